# Optimizing a Trainium2 kernel written in Bass

```python
import math
import jax, jax.numpy as jnp
from jax import lax
import numpy as np

D_MODEL = 1024
BATCH = 32
SEQ = 2048
DEPTH = 1

MEM_LEN = 256
NORM_EPS = 1e-5

SSM_EXPAND = 2
SSM_D_INNER = SSM_EXPAND * D_MODEL
SSM_HEAD_DIM = 64
SSM_HEADS = SSM_D_INNER // SSM_HEAD_DIM
SSM_GROUPS = 8
SSM_STATE = 128
SSM_CONV = 4
SSM_CHUNK = 128
SSM_CONV_DIM = SSM_D_INNER + 2 * SSM_GROUPS * SSM_STATE

DIFF_HEADS = 8
DIFF_HEAD_DIM = 64
DIFF_V_DIM = 2 * DIFF_HEAD_DIM
DIFF_WIDTH = DIFF_HEADS * DIFF_V_DIM
Q_BLOCK = 128

REL_BUCKETS = 32
REL_MAX_DIST = 128

MEM_HEADS = 4
MEM_HEAD_DIM = 256
MEM_WIDTH = MEM_HEADS * MEM_HEAD_DIM

N_BRANCHES = 3

IN_SIZES = (
    SSM_D_INNER,
    SSM_CONV_DIM,
    SSM_HEADS,
    DIFF_WIDTH,
    DIFF_WIDTH,
    DIFF_WIDTH,
    DIFF_WIDTH,
    MEM_WIDTH,
    MEM_WIDTH,
    N_BRANCHES * D_MODEL,
)
IN_DIM = sum(IN_SIZES)
IN_SPLITS = [int(v) for v in np.cumsum(IN_SIZES)[:-1]]

kernel_name = "hybrid_ssd_diffattn_memxattn_gated"


def rms_norm(x, g):
    xf = x.astype(jnp.float32)
    y = xf * lax.rsqrt(jnp.mean(xf * xf, axis=-1, keepdims=True) + NORM_EPS)
    return (y * g.astype(jnp.float32)).astype(x.dtype)


def t5_bucket(rel):
    n = jnp.maximum(rel, 0)
    max_exact = REL_BUCKETS // 2
    nf = jnp.maximum(n, 1).astype(jnp.float32)
    large = max_exact + (jnp.log(nf / max_exact) / math.log(REL_MAX_DIST / max_exact)
                         * (REL_BUCKETS - max_exact)).astype(jnp.int32)
    large = jnp.minimum(large, REL_BUCKETS - 1)
    return jnp.where(n < max_exact, n, large)


def causal_dwconv(u, w, b):
    out = lax.conv_general_dilated(
        u, w[:, None, :], window_strides=(1,), padding=[(SSM_CONV - 1, 0)],
        dimension_numbers=('NWC', 'WIO', 'NWC'), feature_group_count=u.shape[-1])
    return out + b


def ssd_chunked(xs, dt, a, bm, cm):
    bsz, s = xs.shape[0], xs.shape[1]
    nc = s // SSM_CHUNK
    r = SSM_HEADS // SSM_GROUPS
    q = SSM_CHUNK
    xdt = (xs * dt[..., None]).reshape(bsz, nc, q, SSM_GROUPS, r, SSM_HEAD_DIM)
    adt = (dt * a).reshape(bsz, nc, q, SSM_GROUPS, r)
    bmc = bm.reshape(bsz, nc, q, SSM_GROUPS, SSM_STATE)
    cmc = cm.reshape(bsz, nc, q, SSM_GROUPS, SSM_STATE)
    xdt, adt, bmc, cmc = (jnp.moveaxis(t, 1, 0) for t in (xdt, adt, bmc, cmc))
    causal = jnp.tril(jnp.ones((q, q), dtype=bool))

    def step(state, inp):
        xc, ac, bc, cc = inp
        acs = jnp.cumsum(ac, axis=1)
        seg = acs[:, :, None] - acs[:, None, :]
        lmat = jnp.exp(jnp.where(causal[None, :, :, None, None], seg, -jnp.inf))
        cb = jnp.einsum('blgn,bsgn->blsg', cc, bc)
        y_diag = jnp.einsum('blsg,blsgr,bsgrp->blgrp', cb, lmat, xc)
        y_off = jnp.einsum('blgn,bgrpn,blgr->blgrp', cc, state, jnp.exp(acs))
        decay = jnp.exp(acs[:, -1:] - acs)
        new_state = (state * jnp.exp(acs[:, -1])[..., None, None]
                     + jnp.einsum('bsgn,bsgr,bsgrp->bgrpn', bc, decay, xc))
        return new_state.astype(state.dtype), (y_diag + y_off).astype(xc.dtype)

    state0 = jnp.zeros((bsz, SSM_GROUPS, r, SSM_HEAD_DIM, SSM_STATE), xdt.dtype)
    _, ys = lax.scan(step, state0, (xdt, adt, bmc, cmc))
    return jnp.moveaxis(ys, 0, 1).reshape(bsz, s, SSM_HEADS * SSM_HEAD_DIM)


def mamba_branch(z, xbc, dt_raw, conv_w, conv_b, dt_bias, a_log, d_skip, norm_g):
    bsz, s = z.shape[0], z.shape[1]
    xbc = jax.nn.silu(causal_dwconv(xbc, conv_w, conv_b))
    xs, bm, cm = jnp.split(xbc, [SSM_D_INNER, SSM_D_INNER + SSM_GROUPS * SSM_STATE], axis=-1)
    xs = xs.reshape(bsz, s, SSM_HEADS, SSM_HEAD_DIM)
    bm = bm.reshape(bsz, s, SSM_GROUPS, SSM_STATE)
    cm = cm.reshape(bsz, s, SSM_GROUPS, SSM_STATE)
    dt = jax.nn.softplus(dt_raw + dt_bias)
    a = -jnp.exp(a_log)
    y = ssd_chunked(xs, dt, a, bm, cm)
    y = y + (xs * d_skip[:, None]).reshape(bsz, s, SSM_D_INNER)
    y = y * jax.nn.silu(z)
    y = rms_norm(y.reshape(bsz, s, SSM_GROUPS, SSM_D_INNER // SSM_GROUPS),
                 norm_g.reshape(SSM_GROUPS, SSM_D_INNER // SSM_GROUPS))
    return y.reshape(bsz, s, SSM_D_INNER)


def diff_attn_branch(q, k, v, g, lq1, lk1, lq2, lk2, subln_g, rel_bias, lam_init):
    bsz, s = q.shape[0], q.shape[1]
    nb = s // Q_BLOCK
    q = q.reshape(bsz, s, DIFF_HEADS, 2, DIFF_HEAD_DIM).transpose(0, 2, 3, 1, 4)
    k = k.reshape(bsz, s, DIFF_HEADS, 2, DIFF_HEAD_DIM).transpose(0, 2, 3, 1, 4)
    v = v.reshape(bsz, s, DIFF_HEADS, DIFF_V_DIM).transpose(0, 2, 1, 3)
    lam = (jnp.exp(jnp.sum(lq1.astype(jnp.float32) * lk1.astype(jnp.float32)))
           - jnp.exp(jnp.sum(lq2.astype(jnp.float32) * lk2.astype(jnp.float32))) + lam_init)
    qb = jnp.moveaxis(q.reshape(bsz, DIFF_HEADS, 2, nb, Q_BLOCK, DIFF_HEAD_DIM), 3, 0)
    kpos = jnp.arange(s)
    scale = DIFF_HEAD_DIM ** -0.5

    def attend(args):
        qblk, blk = args
        qpos = blk * Q_BLOCK + jnp.arange(Q_BLOCK)
        rel = qpos[:, None] - kpos[None, :]
        bias = jnp.transpose(rel_bias[t5_bucket(rel)], (2, 0, 1)).astype(jnp.float32)
        sc = jnp.einsum('bhcqd,bhckd->bhcqk', qblk, k).astype(jnp.float32) * scale
        sc = jnp.where(rel >= 0, sc + bias[None, :, None], -jnp.inf)
        p = jax.nn.softmax(sc, axis=-1)
        amap = (p[:, :, 0] - lam * p[:, :, 1]).astype(v.dtype)
        return jnp.einsum('bhqk,bhkv->bhqv', amap, v)

    o = lax.map(attend, (qb, jnp.arange(nb)))
    o = o.transpose(1, 0, 3, 2, 4).reshape(bsz, s, DIFF_HEADS, DIFF_V_DIM)
    o = rms_norm(o, subln_g) * (1.0 - lam_init)
    return o.reshape(bsz, s, DIFF_WIDTH) * jax.nn.silu(g)


def mem_branch(q, g, mem, mem_norm_g, w_mem_kv):
    bsz, s = q.shape[0], q.shape[1]
    mem_n = rms_norm(mem, mem_norm_g)
    km, vm = jnp.split(mem_n @ w_mem_kv, 2, axis=-1)
    km = km.reshape(bsz, -1, MEM_HEADS, MEM_HEAD_DIM)
    vm = vm.reshape(bsz, -1, MEM_HEADS, MEM_HEAD_DIM)
    q = q.reshape(bsz, s, MEM_HEADS, MEM_HEAD_DIM)
    sc = jnp.einsum('bshd,bmhd->bhsm', q, km).astype(jnp.float32) * (MEM_HEAD_DIM ** -0.5)
    p = jax.nn.softmax(sc, axis=-1).astype(vm.dtype)
    o = jnp.einsum('bhsm,bmhd->bshd', p, vm).reshape(bsz, s, MEM_WIDTH)
    return o * jax.nn.silu(g)


def setup_inputs(seed: int = 0) -> dict:
    key = jax.random.key(seed)
    ks = jax.random.split(key, 24)
    nrm = jax.random.normal
    f32 = jnp.float32
    x = nrm(ks[0], (BATCH, SEQ, D_MODEL), f32)
    mem = nrm(ks[1], (BATCH, MEM_LEN, D_MODEL), f32)
    norm_gain = 1.0 + 0.02 * nrm(ks[2], (DEPTH, D_MODEL), f32)
    w_in = nrm(ks[3], (DEPTH, D_MODEL, IN_DIM), f32) * D_MODEL ** -0.5
    conv_w = nrm(ks[4], (DEPTH, SSM_CONV, SSM_CONV_DIM), f32) * SSM_CONV ** -0.5
    conv_b = 0.02 * nrm(ks[5], (DEPTH, SSM_CONV_DIM), f32)
    dt0 = jnp.exp(jax.random.uniform(ks[6], (DEPTH, SSM_HEADS), f32)
                  * (math.log(0.1) - math.log(0.001)) + math.log(0.001))
    dt_bias = dt0 + jnp.log(-jnp.expm1(-dt0))
    a_log = jnp.log(jax.random.uniform(ks[7], (DEPTH, SSM_HEADS), f32, minval=1.0, maxval=16.0))
    d_skip = 1.0 + 0.1 * nrm(ks[8], (DEPTH, SSM_HEADS), f32)
    ssm_norm_gain = 1.0 + 0.02 * nrm(ks[9], (DEPTH, SSM_D_INNER), f32)
    lambda_q1 = 0.1 * nrm(ks[10], (DEPTH, DIFF_HEAD_DIM), f32)
    lambda_k1 = 0.1 * nrm(ks[11], (DEPTH, DIFF_HEAD_DIM), f32)
    lambda_q2 = 0.1 * nrm(ks[12], (DEPTH, DIFF_HEAD_DIM), f32)
    lambda_k2 = 0.1 * nrm(ks[13], (DEPTH, DIFF_HEAD_DIM), f32)
    subln_gain = 1.0 + 0.02 * nrm(ks[14], (DEPTH, DIFF_V_DIM), f32)
    mem_norm_gain = 1.0 + 0.02 * nrm(ks[15], (DEPTH, D_MODEL), f32)
    w_mem_kv = nrm(ks[16], (DEPTH, D_MODEL, 2 * MEM_WIDTH), f32) * D_MODEL ** -0.5
    w_br_ssm = nrm(ks[17], (DEPTH, SSM_D_INNER, D_MODEL), f32) * SSM_D_INNER ** -0.5
    w_br_diff = nrm(ks[18], (DEPTH, DIFF_WIDTH, D_MODEL), f32) * DIFF_WIDTH ** -0.5
    w_br_mem = nrm(ks[19], (DEPTH, MEM_WIDTH, D_MODEL), f32) * MEM_WIDTH ** -0.5
    w_out = nrm(ks[20], (DEPTH, D_MODEL, D_MODEL), f32) * D_MODEL ** -0.5
    rel_bias = 0.5 * nrm(ks[21], (REL_BUCKETS, DIFF_HEADS), f32)
    final_norm_gain = 1.0 + 0.02 * nrm(ks[22], (D_MODEL,), f32)
    return {"x": x, "mem": mem, "norm_gain": norm_gain, "w_in": w_in,
            "conv_w": conv_w, "conv_b": conv_b, "dt_bias": dt_bias, "a_log": a_log,
            "d_skip": d_skip, "ssm_norm_gain": ssm_norm_gain,
            "lambda_q1": lambda_q1, "lambda_k1": lambda_k1,
            "lambda_q2": lambda_q2, "lambda_k2": lambda_k2,
            "subln_gain": subln_gain, "mem_norm_gain": mem_norm_gain,
            "w_mem_kv": w_mem_kv, "w_br_ssm": w_br_ssm, "w_br_diff": w_br_diff,
            "w_br_mem": w_br_mem, "w_out": w_out, "rel_bias": rel_bias,
            "final_norm_gain": final_norm_gain}


def reference(x, mem, norm_gain, w_in, conv_w, conv_b, dt_bias, a_log, d_skip,
              ssm_norm_gain, lambda_q1, lambda_k1, lambda_q2, lambda_k2, subln_gain,
              mem_norm_gain, w_mem_kv, w_br_ssm, w_br_diff, w_br_mem, w_out,
              rel_bias, final_norm_gain):
    bsz, s = x.shape[0], x.shape[1]
    for l in range(DEPTH):
        lam_init = 0.8 - 0.6 * math.exp(-0.3 * l)
        h = rms_norm(x, norm_gain[l])
        proj = h @ w_in[l]
        (z, xbc, dt_raw, dq, dk, dv, dg, mq, mg, gate_logits) = jnp.split(proj, IN_SPLITS, axis=-1)
        y_ssm = mamba_branch(z, xbc, dt_raw, conv_w[l], conv_b[l], dt_bias[l],
                             a_log[l], d_skip[l], ssm_norm_gain[l])
        y_diff = diff_attn_branch(dq, dk, dv, dg, lambda_q1[l], lambda_k1[l],
                                  lambda_q2[l], lambda_k2[l], subln_gain[l],
                                  rel_bias, lam_init)
        y_mem = mem_branch(mq, mg, mem, mem_norm_gain[l], w_mem_kv[l])
        gates = jax.nn.sigmoid(gate_logits).reshape(bsz, s, N_BRANCHES, D_MODEL)
        merged = (gates[:, :, 0] * (y_ssm @ w_br_ssm[l])
                  + gates[:, :, 1] * (y_diff @ w_br_diff[l])
                  + gates[:, :, 2] * (y_mem @ w_br_mem[l]))
        x = x + merged @ w_out[l]
    return rms_norm(x, final_norm_gain)
```

```python
import math
import numpy as np
import concourse.bass as bass
import concourse.mybir as mybir
from concourse.bass_utils import run_bass_kernel_spmd

F32 = mybir.dt.float32
BF16 = mybir.dt.bfloat16
ALU = mybir.AluOpType
AF = mybir.ActivationFunctionType

D = 1024
S = 2048
MEM = 256
IN_DIM = 15392
OFF_Z, OFF_XBC, OFF_DT, OFF_DQ, OFF_DK, OFF_DV, OFF_DG, OFF_MQ, OFF_MG, OFF_GATE = (
    0, 2048, 6144, 6176, 7200, 8224, 9248, 10272, 11296, 12320)
EPS = 1e-5
NEG = -30000.0
T = 1024
NT = T // 128
NPART = S // T

_CFG = {"branches": ("ssm", "diff", "mem"), "ncores": 8, "nseq": 4, "same_engine_sync": True}

COMPUTE = ("pe", "act", "dve", "pool")
QUEUES = ("pe", "act", "dve", "pool", "sp")
DMAQ = ("sp", "pool", "act")
NDMASEM = 12


def A(*a, **k):
    return (a, k)


class Buf:
    __slots__ = ("name", "last_w", "readers")

    def __init__(self, name=""):
        self.name = name
        self.last_w = None
        self.readers = []


class Prog:
    def __init__(self, nc, same_engine_sync=True):
        self.nc = nc
        self.q = {e: [] for e in QUEUES}
        self.ndma = {e: 0 for e in DMAQ}
        self.same_engine_sync = same_engine_sync
        self.sem = {e: nc.alloc_semaphore("s_" + e) for e in COMPUTE}
        self.dsem = {e: [nc.alloc_semaphore(f"d_{e}{i}") for i in range(NDMASEM)] for e in DMAQ}

    def _deps(self, reads, writes):
        deps = set()
        for b in reads:
            if b.last_w is not None:
                deps.add(b.last_w)
        for b in writes:
            if b.last_w is not None:
                deps.add(b.last_w)
            deps.update(b.readers)
        return deps

    def _commit(self, me, reads, writes):
        for b in reads:
            b.readers.append(me)
        for b in writes:
            b.last_w = me
            b.readers = []

    def op(self, eng, meth, args, reads=(), writes=()):
        deps = self._deps(reads, writes)
        me = (eng, len(self.q[eng]))
        fn = (meth, args)
        self.q[eng].append([deps, fn, False, None])
        self._commit(me, reads, writes)
        return me

    def dma(self, eng, meth, args, reads=(), writes=()):
        fn = (meth, args)
        deps = self._deps(reads, writes)
        k = self.ndma[eng]
        self.ndma[eng] += 1
        if k >= NDMASEM:
            deps.add(("dma", eng, k - NDMASEM))
        self.q[eng].append([deps, fn, True, k])
        me = ("dma", eng, k)
        self._commit(me, reads, writes)
        return me

    def _all_tails(self):
        deps = set()
        for e in DMAQ:
            n = self.ndma[e]
            for k in range(max(0, n - NDMASEM), n):
                deps.add(("dma", e, k))
        for e in COMPUTE:
            for i in range(len(self.q[e]) - 1, -1, -1):
                if not self.q[e][i][2] and self.q[e][i][1] is not None:
                    deps.add((e, i))
                    break
        return deps

    def barrier(self):
        deps = self._all_tails()
        for e in QUEUES:
            self.q[e].append([set(deps), None, False, None])

    def finish(self):
        self.q["sp"].append([self._all_tails(), None, False, None])

    def emit(self):
        nc = self.nc
        signal = {e: set() for e in COMPUTE}
        plan = {e: [] for e in QUEUES}
        for e in QUEUES:
            seen = {x: -1 for x in COMPUTE}
            seen_slot = {}
            for (deps, fn, is_dma, k) in self.q[e]:
                best = {}
                for d in deps:
                    if d[0] == "dma":
                        slot = (d[1], d[2] % NDMASEM)
                        if seen_slot.get(slot, -1) >= d[2] // NDMASEM:
                            continue
                        if slot not in best or best[slot][2] < d[2]:
                            best[slot] = d
                    else:
                        x, i = d
                        if x == e and (x == "pe" or not self.same_engine_sync):
                            continue
                        if seen[x] >= i:
                            continue
                        if x not in best or best[x][1] < i:
                            best[x] = d
                final = list(best.values())
                for d in final:
                    if d[0] == "dma":
                        seen_slot[(d[1], d[2] % NDMASEM)] = d[2] // NDMASEM
                    else:
                        seen[d[0]] = d[1]
                        signal[d[0]].add(d[1])
                plan[e].append((final, fn, is_dma, k))
        count = {}
        for e in COMPUTE:
            c = 0
            for i in range(len(self.q[e])):
                if i in signal[e]:
                    c += 1
                    count[(e, i)] = c
        self.stats = {e: (len(self.q[e]), len(signal.get(e, ()))) for e in QUEUES}

        def replay(e, eng):
            for i, (final, fn, is_dma, k) in enumerate(plan[e]):
                for d in final:
                    if d[0] == "dma":
                        eng.wait_ge(self.dsem[d[1]][d[2] % NDMASEM], 16 * (d[2] // NDMASEM + 1))
                    else:
                        eng.wait_ge(self.sem[d[0]], count[d])
                if fn is None:
                    continue
                ins = getattr(eng, fn[0])(*fn[1][0], **fn[1][1])
                if is_dma:
                    ins.then_inc(self.dsem[e][k % NDMASEM], 16)
                elif e in signal and i in signal[e]:
                    ins.then_inc(self.sem[e], 1)

        with nc.Block() as block:
            @block.tensor
            def _(eng):
                replay("pe", eng)

            @block.scalar
            def _(eng):
                replay("act", eng)

            @block.vector
            def _(eng):
                replay("dve", eng)

            @block.gpsimd
            def _(eng):
                replay("pool", eng)

            @block.sync
            def _(eng):
                replay("sp", eng)


def t5_bucket_np(rel):
    n = np.maximum(rel, 0)
    max_exact = 16
    nf = np.maximum(n, 1).astype(np.float32)
    large = max_exact + (np.log(nf / max_exact) / np.float32(math.log(128 / max_exact))
                         * (32 - max_exact)).astype(np.int32)
    large = np.minimum(large, 31)
    return np.where(n < max_exact, n, large)


def host_consts():
    k = np.arange(128)[:, None]
    l = np.arange(128)[None, :]
    ident = (k == l).astype(np.float32)
    tril1 = (k <= l).astype(np.float32)
    sup = (k > l).astype(np.float32)
    ones = np.ones((128, 128), np.float32)
    negi = ident * NEG
    slow4 = np.tile(sup, (1, 4))
    cst = np.concatenate([ident, tril1, sup, ones, negi, slow4], axis=1)
    oh = np.zeros((33, 2, 128, 128), np.float32)
    for dlt in range(2):
        rel = 128 * dlt + (l - k)
        b = t5_bucket_np(rel)
        for kk in range(128):
            for qq in range(128):
                if rel[kk, qq] >= 0:
                    oh[b[kk, qq], dlt, kk, qq] = 1.0
                else:
                    oh[32, dlt, kk, qq] = 1.0
    return cst, oh.reshape(33, 2 * 128 * 128)


C_ID, C_TRIL, C_SUP, C_ONES, C_NEGI, C_SLOW4 = 0, 128, 256, 384, 512, 640


def build(nseq, branches, same_engine_sync=True):
    nc = bass.Bass("TRN2", target_bir_lowering=False)
    P = Prog(nc, same_engine_sync)

    def din(name, shape, dt=F32):
        return nc.dram_tensor(name, list(shape), dt, kind="ExternalInput")

    x_d = din("x", [nseq, S, D])
    mem_d = din("mem", [nseq, MEM, D])
    w_in = din("w_in", [D, IN_DIM])
    w_kv = din("w_mem_kv", [D, 2048])
    w_brs = din("w_br_ssm", [2048, D])
    w_brd = din("w_br_diff", [D, D])
    w_brm = din("w_br_mem", [D, D])
    w_out = din("w_out", [D, D])
    gain_d = din("norm_gain", [1, D])
    mgain_d = din("mem_norm_gain", [1, D])
    fgain_d = din("final_norm_gain", [1, D])
    sgain_d = din("ssm_norm_gain", [1, 2048])
    subg_d = din("subln_gain", [1, 128])
    dtb_d = din("dt_bias", [1, 32])
    alog_d = din("a_log", [1, 32])
    dsk_d = din("d_skip", [1, 32])
    lq1_d, lk1_d, lq2_d, lk2_d = (din(n, [1, 64]) for n in ("lambda_q1", "lambda_k1", "lambda_q2", "lambda_k2"))
    rb_d = din("rel_bias", [32, 8])
    cw_d = din("conv_w_l", [128, 32, 4])
    cb_d = din("conv_b_l", [128, 32])
    cst_d = din("cst", [128, 1152])
    oh_d = din("onehot", [33, 32768])
    out_d = nc.dram_tensor("out", [nseq, S, D], F32, kind="ExternalOutput")
    bias_scr = nc.dram_tensor("bias_scr", [8, 32768], F32, kind="Internal")

    def sb(name, shape, dt=F32):
        return nc.alloc_sbuf_tensor("sb_" + name, list(shape), dt)

    cst = sb("cst", [128, 1152]); B_cst = Buf("cst")
    identb = sb("identb", [128, 128], BF16)
    trilb = sb("trilb", [128, 128], BF16)
    biasT = sb("biasT", [128, 8, 2, 128]); B_biasT = Buf("biasT")
    sm = sb("small", [128, 512]); B_sm = Buf("small")
    SM_CFAR, SM_DTB, SM_A, SM_DSK, SM_LAM, SM_NLAM, SM_ONE, SM_EPS = 0, 8, 40, 72, 104, 105, 106, 107
    SM_SUBG = 128
    SM_TMP = 256
    cw = sb("cw", [128, 32, 4]); cb = sb("cb", [128, 32]); B_cw = Buf("cw")
    hT = sb("hT", [128, 8, S], BF16)
    B_hT = [Buf(f"hT{i}") for i in range(S // 128)]
    kmT = sb("kmT", [128, 8, MEM], BF16); B_kmT = Buf("kmT")
    vma = sb("vma", [128, 2, 4, 258], BF16); B_vma = Buf("vma")
    YT = sb("YT", [128, 16, T], BF16); B_YT = Buf("YT")
    mrg = sb("mrg", [128, 8, T], BF16); B_mrg = Buf("mrg")
    state_f = sb("state_f", [128, 8, 256]); B_state = [Buf(f"st{g}") for g in range(8)]
    halo = sb("halo", [128, 32, 4]); B_halo = Buf("halo")
    NWS, NWB = 2, 4
    wst = [sb(f"wst{i}", [128, 2048]) for i in range(NWS)]; B_wst = [Buf(f"wst{i}") for i in range(NWS)]
    wbf = [sb(f"wbf{i}", [128, 2048], BF16) for i in range(NWB)]; B_wbf = [Buf(f"wbf{i}") for i in range(NWB)]
    ARENA_W = 14336
    arena = sb("arena", [128, ARENA_W])
    arena_b = arena[:].bitcast(BF16)

    class Carver:
        def __init__(self):
            self.off = 0

        def f32(self, n):
            a = arena[:, self.off:self.off + n]
            self.off += n
            assert self.off <= ARENA_W, self.off
            return a

        def bf16(self, n):
            w = (n + 1) // 2
            a = arena_b[:, 2 * self.off:2 * self.off + n]
            self.off += w
            assert self.off <= ARENA_W, self.off
            return a

    psum = [nc.alloc_psum_tensor(f"ps{i}", [128, 512], F32) for i in range(8)]
    B_ps = [Buf(f"ps{i}") for i in range(8)]

    def psb(i):
        return psum[i][:].bitcast(BF16)

    wctr = [0, 0]
    cast_rr = [0]

    def load_w(src_ap, nkb, ncols):
        assert nkb * ncols <= 2048
        si = wctr[0] % NWS; wctr[0] += 1
        bi = wctr[1] % NWB; wctr[1] += 1
        st = wst[si][:, 0:nkb * ncols].rearrange("p (k c) -> p k c", k=nkb)
        bf = wbf[bi][:, 0:nkb * ncols].rearrange("p (k c) -> p k c", k=nkb)
        src = src_ap.rearrange("(k p) c -> p k c", p=128)
        P.dma("sp", "dma_start", A(out=st, in_=src), writes=[B_wst[si]])
        P.op("pool", "tensor_copy", A(out=bf, in_=st), reads=[B_wst[si]], writes=[B_wbf[bi]])
        return bf, B_wbf[bi]

    def win(c0, ncols):
        return w_in.ap()[:, c0:c0 + ncols]

    def hbufs(t0, n):
        return B_hT[t0 // 128:(t0 + n + 127) // 128]

    def proj_fm(ps_i, ps_cols, wt, wB, c_lo, ncol, tok0, ntok):
        out = psum[ps_i][0:ncol, ps_cols:ps_cols + ntok]
        for kb in range(8):
            P.op("pe", "matmul", A(out, lhsT=wt[:, kb, c_lo:c_lo + ncol], rhs=hT[:, kb, tok0:tok0 + ntok],
                                                 start=(kb == 0), stop=(kb == 7)),
                 reads=[wB] + hbufs(tok0, ntok), writes=[B_ps[ps_i]])

    def proj_tm(ps_i, ps_cols, wt, wB, c_lo, ncol, tok0):
        out = psum[ps_i][:, ps_cols:ps_cols + ncol]
        for kb in range(8):
            P.op("pe", "matmul", A(out, lhsT=hT[:, kb, tok0:tok0 + 128], rhs=wt[:, kb, c_lo:c_lo + ncol],
                                                 start=(kb == 0), stop=(kb == 7)),
                 reads=[wB] + hbufs(tok0, 128), writes=[B_ps[ps_i]])

    evac_rr = [0]

    def evac(out, in_, reads, writes):
        evac_rr[0] += 1
        if evac_rr[0] % 2:
            P.op("act", "copy", A(out=out, in_=in_), reads=reads, writes=writes)
        else:
            P.op("dve", "tensor_copy", A(out=out, in_=in_), reads=reads, writes=writes)

    def bcast_load(dst, src_row_ap, writes):
        P.dma("sp", "dma_start", A(out=dst, in_=src_row_ap.partition_broadcast(128)), writes=writes)

    def rstd_from_ss(ss_ap, n, inv_count, Bs):
        P.op("dve", "tensor_scalar", A(out=ss_ap, in0=ss_ap, scalar1=inv_count, scalar2=EPS, op0=ALU.mult, op1=ALU.add),
             reads=Bs, writes=Bs)
        P.op("act", "activation", A(out=ss_ap, in_=ss_ap, func=AF.Sqrt), reads=Bs, writes=Bs)
        P.op("dve", "reciprocal", A(out=ss_ap, in_=ss_ap), reads=Bs, writes=Bs)

    P.dma("sp", "dma_start", A(out=cst[:], in_=cst_d.ap()), writes=[B_cst])
    P.op("dve", "tensor_copy", A(out=identb[:], in_=cst[:, C_ID:C_ID + 128]), reads=[B_cst], writes=[B_cst])
    P.op("dve", "tensor_copy", A(out=trilb[:], in_=cst[:, C_TRIL:C_TRIL + 128]), reads=[B_cst], writes=[B_cst])
    P.dma("sp", "dma_start", A(out=cw[:], in_=cw_d.ap()), writes=[B_cw])
    P.dma("sp", "dma_start", A(out=cb[:], in_=cb_d.ap()), writes=[B_cw])
    P.op("pool", "memset", A(sm[:], 0.0), writes=[B_sm])
    P.op("pool", "memset", A(sm[:, SM_ONE:SM_ONE + 1], 1.0), writes=[B_sm])
    P.op("pool", "memset", A(sm[:, SM_EPS:SM_EPS + 1], EPS), writes=[B_sm])
    bcast_load(sm[:, SM_CFAR:SM_CFAR + 8], rb_d.ap()[31:32, :], [B_sm])
    bcast_load(sm[:, SM_DTB:SM_DTB + 32], dtb_d.ap(), [B_sm])
    bcast_load(sm[:, SM_A:SM_A + 32], alog_d.ap(), [B_sm])
    bcast_load(sm[:, SM_DSK:SM_DSK + 32], dsk_d.ap(), [B_sm])
    bcast_load(sm[:, SM_SUBG:SM_SUBG + 128], subg_d.ap(), [B_sm])
    for i, ld in enumerate((lq1_d, lk1_d, lq2_d, lk2_d)):
        bcast_load(sm[:, SM_TMP + 64 * i:SM_TMP + 64 * i + 64], ld.ap(), [B_sm])
    Bs = [B_sm]
    P.op("act", "activation", A(out=sm[:, SM_A:SM_A + 32], in_=sm[:, SM_A:SM_A + 32], func=AF.Exp), reads=Bs, writes=Bs)
    P.op("dve", "tensor_scalar_mul", A(out=sm[:, SM_A:SM_A + 32], in0=sm[:, SM_A:SM_A + 32], scalar1=-1.0), reads=Bs, writes=Bs)
    LAM_INIT = 0.8 - 0.6 * math.exp(-0.3 * 0)
    P.op("dve", "tensor_scalar_mul", A(out=sm[:, SM_SUBG:SM_SUBG + 128], in0=sm[:, SM_SUBG:SM_SUBG + 128], scalar1=1.0 - LAM_INIT), reads=Bs, writes=Bs)
    for i in range(2):
        a0 = SM_TMP + 128 * i
        P.op("dve", "tensor_tensor", A(out=sm[:, a0:a0 + 64], in0=sm[:, a0:a0 + 64], in1=sm[:, a0 + 64:a0 + 128], op=ALU.mult), reads=Bs, writes=Bs)
        P.op("dve", "reduce_sum", A(out=sm[:, 110 + i:111 + i], in_=sm[:, a0:a0 + 64], axis=mybir.AxisListType.X), reads=Bs, writes=Bs)
    P.op("act", "activation", A(out=sm[:, 110:112], in_=sm[:, 110:112], func=AF.Exp), reads=Bs, writes=Bs)
    P.op("dve", "tensor_tensor", A(out=sm[:, SM_LAM:SM_LAM + 1], in0=sm[:, 110:111], in1=sm[:, 111:112], op=ALU.subtract), reads=Bs, writes=Bs)
    P.op("dve", "tensor_scalar_add", A(out=sm[:, SM_LAM:SM_LAM + 1], in0=sm[:, SM_LAM:SM_LAM + 1], scalar1=LAM_INIT), reads=Bs, writes=Bs)
    P.op("dve", "tensor_scalar_mul", A(out=sm[:, SM_NLAM:SM_NLAM + 1], in0=sm[:, SM_LAM:SM_LAM + 1], scalar1=-1.0), reads=Bs, writes=Bs)

    if "diff" in branches:
        cv = Carver()
        rbx = cv.f32(8)
        ohs = cv.f32(4096)
        stg = cv.f32(4096)
        B_rbx, B_ohs, B_stg, B_scr = Buf(), Buf(), Buf(), Buf()
        P.op("pool", "memset", A(rbx[32:33, :], NEG), writes=[B_rbx])
        P.dma("sp", "dma_start", A(out=rbx[0:32, :], in_=rb_d.ap()), writes=[B_rbx])
        for pc in range(8):
            P.dma("sp", "dma_start", A(out=ohs[0:33, :], in_=oh_d.ap()[:, 4096 * pc:4096 * pc + 4096]), writes=[B_ohs])
            for i in range(8):
                pi = i % 2
                P.op("pe", "matmul", A(psum[pi][0:8, :], lhsT=rbx[0:33, :], rhs=ohs[0:33, 512 * i:512 * i + 512], start=True, stop=True),
                     reads=[B_rbx, B_ohs], writes=[B_ps[pi]])
                evac(stg[0:8, 512 * i:512 * i + 512], psum[pi][0:8, :], [B_ps[pi]], [B_stg])
            P.dma("sp", "dma_start", A(out=bias_scr.ap()[:, 4096 * pc:4096 * pc + 4096], in_=stg[0:8, :]), reads=[B_stg], writes=[B_scr])
        for h in range(8):
            P.dma("sp", "dma_start", A(out=biasT[:, h, :, :], in_=bias_scr.ap()[h, :].rearrange("(d k q) -> k d q", d=2, k=128)),
                  reads=[B_scr], writes=[B_biasT])
        P.barrier()

    def prologue(sq):
        cv = Carver()
        gain_b = cv.f32(1024); B_gain = Buf()
        xt = [cv.f32(1024), cv.f32(1024)]; B_xt = [Buf(), Buf()]
        junk = cv.f32(1024); B_junk = Buf()
        hb = cv.bf16(1024); B_hb = Buf()
        ss = cv.f32(2); B_ss = Buf()
        memT = cv.bf16(8 * MEM); B_memT = Buf()
        memT3 = memT.rearrange("p (k m) -> p k m", k=8)

        def norm_transpose(src_ap, i, dst3, dstB, col0):
            xi = i % 2
            P.dma("sp", "dma_start", A(out=xt[xi], in_=src_ap), writes=[B_xt[xi]])
            P.op("act", "activation", A(out=junk, in_=xt[xi], func=AF.Square, accum_out=ss[:, 0:1]), reads=[B_xt[xi]], writes=[B_junk, B_ss])
            rstd_from_ss(ss[:, 0:1], 1, 1.0 / D, [B_ss])
            P.op("dve", "scalar_tensor_tensor", A(out=hb, in0=xt[xi], scalar=ss[:, 0:1], in1=gain_b, op0=ALU.mult, op1=ALU.mult),
                 reads=[B_xt[xi], B_ss, B_gain], writes=[B_hb])
            for kb in range(8):
                P.op("pe", "transpose", A(out=psb(6)[:, 128 * kb:128 * kb + 128], in_=hb[:, 128 * kb:128 * kb + 128], identity=identb[:]),
                     reads=[B_hb, B_cst], writes=[B_ps[6]])
            evac(dst3[:, :, col0:col0 + 128], psb(6)[:, 0:1024].rearrange("p (k t) -> p k t", k=8), [B_ps[6]], dstB)

        bcast_load(gain_b, gain_d.ap(), [B_gain])
        for i in range(S // 128):
            norm_transpose(x_d.ap()[sq, 128 * i:128 * i + 128, :], i, hT, [B_hT[i]], 128 * i)
        if "mem" in branches:
            bcast_load(gain_b, mgain_d.ap(), [B_gain])
            for i in range(2):
                norm_transpose(mem_d.ap()[sq, 128 * i:128 * i + 128, :], i, memT3, [B_memT], 128 * i)
            for fb in range(8):
                wt, wB = load_w(w_kv.ap()[:, 128 * fb:128 * fb + 128], 8, 128)
                pi = fb % 2
                for kb in range(8):
                    P.op("pe", "matmul", A(psum[pi][:, 0:MEM], lhsT=wt[:, kb, :], rhs=memT3[:, kb, :], start=(kb == 0), stop=(kb == 7)),
                         reads=[wB, B_memT], writes=[B_ps[pi]])
                evac(kmT[:, fb, :], psum[pi][:, 0:MEM], [B_ps[pi]], [B_kmT])
            P.op("pool", "memset", A(vma[:, :, :, 256:258], 1.0), writes=[B_vma])
            for hh in range(4):
                wt, wB = load_w(w_kv.ap()[:, 1024 + 256 * hh:1024 + 256 * hh + 256], 8, 256)
                for mt in range(2):
                    pi = mt
                    for kb in range(8):
                        P.op("pe", "matmul", A(psum[pi][:, 0:256], lhsT=memT3[:, kb, 128 * mt:128 * mt + 128], rhs=wt[:, kb, :], start=(kb == 0), stop=(kb == 7)),
                             reads=[wB, B_memT], writes=[B_ps[pi]])
                    evac(vma[:, mt, hh, 0:256], psum[pi][:, 0:256], [B_ps[pi]], [B_vma])
        P.barrier()

    def transpose_to_YT(src_bf, srcB, nblk, yt_blk0, tokcol0, ps_i):
        for j in range(nblk):
            P.op("pe", "transpose", A(out=psb(ps_i)[:, 128 * j:128 * j + 128], in_=src_bf[:, 128 * j:128 * j + 128], identity=identb[:]),
                 reads=[srcB, B_cst], writes=[B_ps[ps_i]])
        evac(YT[:, yt_blk0:yt_blk0 + nblk, tokcol0:tokcol0 + 128], psb(ps_i)[:, 0:128 * nblk].rearrange("p (j t) -> p j t", j=nblk), [B_ps[ps_i]], [B_YT])

    def mem_phase(sq, part):
        t0 = part * T
        cv = Carver()
        QmT = cv.bf16(2 * T).rearrange("p (d t) -> p d t", d=2); B_Qm = Buf()
        Gm = cv.f32(NT * 256).rearrange("p (c f) -> p c f", c=NT); B_Gm = Buf()
        PmT = [cv.bf16(2 * 512).rearrange("p (m t) -> p m t", m=2) for _ in range(2)]; B_Pm = [Buf(), Buf()]
        rr = cv.f32(4); B_rr = Buf()
        ym = [cv.bf16(256), cv.bf16(256)]; B_ym = [Buf(), Buf()]
        it = 0
        for hh in range(4):
            wq, wqB = load_w(win(OFF_MQ + 256 * hh, 256), 8, 256)
            wg, wgB = load_w(win(OFF_MG + 256 * hh, 256), 8, 256)
            for db in range(2):
                for tc in range(T // 512):
                    pi = (2 * db + tc) % 2
                    proj_fm(pi, 0, wq, wqB, 128 * db, 128, t0 + 512 * tc, 512)
                    evac(QmT[:, db, 512 * tc:512 * tc + 512], psum[pi][:, :], [B_ps[pi]], [B_Qm])
            for c in range(NT):
                pi = 2 + c % 2
                proj_tm(pi, 0, wg, wgB, 0, 256, t0 + 128 * c)
                P.op("act", "activation", A(out=Gm[:, c, :], in_=psum[pi][:, 0:256], func=AF.Silu), reads=[B_ps[pi]], writes=[B_Gm])
            for tc in range(T // 512):
                pb = tc % 2
                for mt in range(2):
                    pi = mt
                    for db in range(2):
                        P.op("pe", "matmul", A(psum[pi][:, :], lhsT=kmT[:, 2 * hh + db, 128 * mt:128 * mt + 128], rhs=QmT[:, db, 512 * tc:512 * tc + 512], start=(db == 0), stop=(db == 1)),
                             reads=[B_kmT, B_Qm], writes=[B_ps[pi]])
                    P.op("act", "activation", A(out=PmT[pb][:, mt, :], in_=psum[pi][:, :], func=AF.Exp, scale=1.0 / 16.0), reads=[B_ps[pi]], writes=[B_Pm[pb]])
                for j in range(4):
                    pi = 2 + j % 2
                    for mt in range(2):
                        P.op("pe", "matmul", A(psum[pi][:, 0:257], lhsT=PmT[pb][:, mt, 128 * j:128 * j + 128], rhs=vma[:, mt, hh, 0:257], start=(mt == 0), stop=(mt == 1)),
                             reads=[B_Pm[pb], B_vma], writes=[B_ps[pi]])
                    yi = it % 2; it += 1
                    P.op("dve", "reciprocal", A(out=rr[:, 0:1], in_=psum[pi][:, 256:257]), reads=[B_ps[pi]], writes=[B_rr])
                    P.op("dve", "scalar_tensor_tensor", A(out=ym[yi], in0=psum[pi][:, 0:256], scalar=rr[:, 0:1], in1=Gm[:, 4 * tc + j, :], op0=ALU.mult, op1=ALU.mult),
                         reads=[B_ps[pi], B_rr, B_Gm], writes=[B_ym[yi]])
                    transpose_to_YT(ym[yi], B_ym[yi], 2, 2 * hh, 512 * tc + 128 * j, 6)
        P.barrier()

    def diff_phase(sq, part):
        t0 = part * T
        nk = t0 + T
        cv = Carver()
        KT = cv.bf16(S); B_KT = Buf()
        QT = cv.bf16(T); B_QT = Buf()
        Va = cv.bf16(16 * 130).rearrange("p (t v) -> p t v", t=16); B_Va = Buf()
        Gs = cv.f32(NT * 128).rearrange("p (c f) -> p c f", c=NT); B_Gs = Buf()
        tmpS = [cv.f32(256), cv.f32(256)]; B_tmpS = [Buf(), Buf()]
        PT = [cv.bf16(512), cv.bf16(512)]; B_PT = [Buf(), Buf()]
        Os = [cv.f32(4 * 129).rearrange("p (j v) -> p j v", j=4) for _ in range(2)]; B_Os = [Buf(), Buf()]
        obuf = cv.f32(NT * 128).rearrange("p (c f) -> p c f", c=NT); B_ob = Buf()
        t1 = cv.f32(512).rearrange("p (j v) -> p j v", j=4); t2 = cv.f32(512).rearrange("p (j v) -> p j v", j=4); B_t = Buf()
        rr = cv.f32(8); B_rr = Buf()
        ss = cv.f32(NT); B_ss = Buf()
        junk = cv.f32(128); B_junk = Buf()
        ydt = cv.bf16(NT * 128).rearrange("p (c f) -> p c f", c=NT); B_ydt = Buf()
        P.op("pool", "memset", A(Va[:, :, 128:130], 1.0), writes=[B_Va])
        sidx = 0
        for h in range(8):
            wq, wqB = load_w(win(OFF_DQ + 128 * h, 128), 8, 128)
            wk, wkB = load_w(win(OFF_DK + 128 * h, 128), 8, 128)
            wv, wvB = load_w(win(OFF_DV + 128 * h, 128), 8, 128)
            wg, wgB = load_w(win(OFF_DG + 128 * h, 128), 8, 128)
            for kc in range(nk // 512):
                pi = kc % 2
                proj_fm(pi, 0, wk, wkB, 0, 128, 512 * kc, 512)
                evac(KT[:, 512 * kc:512 * kc + 512], psum[pi][:, :], [B_ps[pi]], [B_KT])
            for tc in range(T // 512):
                pi = tc % 2
                proj_fm(pi, 0, wq, wqB, 0, 128, t0 + 512 * tc, 512)
                evac(QT[:, 512 * tc:512 * tc + 512], psum[pi][:, :], [B_ps[pi]], [B_QT])
            for kt in range(nk // 128):
                pi = kt % 2
                proj_tm(pi, 0, wv, wvB, 0, 128, 128 * kt)
                evac(Va[:, kt, 0:128], psum[pi][:, 0:128], [B_ps[pi]], [B_Va])
            for c in range(NT):
                pi = c % 2
                proj_tm(pi, 0, wg, wgB, 0, 128, t0 + 128 * c)
                P.op("act", "activation", A(out=Gs[:, c, :], in_=psum[pi][:, 0:128], func=AF.Silu), reads=[B_ps[pi]], writes=[B_Gs])
            for qc in range(T // 512):
                qb0 = (t0 + 512 * qc) // 128
                for c in range(2):
                    r0 = 64 * c
                    for kb_ in range(qb0 + 4):
                        j0 = kb_ - qb0
                        jlo = max(j0, 0)
                        si = sidx % 2; sidx += 1
                        Sps = psum[si]
                        P.op("pe", "matmul", A(
                            Sps[:, 128 * jlo:512], lhsT=KT[r0:r0 + 64, 128 * kb_:128 * kb_ + 128],
                            rhs=QT[r0:r0 + 64, 512 * qc + 128 * jlo:512 * qc + 512], start=True, stop=True),
                             reads=[B_KT, B_QT], writes=[B_ps[si]])
                        nsp = 0
                        for j in range(jlo, 4):
                            dl = j - j0
                            if dl <= 1:
                                P.op("dve", "scalar_tensor_tensor", A(
                                    out=tmpS[si][:, 128 * nsp:128 * nsp + 128], in0=Sps[:, 128 * j:128 * j + 128], scalar=0.125,
                                    in1=biasT[:, h, dl, :], op0=ALU.mult, op1=ALU.add),
                                     reads=[B_ps[si], B_biasT], writes=[B_tmpS[si]])
                                nsp += 1
                        if nsp:
                            P.op("act", "activation", A(out=PT[si][:, 128 * jlo:128 * (jlo + nsp)], in_=tmpS[si][:, 0:128 * nsp], func=AF.Exp),
                                 reads=[B_tmpS[si]], writes=[B_PT[si]])
                        jf = max(j0 + 2, 0)
                        if jf < 4:
                            P.op("act", "activation", A(out=PT[si][:, 128 * jf:512], in_=Sps[:, 128 * jf:512], func=AF.Exp, scale=0.125, bias=sm[:, SM_CFAR + h:SM_CFAR + h + 1]),
                                 reads=[B_ps[si], B_sm], writes=[B_PT[si]])
                        for j in range(jlo, 4):
                            P.op("pe", "matmul", A(psum[2 + j][:, 0:129], lhsT=PT[si][:, 128 * j:128 * j + 128], rhs=Va[:, kb_, 0:129],
                                                                             start=(kb_ == 0), stop=(kb_ == qb0 + j)),
                                 reads=[B_PT[si], B_Va], writes=[B_ps[2 + j]])
                    for j in range(4):
                        evac(Os[c][:, j, :], psum[2 + j][:, 0:129], [B_ps[2 + j]], [B_Os[c]])
                P.op("dve", "reciprocal", A(out=rr[:, 0:4], in_=Os[0][:, :, 128]), reads=[B_Os[0]], writes=[B_rr])
                P.op("dve", "reciprocal", A(out=rr[:, 4:8], in_=Os[1][:, :, 128]), reads=[B_Os[1], B_rr], writes=[B_rr])
                P.op("dve", "tensor_scalar", A(out=rr[:, 4:8], in0=rr[:, 4:8], scalar1=sm[:, SM_NLAM:SM_NLAM + 1], scalar2=None, op0=ALU.mult), reads=[B_rr, B_sm], writes=[B_rr])
                P.op("dve", "tensor_tensor", A(out=t1, in0=Os[0][:, :, 0:128], in1=rr[:, 0:4].unsqueeze(2).broadcast_to([128, 4, 128]), op=ALU.mult), reads=[B_Os[0], B_rr, B_t], writes=[B_t])
                P.op("dve", "tensor_tensor", A(out=t2, in0=Os[1][:, :, 0:128], in1=rr[:, 4:8].unsqueeze(2).broadcast_to([128, 4, 128]), op=ALU.mult), reads=[B_Os[1], B_rr, B_t], writes=[B_t])
                P.op("dve", "tensor_tensor", A(out=obuf[:, 4 * qc:4 * qc + 4, :], in0=t1, in1=t2, op=ALU.add), reads=[B_t], writes=[B_ob])
            for c in range(NT):
                P.op("act", "activation", A(out=junk, in_=obuf[:, c, :], func=AF.Square, accum_out=ss[:, c:c + 1]), reads=[B_ob], writes=[B_junk, B_ss])
            rstd_from_ss(ss[:, 0:NT], NT, 1.0 / 128, [B_ss])
            P.op("dve", "tensor_tensor", A(out=obuf, in0=obuf, in1=ss[:, 0:NT].unsqueeze(2).broadcast_to([128, NT, 128]), op=ALU.mult), reads=[B_ob, B_ss], writes=[B_ob])
            P.op("dve", "tensor_tensor", A(out=obuf, in0=obuf, in1=sm[:, SM_SUBG:SM_SUBG + 128].unsqueeze(1).broadcast_to([128, NT, 128]), op=ALU.mult), reads=[B_ob, B_sm], writes=[B_ob])
            P.op("dve", "tensor_tensor", A(out=ydt, in0=obuf, in1=Gs, op=ALU.mult), reads=[B_ob, B_Gs], writes=[B_ydt])
            for c in range(NT):
                transpose_to_YT(ydt[:, c, :], B_ydt, 1, h, 128 * c, 6 + c % 2)
        P.barrier()

    def ssm_phase(sq, part):
        t0 = part * T
        cv = Carver()
        U = cv.f32(T + 4); B_U = Buf()
        acc = cv.f32(T); B_acc = Buf()
        xsT = cv.bf16(T); B_xsT = Buf()
        xs_tok = cv.bf16(NT * 256).rearrange("p (c f) -> p c f", c=NT); B_xs = Buf()
        BT = cv.bf16(T); B_BT = Buf()
        Btok = cv.bf16(NT * 128).rearrange("p (c f) -> p c f", c=NT); B_Btok = Buf()
        CT = cv.bf16(T); B_CT = Buf()
        zs = cv.f32(NT * 256).rearrange("p (c f) -> p c f", c=NT); B_zs = Buf()
        dt = cv.f32(NT * 32).rearrange("p (c f) -> p c f", c=NT)
        adt = cv.f32(NT * 32).rearrange("p (c f) -> p c f", c=NT); B_dt = Buf()
        E = cv.f32(NT * 96).rearrange("p (c f) -> p c f", c=NT); B_E = Buf()
        R0 = cv.f32(512); R = [R0, R0]; B_R0 = Buf(); B_R = [B_R0, B_R0]
        LT = [cv.bf16(512), cv.bf16(512)]; B_LT = [Buf(), Buf()]
        MT = [cv.bf16(512), cv.bf16(512)]; B_MT = [Buf(), Buf()]
        xdt = [cv.bf16(256), cv.bf16(256)]; xdd = [cv.bf16(256), cv.bf16(256)]; B_xd = [Buf(), Buf()]
        y1 = cv.f32(256); B_y1 = Buf()
        y2 = cv.f32(NT * 256).rearrange("p (c f) -> p c f", c=NT); B_y2 = Buf()
        yn = acc.bitcast(BF16)[:, 0:NT * 256].rearrange("p (c f) -> p c f", c=NT); B_yn = B_acc
        st_b = cv.bf16(256); B_stb = Buf()
        stmp = cv.f32(256); B_stmp = Buf()
        dskI = cv.bf16(4 * 128).rearrange("p (h l) -> p h l", h=4); B_dskI = Buf()
        sg_b = cv.f32(256); B_sg = Buf()
        ss = cv.f32(NT); B_ss = Buf()
        junk = cv.f32(256); B_junk = Buf()

        wd, wdB = load_w(win(OFF_DT, 32), 8, 32)
        for c in range(NT):
            pi = c % 2
            proj_tm(pi, 0, wd, wdB, 0, 32, t0 + 128 * c)
            P.op("dve", "tensor_tensor", A(out=dt[:, c, :], in0=psum[pi][:, 0:32], in1=sm[:, SM_DTB:SM_DTB + 32], op=ALU.add), reads=[B_ps[pi], B_sm], writes=[B_dt])
        dtf = dt.rearrange("p c f -> p (c f)")
        P.op("act", "activation", A(out=dtf, in_=dtf, func=AF.Exp), reads=[B_dt], writes=[B_dt])
        P.op("act", "activation", A(out=dtf, in_=dtf, func=AF.Ln, bias=sm[:, SM_ONE:SM_ONE + 1]), reads=[B_dt, B_sm], writes=[B_dt])
        P.op("dve", "tensor_tensor", A(out=adt, in0=dt, in1=sm[:, SM_A:SM_A + 32].unsqueeze(1).broadcast_to([128, NT, 32]), op=ALU.mult), reads=[B_dt, B_sm], writes=[B_dt])
        for c in range(NT):
            pi = c % 2
            for i, cc in enumerate((C_TRIL, C_SUP, C_ONES)):
                P.op("pe", "matmul", A(psum[pi][:, 32 * i:32 * i + 32], lhsT=cst[:, cc:cc + 128], rhs=adt[:, c, :], start=True, stop=True),
                     reads=[B_cst, B_dt], writes=[B_ps[pi]])
            P.op("act", "activation", A(out=E[:, c, :], in_=psum[pi][:, 0:96], func=AF.Exp), reads=[B_ps[pi]], writes=[B_E])

        for g in range(8):
            if part == 0:
                P.op("pool", "memset", A(state_f[:, g, :], 0.0), writes=[B_state[g]])
            bcast_load(sg_b, sgain_d.ap()[:, 256 * g:256 * g + 256], [B_sg])
            for hh in range(4):
                P.op("dve", "tensor_scalar", A(out=dskI[:, hh, :], in0=cst[:, C_ID:C_ID + 128], scalar1=sm[:, SM_DSK + 4 * g + hh:SM_DSK + 4 * g + hh + 1], scalar2=None, op0=ALU.mult),
                     reads=[B_cst, B_sm], writes=[B_dskI])
            blocks = [("xs", 2 * g, OFF_XBC + 256 * g), ("xs", 2 * g + 1, OFF_XBC + 256 * g + 128),
                      ("B", 16 + g, OFF_XBC + 2048 + 128 * g), ("C", 24 + g, OFF_XBC + 3072 + 128 * g)]
            for bi_, (kind, blk, col) in enumerate(blocks):
                wt, wB = load_w(win(col, 128), 8, 128)
                if part == 0:
                    P.op("pool", "memset", A(U[:, 0:4], 0.0), writes=[B_U])
                else:
                    P.op("pool", "tensor_copy", A(out=U[:, 0:4], in_=halo[:, blk, :]), reads=[B_halo], writes=[B_U])
                for tc in range(T // 512):
                    pi = tc % 2
                    proj_fm(pi, 0, wt, wB, 0, 128, t0 + 512 * tc, 512)
                    evac(U[:, 4 + 512 * tc:4 + 512 * tc + 512], psum[pi][:, :], [B_ps[pi]], [B_U])
                P.op("pool", "tensor_copy", A(out=halo[:, blk, :], in_=U[:, T:T + 4]), reads=[B_U], writes=[B_halo])
                P.op("dve", "tensor_scalar", A(out=acc, in0=U[:, 4:4 + T], scalar1=cw[:, blk, 3:4], scalar2=cb[:, blk:blk + 1], op0=ALU.mult, op1=ALU.add),
                     reads=[B_U, B_cw], writes=[B_acc])
                for k in (2, 1, 0):
                    P.op("dve", "scalar_tensor_tensor", A(out=acc, in0=U[:, 1 + k:1 + k + T], scalar=cw[:, blk, k:k + 1], in1=acc, op0=ALU.mult, op1=ALU.add),
                         reads=[B_U, B_cw, B_acc], writes=[B_acc])
                if kind == "C":
                    P.op("act", "activation", A(out=CT, in_=acc, func=AF.Silu), reads=[B_acc], writes=[B_CT])
                elif kind == "B":
                    P.op("act", "activation", A(out=BT, in_=acc, func=AF.Silu), reads=[B_acc], writes=[B_BT])
                    for c in range(NT):
                        P.op("pe", "transpose", A(out=psb(6)[:, 128 * c:128 * c + 128], in_=BT[:, 128 * c:128 * c + 128], identity=identb[:]),
                             reads=[B_BT, B_cst], writes=[B_ps[6]])
                    evac(Btok, psb(6)[:, 0:128 * NT].rearrange("p (c f) -> p c f", c=NT), [B_ps[6]], [B_Btok])
                else:
                    jx = bi_
                    P.op("act", "activation", A(out=xsT, in_=acc, func=AF.Silu), reads=[B_acc], writes=[B_xsT])
                    for c in range(NT):
                        P.op("pe", "transpose", A(out=psb(7)[:, 128 * c:128 * c + 128], in_=xsT[:, 128 * c:128 * c + 128], identity=identb[:]),
                             reads=[B_xsT, B_cst], writes=[B_ps[7]])
                    evac(xs_tok[:, :, 128 * jx:128 * jx + 128], psb(7)[:, 0:128 * NT].rearrange("p (c f) -> p c f", c=NT), [B_ps[7]], [B_xs])
            wz, wzB = load_w(win(OFF_Z + 256 * g, 256), 8, 256)
            for c in range(NT):
                pi = c % 2
                proj_tm(pi, 0, wz, wzB, 0, 256, t0 + 128 * c)
                P.op("act", "activation", A(out=zs[:, c, :], in_=psum[pi][:, 0:256], func=AF.Silu), reads=[B_ps[pi]], writes=[B_zs])
            P.op("act", "copy", A(out=st_b, in_=state_f[:, g, :]), reads=[B_state[g]], writes=[B_stb])
            for c in range(NT):
                i2 = c % 2
                tk = slice(128 * c, 128 * c + 128)
                ag = adt[:, c, 4 * g:4 * g + 4]
                P.op("dve", "tensor_tensor", A(out=R[i2].rearrange("p (h l) -> p h l", h=4), in0=cst[:, C_TRIL:C_TRIL + 128].unsqueeze(1).broadcast_to([128, 4, 128]),
                                                                in1=ag.unsqueeze(2).broadcast_to([128, 4, 128]), op=ALU.mult),
                     reads=[B_cst, B_dt], writes=[B_R[i2]])
                P.op("pe", "matmul", A(psum[0][:, :], lhsT=cst[:, C_SUP:C_SUP + 128], rhs=R[i2], start=True, stop=False), reads=[B_cst, B_R[i2]], writes=[B_ps[0]])
                P.op("pe", "matmul", A(psum[0][:, :], lhsT=cst[:, C_NEGI:C_NEGI + 128], rhs=cst[:, C_SLOW4:C_SLOW4 + 512], start=False, stop=True), reads=[B_cst], writes=[B_ps[0]])
                P.op("act", "activation", A(out=LT[i2], in_=psum[0][:, :], func=AF.Exp), reads=[B_ps[0]], writes=[B_LT[i2]])
                P.op("pe", "matmul", A(psum[1][:, 0:128], lhsT=BT[:, tk], rhs=CT[:, tk], start=True, stop=True), reads=[B_BT, B_CT], writes=[B_ps[1]])
                P.op("dve", "tensor_tensor", A(out=MT[i2].rearrange("p (h l) -> p h l", h=4), in0=LT[i2].rearrange("p (h l) -> p h l", h=4),
                                                         in1=psum[1][:, 0:128].unsqueeze(1).broadcast_to([128, 4, 128]), op=ALU.mult),
                     reads=[B_LT[i2], B_ps[1]], writes=[B_MT[i2]])
                P.op("dve", "tensor_tensor", A(out=xdt[i2].rearrange("p (h q) -> p h q", h=4), in0=xs_tok[:, c, :].rearrange("p (h q) -> p h q", h=4),
                                                              in1=dt[:, c, 4 * g:4 * g + 4].unsqueeze(2).broadcast_to([128, 4, 64]), op=ALU.mult),
                     reads=[B_xs, B_dt], writes=[B_xd[i2]])
                P.op("dve", "tensor_tensor", A(out=xdd[i2].rearrange("p (h q) -> p h q", h=4), in0=xdt[i2].rearrange("p (h q) -> p h q", h=4),
                                                              in1=E[:, c, 32 + 4 * g:32 + 4 * g + 4].unsqueeze(2).broadcast_to([128, 4, 64]), op=ALU.mult),
                     reads=[B_xd[i2], B_E], writes=[B_xd[i2]])
                for hh in range(4):
                    P.op("pe", "matmul", A(psum[2][:, 64 * hh:64 * hh + 64], lhsT=MT[i2][:, 128 * hh:128 * hh + 128], rhs=xdt[i2][:, 64 * hh:64 * hh + 64], start=True, stop=False),
                         reads=[B_MT[i2], B_xd[i2]], writes=[B_ps[2]])
                    P.op("pe", "matmul", A(psum[2][:, 64 * hh:64 * hh + 64], lhsT=dskI[:, hh, :], rhs=xs_tok[:, c, 64 * hh:64 * hh + 64], start=False, stop=True),
                         reads=[B_dskI, B_xs], writes=[B_ps[2]])
                P.op("pe", "matmul", A(psum[3][:, 0:256], lhsT=CT[:, tk], rhs=st_b, start=True, stop=True), reads=[B_CT, B_stb], writes=[B_ps[3]])
                P.op("dve", "tensor_tensor", A(out=y1.rearrange("p (h q) -> p h q", h=4), in0=psum[3][:, 0:256].rearrange("p (h q) -> p h q", h=4),
                                                       in1=E[:, c, 4 * g:4 * g + 4].unsqueeze(2).broadcast_to([128, 4, 64]), op=ALU.mult),
                     reads=[B_ps[3], B_E], writes=[B_y1])
                P.op("dve", "tensor_tensor", A(out=y2[:, c, :], in0=y1, in1=psum[2][:, 0:256], op=ALU.add), reads=[B_y1, B_ps[2]], writes=[B_y2])
                P.op("pe", "matmul", A(psum[4][:, 0:256], lhsT=Btok[:, c, :], rhs=xdd[i2], start=True, stop=True), reads=[B_Btok, B_xd[i2]], writes=[B_ps[4]])
                P.op("dve", "tensor_tensor", A(out=stmp.rearrange("p (h q) -> p h q", h=4), in0=state_f[:, g, :].rearrange("p (h q) -> p h q", h=4),
                                                       in1=E[:, c, 64 + 4 * g:64 + 4 * g + 4].unsqueeze(2).broadcast_to([128, 4, 64]), op=ALU.mult),
                     reads=[B_state[g], B_E], writes=[B_stmp])
                P.op("dve", "tensor_tensor", A(out=state_f[:, g, :], in0=stmp, in1=psum[4][:, 0:256], op=ALU.add), reads=[B_stmp, B_ps[4]], writes=[B_state[g]])
                P.op("act", "copy", A(out=st_b, in_=state_f[:, g, :]), reads=[B_state[g]], writes=[B_stb])
            P.op("dve", "tensor_tensor", A(out=y2, in0=y2, in1=zs, op=ALU.mult), reads=[B_y2, B_zs], writes=[B_y2])
            for c in range(NT):
                P.op("act", "activation", A(out=junk, in_=y2[:, c, :], func=AF.Square, accum_out=ss[:, c:c + 1]), reads=[B_y2], writes=[B_junk, B_ss])
            rstd_from_ss(ss[:, 0:NT], NT, 1.0 / 256, [B_ss])
            P.op("dve", "tensor_tensor", A(out=y2, in0=y2, in1=ss[:, 0:NT].unsqueeze(2).broadcast_to([128, NT, 256]), op=ALU.mult), reads=[B_y2, B_ss], writes=[B_y2])
            P.op("dve", "tensor_tensor", A(out=yn, in0=y2, in1=sg_b.unsqueeze(1).broadcast_to([128, NT, 256]), op=ALU.mult), reads=[B_y2, B_sg], writes=[B_yn])
            for c in range(NT):
                transpose_to_YT(yn[:, c, :], B_yn, 2, 2 * g, 128 * c, 6 + c % 2)
        P.barrier()

    def merge_phase(sq, part, bi, wbr_d, nkb, first):
        t0 = part * T
        cv = Carver()
        sig = [cv.f32(512), cv.f32(512)]; B_sig = [Buf(), Buf()]
        tmp = cv.f32(512); B_tmp = Buf()
        it = 0
        for ob in range(8):
            wb_, wbB = load_w(wbr_d.ap()[:, 128 * ob:128 * ob + 128], nkb, 128)
            wg, wgB = load_w(win(OFF_GATE + 1024 * bi + 128 * ob, 128), 8, 128)
            for tc in range(T // 512):
                pa, pg = 2 * (it % 2), 2 * (it % 2) + 1
                si = it % 2; it += 1
                for kb in range(nkb):
                    P.op("pe", "matmul", A(psum[pa][:, :], lhsT=wb_[:, kb, :], rhs=YT[:, kb, 512 * tc:512 * tc + 512], start=(kb == 0), stop=(kb == nkb - 1)),
                         reads=[wbB, B_YT], writes=[B_ps[pa]])
                proj_fm(pg, 0, wg, wgB, 0, 128, t0 + 512 * tc, 512)
                P.op("act", "activation", A(out=sig[si], in_=psum[pg][:, :], func=AF.Sigmoid), reads=[B_ps[pg]], writes=[B_sig[si]])
                dst = mrg[:, ob, 512 * tc:512 * tc + 512]
                if first:
                    P.op("dve", "tensor_tensor", A(out=dst, in0=sig[si], in1=psum[pa][:, :], op=ALU.mult), reads=[B_sig[si], B_ps[pa]], writes=[B_mrg])
                else:
                    P.op("dve", "tensor_tensor", A(out=tmp, in0=sig[si], in1=psum[pa][:, :], op=ALU.mult), reads=[B_sig[si], B_ps[pa]], writes=[B_tmp])
                    P.op("dve", "tensor_tensor", A(out=dst, in0=dst, in1=tmp, op=ALU.add), reads=[B_tmp, B_mrg], writes=[B_mrg])
        P.barrier()

    def out_phase(sq, part):
        t0 = part * T
        cv = Carver()
        fg_b = cv.f32(1024); B_fg = Buf()
        xt = [cv.f32(1024), cv.f32(1024)]; B_xt = [Buf(), Buf()]
        r = [cv.f32(1024), cv.f32(1024)]; B_r = [Buf(), Buf()]
        junk = cv.f32(1024); B_junk = Buf()
        ss = cv.f32(2); B_ss = Buf()
        bcast_load(fg_b, fgain_d.ap(), [B_fg])
        wts = [load_w(w_out.ap()[:, 256 * i:256 * i + 256], 8, 256) for i in range(4)]
        for c in range(NT):
            xi = c % 2
            rows = slice(t0 + 128 * c, t0 + 128 * c + 128)
            P.dma("sp", "dma_start", A(out=xt[xi], in_=x_d.ap()[sq, rows, :]), writes=[B_xt[xi]])
            for hf in range(2):
                pi = 2 * xi + hf
                for q4 in range(2):
                    wt, wB = wts[2 * hf + q4]
                    for kb in range(8):
                        P.op("pe", "matmul", A(psum[pi][:, 256 * q4:256 * q4 + 256], lhsT=mrg[:, kb, 128 * c:128 * c + 128], rhs=wt[:, kb, :], start=(kb == 0), stop=(kb == 7)),
                             reads=[wB, B_mrg], writes=[B_ps[pi]])
                P.op("dve", "tensor_tensor", A(out=r[xi][:, 512 * hf:512 * hf + 512], in0=xt[xi][:, 512 * hf:512 * hf + 512], in1=psum[pi][:, :], op=ALU.add),
                     reads=[B_xt[xi], B_ps[pi]], writes=[B_r[xi]])
            P.op("act", "activation", A(out=junk, in_=r[xi], func=AF.Square, accum_out=ss[:, 0:1]), reads=[B_r[xi]], writes=[B_junk, B_ss])
            rstd_from_ss(ss[:, 0:1], 1, 1.0 / D, [B_ss])
            P.op("dve", "scalar_tensor_tensor", A(out=r[xi], in0=r[xi], scalar=ss[:, 0:1], in1=fg_b, op0=ALU.mult, op1=ALU.mult), reads=[B_r[xi], B_ss, B_fg], writes=[B_r[xi]])
            P.dma("sp", "dma_start", A(out=out_d.ap()[sq, rows, :], in_=r[xi]), reads=[B_r[xi]])
        P.barrier()

    for sq in range(nseq):
        prologue(sq)
        for part in range(NPART):
            first = True
            for bi, (name, fn, wbr, nkb) in enumerate((("ssm", ssm_phase, w_brs, 16), ("diff", diff_phase, w_brd, 8), ("mem", mem_phase, w_brm, 8))):
                if name not in branches:
                    continue
                fn(sq, part)
                merge_phase(sq, part, bi, wbr, nkb, first)
                first = False
            out_phase(sq, part)
    P.finish()
    P.emit()
    return nc, P


_CACHE = {}


def kernel(**inputs):
    ncores = _CFG["ncores"]; nseq = _CFG["nseq"]
    key = (nseq, tuple(_CFG["branches"]), _CFG["same_engine_sync"])
    if key not in _CACHE:
        _CACHE[key] = build(nseq, _CFG["branches"], _CFG["same_engine_sync"])
    nc, P = _CACHE[key]
    f = lambda a: np.ascontiguousarray(np.asarray(a, dtype=np.float32))
    cst, oh = host_consts()
    shared = {
        "w_in": f(inputs["w_in"][0]), "w_mem_kv": f(inputs["w_mem_kv"][0]), "w_br_ssm": f(inputs["w_br_ssm"][0]),
        "w_br_diff": f(inputs["w_br_diff"][0]), "w_br_mem": f(inputs["w_br_mem"][0]), "w_out": f(inputs["w_out"][0]),
        "norm_gain": f(inputs["norm_gain"]).reshape(1, D), "mem_norm_gain": f(inputs["mem_norm_gain"]).reshape(1, D),
        "final_norm_gain": f(inputs["final_norm_gain"]).reshape(1, D), "ssm_norm_gain": f(inputs["ssm_norm_gain"]).reshape(1, 2048),
        "subln_gain": f(inputs["subln_gain"]).reshape(1, 128), "dt_bias": f(inputs["dt_bias"]).reshape(1, 32),
        "a_log": f(inputs["a_log"]).reshape(1, 32), "d_skip": f(inputs["d_skip"]).reshape(1, 32),
        "lambda_q1": f(inputs["lambda_q1"]).reshape(1, 64), "lambda_k1": f(inputs["lambda_k1"]).reshape(1, 64),
        "lambda_q2": f(inputs["lambda_q2"]).reshape(1, 64), "lambda_k2": f(inputs["lambda_k2"]).reshape(1, 64),
        "rel_bias": f(inputs["rel_bias"]),
        "conv_w_l": f(np.asarray(inputs["conv_w"][0]).reshape(4, 32, 128).transpose(2, 1, 0)),
        "conv_b_l": f(np.asarray(inputs["conv_b"][0]).reshape(32, 128).transpose(1, 0)),
        "cst": cst, "onehot": oh,
    }
    x = np.asarray(inputs["x"]); mem = np.asarray(inputs["mem"])
    in_maps = []
    for c in range(ncores):
        m = dict(shared)
        m["x"] = f(x[c * nseq:(c + 1) * nseq])
        m["mem"] = f(mem[c * nseq:(c + 1) * nseq])
        in_maps.append(m)
    res = run_bass_kernel_spmd(nc, in_maps, core_ids=list(range(ncores)))
    return np.concatenate([np.asarray(r["out"]) for r in res.results], axis=0).astype(np.float32)
```

```python
import math
import numpy as np
import concourse.bass as bass
import concourse.mybir as mybir
from concourse.bass_utils import run_bass_kernel_spmd

F32 = mybir.dt.float32
BF16 = mybir.dt.bfloat16
ALU = mybir.AluOpType
AF = mybir.ActivationFunctionType

D = 1024
S = 2048
MEM = 256
IN_DIM = 15392
OFF_Z, OFF_XBC, OFF_DT, OFF_DQ, OFF_DK, OFF_DV, OFF_DG, OFF_MQ, OFF_MG, OFF_GATE = (
    0, 2048, 6144, 6176, 7200, 8224, 9248, 10272, 11296, 12320)
EPS = 1e-5
NEG = -30000.0
T = 1024
NT = T // 128
NPART = S // T

_CFG = {"branches": ("ssm", "diff", "mem"), "ncores": 8, "nseq": 4, "same_engine_sync": "raw"}

COMPUTE = ("pe", "act", "dve", "pool")
QUEUES = ("pe", "act", "dve", "pool", "sp")
DMAQ = ("sp", "pool", "act")
NDMASEM = 12


def A(*a, **k):
    return (a, k)


class Buf:
    __slots__ = ("name", "last_w", "readers")

    def __init__(self, name=""):
        self.name = name
        self.last_w = None
        self.readers = []


class Prog:
    def __init__(self, nc, same_engine_sync=True):
        self.nc = nc
        self.q = {e: [] for e in QUEUES}
        self.ndma = {e: 0 for e in DMAQ}
        self.same_engine_sync = same_engine_sync
        self.sem = {e: nc.alloc_semaphore("s_" + e) for e in COMPUTE}
        self.dsem = {e: [nc.alloc_semaphore(f"d_{e}{i}") for i in range(NDMASEM)] for e in DMAQ}

    def _deps(self, reads, writes):
        deps = set()
        raw = set()
        for b in reads:
            if b.last_w is not None:
                deps.add(b.last_w)
                raw.add(b.last_w)
        for b in writes:
            if b.last_w is not None:
                deps.add(b.last_w)
            deps.update(b.readers)
        self._raw = raw
        return deps

    def _commit(self, me, reads, writes):
        for b in reads:
            b.readers.append(me)
        for b in writes:
            b.last_w = me
            b.readers = []

    def op(self, eng, meth, args, reads=(), writes=()):
        deps = self._deps(reads, writes)
        me = (eng, len(self.q[eng]))
        fn = (meth, args)
        self.q[eng].append([deps, fn, False, None, self._raw])
        self._commit(me, reads, writes)
        return me

    def dma(self, eng, meth, args, reads=(), writes=()):
        fn = (meth, args)
        deps = self._deps(reads, writes)
        k = self.ndma[eng]
        self.ndma[eng] += 1
        if k >= NDMASEM:
            deps.add(("dma", eng, k - NDMASEM))
        self.q[eng].append([deps, fn, True, k, self._raw])
        me = ("dma", eng, k)
        self._commit(me, reads, writes)
        return me

    def _all_tails(self):
        deps = set()
        for e in DMAQ:
            n = self.ndma[e]
            for k in range(max(0, n - NDMASEM), n):
                deps.add(("dma", e, k))
        for e in COMPUTE:
            for i in range(len(self.q[e]) - 1, -1, -1):
                if not self.q[e][i][2] and self.q[e][i][1] is not None:
                    deps.add((e, i))
                    break
        return deps

    def barrier(self):
        deps = self._all_tails()
        for e in QUEUES:
            self.q[e].append([set(deps), None, False, None, set(deps)])

    def finish(self):
        t = self._all_tails()
        self.q["sp"].append([t, None, False, None, t])

    def emit(self):
        nc = self.nc
        signal = {e: set() for e in COMPUTE}
        plan = {e: [] for e in QUEUES}
        for e in QUEUES:
            seen = {x: -1 for x in COMPUTE}
            seen_slot = {}
            for (deps, fn, is_dma, k, raw) in self.q[e]:
                best = {}
                for d in deps:
                    if d[0] == "dma":
                        slot = (d[1], d[2] % NDMASEM)
                        if seen_slot.get(slot, -1) >= d[2] // NDMASEM:
                            continue
                        if slot not in best or best[slot][2] < d[2]:
                            best[slot] = d
                    else:
                        x, i = d
                        if x == e and (x == "pe" or not self.same_engine_sync):
                            continue
                        if x == e and self.same_engine_sync == "raw" and d not in raw:
                            continue
                        if seen[x] >= i:
                            continue
                        if x not in best or best[x][1] < i:
                            best[x] = d
                final = list(best.values())
                for d in final:
                    if d[0] == "dma":
                        seen_slot[(d[1], d[2] % NDMASEM)] = d[2] // NDMASEM
                    else:
                        seen[d[0]] = d[1]
                        signal[d[0]].add(d[1])
                plan[e].append((final, fn, is_dma, k))
        count = {}
        for e in COMPUTE:
            c = 0
            for i in range(len(self.q[e])):
                if i in signal[e]:
                    c += 1
                    count[(e, i)] = c
        self.stats = {e: (len(self.q[e]), len(signal.get(e, ()))) for e in QUEUES}

        def replay(e, eng):
            for i, (final, fn, is_dma, k) in enumerate(plan[e]):
                for d in final:
                    if d[0] == "dma":
                        eng.wait_ge(self.dsem[d[1]][d[2] % NDMASEM], 16 * (d[2] // NDMASEM + 1))
                    else:
                        eng.wait_ge(self.sem[d[0]], count[d])
                if fn is None:
                    continue
                ins = getattr(eng, fn[0])(*fn[1][0], **fn[1][1])
                if is_dma:
                    ins.then_inc(self.dsem[e][k % NDMASEM], 16)
                elif e in signal and i in signal[e]:
                    ins.then_inc(self.sem[e], 1)

        with nc.Block() as block:
            @block.tensor
            def _(eng):
                replay("pe", eng)

            @block.scalar
            def _(eng):
                replay("act", eng)

            @block.vector
            def _(eng):
                replay("dve", eng)

            @block.gpsimd
            def _(eng):
                replay("pool", eng)

            @block.sync
            def _(eng):
                replay("sp", eng)


def t5_bucket_np(rel):
    n = np.maximum(rel, 0)
    max_exact = 16
    nf = np.maximum(n, 1).astype(np.float32)
    large = max_exact + (np.log(nf / max_exact) / np.float32(math.log(128 / max_exact))
                         * (32 - max_exact)).astype(np.int32)
    large = np.minimum(large, 31)
    return np.where(n < max_exact, n, large)


def host_consts():
    k = np.arange(128)[:, None]
    l = np.arange(128)[None, :]
    ident = (k == l).astype(np.float32)
    tril1 = (k <= l).astype(np.float32)
    sup = (k > l).astype(np.float32)
    ones = np.ones((128, 128), np.float32)
    negi = ident * NEG
    slow4 = np.tile(sup, (1, 4))
    cst = np.concatenate([ident, tril1, sup, ones, negi, slow4], axis=1)
    oh = np.zeros((33, 2, 128, 128), np.float32)
    for dlt in range(2):
        rel = 128 * dlt + (l - k)
        b = t5_bucket_np(rel)
        for kk in range(128):
            for qq in range(128):
                if rel[kk, qq] >= 0:
                    oh[b[kk, qq], dlt, kk, qq] = 1.0
                else:
                    oh[32, dlt, kk, qq] = 1.0
    return cst, oh.reshape(33, 2 * 128 * 128)


C_ID, C_TRIL, C_SUP, C_ONES, C_NEGI, C_SLOW4 = 0, 128, 256, 384, 512, 640


def build(nseq, branches, same_engine_sync=True):
    nc = bass.Bass("TRN2", target_bir_lowering=False)
    P = Prog(nc, same_engine_sync)

    def din(name, shape, dt=F32):
        return nc.dram_tensor(name, list(shape), dt, kind="ExternalInput")

    x_d = din("x", [nseq, S, D])
    mem_d = din("mem", [nseq, MEM, D])
    w_in = din("w_in", [D, IN_DIM])
    w_kv = din("w_mem_kv", [D, 2048])
    w_brs = din("w_br_ssm", [2048, D])
    w_brd = din("w_br_diff", [D, D])
    w_brm = din("w_br_mem", [D, D])
    w_out = din("w_out", [D, D])
    gain_d = din("norm_gain", [1, D])
    mgain_d = din("mem_norm_gain", [1, D])
    fgain_d = din("final_norm_gain", [1, D])
    sgain_d = din("ssm_norm_gain", [1, 2048])
    subg_d = din("subln_gain", [1, 128])
    dtb_d = din("dt_bias", [1, 32])
    alog_d = din("a_log", [1, 32])
    dsk_d = din("d_skip", [1, 32])
    lq1_d, lk1_d, lq2_d, lk2_d = (din(n, [1, 64]) for n in ("lambda_q1", "lambda_k1", "lambda_q2", "lambda_k2"))
    rb_d = din("rel_bias", [32, 8])
    cw_d = din("conv_w_l", [128, 32, 4])
    cb_d = din("conv_b_l", [128, 32])
    cst_d = din("cst", [128, 1152])
    oh_d = din("onehot", [33, 32768])
    out_d = nc.dram_tensor("out", [nseq, S, D], F32, kind="ExternalOutput")
    bias_scr = nc.dram_tensor("bias_scr", [8, 32768], F32, kind="Internal")
    w_in_f, w_kv_f, w_brs_f, w_brd_f, w_brm_f, w_out_f = w_in, w_kv, w_brs, w_brd, w_brm, w_out
    w_in = nc.dram_tensor("w_in_b", [D, IN_DIM], BF16, kind="Internal")
    w_kv = nc.dram_tensor("w_kv_b", [D, 2048], BF16, kind="Internal")
    w_brs = nc.dram_tensor("w_brs_b", [2048, D], BF16, kind="Internal")
    w_brd = nc.dram_tensor("w_brd_b", [D, D], BF16, kind="Internal")
    w_brm = nc.dram_tensor("w_brm_b", [D, D], BF16, kind="Internal")
    w_out = nc.dram_tensor("w_out_b", [D, D], BF16, kind="Internal")

    def sb(name, shape, dt=F32):
        return nc.alloc_sbuf_tensor("sb_" + name, list(shape), dt)

    cst = sb("cst", [128, 1152]); B_cst = Buf("cst")
    identb = sb("identb", [128, 128], BF16)
    trilb = sb("trilb", [128, 128], BF16)
    biasT = sb("biasT", [128, 8, 2, 128]); B_biasT = Buf("biasT")
    sm = sb("small", [128, 512]); B_sm = Buf("small")
    SM_CFAR, SM_DTB, SM_A, SM_DSK, SM_LAM, SM_NLAM, SM_ONE, SM_EPS = 0, 8, 40, 72, 104, 105, 106, 107
    SM_SUBG = 128
    SM_TMP = 256
    cw = sb("cw", [128, 32, 4]); cb = sb("cb", [128, 32]); B_cw = Buf("cw")
    hT = sb("hT", [128, 8, S], BF16)
    B_hT = [Buf(f"hT{i}") for i in range(S // 128)]
    kmT = sb("kmT", [128, 8, MEM], BF16); B_kmT = Buf("kmT")
    vma = sb("vma", [128, 2, 4, 258], BF16); B_vma = Buf("vma")
    YT = sb("YT", [128, 16, T], BF16); B_YT = Buf("YT")
    mrg = sb("mrg", [128, 8, T], BF16); B_mrg = Buf("mrg")
    state_f = sb("state_f", [128, 8, 256]); B_state = [Buf(f"st{g}") for g in range(8)]
    halo = sb("halo", [128, 32, 4]); B_halo = Buf("halo")
    NWB = 6
    wbf = [sb(f"wbf{i}", [128, 2048], BF16) for i in range(NWB)]; B_wbf = [Buf(f"wbf{i}") for i in range(NWB)]
    ARENA_W = 16384
    arena = sb("arena", [128, ARENA_W])
    arena_b = arena[:].bitcast(BF16)

    class Carver:
        def __init__(self):
            self.off = 0

        def f32(self, n):
            a = arena[:, self.off:self.off + n]
            self.off += n
            assert self.off <= ARENA_W, self.off
            return a

        def bf16(self, n):
            w = (n + 1) // 2
            a = arena_b[:, 2 * self.off:2 * self.off + n]
            self.off += w
            assert self.off <= ARENA_W, self.off
            return a

    psum = [nc.alloc_psum_tensor(f"ps{i}", [128, 512], F32) for i in range(8)]
    B_ps = [Buf(f"ps{i}") for i in range(8)]

    def psb(i):
        return psum[i][:].bitcast(BF16)

    wctr = [0, 0]

    def load_w(src_ap, nkb, ncols):
        assert nkb * ncols <= 2048
        bi = wctr[1] % NWB; wctr[1] += 1
        bf = wbf[bi][:, 0:nkb * ncols].rearrange("p (k c) -> p k c", k=nkb)
        src = src_ap.rearrange("(k p) c -> p k c", p=128)
        P.dma("sp", "dma_start", A(out=bf, in_=src), writes=[B_wbf[bi]])
        return bf, B_wbf[bi]

    def win(c0, ncols):
        return w_in.ap()[:, c0:c0 + ncols]

    def hbufs(t0, n):
        return B_hT[t0 // 128:(t0 + n + 127) // 128]

    def proj_fm(ps_i, ps_cols, wt, wB, c_lo, ncol, tok0, ntok):
        out = psum[ps_i][0:ncol, ps_cols:ps_cols + ntok]
        for kb in range(8):
            P.op("pe", "matmul", A(out, lhsT=wt[:, kb, c_lo:c_lo + ncol], rhs=hT[:, kb, tok0:tok0 + ntok],
                                                 start=(kb == 0), stop=(kb == 7)),
                 reads=[wB] + hbufs(tok0, ntok), writes=[B_ps[ps_i]])

    def proj_tm(ps_i, ps_cols, wt, wB, c_lo, ncol, tok0):
        out = psum[ps_i][:, ps_cols:ps_cols + ncol]
        for kb in range(8):
            P.op("pe", "matmul", A(out, lhsT=hT[:, kb, tok0:tok0 + 128], rhs=wt[:, kb, c_lo:c_lo + ncol],
                                                 start=(kb == 0), stop=(kb == 7)),
                 reads=[wB] + hbufs(tok0, 128), writes=[B_ps[ps_i]])

    evac_rr = [0]

    def evac(out, in_, reads, writes):
        evac_rr[0] += 1
        if evac_rr[0] % 2:
            P.op("act", "copy", A(out=out, in_=in_), reads=reads, writes=writes)
        else:
            P.op("dve", "tensor_copy", A(out=out, in_=in_), reads=reads, writes=writes)

    def bcast_load(dst, src_row_ap, writes):
        P.dma("sp", "dma_start", A(out=dst, in_=src_row_ap.partition_broadcast(128)), writes=writes)

    def rstd_from_ss(ss_ap, n, inv_count, Bs):
        P.op("dve", "tensor_scalar", A(out=ss_ap, in0=ss_ap, scalar1=inv_count, scalar2=EPS, op0=ALU.mult, op1=ALU.add),
             reads=Bs, writes=Bs)
        P.op("act", "activation", A(out=ss_ap, in_=ss_ap, func=AF.Sqrt), reads=Bs, writes=Bs)
        P.op("dve", "reciprocal", A(out=ss_ap, in_=ss_ap), reads=Bs, writes=Bs)

    P.dma("sp", "dma_start", A(out=cst[:], in_=cst_d.ap()), writes=[B_cst])
    P.op("dve", "tensor_copy", A(out=identb[:], in_=cst[:, C_ID:C_ID + 128]), reads=[B_cst], writes=[B_cst])
    P.op("dve", "tensor_copy", A(out=trilb[:], in_=cst[:, C_TRIL:C_TRIL + 128]), reads=[B_cst], writes=[B_cst])
    P.dma("sp", "dma_start", A(out=cw[:], in_=cw_d.ap()), writes=[B_cw])
    P.dma("sp", "dma_start", A(out=cb[:], in_=cb_d.ap()), writes=[B_cw])
    P.op("pool", "memset", A(sm[:], 0.0), writes=[B_sm])
    P.op("pool", "memset", A(sm[:, SM_ONE:SM_ONE + 1], 1.0), writes=[B_sm])
    P.op("pool", "memset", A(sm[:, SM_EPS:SM_EPS + 1], EPS), writes=[B_sm])
    bcast_load(sm[:, SM_CFAR:SM_CFAR + 8], rb_d.ap()[31:32, :], [B_sm])
    bcast_load(sm[:, SM_DTB:SM_DTB + 32], dtb_d.ap(), [B_sm])
    bcast_load(sm[:, SM_A:SM_A + 32], alog_d.ap(), [B_sm])
    bcast_load(sm[:, SM_DSK:SM_DSK + 32], dsk_d.ap(), [B_sm])
    bcast_load(sm[:, SM_SUBG:SM_SUBG + 128], subg_d.ap(), [B_sm])
    for i, ld in enumerate((lq1_d, lk1_d, lq2_d, lk2_d)):
        bcast_load(sm[:, SM_TMP + 64 * i:SM_TMP + 64 * i + 64], ld.ap(), [B_sm])
    Bs = [B_sm]
    P.op("act", "activation", A(out=sm[:, SM_A:SM_A + 32], in_=sm[:, SM_A:SM_A + 32], func=AF.Exp), reads=Bs, writes=Bs)
    P.op("dve", "tensor_scalar_mul", A(out=sm[:, SM_A:SM_A + 32], in0=sm[:, SM_A:SM_A + 32], scalar1=-1.0), reads=Bs, writes=Bs)
    LAM_INIT = 0.8 - 0.6 * math.exp(-0.3 * 0)
    P.op("dve", "tensor_scalar_mul", A(out=sm[:, SM_SUBG:SM_SUBG + 128], in0=sm[:, SM_SUBG:SM_SUBG + 128], scalar1=1.0 - LAM_INIT), reads=Bs, writes=Bs)
    for i in range(2):
        a0 = SM_TMP + 128 * i
        P.op("dve", "tensor_tensor", A(out=sm[:, a0:a0 + 64], in0=sm[:, a0:a0 + 64], in1=sm[:, a0 + 64:a0 + 128], op=ALU.mult), reads=Bs, writes=Bs)
        P.op("dve", "reduce_sum", A(out=sm[:, 110 + i:111 + i], in_=sm[:, a0:a0 + 64], axis=mybir.AxisListType.X), reads=Bs, writes=Bs)
    P.op("act", "activation", A(out=sm[:, 110:112], in_=sm[:, 110:112], func=AF.Exp), reads=Bs, writes=Bs)
    P.op("dve", "tensor_tensor", A(out=sm[:, SM_LAM:SM_LAM + 1], in0=sm[:, 110:111], in1=sm[:, 111:112], op=ALU.subtract), reads=Bs, writes=Bs)
    P.op("dve", "tensor_scalar_add", A(out=sm[:, SM_LAM:SM_LAM + 1], in0=sm[:, SM_LAM:SM_LAM + 1], scalar1=LAM_INIT), reads=Bs, writes=Bs)
    P.op("dve", "tensor_scalar_mul", A(out=sm[:, SM_NLAM:SM_NLAM + 1], in0=sm[:, SM_LAM:SM_LAM + 1], scalar1=-1.0), reads=Bs, writes=Bs)

    cv = Carver()
    pst = [cv.f32(2048), cv.f32(2048)]; B_pst = [Buf(), Buf()]
    pbf = [cv.bf16(2048), cv.bf16(2048)]; B_pbf = [Buf(), Buf()]
    pc_i = 0
    for (srcw, dstw, rows, cols) in ((w_in_f, w_in, D, IN_DIM), (w_kv_f, w_kv, D, 2048), (w_brs_f, w_brs, 2048, D),
                                     (w_brd_f, w_brd, D, D), (w_brm_f, w_brm, D, D), (w_out_f, w_out, D, D)):
        for rb in range(rows // 128):
            for c0 in range(0, cols, 2048):
                n = min(2048, cols - c0)
                si = pc_i % 2
                P.dma("sp", "dma_start", A(out=pst[si][:, 0:n], in_=srcw.ap()[128 * rb:128 * rb + 128, c0:c0 + n]), writes=[B_pst[si]])
                ce = ("pool", "act", "dve")[pc_i % 3]
                P.op(ce, "copy" if ce == "act" else "tensor_copy", A(out=pbf[si][:, 0:n], in_=pst[si][:, 0:n]), reads=[B_pst[si]], writes=[B_pbf[si]])
                P.dma("sp", "dma_start", A(out=dstw.ap()[128 * rb:128 * rb + 128, c0:c0 + n], in_=pbf[si][:, 0:n]), reads=[B_pbf[si]])
                pc_i += 1
    P.barrier()

    if "diff" in branches:
        cv = Carver()
        rbx = cv.f32(8)
        ohs = cv.f32(4096)
        stg = cv.f32(4096)
        B_rbx, B_ohs, B_stg, B_scr = Buf(), Buf(), Buf(), Buf()
        P.op("pool", "memset", A(rbx[32:33, :], NEG), writes=[B_rbx])
        P.dma("sp", "dma_start", A(out=rbx[0:32, :], in_=rb_d.ap()), writes=[B_rbx])
        for pc in range(8):
            P.dma("sp", "dma_start", A(out=ohs[0:33, :], in_=oh_d.ap()[:, 4096 * pc:4096 * pc + 4096]), writes=[B_ohs])
            for i in range(8):
                pi = i % 2
                P.op("pe", "matmul", A(psum[pi][0:8, :], lhsT=rbx[0:33, :], rhs=ohs[0:33, 512 * i:512 * i + 512], start=True, stop=True),
                     reads=[B_rbx, B_ohs], writes=[B_ps[pi]])
                evac(stg[0:8, 512 * i:512 * i + 512], psum[pi][0:8, :], [B_ps[pi]], [B_stg])
            P.dma("sp", "dma_start", A(out=bias_scr.ap()[:, 4096 * pc:4096 * pc + 4096], in_=stg[0:8, :]), reads=[B_stg], writes=[B_scr])
        for h in range(8):
            P.dma("sp", "dma_start", A(out=biasT[:, h, :, :], in_=bias_scr.ap()[h, :].rearrange("(d k q) -> k d q", d=2, k=128)),
                  reads=[B_scr], writes=[B_biasT])
        P.barrier()

    def prologue(sq):
        cv = Carver()
        gain_b = cv.f32(1024); B_gain = Buf()
        xt = [cv.f32(1024), cv.f32(1024)]; B_xt = [Buf(), Buf()]
        junk = cv.f32(1024); B_junk = Buf()
        hb = cv.bf16(1024); B_hb = Buf()
        ss = cv.f32(2); B_ss = Buf()
        memT = cv.bf16(8 * MEM); B_memT = Buf()
        memT3 = memT.rearrange("p (k m) -> p k m", k=8)

        def norm_transpose(src_ap, i, dst3, dstB, col0):
            xi = i % 2
            P.dma("sp", "dma_start", A(out=xt[xi], in_=src_ap), writes=[B_xt[xi]])
            P.op("act", "activation", A(out=junk, in_=xt[xi], func=AF.Square, accum_out=ss[:, 0:1]), reads=[B_xt[xi]], writes=[B_junk, B_ss])
            rstd_from_ss(ss[:, 0:1], 1, 1.0 / D, [B_ss])
            P.op("dve", "scalar_tensor_tensor", A(out=hb, in0=xt[xi], scalar=ss[:, 0:1], in1=gain_b, op0=ALU.mult, op1=ALU.mult),
                 reads=[B_xt[xi], B_ss, B_gain], writes=[B_hb])
            for kb in range(8):
                P.op("pe", "transpose", A(out=psb(6)[:, 128 * kb:128 * kb + 128], in_=hb[:, 128 * kb:128 * kb + 128], identity=identb[:]),
                     reads=[B_hb, B_cst], writes=[B_ps[6]])
            evac(dst3[:, :, col0:col0 + 128], psb(6)[:, 0:1024].rearrange("p (k t) -> p k t", k=8), [B_ps[6]], dstB)

        bcast_load(gain_b, gain_d.ap(), [B_gain])
        for i in range(S // 128):
            norm_transpose(x_d.ap()[sq, 128 * i:128 * i + 128, :], i, hT, [B_hT[i]], 128 * i)
        if "mem" in branches:
            bcast_load(gain_b, mgain_d.ap(), [B_gain])
            for i in range(2):
                norm_transpose(mem_d.ap()[sq, 128 * i:128 * i + 128, :], i, memT3, [B_memT], 128 * i)
            for fb in range(8):
                wt, wB = load_w(w_kv.ap()[:, 128 * fb:128 * fb + 128], 8, 128)
                pi = fb % 2
                for kb in range(8):
                    P.op("pe", "matmul", A(psum[pi][:, 0:MEM], lhsT=wt[:, kb, :], rhs=memT3[:, kb, :], start=(kb == 0), stop=(kb == 7)),
                         reads=[wB, B_memT], writes=[B_ps[pi]])
                evac(kmT[:, fb, :], psum[pi][:, 0:MEM], [B_ps[pi]], [B_kmT])
            P.op("pool", "memset", A(vma[:, :, :, 256:258], 1.0), writes=[B_vma])
            for hh in range(4):
                wt, wB = load_w(w_kv.ap()[:, 1024 + 256 * hh:1024 + 256 * hh + 256], 8, 256)
                for mt in range(2):
                    pi = mt
                    for kb in range(8):
                        P.op("pe", "matmul", A(psum[pi][:, 0:256], lhsT=memT3[:, kb, 128 * mt:128 * mt + 128], rhs=wt[:, kb, :], start=(kb == 0), stop=(kb == 7)),
                             reads=[wB, B_memT], writes=[B_ps[pi]])
                    evac(vma[:, mt, hh, 0:256], psum[pi][:, 0:256], [B_ps[pi]], [B_vma])
        P.barrier()

    def transpose_to_YT(src_bf, srcB, nblk, yt_blk0, tokcol0, ps_i):
        for j in range(nblk):
            P.op("pe", "transpose", A(out=psb(ps_i)[:, 128 * j:128 * j + 128], in_=src_bf[:, 128 * j:128 * j + 128], identity=identb[:]),
                 reads=[srcB, B_cst], writes=[B_ps[ps_i]])
        evac(YT[:, yt_blk0:yt_blk0 + nblk, tokcol0:tokcol0 + 128], psb(ps_i)[:, 0:128 * nblk].rearrange("p (j t) -> p j t", j=nblk), [B_ps[ps_i]], [B_YT])

    def mem_phase(sq, part):
        t0 = part * T
        cv = Carver()
        QmT = cv.bf16(2 * T).rearrange("p (d t) -> p d t", d=2); B_Qm = Buf()
        Gm = cv.f32(NT * 256).rearrange("p (c f) -> p c f", c=NT); B_Gm = Buf()
        PmT = [cv.bf16(2 * 512).rearrange("p (m t) -> p m t", m=2) for _ in range(2)]; B_Pm = [Buf(), Buf()]
        rr = cv.f32(4); B_rr = Buf()
        ym = [cv.bf16(256), cv.bf16(256)]; B_ym = [Buf(), Buf()]
        it = 0
        for hh in range(4):
            wq, wqB = load_w(win(OFF_MQ + 256 * hh, 256), 8, 256)
            wg, wgB = load_w(win(OFF_MG + 256 * hh, 256), 8, 256)
            for db in range(2):
                for tc in range(T // 512):
                    pi = (2 * db + tc) % 2
                    proj_fm(pi, 0, wq, wqB, 128 * db, 128, t0 + 512 * tc, 512)
                    evac(QmT[:, db, 512 * tc:512 * tc + 512], psum[pi][:, :], [B_ps[pi]], [B_Qm])
            for c in range(NT):
                pi = 2 + c % 2
                proj_tm(pi, 0, wg, wgB, 0, 256, t0 + 128 * c)
                P.op("act", "activation", A(out=Gm[:, c, :], in_=psum[pi][:, 0:256], func=AF.Silu), reads=[B_ps[pi]], writes=[B_Gm])
            for tc in range(T // 512):
                pb = tc % 2
                for mt in range(2):
                    pi = mt
                    for db in range(2):
                        P.op("pe", "matmul", A(psum[pi][:, :], lhsT=kmT[:, 2 * hh + db, 128 * mt:128 * mt + 128], rhs=QmT[:, db, 512 * tc:512 * tc + 512], start=(db == 0), stop=(db == 1)),
                             reads=[B_kmT, B_Qm], writes=[B_ps[pi]])
                    P.op("act", "activation", A(out=PmT[pb][:, mt, :], in_=psum[pi][:, :], func=AF.Exp, scale=1.0 / 16.0), reads=[B_ps[pi]], writes=[B_Pm[pb]])
                for j in range(4):
                    pi = 2 + j % 2
                    for mt in range(2):
                        P.op("pe", "matmul", A(psum[pi][:, 0:257], lhsT=PmT[pb][:, mt, 128 * j:128 * j + 128], rhs=vma[:, mt, hh, 0:257], start=(mt == 0), stop=(mt == 1)),
                             reads=[B_Pm[pb], B_vma], writes=[B_ps[pi]])
                    yi = it % 2; it += 1
                    P.op("dve", "reciprocal", A(out=rr[:, 0:1], in_=psum[pi][:, 256:257]), reads=[B_ps[pi]], writes=[B_rr])
                    P.op("dve", "scalar_tensor_tensor", A(out=ym[yi], in0=psum[pi][:, 0:256], scalar=rr[:, 0:1], in1=Gm[:, 4 * tc + j, :], op0=ALU.mult, op1=ALU.mult),
                         reads=[B_ps[pi], B_rr, B_Gm], writes=[B_ym[yi]])
                    transpose_to_YT(ym[yi], B_ym[yi], 2, 2 * hh, 512 * tc + 128 * j, 6)
        P.barrier()

    def diff_phase(sq, part):
        t0 = part * T
        nk = t0 + T
        cv = Carver()
        KT = cv.bf16(S); B_KT = Buf()
        QT = cv.bf16(T); B_QT = Buf()
        Va = cv.bf16(16 * 130).rearrange("p (t v) -> p t v", t=16); B_Va = Buf()
        Gs = cv.f32(NT * 128).rearrange("p (c f) -> p c f", c=NT); B_Gs = Buf()
        tmpS = [cv.f32(256), cv.f32(256)]; B_tmpS = [Buf(), Buf()]
        PT = [cv.bf16(512), cv.bf16(512)]; B_PT = [Buf(), Buf()]
        Os = [cv.f32(4 * 129).rearrange("p (j v) -> p j v", j=4) for _ in range(2)]; B_Os = [Buf(), Buf()]
        obuf = cv.f32(NT * 128).rearrange("p (c f) -> p c f", c=NT); B_ob = Buf()
        t1 = cv.f32(512).rearrange("p (j v) -> p j v", j=4); t2 = cv.f32(512).rearrange("p (j v) -> p j v", j=4); B_t = Buf()
        rr = cv.f32(8); B_rr = Buf()
        ss = cv.f32(NT); B_ss = Buf()
        junk = cv.f32(128); B_junk = Buf()
        ydt = cv.bf16(NT * 128).rearrange("p (c f) -> p c f", c=NT); B_ydt = Buf()
        P.op("pool", "memset", A(Va[:, :, 128:130], 1.0), writes=[B_Va])
        sidx = 0
        for h in range(8):
            wq, wqB = load_w(win(OFF_DQ + 128 * h, 128), 8, 128)
            wk, wkB = load_w(win(OFF_DK + 128 * h, 128), 8, 128)
            wv, wvB = load_w(win(OFF_DV + 128 * h, 128), 8, 128)
            wg, wgB = load_w(win(OFF_DG + 128 * h, 128), 8, 128)
            for kc in range(nk // 512):
                pi = kc % 2
                proj_fm(pi, 0, wk, wkB, 0, 128, 512 * kc, 512)
                evac(KT[:, 512 * kc:512 * kc + 512], psum[pi][:, :], [B_ps[pi]], [B_KT])
            for tc in range(T // 512):
                pi = tc % 2
                proj_fm(pi, 0, wq, wqB, 0, 128, t0 + 512 * tc, 512)
                evac(QT[:, 512 * tc:512 * tc + 512], psum[pi][:, :], [B_ps[pi]], [B_QT])
            for kt in range(nk // 128):
                pi = kt % 2
                proj_tm(pi, 0, wv, wvB, 0, 128, 128 * kt)
                evac(Va[:, kt, 0:128], psum[pi][:, 0:128], [B_ps[pi]], [B_Va])
            for c in range(NT):
                pi = c % 2
                proj_tm(pi, 0, wg, wgB, 0, 128, t0 + 128 * c)
                P.op("act", "activation", A(out=Gs[:, c, :], in_=psum[pi][:, 0:128], func=AF.Silu), reads=[B_ps[pi]], writes=[B_Gs])
            for qc in range(T // 512):
                qb0 = (t0 + 512 * qc) // 128
                for c in range(2):
                    r0 = 64 * c
                    nkb_ = qb0 + 4
                    sis = []
                    for kk in range(nkb_):
                        sis.append(sidx % 2); sidx += 1

                    def emit_S(kb_):
                        j0 = kb_ - qb0
                        jlo = max(j0, 0)
                        si = sis[kb_]
                        P.op("pe", "matmul", A(
                            psum[si][:, 128 * jlo:512], lhsT=KT[r0:r0 + 64, 128 * kb_:128 * kb_ + 128],
                            rhs=QT[r0:r0 + 64, 512 * qc + 128 * jlo:512 * qc + 512], start=True, stop=True),
                             reads=[B_KT, B_QT], writes=[B_ps[si]])

                    emit_S(0)
                    for kb_ in range(nkb_):
                        if kb_ + 1 < nkb_:
                            emit_S(kb_ + 1)
                        j0 = kb_ - qb0
                        jlo = max(j0, 0)
                        si = sis[kb_]
                        Sps = psum[si]
                        nsp = 0
                        for j in range(jlo, 4):
                            dl = j - j0
                            if dl <= 1:
                                P.op("dve", "scalar_tensor_tensor", A(
                                    out=tmpS[si][:, 128 * nsp:128 * nsp + 128], in0=Sps[:, 128 * j:128 * j + 128], scalar=0.125,
                                    in1=biasT[:, h, dl, :], op0=ALU.mult, op1=ALU.add),
                                     reads=[B_ps[si], B_biasT], writes=[B_tmpS[si]])
                                nsp += 1
                        if nsp:
                            P.op("act", "activation", A(out=PT[si][:, 128 * jlo:128 * (jlo + nsp)], in_=tmpS[si][:, 0:128 * nsp], func=AF.Exp),
                                 reads=[B_tmpS[si]], writes=[B_PT[si]])
                        jf = max(j0 + 2, 0)
                        if jf < 4:
                            P.op("act", "activation", A(out=PT[si][:, 128 * jf:512], in_=Sps[:, 128 * jf:512], func=AF.Exp, scale=0.125, bias=sm[:, SM_CFAR + h:SM_CFAR + h + 1]),
                                 reads=[B_ps[si], B_sm], writes=[B_PT[si]])
                        for j in range(jlo, 4):
                            P.op("pe", "matmul", A(psum[2 + j][:, 0:129], lhsT=PT[si][:, 128 * j:128 * j + 128], rhs=Va[:, kb_, 0:129],
                                                   start=(kb_ == 0), stop=(kb_ == qb0 + j)),
                                 reads=[B_PT[si], B_Va], writes=[B_ps[2 + j]])
                    for j in range(4):
                        evac(Os[c][:, j, :], psum[2 + j][:, 0:129], [B_ps[2 + j]], [B_Os[c]])
                P.op("dve", "reciprocal", A(out=rr[:, 0:4], in_=Os[0][:, :, 128]), reads=[B_Os[0]], writes=[B_rr])
                P.op("dve", "reciprocal", A(out=rr[:, 4:8], in_=Os[1][:, :, 128]), reads=[B_Os[1], B_rr], writes=[B_rr])
                P.op("dve", "tensor_scalar", A(out=rr[:, 4:8], in0=rr[:, 4:8], scalar1=sm[:, SM_NLAM:SM_NLAM + 1], scalar2=None, op0=ALU.mult), reads=[B_rr, B_sm], writes=[B_rr])
                P.op("dve", "tensor_tensor", A(out=t1, in0=Os[0][:, :, 0:128], in1=rr[:, 0:4].unsqueeze(2).broadcast_to([128, 4, 128]), op=ALU.mult), reads=[B_Os[0], B_rr, B_t], writes=[B_t])
                P.op("dve", "tensor_tensor", A(out=t2, in0=Os[1][:, :, 0:128], in1=rr[:, 4:8].unsqueeze(2).broadcast_to([128, 4, 128]), op=ALU.mult), reads=[B_Os[1], B_rr, B_t], writes=[B_t])
                P.op("dve", "tensor_tensor", A(out=obuf[:, 4 * qc:4 * qc + 4, :], in0=t1, in1=t2, op=ALU.add), reads=[B_t], writes=[B_ob])
            for c in range(NT):
                P.op("act", "activation", A(out=junk, in_=obuf[:, c, :], func=AF.Square, accum_out=ss[:, c:c + 1]), reads=[B_ob], writes=[B_junk, B_ss])
            rstd_from_ss(ss[:, 0:NT], NT, 1.0 / 128, [B_ss])
            P.op("dve", "tensor_tensor", A(out=obuf, in0=obuf, in1=ss[:, 0:NT].unsqueeze(2).broadcast_to([128, NT, 128]), op=ALU.mult), reads=[B_ob, B_ss], writes=[B_ob])
            P.op("dve", "tensor_tensor", A(out=obuf, in0=obuf, in1=sm[:, SM_SUBG:SM_SUBG + 128].unsqueeze(1).broadcast_to([128, NT, 128]), op=ALU.mult), reads=[B_ob, B_sm], writes=[B_ob])
            P.op("dve", "tensor_tensor", A(out=ydt, in0=obuf, in1=Gs, op=ALU.mult), reads=[B_ob, B_Gs], writes=[B_ydt])
            for c in range(NT):
                transpose_to_YT(ydt[:, c, :], B_ydt, 1, h, 128 * c, 6 + c % 2)
        P.barrier()

    def ssm_phase(sq, part):
        t0 = part * T
        cv = Carver()
        Us = [cv.f32(T + 4), cv.f32(T + 4)]; B_Us = [Buf(), Buf()]
        acc = cv.f32(T); B_acc = Buf()
        xsT = cv.bf16(T); B_xsT = Buf()
        xs_tok = cv.bf16(NT * 256).rearrange("p (c f) -> p c f", c=NT); B_xs = Buf()
        BT = cv.bf16(T); B_BT = Buf()
        Btok = cv.bf16(NT * 128).rearrange("p (c f) -> p c f", c=NT); B_Btok = Buf()
        CT = cv.bf16(T); B_CT = Buf()
        zs = cv.f32(NT * 256).rearrange("p (c f) -> p c f", c=NT); B_zs = Buf()
        dt = cv.f32(NT * 32).rearrange("p (c f) -> p c f", c=NT)
        adt = cv.f32(NT * 32).rearrange("p (c f) -> p c f", c=NT); B_dt = Buf()
        E = cv.f32(NT * 96).rearrange("p (c f) -> p c f", c=NT); B_E = Buf()
        R = [cv.f32(512), cv.f32(512)]; B_R = [Buf(), Buf()]
        LT = [cv.bf16(512), cv.bf16(512)]; B_LT = [Buf(), Buf()]
        MT = [cv.bf16(512), cv.bf16(512)]; B_MT = [Buf(), Buf()]
        xdt = [cv.bf16(256), cv.bf16(256)]; xdd = [cv.bf16(256), cv.bf16(256)]; B_xd = [Buf(), Buf()]; B_xdd = [Buf(), Buf()]
        y1 = cv.f32(256); B_y1 = Buf()
        y2 = cv.f32(NT * 256).rearrange("p (c f) -> p c f", c=NT); B_y2c = [Buf() for _ in range(NT)]; B_y2 = B_y2c
        yn = acc.bitcast(BF16)[:, 0:NT * 256].rearrange("p (c f) -> p c f", c=NT); B_yn = B_acc
        st_b = cv.bf16(256); B_stb = Buf()
        stmp = cv.f32(256); B_stmp = Buf()
        dskI = cv.bf16(4 * 128).rearrange("p (h l) -> p h l", h=4); B_dskI = Buf()
        sg_b = cv.f32(256); B_sg = Buf()
        ss = cv.f32(NT); B_ss = Buf()
        junk = cv.f32(256); B_junk = Buf()
        print("ssm arena words", cv.off) if (sq == 0 and part == 0) else None

        wd, wdB = load_w(win(OFF_DT, 32), 8, 32)
        for c in range(NT):
            pi = c % 2
            proj_tm(pi, 0, wd, wdB, 0, 32, t0 + 128 * c)
            P.op("dve", "tensor_tensor", A(out=dt[:, c, :], in0=psum[pi][:, 0:32], in1=sm[:, SM_DTB:SM_DTB + 32], op=ALU.add), reads=[B_ps[pi], B_sm], writes=[B_dt])
        dtf = dt.rearrange("p c f -> p (c f)")
        P.op("act", "activation", A(out=dtf, in_=dtf, func=AF.Exp), reads=[B_dt], writes=[B_dt])
        P.op("act", "activation", A(out=dtf, in_=dtf, func=AF.Ln, bias=sm[:, SM_ONE:SM_ONE + 1]), reads=[B_dt, B_sm], writes=[B_dt])
        P.op("dve", "tensor_tensor", A(out=adt, in0=dt, in1=sm[:, SM_A:SM_A + 32].unsqueeze(1).broadcast_to([128, NT, 32]), op=ALU.mult), reads=[B_dt, B_sm], writes=[B_dt])
        for c in range(NT):
            pi = c % 2
            for i, cc in enumerate((C_TRIL, C_SUP, C_ONES)):
                P.op("pe", "matmul", A(psum[pi][:, 32 * i:32 * i + 32], lhsT=cst[:, cc:cc + 128], rhs=adt[:, c, :], start=True, stop=True),
                     reads=[B_cst, B_dt], writes=[B_ps[pi]])
            P.op("act", "activation", A(out=E[:, c, :], in_=psum[pi][:, 0:96], func=AF.Exp), reads=[B_ps[pi]], writes=[B_E])

        for g in range(8):
            if part == 0:
                P.op("pool", "memset", A(state_f[:, g, :], 0.0), writes=[B_state[g]])
            bcast_load(sg_b, sgain_d.ap()[:, 256 * g:256 * g + 256], [B_sg])
            for hh in range(4):
                P.op("dve", "tensor_scalar", A(out=dskI[:, hh, :], in0=cst[:, C_ID:C_ID + 128], scalar1=sm[:, SM_DSK + 4 * g + hh:SM_DSK + 4 * g + hh + 1], scalar2=None, op0=ALU.mult),
                     reads=[B_cst, B_sm], writes=[B_dskI])
            blocks = [("xs", 2 * g, OFF_XBC + 256 * g), ("xs", 2 * g + 1, OFF_XBC + 256 * g + 128),
                      ("B", 16 + g, OFF_XBC + 2048 + 128 * g), ("C", 24 + g, OFF_XBC + 3072 + 128 * g)]
            for bi_, (kind, blk, col) in enumerate(blocks):
                wt, wB = load_w(win(col, 128), 8, 128)
                U = Us[bi_ % 2]; B_U = B_Us[bi_ % 2]
                if part == 0:
                    P.op("pool", "memset", A(U[:, 0:4], 0.0), writes=[B_U])
                else:
                    P.op("pool", "tensor_copy", A(out=U[:, 0:4], in_=halo[:, blk, :]), reads=[B_halo], writes=[B_U])
                for tc in range(T // 512):
                    pi = tc % 2
                    proj_fm(pi, 0, wt, wB, 0, 128, t0 + 512 * tc, 512)
                    evac(U[:, 4 + 512 * tc:4 + 512 * tc + 512], psum[pi][:, :], [B_ps[pi]], [B_U])
                P.op("pool", "tensor_copy", A(out=halo[:, blk, :], in_=U[:, T:T + 4]), reads=[B_U], writes=[B_halo])
                P.op("dve", "tensor_scalar", A(out=acc, in0=U[:, 4:4 + T], scalar1=cw[:, blk, 3:4], scalar2=cb[:, blk:blk + 1], op0=ALU.mult, op1=ALU.add),
                     reads=[B_U, B_cw], writes=[B_acc])
                for k in (2, 1, 0):
                    P.op("dve", "scalar_tensor_tensor", A(out=acc, in0=U[:, 1 + k:1 + k + T], scalar=cw[:, blk, k:k + 1], in1=acc, op0=ALU.mult, op1=ALU.add),
                         reads=[B_U, B_cw, B_acc], writes=[B_acc])
                if kind == "C":
                    P.op("act", "activation", A(out=CT, in_=acc, func=AF.Silu), reads=[B_acc], writes=[B_CT])
                elif kind == "B":
                    P.op("act", "activation", A(out=BT, in_=acc, func=AF.Silu), reads=[B_acc], writes=[B_BT])
                    for c in range(NT):
                        P.op("pe", "transpose", A(out=psb(6)[:, 128 * c:128 * c + 128], in_=BT[:, 128 * c:128 * c + 128], identity=identb[:]),
                             reads=[B_BT, B_cst], writes=[B_ps[6]])
                    evac(Btok, psb(6)[:, 0:128 * NT].rearrange("p (c f) -> p c f", c=NT), [B_ps[6]], [B_Btok])
                else:
                    jx = bi_
                    P.op("act", "activation", A(out=xsT, in_=acc, func=AF.Silu), reads=[B_acc], writes=[B_xsT])
                    for c in range(NT):
                        P.op("pe", "transpose", A(out=psb(7)[:, 128 * c:128 * c + 128], in_=xsT[:, 128 * c:128 * c + 128], identity=identb[:]),
                             reads=[B_xsT, B_cst], writes=[B_ps[7]])
                    evac(xs_tok[:, :, 128 * jx:128 * jx + 128], psb(7)[:, 0:128 * NT].rearrange("p (c f) -> p c f", c=NT), [B_ps[7]], [B_xs])
            wz, wzB = load_w(win(OFF_Z + 256 * g, 256), 8, 256)
            for c in range(NT):
                pi = c % 2
                proj_tm(pi, 0, wz, wzB, 0, 256, t0 + 128 * c)
                P.op("act", "activation", A(out=zs[:, c, :], in_=psum[pi][:, 0:256], func=AF.Silu), reads=[B_ps[pi]], writes=[B_zs])
            P.op("act", "copy", A(out=st_b, in_=state_f[:, g, :]), reads=[B_state[g]], writes=[B_stb])
            CBb = (1, 5)

            def S1(c):
                i2 = c % 2
                tk = slice(128 * c, 128 * c + 128)
                ag = adt[:, c, 4 * g:4 * g + 4]
                P.op("dve", "tensor_tensor", A(out=R[i2].rearrange("p (h l) -> p h l", h=4), in0=cst[:, C_TRIL:C_TRIL + 128].unsqueeze(1).broadcast_to([128, 4, 128]),
                                               in1=ag.unsqueeze(2).broadcast_to([128, 4, 128]), op=ALU.mult),
                     reads=[B_cst, B_dt], writes=[B_R[i2]])
                P.op("pe", "matmul", A(psum[0][:, :], lhsT=cst[:, C_SUP:C_SUP + 128], rhs=R[i2], start=True, stop=False), reads=[B_cst, B_R[i2]], writes=[B_ps[0]])
                P.op("pe", "matmul", A(psum[0][:, :], lhsT=cst[:, C_NEGI:C_NEGI + 128], rhs=cst[:, C_SLOW4:C_SLOW4 + 512], start=False, stop=True), reads=[B_cst], writes=[B_ps[0]])
                P.op("act", "activation", A(out=LT[i2], in_=psum[0][:, :], func=AF.Exp), reads=[B_ps[0]], writes=[B_LT[i2]])
                P.op("pe", "matmul", A(psum[CBb[i2]][:, 0:128], lhsT=BT[:, tk], rhs=CT[:, tk], start=True, stop=True), reads=[B_BT, B_CT], writes=[B_ps[CBb[i2]]])

            def S2(c):
                i2 = c % 2
                P.op("dve", "tensor_tensor", A(out=xdt[i2].rearrange("p (h q) -> p h q", h=4), in0=xs_tok[:, c, :].rearrange("p (h q) -> p h q", h=4),
                                               in1=dt[:, c, 4 * g:4 * g + 4].unsqueeze(2).broadcast_to([128, 4, 64]), op=ALU.mult),
                     reads=[B_xs, B_dt], writes=[B_xd[i2]])
                P.op("dve", "tensor_tensor", A(out=xdd[i2].rearrange("p (h q) -> p h q", h=4), in0=xdt[i2].rearrange("p (h q) -> p h q", h=4),
                                               in1=E[:, c, 32 + 4 * g:32 + 4 * g + 4].unsqueeze(2).broadcast_to([128, 4, 64]), op=ALU.mult),
                     reads=[B_xd[i2], B_E], writes=[B_xdd[i2]])
                P.op("dve", "tensor_tensor", A(out=MT[i2].rearrange("p (h l) -> p h l", h=4), in0=LT[i2].rearrange("p (h l) -> p h l", h=4),
                                               in1=psum[CBb[i2]][:, 0:128].unsqueeze(1).broadcast_to([128, 4, 128]), op=ALU.mult),
                     reads=[B_LT[i2], B_ps[CBb[i2]]], writes=[B_MT[i2]])

            def S3(c):
                i2 = c % 2
                tk = slice(128 * c, 128 * c + 128)
                for hh in range(4):
                    P.op("pe", "matmul", A(psum[2][:, 64 * hh:64 * hh + 64], lhsT=MT[i2][:, 128 * hh:128 * hh + 128], rhs=xdt[i2][:, 64 * hh:64 * hh + 64], start=True, stop=False),
                         reads=[B_MT[i2], B_xd[i2]], writes=[B_ps[2]])
                    P.op("pe", "matmul", A(psum[2][:, 64 * hh:64 * hh + 64], lhsT=dskI[:, hh, :], rhs=xs_tok[:, c, 64 * hh:64 * hh + 64], start=False, stop=True),
                         reads=[B_dskI, B_xs], writes=[B_ps[2]])
                P.op("pe", "matmul", A(psum[4][:, 0:256], lhsT=Btok[:, c, :], rhs=xdd[i2], start=True, stop=True), reads=[B_Btok, B_xdd[i2]], writes=[B_ps[4]])
                P.op("pe", "matmul", A(psum[3][:, 0:256], lhsT=CT[:, tk], rhs=st_b, start=True, stop=True), reads=[B_CT, B_stb], writes=[B_ps[3]])
                P.op("dve", "tensor_tensor", A(out=y1.rearrange("p (h q) -> p h q", h=4), in0=psum[3][:, 0:256].rearrange("p (h q) -> p h q", h=4),
                                               in1=E[:, c, 4 * g:4 * g + 4].unsqueeze(2).broadcast_to([128, 4, 64]), op=ALU.mult),
                     reads=[B_ps[3], B_E], writes=[B_y1])
                P.op("dve", "tensor_tensor", A(out=stmp.rearrange("p (h q) -> p h q", h=4), in0=state_f[:, g, :].rearrange("p (h q) -> p h q", h=4),
                                               in1=E[:, c, 64 + 4 * g:64 + 4 * g + 4].unsqueeze(2).broadcast_to([128, 4, 64]), op=ALU.mult),
                     reads=[B_state[g], B_E], writes=[B_stmp])
                P.op("dve", "tensor_tensor", A(out=state_f[:, g, :], in0=stmp, in1=psum[4][:, 0:256], op=ALU.add), reads=[B_stmp, B_ps[4]], writes=[B_state[g]])
                P.op("act", "copy", A(out=st_b, in_=state_f[:, g, :]), reads=[B_state[g]], writes=[B_stb])
                P.op("dve", "tensor_tensor", A(out=y2[:, c, :], in0=y1, in1=psum[2][:, 0:256], op=ALU.add), reads=[B_y1, B_ps[2]], writes=[B_y2c[c]])

            S1(0)
            if NT > 1:
                S1(1)
            S2(0)
            for c in range(NT):
                if c + 2 < NT:
                    S1(c + 2)
                if c + 1 < NT:
                    S2(c + 1)
                S3(c)
            P.op("dve", "tensor_tensor", A(out=y2, in0=y2, in1=zs, op=ALU.mult), reads=B_y2 + [B_zs], writes=B_y2)
            for c in range(NT):
                P.op("act", "activation", A(out=junk, in_=y2[:, c, :], func=AF.Square, accum_out=ss[:, c:c + 1]), reads=B_y2, writes=[B_junk, B_ss])
            rstd_from_ss(ss[:, 0:NT], NT, 1.0 / 256, [B_ss])
            P.op("dve", "tensor_tensor", A(out=y2, in0=y2, in1=ss[:, 0:NT].unsqueeze(2).broadcast_to([128, NT, 256]), op=ALU.mult), reads=B_y2 + [B_ss], writes=B_y2)
            P.op("dve", "tensor_tensor", A(out=yn, in0=y2, in1=sg_b.unsqueeze(1).broadcast_to([128, NT, 256]), op=ALU.mult), reads=B_y2 + [B_sg], writes=[B_yn])
            for c in range(NT):
                transpose_to_YT(yn[:, c, :], B_yn, 2, 2 * g, 128 * c, 6 + c % 2)
        P.barrier()

    def merge_phase(sq, part, bi, wbr_d, nkb, first):
        t0 = part * T
        cv = Carver()
        sig = [cv.f32(512), cv.f32(512)]; B_sig = [Buf(), Buf()]
        tmp = cv.f32(512); B_tmp = Buf()
        it = 0
        for ob in range(8):
            wb_, wbB = load_w(wbr_d.ap()[:, 128 * ob:128 * ob + 128], nkb, 128)
            wg, wgB = load_w(win(OFF_GATE + 1024 * bi + 128 * ob, 128), 8, 128)
            for tc in range(T // 512):
                pa, pg = 2 * (it % 2), 2 * (it % 2) + 1
                si = it % 2; it += 1
                for kb in range(nkb):
                    P.op("pe", "matmul", A(psum[pa][:, :], lhsT=wb_[:, kb, :], rhs=YT[:, kb, 512 * tc:512 * tc + 512], start=(kb == 0), stop=(kb == nkb - 1)),
                         reads=[wbB, B_YT], writes=[B_ps[pa]])
                proj_fm(pg, 0, wg, wgB, 0, 128, t0 + 512 * tc, 512)
                P.op("act", "activation", A(out=sig[si], in_=psum[pg][:, :], func=AF.Sigmoid), reads=[B_ps[pg]], writes=[B_sig[si]])
                dst = mrg[:, ob, 512 * tc:512 * tc + 512]
                if first:
                    P.op("dve", "tensor_tensor", A(out=dst, in0=sig[si], in1=psum[pa][:, :], op=ALU.mult), reads=[B_sig[si], B_ps[pa]], writes=[B_mrg])
                else:
                    P.op("dve", "tensor_tensor", A(out=tmp, in0=sig[si], in1=psum[pa][:, :], op=ALU.mult), reads=[B_sig[si], B_ps[pa]], writes=[B_tmp])
                    P.op("dve", "tensor_tensor", A(out=dst, in0=dst, in1=tmp, op=ALU.add), reads=[B_tmp, B_mrg], writes=[B_mrg])
        P.barrier()

    def out_phase(sq, part):
        t0 = part * T
        cv = Carver()
        fg_b = cv.f32(1024); B_fg = Buf()
        xt = [cv.f32(1024), cv.f32(1024)]; B_xt = [Buf(), Buf()]
        r = [cv.f32(1024), cv.f32(1024)]; B_r = [Buf(), Buf()]
        junk = cv.f32(1024); B_junk = Buf()
        ss = cv.f32(2); B_ss = Buf()
        bcast_load(fg_b, fgain_d.ap(), [B_fg])
        wts = [load_w(w_out.ap()[:, 256 * i:256 * i + 256], 8, 256) for i in range(4)]
        for c in range(NT):
            xi = c % 2
            rows = slice(t0 + 128 * c, t0 + 128 * c + 128)
            P.dma("sp", "dma_start", A(out=xt[xi], in_=x_d.ap()[sq, rows, :]), writes=[B_xt[xi]])
            for hf in range(2):
                pi = 2 * xi + hf
                for q4 in range(2):
                    wt, wB = wts[2 * hf + q4]
                    for kb in range(8):
                        P.op("pe", "matmul", A(psum[pi][:, 256 * q4:256 * q4 + 256], lhsT=mrg[:, kb, 128 * c:128 * c + 128], rhs=wt[:, kb, :], start=(kb == 0), stop=(kb == 7)),
                             reads=[wB, B_mrg], writes=[B_ps[pi]])
                P.op("dve", "tensor_tensor", A(out=r[xi][:, 512 * hf:512 * hf + 512], in0=xt[xi][:, 512 * hf:512 * hf + 512], in1=psum[pi][:, :], op=ALU.add),
                     reads=[B_xt[xi], B_ps[pi]], writes=[B_r[xi]])
            P.op("act", "activation", A(out=junk, in_=r[xi], func=AF.Square, accum_out=ss[:, 0:1]), reads=[B_r[xi]], writes=[B_junk, B_ss])
            rstd_from_ss(ss[:, 0:1], 1, 1.0 / D, [B_ss])
            P.op("dve", "scalar_tensor_tensor", A(out=r[xi], in0=r[xi], scalar=ss[:, 0:1], in1=fg_b, op0=ALU.mult, op1=ALU.mult), reads=[B_r[xi], B_ss, B_fg], writes=[B_r[xi]])
            P.dma("sp", "dma_start", A(out=out_d.ap()[sq, rows, :], in_=r[xi]), reads=[B_r[xi]])
        P.barrier()

    for sq in range(nseq):
        prologue(sq)
        for part in range(NPART):
            first = True
            for bi, (name, fn, wbr, nkb) in enumerate((("ssm", ssm_phase, w_brs, 16), ("diff", diff_phase, w_brd, 8), ("mem", mem_phase, w_brm, 8))):
                if name not in branches:
                    continue
                fn(sq, part)
                merge_phase(sq, part, bi, wbr, nkb, first)
                first = False
            out_phase(sq, part)
    P.finish()
    P.emit()
    return nc, P


_CACHE = {}


def kernel(**inputs):
    ncores = _CFG["ncores"]; nseq = _CFG["nseq"]
    key = (nseq, tuple(_CFG["branches"]), _CFG["same_engine_sync"])
    if key not in _CACHE:
        _CACHE[key] = build(nseq, _CFG["branches"], _CFG["same_engine_sync"])
    nc, P = _CACHE[key]
    f = lambda a: np.ascontiguousarray(np.asarray(a, dtype=np.float32))
    cst, oh = host_consts()
    shared = {
        "w_in": f(inputs["w_in"][0]), "w_mem_kv": f(inputs["w_mem_kv"][0]), "w_br_ssm": f(inputs["w_br_ssm"][0]),
        "w_br_diff": f(inputs["w_br_diff"][0]), "w_br_mem": f(inputs["w_br_mem"][0]), "w_out": f(inputs["w_out"][0]),
        "norm_gain": f(inputs["norm_gain"]).reshape(1, D), "mem_norm_gain": f(inputs["mem_norm_gain"]).reshape(1, D),
        "final_norm_gain": f(inputs["final_norm_gain"]).reshape(1, D), "ssm_norm_gain": f(inputs["ssm_norm_gain"]).reshape(1, 2048),
        "subln_gain": f(inputs["subln_gain"]).reshape(1, 128), "dt_bias": f(inputs["dt_bias"]).reshape(1, 32),
        "a_log": f(inputs["a_log"]).reshape(1, 32), "d_skip": f(inputs["d_skip"]).reshape(1, 32),
        "lambda_q1": f(inputs["lambda_q1"]).reshape(1, 64), "lambda_k1": f(inputs["lambda_k1"]).reshape(1, 64),
        "lambda_q2": f(inputs["lambda_q2"]).reshape(1, 64), "lambda_k2": f(inputs["lambda_k2"]).reshape(1, 64),
        "rel_bias": f(inputs["rel_bias"]),
        "conv_w_l": f(np.asarray(inputs["conv_w"][0]).reshape(4, 32, 128).transpose(2, 1, 0)),
        "conv_b_l": f(np.asarray(inputs["conv_b"][0]).reshape(32, 128).transpose(1, 0)),
        "cst": cst, "onehot": oh,
    }
    x = np.asarray(inputs["x"]); mem = np.asarray(inputs["mem"])
    in_maps = []
    for c in range(ncores):
        m = dict(shared)
        m["x"] = f(x[c * nseq:(c + 1) * nseq])
        m["mem"] = f(mem[c * nseq:(c + 1) * nseq])
        in_maps.append(m)
    if _CFG.get("trace"):
        res = run_bass_kernel_spmd(nc, in_maps, core_ids=list(range(ncores)), trace=True)
        print("EXEC_TIME_NS", res.exec_time_ns)
    else:
        res = run_bass_kernel_spmd(nc, in_maps, core_ids=list(range(ncores)))
    return np.concatenate([np.asarray(r["out"]) for r in res.results], axis=0).astype(np.float32)
```

```python
import math
import numpy as np
import concourse.bass as bass
import concourse.mybir as mybir
from concourse.bass_utils import run_bass_kernel_spmd

F32 = mybir.dt.float32
BF16 = mybir.dt.bfloat16
ALU = mybir.AluOpType
AF = mybir.ActivationFunctionType

D = 1024
S = 2048
MEM = 256
IN_DIM = 15392
OFF_Z, OFF_XBC, OFF_DT, OFF_DQ, OFF_DK, OFF_DV, OFF_DG, OFF_MQ, OFF_MG, OFF_GATE = (
    0, 2048, 6144, 6176, 7200, 8224, 9248, 10272, 11296, 12320)
EPS = 1e-5
NEG = -30000.0
T = 1024
NT = T // 128
NPART = S // T

_CFG = {"branches": ("ssm", "diff", "mem"), "ncores": 8, "nseq": 4, "same_engine_sync": "raw", "schedule": True}

COMPUTE = ("pe", "act", "dve", "pool")
QUEUES = ("pe", "act", "dve", "pool", "sp")
DMAQ = ("sp", "pool", "act")
NDMASEM = 12


def A(*a, **k):
    return (a, k)


class Buf:
    __slots__ = ("name", "last_w", "readers")

    def __init__(self, name=""):
        self.name = name
        self.last_w = None
        self.readers = []


def _free(ap):
    n = 1
    for d in ap.shape[1:]:
        n *= int(d)
    return n


class Prog:
    def __init__(self, nc, same_engine_sync="raw", schedule=True):
        self.nc = nc
        self.ins = []
        self.segs = [[]]
        self.same_engine_sync = same_engine_sync
        self.schedule = schedule
        self.sem = {e: nc.alloc_semaphore("s_" + e) for e in COMPUTE}
        self.dsem = {e: [nc.alloc_semaphore(f"d_{e}{i}") for i in range(NDMASEM)] for e in DMAQ}

    def _deps(self, reads, writes):
        deps = set()
        raw = set()
        for b in reads:
            if b.last_w is not None:
                deps.add(b.last_w)
                raw.add(b.last_w)
        for b in writes:
            if b.last_w is not None:
                deps.add(b.last_w)
            deps.update(b.readers)
        return deps, raw

    def _commit(self, me, reads, writes):
        for b in reads:
            b.readers.append(me)
        for b in writes:
            b.last_w = me
            b.readers = []

    def _est(self, eng, meth, args, is_dma):
        a, k = args
        try:
            out = k.get("out", a[0] if a else None)
            n = _free(out)
            if is_dma:
                return 60.0, 2000.0 + n * (4 if out.dtype == F32 else 2) * 128 / 150.0
            if eng == "pe":
                if meth == "transpose":
                    return 110.0, 0.0
                f32 = k["lhsT"].dtype == F32
                return max(64.0, n * 0.42) * (4 if f32 else 1), 0.0
            if eng == "act":
                return 224.0 + 0.75 * n, 0.0
            if eng == "dve":
                return 60.0 + 1.05 * n, 0.0
            return 100.0 + 2.3 * n, 0.0
        except Exception:
            return 300.0, 0.0

    def _add(self, eng, meth, args, reads, writes, is_dma):
        deps, raw = self._deps(reads, writes)
        gid = len(self.ins)
        dur, lat = self._est(eng, meth, args, is_dma)
        tab = None
        if eng == "act" and meth == "activation":
            f = args[1].get("func")
            tab = "explog" if f in (AF.Exp, AF.Ln) else str(f)
        self.ins.append(dict(eng=eng, fn=(meth, args), deps=deps, raw=raw, dma=is_dma, dur=dur, lat=lat, tab=tab))
        self.segs[-1].append(gid)
        self._commit(gid, reads, writes)
        return gid

    def op(self, eng, meth, args, reads=(), writes=()):
        return self._add(eng, meth, args, reads, writes, False)

    def dma(self, eng, meth, args, reads=(), writes=()):
        return self._add(eng, meth, args, reads, writes, True)

    def barrier(self):
        self.segs.append([])

    def finish(self):
        self.segs.append([])

    def _sched(self, seg):
        ins = self.ins
        pend = {e: [] for e in QUEUES}
        for g in seg:
            pend[ins[g]["eng"]].append(g)
        if not self.schedule:
            return pend
        segset = set(seg)
        W = 24
        fin = {}
        nun = {}
        users = {}
        for g in seg:
            c = 0
            for d in ins[g]["deps"]:
                if d in segset:
                    c += 1
                    users.setdefault(d, []).append(g)
            nun[g] = c
        free = {e: 0.0 for e in QUEUES}
        curtab = [None]
        order = {e: [] for e in QUEUES}
        cand = {e: None for e in QUEUES}
        dirty = set(QUEUES)
        remaining = len(seg)
        while remaining:
            for e in list(dirty):
                best = None
                lst = pend[e]
                for g in lst[:W]:
                    if nun[g]:
                        continue
                    I = ins[g]
                    r = 0.0
                    for d in I["deps"]:
                        if d in fin:
                            t = fin[d] + (150.0 if ins[d]["eng"] != e else 0.0)
                            if t > r:
                                r = t
                    st = r if r > free[e] else free[e]
                    if e == "act" and I["tab"] is not None and I["tab"] != curtab[0]:
                        st += 1300.0
                    if best is None or st < best[0]:
                        best = (st, g)
                cand[e] = best
            dirty.clear()
            be = None
            for e in QUEUES:
                c = cand[e]
                if c is not None and (be is None or c[0] < cand[be][0]):
                    be = e
            if be is None:
                raise RuntimeError("scheduler deadlock")
            st, g = cand[be]
            I = ins[g]
            if be == "act" and I["tab"] is not None:
                curtab[0] = I["tab"]
            free[be] = st + I["dur"]
            fin[g] = st + I["dur"] + I["lat"]
            pend[be].remove(g)
            order[be].append(g)
            remaining -= 1
            dirty.add(be)
            for u in users.get(g, ()):
                nun[u] -= 1
                if nun[u] == 0:
                    dirty.add(ins[u]["eng"])
        return order

    def emit(self):
        nc = self.nc
        ins = self.ins
        stream = {e: [] for e in QUEUES}
        pos = {}
        dma_list = {e: [] for e in DMAQ}

        def tails():
            t = set()
            for e in QUEUES:
                for x in reversed(stream[e]):
                    if not isinstance(x, tuple) and not ins[x]["dma"]:
                        t.add(x)
                        break
            for e in DMAQ:
                t.update(dma_list[e][-NDMASEM:])
            return t

        nseg = len(self.segs)
        for si, seg in enumerate(self.segs):
            if seg:
                order = self._sched(seg)
                for e in QUEUES:
                    for g in order[e]:
                        I = ins[g]
                        if I["dma"]:
                            dl = dma_list[e]
                            I["k"] = len(dl)
                            if len(dl) >= NDMASEM:
                                I["deps"] = set(I["deps"]) | {dl[-NDMASEM]}
                            dl.append(g)
                        pos[g] = len(stream[e])
                        stream[e].append(g)
            if si < nseg - 1:
                t = tails()
                last = (si == nseg - 2)
                for e in (("sp",) if last else QUEUES):
                    stream[e].append(("wait", t))
        signal = set()
        plan = {e: [] for e in QUEUES}
        for e in QUEUES:
            seen = {x: -1 for x in COMPUTE}
            seen_slot = {}
            for ent in stream[e]:
                if isinstance(ent, tuple):
                    deps, raw, g = ent[1], ent[1], None
                else:
                    g = ent
                    deps, raw = ins[g]["deps"], ins[g]["raw"]
                best = {}
                for d in deps:
                    D = ins[d]
                    if D["dma"]:
                        slot = (D["eng"], D["k"] % NDMASEM)
                        if seen_slot.get(slot, -1) >= D["k"] // NDMASEM:
                            continue
                        if slot not in best or ins[best[slot]]["k"] < D["k"]:
                            best[slot] = d
                    else:
                        x = D["eng"]
                        if x == e and (x == "pe" or not self.same_engine_sync):
                            continue
                        if x == e and self.same_engine_sync == "raw" and d not in raw:
                            continue
                        if seen[x] >= pos[d]:
                            continue
                        if x not in best or pos[best[x]] < pos[d]:
                            best[x] = d
                final = list(best.values())
                for d in final:
                    D = ins[d]
                    if D["dma"]:
                        seen_slot[(D["eng"], D["k"] % NDMASEM)] = D["k"] // NDMASEM
                    else:
                        seen[D["eng"]] = pos[d]
                        signal.add(d)
                plan[e].append((final, g))
        count = {}
        for e in COMPUTE:
            c = 0
            for ent in stream[e]:
                if not isinstance(ent, tuple) and ent in signal:
                    c += 1
                    count[ent] = c
        self.stats = {e: len(stream[e]) for e in QUEUES}

        def replay(e, eng):
            for final, g in plan[e]:
                for d in final:
                    D = ins[d]
                    if D["dma"]:
                        eng.wait_ge(self.dsem[D["eng"]][D["k"] % NDMASEM], 16 * (D["k"] // NDMASEM + 1))
                    else:
                        eng.wait_ge(self.sem[D["eng"]], count[d])
                if g is None:
                    continue
                I = ins[g]
                fn = I["fn"]
                o = getattr(eng, fn[0])(*fn[1][0], **fn[1][1])
                if I["dma"]:
                    o.then_inc(self.dsem[e][I["k"] % NDMASEM], 16)
                elif g in signal:
                    o.then_inc(self.sem[e], 1)

        with nc.Block() as block:
            @block.tensor
            def _(eng):
                replay("pe", eng)

            @block.scalar
            def _(eng):
                replay("act", eng)

            @block.vector
            def _(eng):
                replay("dve", eng)

            @block.gpsimd
            def _(eng):
                replay("pool", eng)

            @block.sync
            def _(eng):
                replay("sp", eng)


def t5_bucket_np(rel):
    n = np.maximum(rel, 0)
    max_exact = 16
    nf = np.maximum(n, 1).astype(np.float32)
    large = max_exact + (np.log(nf / max_exact) / np.float32(math.log(128 / max_exact))
                         * (32 - max_exact)).astype(np.int32)
    large = np.minimum(large, 31)
    return np.where(n < max_exact, n, large)


def host_consts():
    k = np.arange(128)[:, None]
    l = np.arange(128)[None, :]
    ident = (k == l).astype(np.float32)
    tril1 = (k <= l).astype(np.float32)
    sup = (k > l).astype(np.float32)
    ones = np.ones((128, 128), np.float32)
    negi = ident * NEG
    slow4 = np.tile(sup, (1, 4))
    cst = np.concatenate([ident, tril1, sup, ones, negi, slow4], axis=1)
    oh = np.zeros((33, 2, 128, 128), np.float32)
    for dlt in range(2):
        rel = 128 * dlt + (l - k)
        b = t5_bucket_np(rel)
        for kk in range(128):
            for qq in range(128):
                if rel[kk, qq] >= 0:
                    oh[b[kk, qq], dlt, kk, qq] = 1.0
                else:
                    oh[32, dlt, kk, qq] = 1.0
    return cst, oh.reshape(33, 2 * 128 * 128)


C_ID, C_TRIL, C_SUP, C_ONES, C_NEGI, C_SLOW4 = 0, 128, 256, 384, 512, 640


def build(nseq, branches, same_engine_sync="raw", schedule=True):
    nc = bass.Bass("TRN2", target_bir_lowering=False)
    P = Prog(nc, same_engine_sync, schedule)

    def din(name, shape, dt=F32):
        return nc.dram_tensor(name, list(shape), dt, kind="ExternalInput")

    x_d = din("x", [nseq, S, D])
    mem_d = din("mem", [nseq, MEM, D])
    w_in = din("w_in", [D, IN_DIM])
    w_kv = din("w_mem_kv", [D, 2048])
    w_brs = din("w_br_ssm", [2048, D])
    w_brd = din("w_br_diff", [D, D])
    w_brm = din("w_br_mem", [D, D])
    w_out = din("w_out", [D, D])
    gain_d = din("norm_gain", [1, D])
    mgain_d = din("mem_norm_gain", [1, D])
    fgain_d = din("final_norm_gain", [1, D])
    sgain_d = din("ssm_norm_gain", [1, 2048])
    subg_d = din("subln_gain", [1, 128])
    dtb_d = din("dt_bias", [1, 32])
    alog_d = din("a_log", [1, 32])
    dsk_d = din("d_skip", [1, 32])
    lq1_d, lk1_d, lq2_d, lk2_d = (din(n, [1, 64]) for n in ("lambda_q1", "lambda_k1", "lambda_q2", "lambda_k2"))
    rb_d = din("rel_bias", [32, 8])
    cw_d = din("conv_w_l", [128, 32, 4])
    cb_d = din("conv_b_l", [128, 32])
    cst_d = din("cst", [128, 1152])
    oh_d = din("onehot", [33, 32768])
    out_d = nc.dram_tensor("out", [nseq, S, D], F32, kind="ExternalOutput")
    bias_scr = nc.dram_tensor("bias_scr", [8, 32768], F32, kind="Internal")
    w_in_f, w_kv_f, w_brs_f, w_brd_f, w_brm_f, w_out_f = w_in, w_kv, w_brs, w_brd, w_brm, w_out
    w_in = nc.dram_tensor("w_in_b", [D, IN_DIM], BF16, kind="Internal")
    w_kv = nc.dram_tensor("w_kv_b", [D, 2048], BF16, kind="Internal")
    w_brs = nc.dram_tensor("w_brs_b", [2048, D], BF16, kind="Internal")
    w_brd = nc.dram_tensor("w_brd_b", [D, D], BF16, kind="Internal")
    w_brm = nc.dram_tensor("w_brm_b", [D, D], BF16, kind="Internal")
    w_out = nc.dram_tensor("w_out_b", [D, D], BF16, kind="Internal")

    def sb(name, shape, dt=F32):
        return nc.alloc_sbuf_tensor("sb_" + name, list(shape), dt)

    cst = sb("cst", [128, 1152]); B_cst = Buf("cst")
    identb = sb("identb", [128, 128], BF16)
    trilb = sb("trilb", [128, 128], BF16)
    sm = sb("small", [128, 512]); B_sm = Buf("small")
    SM_CFAR, SM_DTB, SM_A, SM_DSK, SM_LAM, SM_NLAM, SM_ONE, SM_EPS = 0, 8, 40, 72, 104, 105, 106, 107
    SM_SUBG = 128
    SM_TMP = 256
    cw = sb("cw", [128, 32, 4]); cb = sb("cb", [128, 32]); B_cw = Buf("cw")
    hT = sb("hT", [128, 8, S], BF16)
    B_hT = [Buf(f"hT{i}") for i in range(S // 128)]
    kmT = sb("kmT", [128, 8, MEM], BF16); B_kmT = Buf("kmT")
    vma = sb("vma", [128, 2, 4, 258], BF16); B_vma = Buf("vma")
    YT = sb("YT", [128, 16, T], BF16); B_YT = Buf("YT")
    mrg = sb("mrg", [128, 8, T], BF16); B_mrg = Buf("mrg")
    state_f = sb("state_f", [128, 8, 256]); B_state = [Buf(f"st{g}") for g in range(8)]
    halo = sb("halo", [128, 32, 4]); B_halo = Buf("halo")
    NWB = 6
    wbf = [sb(f"wbf{i}", [128, 2048], BF16) for i in range(NWB)]; B_wbf = [Buf(f"wbf{i}") for i in range(NWB)]
    ARENA_W = 20000
    arena = sb("arena", [128, ARENA_W])
    arena_b = arena[:].bitcast(BF16)

    class Carver:
        def __init__(self):
            self.off = 0

        def f32(self, n):
            a = arena[:, self.off:self.off + n]
            self.off += n
            assert self.off <= ARENA_W, self.off
            return a

        def bf16(self, n):
            w = (n + 1) // 2
            a = arena_b[:, 2 * self.off:2 * self.off + n]
            self.off += w
            assert self.off <= ARENA_W, self.off
            return a

    psum = [nc.alloc_psum_tensor(f"ps{i}", [128, 512], F32) for i in range(8)]
    B_ps = [Buf(f"ps{i}") for i in range(8)]

    def psb(i):
        return psum[i][:].bitcast(BF16)

    wctr = [0, 0]

    def load_w(src_ap, nkb, ncols):
        assert nkb * ncols <= 2048
        bi = wctr[1] % NWB; wctr[1] += 1
        bf = wbf[bi][:, 0:nkb * ncols].rearrange("p (k c) -> p k c", k=nkb)
        src = src_ap.rearrange("(k p) c -> p k c", p=128)
        P.dma("sp", "dma_start", A(out=bf, in_=src), writes=[B_wbf[bi]])
        return bf, B_wbf[bi]

    def win(c0, ncols):
        return w_in.ap()[:, c0:c0 + ncols]

    def hbufs(t0, n):
        return B_hT[t0 // 128:(t0 + n + 127) // 128]

    def proj_fm(ps_i, ps_cols, wt, wB, c_lo, ncol, tok0, ntok):
        out = psum[ps_i][0:ncol, ps_cols:ps_cols + ntok]
        for kb in range(8):
            P.op("pe", "matmul", A(out, lhsT=wt[:, kb, c_lo:c_lo + ncol], rhs=hT[:, kb, tok0:tok0 + ntok],
                                                 start=(kb == 0), stop=(kb == 7)),
                 reads=[wB] + hbufs(tok0, ntok), writes=[B_ps[ps_i]])

    def proj_tm(ps_i, ps_cols, wt, wB, c_lo, ncol, tok0):
        out = psum[ps_i][:, ps_cols:ps_cols + ncol]
        for kb in range(8):
            P.op("pe", "matmul", A(out, lhsT=hT[:, kb, tok0:tok0 + 128], rhs=wt[:, kb, c_lo:c_lo + ncol],
                                                 start=(kb == 0), stop=(kb == 7)),
                 reads=[wB] + hbufs(tok0, 128), writes=[B_ps[ps_i]])

    evac_rr = [0]

    def evac(out, in_, reads, writes):
        evac_rr[0] += 1
        if evac_rr[0] % 2:
            P.op("act", "copy", A(out=out, in_=in_), reads=reads, writes=writes)
        else:
            P.op("dve", "tensor_copy", A(out=out, in_=in_), reads=reads, writes=writes)

    def bcast_load(dst, src_row_ap, writes):
        P.dma("sp", "dma_start", A(out=dst, in_=src_row_ap.partition_broadcast(128)), writes=writes)

    def rstd_from_ss(ss_ap, n, inv_count, Bs):
        P.op("dve", "tensor_scalar", A(out=ss_ap, in0=ss_ap, scalar1=inv_count, scalar2=EPS, op0=ALU.mult, op1=ALU.add),
             reads=Bs, writes=Bs)
        P.op("act", "activation", A(out=ss_ap, in_=ss_ap, func=AF.Sqrt), reads=Bs, writes=Bs)
        P.op("dve", "reciprocal", A(out=ss_ap, in_=ss_ap), reads=Bs, writes=Bs)

    P.dma("sp", "dma_start", A(out=cst[:], in_=cst_d.ap()), writes=[B_cst])
    P.op("dve", "tensor_copy", A(out=identb[:], in_=cst[:, C_ID:C_ID + 128]), reads=[B_cst], writes=[B_cst])
    P.op("dve", "tensor_copy", A(out=trilb[:], in_=cst[:, C_TRIL:C_TRIL + 128]), reads=[B_cst], writes=[B_cst])
    P.dma("sp", "dma_start", A(out=cw[:], in_=cw_d.ap()), writes=[B_cw])
    P.dma("sp", "dma_start", A(out=cb[:], in_=cb_d.ap()), writes=[B_cw])
    P.op("pool", "memset", A(sm[:], 0.0), writes=[B_sm])
    P.op("pool", "memset", A(sm[:, SM_ONE:SM_ONE + 1], 1.0), writes=[B_sm])
    P.op("pool", "memset", A(sm[:, SM_EPS:SM_EPS + 1], EPS), writes=[B_sm])
    bcast_load(sm[:, SM_CFAR:SM_CFAR + 8], rb_d.ap()[31:32, :], [B_sm])
    bcast_load(sm[:, SM_DTB:SM_DTB + 32], dtb_d.ap(), [B_sm])
    bcast_load(sm[:, SM_A:SM_A + 32], alog_d.ap(), [B_sm])
    bcast_load(sm[:, SM_DSK:SM_DSK + 32], dsk_d.ap(), [B_sm])
    bcast_load(sm[:, SM_SUBG:SM_SUBG + 128], subg_d.ap(), [B_sm])
    for i, ld in enumerate((lq1_d, lk1_d, lq2_d, lk2_d)):
        bcast_load(sm[:, SM_TMP + 64 * i:SM_TMP + 64 * i + 64], ld.ap(), [B_sm])
    Bs = [B_sm]
    P.op("act", "activation", A(out=sm[:, SM_A:SM_A + 32], in_=sm[:, SM_A:SM_A + 32], func=AF.Exp), reads=Bs, writes=Bs)
    P.op("dve", "tensor_scalar_mul", A(out=sm[:, SM_A:SM_A + 32], in0=sm[:, SM_A:SM_A + 32], scalar1=-1.0), reads=Bs, writes=Bs)
    LAM_INIT = 0.8 - 0.6 * math.exp(-0.3 * 0)
    P.op("dve", "tensor_scalar_mul", A(out=sm[:, SM_SUBG:SM_SUBG + 128], in0=sm[:, SM_SUBG:SM_SUBG + 128], scalar1=1.0 - LAM_INIT), reads=Bs, writes=Bs)
    for i in range(2):
        a0 = SM_TMP + 128 * i
        P.op("dve", "tensor_tensor", A(out=sm[:, a0:a0 + 64], in0=sm[:, a0:a0 + 64], in1=sm[:, a0 + 64:a0 + 128], op=ALU.mult), reads=Bs, writes=Bs)
        P.op("dve", "reduce_sum", A(out=sm[:, 110 + i:111 + i], in_=sm[:, a0:a0 + 64], axis=mybir.AxisListType.X), reads=Bs, writes=Bs)
    P.op("act", "activation", A(out=sm[:, 110:112], in_=sm[:, 110:112], func=AF.Exp), reads=Bs, writes=Bs)
    P.op("dve", "tensor_tensor", A(out=sm[:, SM_LAM:SM_LAM + 1], in0=sm[:, 110:111], in1=sm[:, 111:112], op=ALU.subtract), reads=Bs, writes=Bs)
    P.op("dve", "tensor_scalar_add", A(out=sm[:, SM_LAM:SM_LAM + 1], in0=sm[:, SM_LAM:SM_LAM + 1], scalar1=LAM_INIT), reads=Bs, writes=Bs)
    P.op("dve", "tensor_scalar_mul", A(out=sm[:, SM_NLAM:SM_NLAM + 1], in0=sm[:, SM_LAM:SM_LAM + 1], scalar1=-1.0), reads=Bs, writes=Bs)

    cv = Carver()
    NPS = 4
    pst = [cv.f32(2048) for _ in range(NPS)]; B_pst = [Buf() for _ in range(NPS)]
    pbf = [cv.bf16(2048) for _ in range(NPS)]; B_pbf = [Buf() for _ in range(NPS)]
    pc_i = 0
    for (srcw, dstw, rows, cols) in ((w_in_f, w_in, D, IN_DIM), (w_kv_f, w_kv, D, 2048), (w_brs_f, w_brs, 2048, D),
                                     (w_brd_f, w_brd, D, D), (w_brm_f, w_brm, D, D), (w_out_f, w_out, D, D)):
        for rb in range(rows // 128):
            for c0 in range(0, cols, 2048):
                n = min(2048, cols - c0)
                si = pc_i % NPS
                P.dma("sp", "dma_start", A(out=pst[si][:, 0:n], in_=srcw.ap()[128 * rb:128 * rb + 128, c0:c0 + n]), writes=[B_pst[si]])
                ce = ("pool", "act", "dve")[pc_i % 3]
                P.op(ce, "copy" if ce == "act" else "tensor_copy", A(out=pbf[si][:, 0:n], in_=pst[si][:, 0:n]), reads=[B_pst[si]], writes=[B_pbf[si]])
                P.dma("sp", "dma_start", A(out=dstw.ap()[128 * rb:128 * rb + 128, c0:c0 + n], in_=pbf[si][:, 0:n]), reads=[B_pbf[si]])
                pc_i += 1
    P.barrier()

    if "diff" in branches:
        cv = Carver()
        rbx = cv.f32(8)
        ohs = cv.f32(4096)
        stg = cv.f32(4096)
        B_rbx, B_ohs, B_stg, B_scr = Buf(), Buf(), Buf(), Buf()
        P.op("pool", "memset", A(rbx[32:33, :], NEG), writes=[B_rbx])
        P.dma("sp", "dma_start", A(out=rbx[0:32, :], in_=rb_d.ap()), writes=[B_rbx])
        for pc in range(8):
            P.dma("sp", "dma_start", A(out=ohs[0:33, :], in_=oh_d.ap()[:, 4096 * pc:4096 * pc + 4096]), writes=[B_ohs])
            for i in range(8):
                pi = i % 2
                P.op("pe", "matmul", A(psum[pi][0:8, :], lhsT=rbx[0:33, :], rhs=ohs[0:33, 512 * i:512 * i + 512], start=True, stop=True),
                     reads=[B_rbx, B_ohs], writes=[B_ps[pi]])
                evac(stg[0:8, 512 * i:512 * i + 512], psum[pi][0:8, :], [B_ps[pi]], [B_stg])
            P.dma("sp", "dma_start", A(out=bias_scr.ap()[:, 4096 * pc:4096 * pc + 4096], in_=stg[0:8, :]), reads=[B_stg], writes=[B_scr])
        P.barrier()

    def prologue(sq):
        cv = Carver()
        gain_b = cv.f32(1024); B_gain = Buf()
        xt = [cv.f32(1024), cv.f32(1024)]; B_xt = [Buf(), Buf()]
        junk = cv.f32(1024); B_junk = Buf()
        hb = cv.bf16(1024); B_hb = Buf()
        ss = cv.f32(2); B_ss = Buf()
        memT = cv.bf16(8 * MEM); B_memT = Buf()
        memT3 = memT.rearrange("p (k m) -> p k m", k=8)

        def norm_transpose(src_ap, i, dst3, dstB, col0):
            xi = i % 2
            P.dma("sp", "dma_start", A(out=xt[xi], in_=src_ap), writes=[B_xt[xi]])
            P.op("act", "activation", A(out=junk, in_=xt[xi], func=AF.Square, accum_out=ss[:, 0:1]), reads=[B_xt[xi]], writes=[B_junk, B_ss])
            rstd_from_ss(ss[:, 0:1], 1, 1.0 / D, [B_ss])
            P.op("dve", "scalar_tensor_tensor", A(out=hb, in0=xt[xi], scalar=ss[:, 0:1], in1=gain_b, op0=ALU.mult, op1=ALU.mult),
                 reads=[B_xt[xi], B_ss, B_gain], writes=[B_hb])
            for kb in range(8):
                P.op("pe", "transpose", A(out=psb(6)[:, 128 * kb:128 * kb + 128], in_=hb[:, 128 * kb:128 * kb + 128], identity=identb[:]),
                     reads=[B_hb, B_cst], writes=[B_ps[6]])
            evac(dst3[:, :, col0:col0 + 128], psb(6)[:, 0:1024].rearrange("p (k t) -> p k t", k=8), [B_ps[6]], dstB)

        bcast_load(gain_b, gain_d.ap(), [B_gain])
        for i in range(S // 128):
            norm_transpose(x_d.ap()[sq, 128 * i:128 * i + 128, :], i, hT, [B_hT[i]], 128 * i)
        if "mem" in branches:
            bcast_load(gain_b, mgain_d.ap(), [B_gain])
            for i in range(2):
                norm_transpose(mem_d.ap()[sq, 128 * i:128 * i + 128, :], i, memT3, [B_memT], 128 * i)
            for fb in range(8):
                wt, wB = load_w(w_kv.ap()[:, 128 * fb:128 * fb + 128], 8, 128)
                pi = fb % 2
                for kb in range(8):
                    P.op("pe", "matmul", A(psum[pi][:, 0:MEM], lhsT=wt[:, kb, :], rhs=memT3[:, kb, :], start=(kb == 0), stop=(kb == 7)),
                         reads=[wB, B_memT], writes=[B_ps[pi]])
                evac(kmT[:, fb, :], psum[pi][:, 0:MEM], [B_ps[pi]], [B_kmT])
            P.op("pool", "memset", A(vma[:, :, :, 256:258], 1.0), writes=[B_vma])
            for hh in range(4):
                wt, wB = load_w(w_kv.ap()[:, 1024 + 256 * hh:1024 + 256 * hh + 256], 8, 256)
                for mt in range(2):
                    pi = mt
                    for kb in range(8):
                        P.op("pe", "matmul", A(psum[pi][:, 0:256], lhsT=memT3[:, kb, 128 * mt:128 * mt + 128], rhs=wt[:, kb, :], start=(kb == 0), stop=(kb == 7)),
                             reads=[wB, B_memT], writes=[B_ps[pi]])
                    evac(vma[:, mt, hh, 0:256], psum[pi][:, 0:256], [B_ps[pi]], [B_vma])
        P.barrier()

    def transpose_to_YT(src_bf, srcB, nblk, yt_blk0, tokcol0, ps_i):
        for j in range(nblk):
            P.op("pe", "transpose", A(out=psb(ps_i)[:, 128 * j:128 * j + 128], in_=src_bf[:, 128 * j:128 * j + 128], identity=identb[:]),
                 reads=[srcB, B_cst], writes=[B_ps[ps_i]])
        evac(YT[:, yt_blk0:yt_blk0 + nblk, tokcol0:tokcol0 + 128], psb(ps_i)[:, 0:128 * nblk].rearrange("p (j t) -> p j t", j=nblk), [B_ps[ps_i]], [B_YT])

    def mem_phase(sq, part):
        t0 = part * T
        cv = Carver()
        QmT = cv.bf16(2 * T).rearrange("p (d t) -> p d t", d=2); B_Qm = Buf()
        Gm = cv.f32(NT * 256).rearrange("p (c f) -> p c f", c=NT); B_Gm = Buf()
        PmT = [cv.bf16(2 * 512).rearrange("p (m t) -> p m t", m=2) for _ in range(2)]; B_Pm = [Buf(), Buf()]
        rr = cv.f32(4); B_rr = Buf()
        ym = [cv.bf16(256), cv.bf16(256)]; B_ym = [Buf(), Buf()]
        it = 0
        for hh in range(4):
            wq, wqB = load_w(win(OFF_MQ + 256 * hh, 256), 8, 256)
            wg, wgB = load_w(win(OFF_MG + 256 * hh, 256), 8, 256)
            for db in range(2):
                for tc in range(T // 512):
                    pi = (2 * db + tc) % 2
                    proj_fm(pi, 0, wq, wqB, 128 * db, 128, t0 + 512 * tc, 512)
                    evac(QmT[:, db, 512 * tc:512 * tc + 512], psum[pi][:, :], [B_ps[pi]], [B_Qm])
            for c in range(NT):
                pi = 2 + c % 2
                proj_tm(pi, 0, wg, wgB, 0, 256, t0 + 128 * c)
                P.op("act", "activation", A(out=Gm[:, c, :], in_=psum[pi][:, 0:256], func=AF.Silu), reads=[B_ps[pi]], writes=[B_Gm])
            for tc in range(T // 512):
                pb = tc % 2
                for mt in range(2):
                    pi = mt
                    for db in range(2):
                        P.op("pe", "matmul", A(psum[pi][:, :], lhsT=kmT[:, 2 * hh + db, 128 * mt:128 * mt + 128], rhs=QmT[:, db, 512 * tc:512 * tc + 512], start=(db == 0), stop=(db == 1)),
                             reads=[B_kmT, B_Qm], writes=[B_ps[pi]])
                    P.op("act", "activation", A(out=PmT[pb][:, mt, :], in_=psum[pi][:, :], func=AF.Exp, scale=1.0 / 16.0), reads=[B_ps[pi]], writes=[B_Pm[pb]])
                for j in range(4):
                    pi = 2 + j % 2
                    for mt in range(2):
                        P.op("pe", "matmul", A(psum[pi][:, 0:257], lhsT=PmT[pb][:, mt, 128 * j:128 * j + 128], rhs=vma[:, mt, hh, 0:257], start=(mt == 0), stop=(mt == 1)),
                             reads=[B_Pm[pb], B_vma], writes=[B_ps[pi]])
                    yi = it % 2; it += 1
                    P.op("dve", "reciprocal", A(out=rr[:, 0:1], in_=psum[pi][:, 256:257]), reads=[B_ps[pi]], writes=[B_rr])
                    P.op("dve", "scalar_tensor_tensor", A(out=ym[yi], in0=psum[pi][:, 0:256], scalar=rr[:, 0:1], in1=Gm[:, 4 * tc + j, :], op0=ALU.mult, op1=ALU.mult),
                         reads=[B_ps[pi], B_rr, B_Gm], writes=[B_ym[yi]])
                    transpose_to_YT(ym[yi], B_ym[yi], 2, 2 * hh, 512 * tc + 128 * j, 6)
        P.barrier()

    def diff_phase(sq, part):
        t0 = part * T
        nk = t0 + T
        cv = Carver()
        KT = cv.bf16(S); B_KT = Buf()
        QT = cv.bf16(T); B_QT = Buf()
        Va = cv.bf16(16 * 130).rearrange("p (t v) -> p t v", t=16); B_Va = Buf()
        Gs = cv.f32(NT * 128).rearrange("p (c f) -> p c f", c=NT); B_Gs = Buf()
        tmpS = [cv.f32(256), cv.f32(256)]; B_tmpS = [Buf(), Buf()]
        PT = [cv.bf16(512), cv.bf16(512)]; B_PT = [Buf(), Buf()]
        Os = [cv.f32(4 * 129).rearrange("p (j v) -> p j v", j=4) for _ in range(2)]; B_Os = [Buf(), Buf()]
        obuf = cv.f32(NT * 128).rearrange("p (c f) -> p c f", c=NT); B_ob = Buf()
        t1 = cv.f32(512).rearrange("p (j v) -> p j v", j=4); t2 = cv.f32(512).rearrange("p (j v) -> p j v", j=4); B_t = Buf()
        rr = cv.f32(8); B_rr = Buf()
        ss = cv.f32(NT); B_ss = Buf()
        junk = cv.f32(128); B_junk = Buf()
        ydt = cv.bf16(NT * 128).rearrange("p (c f) -> p c f", c=NT); B_ydt = Buf()
        biasT = cv.f32(8 * 2 * 128).rearrange("p (h d q) -> p h d q", h=8, d=2); B_biasT = Buf()
        for h in range(8):
            P.dma("sp", "dma_start", A(out=biasT[:, h, :, :], in_=bias_scr.ap()[h, :].rearrange("(d k q) -> k d q", d=2, k=128)), writes=[B_biasT])
        if sq == 0 and part == 0:
            print("diff arena words", cv.off)
        P.op("pool", "memset", A(Va[:, :, 128:130], 1.0), writes=[B_Va])
        sidx = 0
        for h in range(8):
            wq, wqB = load_w(win(OFF_DQ + 128 * h, 128), 8, 128)
            wk, wkB = load_w(win(OFF_DK + 128 * h, 128), 8, 128)
            wv, wvB = load_w(win(OFF_DV + 128 * h, 128), 8, 128)
            wg, wgB = load_w(win(OFF_DG + 128 * h, 128), 8, 128)
            for kc in range(nk // 512):
                pi = kc % 2
                proj_fm(pi, 0, wk, wkB, 0, 128, 512 * kc, 512)
                evac(KT[:, 512 * kc:512 * kc + 512], psum[pi][:, :], [B_ps[pi]], [B_KT])
            for tc in range(T // 512):
                pi = tc % 2
                proj_fm(pi, 0, wq, wqB, 0, 128, t0 + 512 * tc, 512)
                evac(QT[:, 512 * tc:512 * tc + 512], psum[pi][:, :], [B_ps[pi]], [B_QT])
            for kt in range(nk // 128):
                pi = kt % 2
                proj_tm(pi, 0, wv, wvB, 0, 128, 128 * kt)
                evac(Va[:, kt, 0:128], psum[pi][:, 0:128], [B_ps[pi]], [B_Va])
            for c in range(NT):
                pi = c % 2
                proj_tm(pi, 0, wg, wgB, 0, 128, t0 + 128 * c)
                P.op("act", "activation", A(out=Gs[:, c, :], in_=psum[pi][:, 0:128], func=AF.Silu), reads=[B_ps[pi]], writes=[B_Gs])
            for qc in range(T // 512):
                qb0 = (t0 + 512 * qc) // 128
                for c in range(2):
                    r0 = 64 * c
                    nkb_ = qb0 + 4
                    sis = []
                    for kk in range(nkb_):
                        sis.append(sidx % 2); sidx += 1

                    def emit_S(kb_):
                        j0 = kb_ - qb0
                        jlo = max(j0, 0)
                        si = sis[kb_]
                        P.op("pe", "matmul", A(
                            psum[si][:, 128 * jlo:512], lhsT=KT[r0:r0 + 64, 128 * kb_:128 * kb_ + 128],
                            rhs=QT[r0:r0 + 64, 512 * qc + 128 * jlo:512 * qc + 512], start=True, stop=True),
                             reads=[B_KT, B_QT], writes=[B_ps[si]])

                    emit_S(0)
                    for kb_ in range(nkb_):
                        if kb_ + 1 < nkb_:
                            emit_S(kb_ + 1)
                        j0 = kb_ - qb0
                        jlo = max(j0, 0)
                        si = sis[kb_]
                        Sps = psum[si]
                        nsp = 0
                        for j in range(jlo, 4):
                            dl = j - j0
                            if dl <= 1:
                                P.op("dve", "scalar_tensor_tensor", A(
                                    out=tmpS[si][:, 128 * nsp:128 * nsp + 128], in0=Sps[:, 128 * j:128 * j + 128], scalar=0.125,
                                    in1=biasT[:, h, dl, :], op0=ALU.mult, op1=ALU.add),
                                     reads=[B_ps[si], B_biasT], writes=[B_tmpS[si]])
                                nsp += 1
                        if nsp:
                            P.op("act", "activation", A(out=PT[si][:, 128 * jlo:128 * (jlo + nsp)], in_=tmpS[si][:, 0:128 * nsp], func=AF.Exp),
                                 reads=[B_tmpS[si]], writes=[B_PT[si]])
                        jf = max(j0 + 2, 0)
                        if jf < 4:
                            P.op("act", "activation", A(out=PT[si][:, 128 * jf:512], in_=Sps[:, 128 * jf:512], func=AF.Exp, scale=0.125, bias=sm[:, SM_CFAR + h:SM_CFAR + h + 1]),
                                 reads=[B_ps[si], B_sm], writes=[B_PT[si]])
                        for j in range(jlo, 4):
                            P.op("pe", "matmul", A(psum[2 + j][:, 0:129], lhsT=PT[si][:, 128 * j:128 * j + 128], rhs=Va[:, kb_, 0:129],
                                                   start=(kb_ == 0), stop=(kb_ == qb0 + j)),
                                 reads=[B_PT[si], B_Va], writes=[B_ps[2 + j]])
                    for j in range(4):
                        evac(Os[c][:, j, :], psum[2 + j][:, 0:129], [B_ps[2 + j]], [B_Os[c]])
                P.op("dve", "reciprocal", A(out=rr[:, 0:4], in_=Os[0][:, :, 128]), reads=[B_Os[0]], writes=[B_rr])
                P.op("dve", "reciprocal", A(out=rr[:, 4:8], in_=Os[1][:, :, 128]), reads=[B_Os[1], B_rr], writes=[B_rr])
                P.op("dve", "tensor_scalar", A(out=rr[:, 4:8], in0=rr[:, 4:8], scalar1=sm[:, SM_NLAM:SM_NLAM + 1], scalar2=None, op0=ALU.mult), reads=[B_rr, B_sm], writes=[B_rr])
                P.op("dve", "tensor_tensor", A(out=t1, in0=Os[0][:, :, 0:128], in1=rr[:, 0:4].unsqueeze(2).broadcast_to([128, 4, 128]), op=ALU.mult), reads=[B_Os[0], B_rr, B_t], writes=[B_t])
                P.op("dve", "tensor_tensor", A(out=t2, in0=Os[1][:, :, 0:128], in1=rr[:, 4:8].unsqueeze(2).broadcast_to([128, 4, 128]), op=ALU.mult), reads=[B_Os[1], B_rr, B_t], writes=[B_t])
                P.op("dve", "tensor_tensor", A(out=obuf[:, 4 * qc:4 * qc + 4, :], in0=t1, in1=t2, op=ALU.add), reads=[B_t], writes=[B_ob])
            for c in range(NT):
                P.op("act", "activation", A(out=junk, in_=obuf[:, c, :], func=AF.Square, accum_out=ss[:, c:c + 1]), reads=[B_ob], writes=[B_junk, B_ss])
            rstd_from_ss(ss[:, 0:NT], NT, 1.0 / 128, [B_ss])
            P.op("dve", "tensor_tensor", A(out=obuf, in0=obuf, in1=ss[:, 0:NT].unsqueeze(2).broadcast_to([128, NT, 128]), op=ALU.mult), reads=[B_ob, B_ss], writes=[B_ob])
            P.op("dve", "tensor_tensor", A(out=obuf, in0=obuf, in1=sm[:, SM_SUBG:SM_SUBG + 128].unsqueeze(1).broadcast_to([128, NT, 128]), op=ALU.mult), reads=[B_ob, B_sm], writes=[B_ob])
            P.op("dve", "tensor_tensor", A(out=ydt, in0=obuf, in1=Gs, op=ALU.mult), reads=[B_ob, B_Gs], writes=[B_ydt])
            for c in range(NT):
                transpose_to_YT(ydt[:, c, :], B_ydt, 1, h, 128 * c, 6 + c % 2)
        P.barrier()

    def ssm_phase(sq, part):
        t0 = part * T
        cv = Carver()
        Us = [cv.f32(T + 4), cv.f32(T + 4)]; B_Us = [Buf(), Buf()]
        acc = cv.f32(T); B_acc = Buf()
        xsT = cv.bf16(T); B_xsT = Buf()
        xs_tok2 = [cv.bf16(NT * 256).rearrange("p (c f) -> p c f", c=NT) for _ in range(2)]; B_xs2 = [Buf(), Buf()]
        BT2 = [cv.bf16(T), cv.bf16(T)]; B_BT2 = [Buf(), Buf()]
        Btok2 = [cv.bf16(NT * 128).rearrange("p (c f) -> p c f", c=NT) for _ in range(2)]; B_Btok2 = [Buf(), Buf()]
        CT2 = [cv.bf16(T), cv.bf16(T)]; B_CT2 = [Buf(), Buf()]
        zs2 = [cv.bf16(NT * 256).rearrange("p (c f) -> p c f", c=NT) for _ in range(2)]; B_zs2 = [Buf(), Buf()]
        dskI2 = [cv.bf16(4 * 128).rearrange("p (h l) -> p h l", h=4) for _ in range(2)]; B_dskI2 = [Buf(), Buf()]
        sg_b2 = [cv.f32(256), cv.f32(256)]; B_sg2 = [Buf(), Buf()]
        dt = cv.f32(NT * 32).rearrange("p (c f) -> p c f", c=NT)
        adt = cv.f32(NT * 32).rearrange("p (c f) -> p c f", c=NT); B_dt = Buf()
        E = cv.f32(NT * 96).rearrange("p (c f) -> p c f", c=NT); B_E = Buf()
        R = [cv.f32(512), cv.f32(512)]; B_R = [Buf(), Buf()]
        LT = [cv.bf16(512), cv.bf16(512)]; B_LT = [Buf(), Buf()]
        MT = [cv.bf16(512), cv.bf16(512)]; B_MT = [Buf(), Buf()]
        xdt = [cv.bf16(256), cv.bf16(256)]; xdd = [cv.bf16(256), cv.bf16(256)]; B_xd = [Buf(), Buf()]; B_xdd = [Buf(), Buf()]
        y1 = cv.f32(256); B_y1 = Buf()
        y2 = cv.f32(NT * 256).rearrange("p (c f) -> p c f", c=NT); B_y2c = [Buf() for _ in range(NT)]; B_y2 = B_y2c
        yn = cv.bf16(NT * 256).rearrange("p (c f) -> p c f", c=NT); B_yn = Buf()
        st_b = cv.bf16(256); B_stb = Buf()
        stmp = cv.f32(256); B_stmp = Buf()
        ss = cv.f32(NT); B_ss = Buf()
        junk = cv.bf16(256); B_junk = Buf()
        if sq == 0 and part == 0:
            print("ssm arena words", cv.off)

        wd, wdB = load_w(win(OFF_DT, 32), 8, 32)
        for c in range(NT):
            pi = c % 2
            proj_tm(pi, 0, wd, wdB, 0, 32, t0 + 128 * c)
            P.op("dve", "tensor_tensor", A(out=dt[:, c, :], in0=psum[pi][:, 0:32], in1=sm[:, SM_DTB:SM_DTB + 32], op=ALU.add), reads=[B_ps[pi], B_sm], writes=[B_dt])
        dtf = dt.rearrange("p c f -> p (c f)")
        P.op("act", "activation", A(out=dtf, in_=dtf, func=AF.Exp), reads=[B_dt], writes=[B_dt])
        P.op("act", "activation", A(out=dtf, in_=dtf, func=AF.Ln, bias=sm[:, SM_ONE:SM_ONE + 1]), reads=[B_dt, B_sm], writes=[B_dt])
        P.op("dve", "tensor_tensor", A(out=adt, in0=dt, in1=sm[:, SM_A:SM_A + 32].unsqueeze(1).broadcast_to([128, NT, 32]), op=ALU.mult), reads=[B_dt, B_sm], writes=[B_dt])
        for c in range(NT):
            pi = c % 2
            for i, cc in enumerate((C_TRIL, C_SUP, C_ONES)):
                P.op("pe", "matmul", A(psum[pi][:, 32 * i:32 * i + 32], lhsT=cst[:, cc:cc + 128], rhs=adt[:, c, :], start=True, stop=True),
                     reads=[B_cst, B_dt], writes=[B_ps[pi]])
            P.op("act", "activation", A(out=E[:, c, :], in_=psum[pi][:, 0:96], func=AF.Exp), reads=[B_ps[pi]], writes=[B_E])

        def prep(g):
            q = g % 2
            xs_tok, BT, Btok, CT, zs, dskI, sg_b = xs_tok2[q], BT2[q], Btok2[q], CT2[q], zs2[q], dskI2[q], sg_b2[q]
            B_xs, B_BT, B_Btok, B_CT, B_zs, B_dskI, B_sg = B_xs2[q], B_BT2[q], B_Btok2[q], B_CT2[q], B_zs2[q], B_dskI2[q], B_sg2[q]
            if part == 0:
                P.op("pool", "memset", A(state_f[:, g, :], 0.0), writes=[B_state[g]])
            bcast_load(sg_b, sgain_d.ap()[:, 256 * g:256 * g + 256], [B_sg])
            for hh in range(4):
                P.op("pool", "tensor_scalar", A(out=dskI[:, hh, :], in0=cst[:, C_ID:C_ID + 128], scalar1=sm[:, SM_DSK + 4 * g + hh:SM_DSK + 4 * g + hh + 1], scalar2=None, op0=ALU.mult),
                     reads=[B_cst, B_sm], writes=[B_dskI])
            blocks = [("xs", 2 * g, OFF_XBC + 256 * g), ("xs", 2 * g + 1, OFF_XBC + 256 * g + 128),
                      ("B", 16 + g, OFF_XBC + 2048 + 128 * g), ("C", 24 + g, OFF_XBC + 3072 + 128 * g)]
            for bi_, (kind, blk, col) in enumerate(blocks):
                wt, wB = load_w(win(col, 128), 8, 128)
                U = Us[bi_ % 2]; B_U = B_Us[bi_ % 2]
                if part == 0:
                    P.op("pool", "memset", A(U[:, 0:4], 0.0), writes=[B_U])
                else:
                    P.op("pool", "tensor_copy", A(out=U[:, 0:4], in_=halo[:, blk, :]), reads=[B_halo], writes=[B_U])
                for tc in range(T // 512):
                    pi = 6 + tc % 2
                    proj_fm(pi, 0, wt, wB, 0, 128, t0 + 512 * tc, 512)
                    evac(U[:, 4 + 512 * tc:4 + 512 * tc + 512], psum[pi][:, :], [B_ps[pi]], [B_U])
                    yield
                P.op("pool", "tensor_copy", A(out=halo[:, blk, :], in_=U[:, T:T + 4]), reads=[B_U], writes=[B_halo])
                P.op("dve", "tensor_scalar", A(out=acc, in0=U[:, 4:4 + T], scalar1=cw[:, blk, 3:4], scalar2=cb[:, blk:blk + 1], op0=ALU.mult, op1=ALU.add),
                     reads=[B_U, B_cw], writes=[B_acc])
                for k in (2, 1, 0):
                    P.op("dve", "scalar_tensor_tensor", A(out=acc, in0=U[:, 1 + k:1 + k + T], scalar=cw[:, blk, k:k + 1], in1=acc, op0=ALU.mult, op1=ALU.add),
                         reads=[B_U, B_cw, B_acc], writes=[B_acc])
                    if k == 1:
                        yield
                if kind == "C":
                    P.op("act", "activation", A(out=CT, in_=acc, func=AF.Silu), reads=[B_acc], writes=[B_CT])
                elif kind == "B":
                    P.op("act", "activation", A(out=BT, in_=acc, func=AF.Silu), reads=[B_acc], writes=[B_BT])
                    for c in range(NT):
                        P.op("pe", "transpose", A(out=psb(6)[:, 128 * c:128 * c + 128], in_=BT[:, 128 * c:128 * c + 128], identity=identb[:]),
                             reads=[B_BT, B_cst], writes=[B_ps[6]])
                    evac(Btok, psb(6)[:, 0:128 * NT].rearrange("p (c f) -> p c f", c=NT), [B_ps[6]], [B_Btok])
                else:
                    jx = bi_
                    P.op("act", "activation", A(out=xsT, in_=acc, func=AF.Silu), reads=[B_acc], writes=[B_xsT])
                    for c in range(NT):
                        P.op("pe", "transpose", A(out=psb(7)[:, 128 * c:128 * c + 128], in_=xsT[:, 128 * c:128 * c + 128], identity=identb[:]),
                             reads=[B_xsT, B_cst], writes=[B_ps[7]])
                    evac(xs_tok[:, :, 128 * jx:128 * jx + 128], psb(7)[:, 0:128 * NT].rearrange("p (c f) -> p c f", c=NT), [B_ps[7]], [B_xs])
                yield
            wz, wzB = load_w(win(OFF_Z + 256 * g, 256), 8, 256)
            for c in range(NT):
                pi = 6 + c % 2
                proj_tm(pi, 0, wz, wzB, 0, 256, t0 + 128 * c)
                P.op("act", "activation", A(out=zs[:, c, :], in_=psum[pi][:, 0:256], func=AF.Silu), reads=[B_ps[pi]], writes=[B_zs])
                if c % 2:
                    yield

        def scan(g):
            q = g % 2
            xs_tok, BT, Btok, CT, zs, dskI, sg_b = xs_tok2[q], BT2[q], Btok2[q], CT2[q], zs2[q], dskI2[q], sg_b2[q]
            B_xs, B_BT, B_Btok, B_CT, B_zs, B_dskI, B_sg = B_xs2[q], B_BT2[q], B_Btok2[q], B_CT2[q], B_zs2[q], B_dskI2[q], B_sg2[q]
            P.op("act", "copy", A(out=st_b, in_=state_f[:, g, :]), reads=[B_state[g]], writes=[B_stb])
            CBb = (1, 5)

            def S1(c):
                i2 = c % 2
                tk = slice(128 * c, 128 * c + 128)
                ag = adt[:, c, 4 * g:4 * g + 4]
                P.op("dve", "tensor_tensor", A(out=R[i2].rearrange("p (h l) -> p h l", h=4), in0=cst[:, C_TRIL:C_TRIL + 128].unsqueeze(1).broadcast_to([128, 4, 128]),
                                               in1=ag.unsqueeze(2).broadcast_to([128, 4, 128]), op=ALU.mult),
                     reads=[B_cst, B_dt], writes=[B_R[i2]])
                P.op("pe", "matmul", A(psum[0][:, :], lhsT=cst[:, C_SUP:C_SUP + 128], rhs=R[i2], start=True, stop=False), reads=[B_cst, B_R[i2]], writes=[B_ps[0]])
                P.op("pe", "matmul", A(psum[0][:, :], lhsT=cst[:, C_NEGI:C_NEGI + 128], rhs=cst[:, C_SLOW4:C_SLOW4 + 512], start=False, stop=True), reads=[B_cst], writes=[B_ps[0]])
                P.op("act", "activation", A(out=LT[i2], in_=psum[0][:, :], func=AF.Exp), reads=[B_ps[0]], writes=[B_LT[i2]])
                P.op("pe", "matmul", A(psum[CBb[i2]][:, 0:128], lhsT=BT[:, tk], rhs=CT[:, tk], start=True, stop=True), reads=[B_BT, B_CT], writes=[B_ps[CBb[i2]]])

            def S2(c):
                i2 = c % 2
                P.op("dve", "tensor_tensor", A(out=xdt[i2].rearrange("p (h q) -> p h q", h=4), in0=xs_tok[:, c, :].rearrange("p (h q) -> p h q", h=4),
                                               in1=dt[:, c, 4 * g:4 * g + 4].unsqueeze(2).broadcast_to([128, 4, 64]), op=ALU.mult),
                     reads=[B_xs, B_dt], writes=[B_xd[i2]])
                P.op("dve", "tensor_tensor", A(out=xdd[i2].rearrange("p (h q) -> p h q", h=4), in0=xdt[i2].rearrange("p (h q) -> p h q", h=4),
                                               in1=E[:, c, 32 + 4 * g:32 + 4 * g + 4].unsqueeze(2).broadcast_to([128, 4, 64]), op=ALU.mult),
                     reads=[B_xd[i2], B_E], writes=[B_xdd[i2]])
                P.op("dve", "tensor_tensor", A(out=MT[i2].rearrange("p (h l) -> p h l", h=4), in0=LT[i2].rearrange("p (h l) -> p h l", h=4),
                                               in1=psum[CBb[i2]][:, 0:128].unsqueeze(1).broadcast_to([128, 4, 128]), op=ALU.mult),
                     reads=[B_LT[i2], B_ps[CBb[i2]]], writes=[B_MT[i2]])

            def S3(c):
                i2 = c % 2
                tk = slice(128 * c, 128 * c + 128)
                for hh in range(4):
                    P.op("pe", "matmul", A(psum[2][:, 64 * hh:64 * hh + 64], lhsT=MT[i2][:, 128 * hh:128 * hh + 128], rhs=xdt[i2][:, 64 * hh:64 * hh + 64], start=True, stop=False),
                         reads=[B_MT[i2], B_xd[i2]], writes=[B_ps[2]])
                    P.op("pe", "matmul", A(psum[2][:, 64 * hh:64 * hh + 64], lhsT=dskI[:, hh, :], rhs=xs_tok[:, c, 64 * hh:64 * hh + 64], start=False, stop=True),
                         reads=[B_dskI, B_xs], writes=[B_ps[2]])
                P.op("pe", "matmul", A(psum[4][:, 0:256], lhsT=Btok[:, c, :], rhs=xdd[i2], start=True, stop=True), reads=[B_Btok, B_xdd[i2]], writes=[B_ps[4]])
                P.op("pe", "matmul", A(psum[3][:, 0:256], lhsT=CT[:, tk], rhs=st_b, start=True, stop=True), reads=[B_CT, B_stb], writes=[B_ps[3]])
                P.op("dve", "tensor_tensor", A(out=y1.rearrange("p (h q) -> p h q", h=4), in0=psum[3][:, 0:256].rearrange("p (h q) -> p h q", h=4),
                                               in1=E[:, c, 4 * g:4 * g + 4].unsqueeze(2).broadcast_to([128, 4, 64]), op=ALU.mult),
                     reads=[B_ps[3], B_E], writes=[B_y1])
                P.op("dve", "tensor_tensor", A(out=stmp.rearrange("p (h q) -> p h q", h=4), in0=state_f[:, g, :].rearrange("p (h q) -> p h q", h=4),
                                               in1=E[:, c, 64 + 4 * g:64 + 4 * g + 4].unsqueeze(2).broadcast_to([128, 4, 64]), op=ALU.mult),
                     reads=[B_state[g], B_E], writes=[B_stmp])
                P.op("dve", "tensor_tensor", A(out=state_f[:, g, :], in0=stmp, in1=psum[4][:, 0:256], op=ALU.add), reads=[B_stmp, B_ps[4]], writes=[B_state[g]])
                P.op("act", "copy", A(out=st_b, in_=state_f[:, g, :]), reads=[B_state[g]], writes=[B_stb])
                P.op("dve", "tensor_tensor", A(out=y2[:, c, :], in0=y1, in1=psum[2][:, 0:256], op=ALU.add), reads=[B_y1, B_ps[2]], writes=[B_y2c[c]])

            S1(0)
            if NT > 1:
                S1(1)
            S2(0)
            for c in range(NT):
                if c + 2 < NT:
                    S1(c + 2)
                if c + 1 < NT:
                    S2(c + 1)
                S3(c)
                yield
            P.op("dve", "tensor_tensor", A(out=y2, in0=y2, in1=zs, op=ALU.mult), reads=B_y2 + [B_zs], writes=B_y2)
            for c in range(NT):
                P.op("act", "activation", A(out=junk, in_=y2[:, c, :], func=AF.Square, accum_out=ss[:, c:c + 1]), reads=B_y2, writes=[B_junk, B_ss])
            rstd_from_ss(ss[:, 0:NT], NT, 1.0 / 256, [B_ss])
            yield
            P.op("dve", "tensor_tensor", A(out=y2, in0=y2, in1=ss[:, 0:NT].unsqueeze(2).broadcast_to([128, NT, 256]), op=ALU.mult), reads=B_y2 + [B_ss], writes=B_y2)
            P.op("dve", "tensor_tensor", A(out=yn, in0=y2, in1=sg_b.unsqueeze(1).broadcast_to([128, NT, 256]), op=ALU.mult), reads=B_y2 + [B_sg], writes=[B_yn])
            yield
            for c in range(NT):
                transpose_to_YT(yn[:, c, :], B_yn, 2, 2 * g, 128 * c, c % 2)
                if c % 2:
                    yield

        def run_interleaved(gens):
            gens = [g_ for g_ in gens if g_ is not None]
            while gens:
                for g_ in list(gens):
                    try:
                        next(g_)
                    except StopIteration:
                        gens.remove(g_)

        run_interleaved([prep(0)])
        for g in range(8):
            run_interleaved([scan(g), prep(g + 1) if g + 1 < 8 else None])
        P.barrier()

    def merge_phase(sq, part, bi, wbr_d, nkb, first):
        t0 = part * T
        cv = Carver()
        sig = [cv.f32(512), cv.f32(512)]; B_sig = [Buf(), Buf()]
        tmp = cv.f32(512); B_tmp = Buf()
        it = 0
        for ob in range(8):
            wb_, wbB = load_w(wbr_d.ap()[:, 128 * ob:128 * ob + 128], nkb, 128)
            wg, wgB = load_w(win(OFF_GATE + 1024 * bi + 128 * ob, 128), 8, 128)
            for tc in range(T // 512):
                pa, pg = 2 * (it % 2), 2 * (it % 2) + 1
                si = it % 2; it += 1
                for kb in range(nkb):
                    P.op("pe", "matmul", A(psum[pa][:, :], lhsT=wb_[:, kb, :], rhs=YT[:, kb, 512 * tc:512 * tc + 512], start=(kb == 0), stop=(kb == nkb - 1)),
                         reads=[wbB, B_YT], writes=[B_ps[pa]])
                proj_fm(pg, 0, wg, wgB, 0, 128, t0 + 512 * tc, 512)
                P.op("act", "activation", A(out=sig[si], in_=psum[pg][:, :], func=AF.Sigmoid), reads=[B_ps[pg]], writes=[B_sig[si]])
                dst = mrg[:, ob, 512 * tc:512 * tc + 512]
                if first:
                    P.op("dve", "tensor_tensor", A(out=dst, in0=sig[si], in1=psum[pa][:, :], op=ALU.mult), reads=[B_sig[si], B_ps[pa]], writes=[B_mrg])
                else:
                    P.op("dve", "tensor_tensor", A(out=tmp, in0=sig[si], in1=psum[pa][:, :], op=ALU.mult), reads=[B_sig[si], B_ps[pa]], writes=[B_tmp])
                    P.op("dve", "tensor_tensor", A(out=dst, in0=dst, in1=tmp, op=ALU.add), reads=[B_tmp, B_mrg], writes=[B_mrg])
        P.barrier()

    def out_phase(sq, part):
        t0 = part * T
        cv = Carver()
        fg_b = cv.f32(1024); B_fg = Buf()
        xt = [cv.f32(1024), cv.f32(1024)]; B_xt = [Buf(), Buf()]
        r = [cv.f32(1024), cv.f32(1024)]; B_r = [Buf(), Buf()]
        junk = cv.f32(1024); B_junk = Buf()
        ss = cv.f32(2); B_ss = Buf()
        bcast_load(fg_b, fgain_d.ap(), [B_fg])
        wts = [load_w(w_out.ap()[:, 256 * i:256 * i + 256], 8, 256) for i in range(4)]
        for c in range(NT):
            xi = c % 2
            rows = slice(t0 + 128 * c, t0 + 128 * c + 128)
            P.dma("sp", "dma_start", A(out=xt[xi], in_=x_d.ap()[sq, rows, :]), writes=[B_xt[xi]])
            for hf in range(2):
                pi = 2 * xi + hf
                for q4 in range(2):
                    wt, wB = wts[2 * hf + q4]
                    for kb in range(8):
                        P.op("pe", "matmul", A(psum[pi][:, 256 * q4:256 * q4 + 256], lhsT=mrg[:, kb, 128 * c:128 * c + 128], rhs=wt[:, kb, :], start=(kb == 0), stop=(kb == 7)),
                             reads=[wB, B_mrg], writes=[B_ps[pi]])
                P.op("dve", "tensor_tensor", A(out=r[xi][:, 512 * hf:512 * hf + 512], in0=xt[xi][:, 512 * hf:512 * hf + 512], in1=psum[pi][:, :], op=ALU.add),
                     reads=[B_xt[xi], B_ps[pi]], writes=[B_r[xi]])
            P.op("act", "activation", A(out=junk, in_=r[xi], func=AF.Square, accum_out=ss[:, 0:1]), reads=[B_r[xi]], writes=[B_junk, B_ss])
            rstd_from_ss(ss[:, 0:1], 1, 1.0 / D, [B_ss])
            P.op("dve", "scalar_tensor_tensor", A(out=r[xi], in0=r[xi], scalar=ss[:, 0:1], in1=fg_b, op0=ALU.mult, op1=ALU.mult), reads=[B_r[xi], B_ss, B_fg], writes=[B_r[xi]])
            P.dma("sp", "dma_start", A(out=out_d.ap()[sq, rows, :], in_=r[xi]), reads=[B_r[xi]])
        P.barrier()

    for sq in range(nseq):
        prologue(sq)
        for part in range(NPART):
            first = True
            for bi, (name, fn, wbr, nkb) in enumerate((("ssm", ssm_phase, w_brs, 16), ("diff", diff_phase, w_brd, 8), ("mem", mem_phase, w_brm, 8))):
                if name not in branches:
                    continue
                fn(sq, part)
                merge_phase(sq, part, bi, wbr, nkb, first)
                first = False
            out_phase(sq, part)
    P.finish()
    P.emit()
    return nc, P


_CACHE = {}


def kernel(**inputs):
    ncores = _CFG["ncores"]; nseq = _CFG["nseq"]
    key = (nseq, tuple(_CFG["branches"]), _CFG["same_engine_sync"], _CFG["schedule"])
    if key not in _CACHE:
        _CACHE[key] = build(nseq, _CFG["branches"], _CFG["same_engine_sync"], _CFG["schedule"])
    nc, P = _CACHE[key]
    f = lambda a: np.ascontiguousarray(np.asarray(a, dtype=np.float32))
    cst, oh = host_consts()
    shared = {
        "w_in": f(inputs["w_in"][0]), "w_mem_kv": f(inputs["w_mem_kv"][0]), "w_br_ssm": f(inputs["w_br_ssm"][0]),
        "w_br_diff": f(inputs["w_br_diff"][0]), "w_br_mem": f(inputs["w_br_mem"][0]), "w_out": f(inputs["w_out"][0]),
        "norm_gain": f(inputs["norm_gain"]).reshape(1, D), "mem_norm_gain": f(inputs["mem_norm_gain"]).reshape(1, D),
        "final_norm_gain": f(inputs["final_norm_gain"]).reshape(1, D), "ssm_norm_gain": f(inputs["ssm_norm_gain"]).reshape(1, 2048),
        "subln_gain": f(inputs["subln_gain"]).reshape(1, 128), "dt_bias": f(inputs["dt_bias"]).reshape(1, 32),
        "a_log": f(inputs["a_log"]).reshape(1, 32), "d_skip": f(inputs["d_skip"]).reshape(1, 32),
        "lambda_q1": f(inputs["lambda_q1"]).reshape(1, 64), "lambda_k1": f(inputs["lambda_k1"]).reshape(1, 64),
        "lambda_q2": f(inputs["lambda_q2"]).reshape(1, 64), "lambda_k2": f(inputs["lambda_k2"]).reshape(1, 64),
        "rel_bias": f(inputs["rel_bias"]),
        "conv_w_l": f(np.asarray(inputs["conv_w"][0]).reshape(4, 32, 128).transpose(2, 1, 0)),
        "conv_b_l": f(np.asarray(inputs["conv_b"][0]).reshape(32, 128).transpose(1, 0)),
        "cst": cst, "onehot": oh,
    }
    x = np.asarray(inputs["x"]); mem = np.asarray(inputs["mem"])
    in_maps = []
    for c in range(ncores):
        m = dict(shared)
        m["x"] = f(x[c * nseq:(c + 1) * nseq])
        m["mem"] = f(mem[c * nseq:(c + 1) * nseq])
        in_maps.append(m)
    if _CFG.get("trace"):
        res = run_bass_kernel_spmd(nc, in_maps, core_ids=list(range(ncores)), trace=True)
        print("EXEC_TIME_NS", res.exec_time_ns)
    else:
        res = run_bass_kernel_spmd(nc, in_maps, core_ids=list(range(ncores)))
    return np.concatenate([np.asarray(r["out"]) for r in res.results], axis=0).astype(np.float32)
```

```python
import math
import numpy as np
import concourse.bass as bass
import concourse.mybir as mybir
from concourse.bass_utils import run_bass_kernel_spmd

F32 = mybir.dt.float32
BF16 = mybir.dt.bfloat16
ALU = mybir.AluOpType
AF = mybir.ActivationFunctionType

D = 1024
S = 2048
MEM = 256
IN_DIM = 15392
OFF_Z, OFF_XBC, OFF_DT, OFF_DQ, OFF_DK, OFF_DV, OFF_DG, OFF_MQ, OFF_MG, OFF_GATE = (
    0, 2048, 6144, 6176, 7200, 8224, 9248, 10272, 11296, 12320)
EPS = 1e-5
NEG = -30000.0
T = 1024
NT = T // 128
NPART = S // T

_CFG = {"branches": ("ssm", "diff", "mem"), "ncores": 8, "nseq": 4, "same_engine_sync": "raw", "schedule": True}

COMPUTE = ("pe", "act", "dve", "pool")
QUEUES = ("pe", "act", "dve", "pool", "sp")
DMAQ = ("sp", "pool", "act")
NDMASEM = 12


def A(*a, **k):
    return (a, k)


class Buf:
    __slots__ = ("name", "last_w", "readers")

    def __init__(self, name=""):
        self.name = name
        self.last_w = None
        self.readers = []


def _free(ap):
    n = 1
    for d in ap.shape[1:]:
        n *= int(d)
    return n


class Prog:
    def __init__(self, nc, same_engine_sync="raw", schedule=True):
        self.nc = nc
        self.ins = []
        self.segs = [[]]
        self.same_engine_sync = same_engine_sync
        self.schedule = schedule
        self.sem = {e: nc.alloc_semaphore("s_" + e) for e in COMPUTE}
        self.dsem = {e: [nc.alloc_semaphore(f"d_{e}{i}") for i in range(NDMASEM)] for e in DMAQ}

    def _deps(self, reads, writes):
        deps = set()
        raw = set()
        for b in reads:
            if b.last_w is not None:
                deps.add(b.last_w)
                raw.add(b.last_w)
        for b in writes:
            if b.last_w is not None:
                deps.add(b.last_w)
            deps.update(b.readers)
        return deps, raw

    def _commit(self, me, reads, writes):
        for b in reads:
            b.readers.append(me)
        for b in writes:
            b.last_w = me
            b.readers = []

    def _est(self, eng, meth, args, is_dma):
        a, k = args
        try:
            out = k.get("out", a[0] if a else None)
            n = _free(out)
            if is_dma:
                return 60.0, 2000.0 + n * (4 if out.dtype == F32 else 2) * 128 / 150.0
            if eng == "pe":
                if meth == "transpose":
                    return 110.0, 0.0
                f32 = k["lhsT"].dtype == F32
                return max(64.0, n * 0.42) * (4 if f32 else 1), 0.0
            if eng == "act":
                return 224.0 + 0.75 * n, 0.0
            if eng == "dve":
                return 60.0 + 1.05 * n, 0.0
            return 100.0 + 2.3 * n, 0.0
        except Exception:
            return 300.0, 0.0

    def _add(self, eng, meth, args, reads, writes, is_dma):
        deps, raw = self._deps(reads, writes)
        gid = len(self.ins)
        dur, lat = self._est(eng, meth, args, is_dma)
        tab = None
        if eng == "act" and meth == "activation":
            f = args[1].get("func")
            tab = "explog" if f in (AF.Exp, AF.Ln) else str(f)
        self.ins.append(dict(eng=eng, fn=(meth, args), deps=deps, raw=raw, dma=is_dma, dur=dur, lat=lat, tab=tab))
        self.segs[-1].append(gid)
        self._commit(gid, reads, writes)
        return gid

    def op(self, eng, meth, args, reads=(), writes=()):
        return self._add(eng, meth, args, reads, writes, False)

    def dma(self, eng, meth, args, reads=(), writes=()):
        return self._add(eng, meth, args, reads, writes, True)

    def barrier(self):
        self.segs.append([])

    def finish(self):
        self.segs.append([])

    def _sched(self, seg):
        ins = self.ins
        pend = {e: [] for e in QUEUES}
        for g in seg:
            pend[ins[g]["eng"]].append(g)
        if not self.schedule:
            return pend
        segset = set(seg)
        W = 24
        fin = {}
        nun = {}
        users = {}
        for g in seg:
            c = 0
            for d in ins[g]["deps"]:
                if d in segset:
                    c += 1
                    users.setdefault(d, []).append(g)
            nun[g] = c
        free = {e: 0.0 for e in QUEUES}
        curtab = [None]
        order = {e: [] for e in QUEUES}
        cand = {e: None for e in QUEUES}
        dirty = set(QUEUES)
        remaining = len(seg)
        while remaining:
            for e in list(dirty):
                best = None
                lst = pend[e]
                for g in lst[:W]:
                    if nun[g]:
                        continue
                    I = ins[g]
                    r = 0.0
                    for d in I["deps"]:
                        if d in fin:
                            t = fin[d] + (150.0 if ins[d]["eng"] != e else 0.0)
                            if t > r:
                                r = t
                    st = r if r > free[e] else free[e]
                    if e == "act" and I["tab"] is not None and I["tab"] != curtab[0]:
                        st += 1300.0
                    if best is None or st < best[0]:
                        best = (st, g)
                cand[e] = best
            dirty.clear()
            be = None
            for e in QUEUES:
                c = cand[e]
                if c is not None and (be is None or c[0] < cand[be][0]):
                    be = e
            if be is None:
                raise RuntimeError("scheduler deadlock")
            st, g = cand[be]
            I = ins[g]
            if be == "act" and I["tab"] is not None:
                curtab[0] = I["tab"]
            free[be] = st + I["dur"]
            fin[g] = st + I["dur"] + I["lat"]
            pend[be].remove(g)
            order[be].append(g)
            remaining -= 1
            dirty.add(be)
            for u in users.get(g, ()):
                nun[u] -= 1
                if nun[u] == 0:
                    dirty.add(ins[u]["eng"])
        return order

    def emit(self):
        nc = self.nc
        ins = self.ins
        stream = {e: [] for e in QUEUES}
        pos = {}
        dma_list = {e: [] for e in DMAQ}

        def tails():
            t = set()
            for e in QUEUES:
                for x in reversed(stream[e]):
                    if not isinstance(x, tuple) and not ins[x]["dma"]:
                        t.add(x)
                        break
            for e in DMAQ:
                t.update(dma_list[e][-NDMASEM:])
            return t

        nseg = len(self.segs)
        for si, seg in enumerate(self.segs):
            if seg:
                order = self._sched(seg)
                for e in QUEUES:
                    for g in order[e]:
                        I = ins[g]
                        if I["dma"]:
                            dl = dma_list[e]
                            I["k"] = len(dl)
                            if len(dl) >= NDMASEM:
                                I["deps"] = set(I["deps"]) | {dl[-NDMASEM]}
                            dl.append(g)
                        pos[g] = len(stream[e])
                        stream[e].append(g)
            if si < nseg - 1:
                t = tails()
                last = (si == nseg - 2)
                for e in (("sp",) if last else QUEUES):
                    stream[e].append(("wait", t))
        signal = set()
        plan = {e: [] for e in QUEUES}
        for e in QUEUES:
            seen = {x: -1 for x in COMPUTE}
            seen_slot = {}
            for ent in stream[e]:
                if isinstance(ent, tuple):
                    deps, raw, g = ent[1], ent[1], None
                else:
                    g = ent
                    deps, raw = ins[g]["deps"], ins[g]["raw"]
                best = {}
                for d in deps:
                    D = ins[d]
                    if D["dma"]:
                        slot = (D["eng"], D["k"] % NDMASEM)
                        if seen_slot.get(slot, -1) >= D["k"] // NDMASEM:
                            continue
                        if slot not in best or ins[best[slot]]["k"] < D["k"]:
                            best[slot] = d
                    else:
                        x = D["eng"]
                        if x == e and (x == "pe" or not self.same_engine_sync):
                            continue
                        if x == e and self.same_engine_sync == "raw" and d not in raw:
                            continue
                        if seen[x] >= pos[d]:
                            continue
                        if x not in best or pos[best[x]] < pos[d]:
                            best[x] = d
                final = list(best.values())
                for d in final:
                    D = ins[d]
                    if D["dma"]:
                        seen_slot[(D["eng"], D["k"] % NDMASEM)] = D["k"] // NDMASEM
                    else:
                        seen[D["eng"]] = pos[d]
                        signal.add(d)
                plan[e].append((final, g))
        count = {}
        for e in COMPUTE:
            c = 0
            for ent in stream[e]:
                if not isinstance(ent, tuple) and ent in signal:
                    c += 1
                    count[ent] = c
        self.stats = {e: len(stream[e]) for e in QUEUES}

        def replay(e, eng):
            for final, g in plan[e]:
                for d in final:
                    D = ins[d]
                    if D["dma"]:
                        eng.wait_ge(self.dsem[D["eng"]][D["k"] % NDMASEM], 16 * (D["k"] // NDMASEM + 1))
                    else:
                        eng.wait_ge(self.sem[D["eng"]], count[d])
                if g is None:
                    continue
                I = ins[g]
                fn = I["fn"]
                o = getattr(eng, fn[0])(*fn[1][0], **fn[1][1])
                if I["dma"]:
                    o.then_inc(self.dsem[e][I["k"] % NDMASEM], 16)
                elif g in signal:
                    o.then_inc(self.sem[e], 1)

        with nc.Block() as block:
            @block.tensor
            def _(eng):
                replay("pe", eng)

            @block.scalar
            def _(eng):
                replay("act", eng)

            @block.vector
            def _(eng):
                replay("dve", eng)

            @block.gpsimd
            def _(eng):
                replay("pool", eng)

            @block.sync
            def _(eng):
                replay("sp", eng)


def t5_bucket_np(rel):
    n = np.maximum(rel, 0)
    max_exact = 16
    nf = np.maximum(n, 1).astype(np.float32)
    large = max_exact + (np.log(nf / max_exact) / np.float32(math.log(128 / max_exact))
                         * (32 - max_exact)).astype(np.int32)
    large = np.minimum(large, 31)
    return np.where(n < max_exact, n, large)


def host_consts():
    k = np.arange(128)[:, None]
    l = np.arange(128)[None, :]
    ident = (k == l).astype(np.float32)
    tril1 = (k <= l).astype(np.float32)
    sup = (k > l).astype(np.float32)
    ones = np.ones((128, 128), np.float32)
    negi = ident * NEG
    slow4 = np.tile(sup, (1, 4))
    cst = np.concatenate([ident, tril1, sup, ones, negi, slow4], axis=1)
    oh = np.zeros((33, 2, 128, 128), np.float32)
    for dlt in range(2):
        rel = 128 * dlt + (l - k)
        b = t5_bucket_np(rel)
        for kk in range(128):
            for qq in range(128):
                if rel[kk, qq] >= 0:
                    oh[b[kk, qq], dlt, kk, qq] = 1.0
                else:
                    oh[32, dlt, kk, qq] = 1.0
    return cst, oh.reshape(33, 2 * 128 * 128)


C_ID, C_TRIL, C_SUP, C_ONES, C_NEGI, C_SLOW4 = 0, 128, 256, 384, 512, 640


def build(nseq, branches, same_engine_sync="raw", schedule=True):
    nc = bass.Bass("TRN2", target_bir_lowering=False)
    P = Prog(nc, same_engine_sync, schedule)

    def din(name, shape, dt=F32):
        return nc.dram_tensor(name, list(shape), dt, kind="ExternalInput")

    x_d = din("x", [nseq, S, D])
    mem_d = din("mem", [nseq, MEM, D])
    w_in = din("w_in", [D, IN_DIM])
    w_kv = din("w_mem_kv", [D, 2048])
    w_brs = din("w_br_ssm", [2048, D])
    w_brd = din("w_br_diff", [D, D])
    w_brm = din("w_br_mem", [D, D])
    w_out = din("w_out", [D, D])
    gain_d = din("norm_gain", [1, D])
    mgain_d = din("mem_norm_gain", [1, D])
    fgain_d = din("final_norm_gain", [1, D])
    sgain_d = din("ssm_norm_gain", [1, 2048])
    subg_d = din("subln_gain", [1, 128])
    dtb_d = din("dt_bias", [1, 32])
    alog_d = din("a_log", [1, 32])
    dsk_d = din("d_skip", [1, 32])
    lq1_d, lk1_d, lq2_d, lk2_d = (din(n, [1, 64]) for n in ("lambda_q1", "lambda_k1", "lambda_q2", "lambda_k2"))
    rb_d = din("rel_bias", [32, 8])
    cw_d = din("conv_w_l", [128, 32, 4])
    cb_d = din("conv_b_l", [128, 32])
    cst_d = din("cst", [128, 1152])
    oh_d = din("onehot", [33, 32768])
    out_d = nc.dram_tensor("out", [nseq, S, D], F32, kind="ExternalOutput")
    bias_scr = nc.dram_tensor("bias_scr", [8, 32768], F32, kind="Internal")
    w_in_f, w_kv_f, w_brs_f, w_brd_f, w_brm_f, w_out_f = w_in, w_kv, w_brs, w_brd, w_brm, w_out
    w_in = nc.dram_tensor("w_in_b", [D, IN_DIM], BF16, kind="Internal")
    w_kv = nc.dram_tensor("w_kv_b", [D, 2048], BF16, kind="Internal")
    w_brs = nc.dram_tensor("w_brs_b", [2048, D], BF16, kind="Internal")
    w_brd = nc.dram_tensor("w_brd_b", [D, D], BF16, kind="Internal")
    w_brm = nc.dram_tensor("w_brm_b", [D, D], BF16, kind="Internal")
    w_out = nc.dram_tensor("w_out_b", [D, D], BF16, kind="Internal")

    def sb(name, shape, dt=F32):
        return nc.alloc_sbuf_tensor("sb_" + name, list(shape), dt)

    cst = sb("cst", [128, 1152]); B_cst = Buf("cst")
    identb = sb("identb", [128, 128], BF16)
    trilb = sb("trilb", [128, 128], BF16)
    cstb = sb("cstb", [128, 768], BF16)
    sm = sb("small", [128, 512]); B_sm = Buf("small")
    SM_CFAR, SM_DTB, SM_A, SM_DSK, SM_LAM, SM_NLAM, SM_ONE, SM_EPS = 0, 8, 40, 72, 104, 105, 106, 107
    SM_SUBG = 128
    SM_TMP = 256
    cw = sb("cw", [128, 32, 4]); cb = sb("cb", [128, 32]); B_cw = Buf("cw")
    hT = sb("hT", [128, 8, S], BF16)
    B_hT = [Buf(f"hT{i}") for i in range(S // 128)]
    kmT = sb("kmT", [128, 8, MEM], BF16); B_kmT = Buf("kmT")
    vma = sb("vma", [128, 2, 4, 258], BF16); B_vma = Buf("vma")
    YT = sb("YT", [128, 16, T], BF16); B_YT = Buf("YT")
    mrg = sb("mrg", [128, 8, T], BF16); B_mrg = Buf("mrg")
    state_f = sb("state_f", [128, 8, 256]); B_state = [Buf(f"st{g}") for g in range(8)]
    halo = sb("halo", [128, 32, 4]); B_halo = Buf("halo")
    NWB = 6
    wbf = [sb(f"wbf{i}", [128, 2048], BF16) for i in range(NWB)]; B_wbf = [Buf(f"wbf{i}") for i in range(NWB)]
    ARENA_W = 18976
    arena = sb("arena", [128, ARENA_W])
    arena_b = arena[:].bitcast(BF16)

    class Carver:
        def __init__(self):
            self.off = 0

        def f32(self, n):
            a = arena[:, self.off:self.off + n]
            self.off += n
            assert self.off <= ARENA_W, self.off
            return a

        def bf16(self, n):
            w = (n + 1) // 2
            a = arena_b[:, 2 * self.off:2 * self.off + n]
            self.off += w
            assert self.off <= ARENA_W, self.off
            return a

    msig = sb("msig", [128, 512]); B_msig = Buf("msig")
    mtmp = sb("mtmp", [128, 512]); B_mtmp = Buf("mtmp")
    psum = [nc.alloc_psum_tensor(f"ps{i}", [128, 512], F32) for i in range(8)]
    B_ps = [Buf(f"ps{i}") for i in range(8)]

    def psb(i):
        return psum[i][:].bitcast(BF16)

    wctr = [0, 0]

    def load_w(src_ap, nkb, ncols):
        assert nkb * ncols <= 2048
        bi = wctr[1] % NWB; wctr[1] += 1
        bf = wbf[bi][:, 0:nkb * ncols].rearrange("p (k c) -> p k c", k=nkb)
        src = src_ap.rearrange("(k p) c -> p k c", p=128)
        P.dma("sp", "dma_start", A(out=bf, in_=src), writes=[B_wbf[bi]])
        return bf, B_wbf[bi]

    def win(c0, ncols):
        return w_in.ap()[:, c0:c0 + ncols]

    def hbufs(t0, n):
        return B_hT[t0 // 128:(t0 + n + 127) // 128]

    def proj_fm(ps_i, ps_cols, wt, wB, c_lo, ncol, tok0, ntok):
        out = psum[ps_i][0:ncol, ps_cols:ps_cols + ntok]
        for kb in range(8):
            P.op("pe", "matmul", A(out, lhsT=wt[:, kb, c_lo:c_lo + ncol], rhs=hT[:, kb, tok0:tok0 + ntok],
                                                 start=(kb == 0), stop=(kb == 7)),
                 reads=[wB] + hbufs(tok0, ntok), writes=[B_ps[ps_i]])

    def proj_tm(ps_i, ps_cols, wt, wB, c_lo, ncol, tok0):
        out = psum[ps_i][:, ps_cols:ps_cols + ncol]
        for kb in range(8):
            P.op("pe", "matmul", A(out, lhsT=hT[:, kb, tok0:tok0 + 128], rhs=wt[:, kb, c_lo:c_lo + ncol],
                                                 start=(kb == 0), stop=(kb == 7)),
                 reads=[wB] + hbufs(tok0, 128), writes=[B_ps[ps_i]])

    evac_rr = [0]

    def evac(out, in_, reads, writes):
        evac_rr[0] += 1
        if evac_rr[0] % 2:
            P.op("act", "copy", A(out=out, in_=in_), reads=reads, writes=writes)
        else:
            P.op("dve", "tensor_copy", A(out=out, in_=in_), reads=reads, writes=writes)

    def bcast_load(dst, src_row_ap, writes):
        P.dma("sp", "dma_start", A(out=dst, in_=src_row_ap.partition_broadcast(128)), writes=writes)

    def rstd_from_ss(ss_ap, n, inv_count, Bs):
        P.op("dve", "tensor_scalar", A(out=ss_ap, in0=ss_ap, scalar1=inv_count, scalar2=EPS, op0=ALU.mult, op1=ALU.add),
             reads=Bs, writes=Bs)
        P.op("act", "activation", A(out=ss_ap, in_=ss_ap, func=AF.Sqrt), reads=Bs, writes=Bs)
        P.op("dve", "reciprocal", A(out=ss_ap, in_=ss_ap), reads=Bs, writes=Bs)

    P.dma("sp", "dma_start", A(out=cst[:], in_=cst_d.ap()), writes=[B_cst])
    P.op("dve", "tensor_copy", A(out=identb[:], in_=cst[:, C_ID:C_ID + 128]), reads=[B_cst], writes=[B_cst])
    P.op("dve", "tensor_copy", A(out=trilb[:], in_=cst[:, C_TRIL:C_TRIL + 128]), reads=[B_cst], writes=[B_cst])
    P.op("dve", "tensor_copy", A(out=cstb[:, 0:128], in_=cst[:, C_SUP:C_SUP + 128]), reads=[B_cst], writes=[B_cst])
    P.op("dve", "tensor_copy", A(out=cstb[:, 128:768], in_=cst[:, C_NEGI:C_NEGI + 640]), reads=[B_cst], writes=[B_cst])
    P.dma("sp", "dma_start", A(out=cw[:], in_=cw_d.ap()), writes=[B_cw])
    P.dma("sp", "dma_start", A(out=cb[:], in_=cb_d.ap()), writes=[B_cw])
    P.op("pool", "memset", A(sm[:], 0.0), writes=[B_sm])
    P.op("pool", "memset", A(sm[:, SM_ONE:SM_ONE + 1], 1.0), writes=[B_sm])
    P.op("pool", "memset", A(sm[:, SM_EPS:SM_EPS + 1], EPS), writes=[B_sm])
    bcast_load(sm[:, SM_CFAR:SM_CFAR + 8], rb_d.ap()[31:32, :], [B_sm])
    bcast_load(sm[:, SM_DTB:SM_DTB + 32], dtb_d.ap(), [B_sm])
    bcast_load(sm[:, SM_A:SM_A + 32], alog_d.ap(), [B_sm])
    bcast_load(sm[:, SM_DSK:SM_DSK + 32], dsk_d.ap(), [B_sm])
    bcast_load(sm[:, SM_SUBG:SM_SUBG + 128], subg_d.ap(), [B_sm])
    for i, ld in enumerate((lq1_d, lk1_d, lq2_d, lk2_d)):
        bcast_load(sm[:, SM_TMP + 64 * i:SM_TMP + 64 * i + 64], ld.ap(), [B_sm])
    Bs = [B_sm]
    P.op("act", "activation", A(out=sm[:, SM_A:SM_A + 32], in_=sm[:, SM_A:SM_A + 32], func=AF.Exp), reads=Bs, writes=Bs)
    P.op("dve", "tensor_scalar_mul", A(out=sm[:, SM_A:SM_A + 32], in0=sm[:, SM_A:SM_A + 32], scalar1=-1.0), reads=Bs, writes=Bs)
    LAM_INIT = 0.8 - 0.6 * math.exp(-0.3 * 0)
    P.op("dve", "tensor_scalar_mul", A(out=sm[:, SM_SUBG:SM_SUBG + 128], in0=sm[:, SM_SUBG:SM_SUBG + 128], scalar1=1.0 - LAM_INIT), reads=Bs, writes=Bs)
    for i in range(2):
        a0 = SM_TMP + 128 * i
        P.op("dve", "tensor_tensor", A(out=sm[:, a0:a0 + 64], in0=sm[:, a0:a0 + 64], in1=sm[:, a0 + 64:a0 + 128], op=ALU.mult), reads=Bs, writes=Bs)
        P.op("dve", "reduce_sum", A(out=sm[:, 110 + i:111 + i], in_=sm[:, a0:a0 + 64], axis=mybir.AxisListType.X), reads=Bs, writes=Bs)
    P.op("act", "activation", A(out=sm[:, 110:112], in_=sm[:, 110:112], func=AF.Exp), reads=Bs, writes=Bs)
    P.op("dve", "tensor_tensor", A(out=sm[:, SM_LAM:SM_LAM + 1], in0=sm[:, 110:111], in1=sm[:, 111:112], op=ALU.subtract), reads=Bs, writes=Bs)
    P.op("dve", "tensor_scalar_add", A(out=sm[:, SM_LAM:SM_LAM + 1], in0=sm[:, SM_LAM:SM_LAM + 1], scalar1=LAM_INIT), reads=Bs, writes=Bs)
    P.op("dve", "tensor_scalar_mul", A(out=sm[:, SM_NLAM:SM_NLAM + 1], in0=sm[:, SM_LAM:SM_LAM + 1], scalar1=-1.0), reads=Bs, writes=Bs)

    cv = Carver()
    NPS = 4
    pst = [cv.f32(2048) for _ in range(NPS)]; B_pst = [Buf() for _ in range(NPS)]
    pbf = [cv.bf16(2048) for _ in range(NPS)]; B_pbf = [Buf() for _ in range(NPS)]
    pc_i = 0
    for (srcw, dstw, rows, cols) in ((w_in_f, w_in, D, IN_DIM), (w_kv_f, w_kv, D, 2048), (w_brs_f, w_brs, 2048, D),
                                     (w_brd_f, w_brd, D, D), (w_brm_f, w_brm, D, D), (w_out_f, w_out, D, D)):
        for rb in range(rows // 128):
            for c0 in range(0, cols, 2048):
                n = min(2048, cols - c0)
                si = pc_i % NPS
                P.dma("sp", "dma_start", A(out=pst[si][:, 0:n], in_=srcw.ap()[128 * rb:128 * rb + 128, c0:c0 + n]), writes=[B_pst[si]])
                ce = ("pool", "act", "dve")[pc_i % 3]
                P.op(ce, "copy" if ce == "act" else "tensor_copy", A(out=pbf[si][:, 0:n], in_=pst[si][:, 0:n]), reads=[B_pst[si]], writes=[B_pbf[si]])
                P.dma("sp", "dma_start", A(out=dstw.ap()[128 * rb:128 * rb + 128, c0:c0 + n], in_=pbf[si][:, 0:n]), reads=[B_pbf[si]])
                pc_i += 1
    P.barrier()

    if "diff" in branches:
        cv = Carver()
        rbx = cv.f32(8)
        ohs = cv.f32(4096)
        stg = cv.f32(4096)
        B_rbx, B_ohs, B_stg, B_scr = Buf(), Buf(), Buf(), Buf()
        P.op("pool", "memset", A(rbx[32:33, :], NEG), writes=[B_rbx])
        P.dma("sp", "dma_start", A(out=rbx[0:32, :], in_=rb_d.ap()), writes=[B_rbx])
        for pc in range(8):
            P.dma("sp", "dma_start", A(out=ohs[0:33, :], in_=oh_d.ap()[:, 4096 * pc:4096 * pc + 4096]), writes=[B_ohs])
            for i in range(8):
                pi = i % 2
                P.op("pe", "matmul", A(psum[pi][0:8, :], lhsT=rbx[0:33, :], rhs=ohs[0:33, 512 * i:512 * i + 512], start=True, stop=True),
                     reads=[B_rbx, B_ohs], writes=[B_ps[pi]])
                evac(stg[0:8, 512 * i:512 * i + 512], psum[pi][0:8, :], [B_ps[pi]], [B_stg])
            P.dma("sp", "dma_start", A(out=bias_scr.ap()[:, 4096 * pc:4096 * pc + 4096], in_=stg[0:8, :]), reads=[B_stg], writes=[B_scr])
        P.barrier()

    def prologue(sq):
        cv = Carver()
        gain_b = cv.f32(1024); B_gain = Buf()
        xt = [cv.f32(1024), cv.f32(1024)]; B_xt = [Buf(), Buf()]
        junk = cv.f32(1024); B_junk = Buf()
        hb = cv.bf16(1024); B_hb = Buf()
        ss = cv.f32(2); B_ss = Buf()
        memT = cv.bf16(8 * MEM); B_memT = Buf()
        memT3 = memT.rearrange("p (k m) -> p k m", k=8)

        def norm_transpose(src_ap, i, dst3, dstB, col0):
            xi = i % 2
            P.dma("sp", "dma_start", A(out=xt[xi], in_=src_ap), writes=[B_xt[xi]])
            P.op("act", "activation", A(out=junk, in_=xt[xi], func=AF.Square, accum_out=ss[:, 0:1]), reads=[B_xt[xi]], writes=[B_junk, B_ss])
            rstd_from_ss(ss[:, 0:1], 1, 1.0 / D, [B_ss])
            P.op("dve", "scalar_tensor_tensor", A(out=hb, in0=xt[xi], scalar=ss[:, 0:1], in1=gain_b, op0=ALU.mult, op1=ALU.mult),
                 reads=[B_xt[xi], B_ss, B_gain], writes=[B_hb])
            for kb in range(8):
                P.op("pe", "transpose", A(out=psb(6)[:, 128 * kb:128 * kb + 128], in_=hb[:, 128 * kb:128 * kb + 128], identity=identb[:]),
                     reads=[B_hb, B_cst], writes=[B_ps[6]])
            evac(dst3[:, :, col0:col0 + 128], psb(6)[:, 0:1024].rearrange("p (k t) -> p k t", k=8), [B_ps[6]], dstB)

        bcast_load(gain_b, gain_d.ap(), [B_gain])
        for i in range(S // 128):
            norm_transpose(x_d.ap()[sq, 128 * i:128 * i + 128, :], i, hT, [B_hT[i]], 128 * i)
        if "mem" in branches:
            bcast_load(gain_b, mgain_d.ap(), [B_gain])
            for i in range(2):
                norm_transpose(mem_d.ap()[sq, 128 * i:128 * i + 128, :], i, memT3, [B_memT], 128 * i)
            for fb in range(8):
                wt, wB = load_w(w_kv.ap()[:, 128 * fb:128 * fb + 128], 8, 128)
                pi = fb % 2
                for kb in range(8):
                    P.op("pe", "matmul", A(psum[pi][:, 0:MEM], lhsT=wt[:, kb, :], rhs=memT3[:, kb, :], start=(kb == 0), stop=(kb == 7)),
                         reads=[wB, B_memT], writes=[B_ps[pi]])
                evac(kmT[:, fb, :], psum[pi][:, 0:MEM], [B_ps[pi]], [B_kmT])
            P.op("pool", "memset", A(vma[:, :, :, 256:258], 1.0), writes=[B_vma])
            for hh in range(4):
                wt, wB = load_w(w_kv.ap()[:, 1024 + 256 * hh:1024 + 256 * hh + 256], 8, 256)
                for mt in range(2):
                    pi = mt
                    for kb in range(8):
                        P.op("pe", "matmul", A(psum[pi][:, 0:256], lhsT=memT3[:, kb, 128 * mt:128 * mt + 128], rhs=wt[:, kb, :], start=(kb == 0), stop=(kb == 7)),
                             reads=[wB, B_memT], writes=[B_ps[pi]])
                    evac(vma[:, mt, hh, 0:256], psum[pi][:, 0:256], [B_ps[pi]], [B_vma])
        P.barrier()

    def transpose_to_YT(src_bf, srcB, nblk, yt_blk0, tokcol0, ps_i):
        for j in range(nblk):
            P.op("pe", "transpose", A(out=psb(ps_i)[:, 128 * j:128 * j + 128], in_=src_bf[:, 128 * j:128 * j + 128], identity=identb[:]),
                 reads=[srcB, B_cst], writes=[B_ps[ps_i]])
        evac(YT[:, yt_blk0:yt_blk0 + nblk, tokcol0:tokcol0 + 128], psb(ps_i)[:, 0:128 * nblk].rearrange("p (j t) -> p j t", j=nblk), [B_ps[ps_i]], [B_YT])

    def mem_phase(sq, part):
        t0 = part * T
        cv = Carver()
        QmT = cv.bf16(2 * T).rearrange("p (d t) -> p d t", d=2); B_Qm = Buf()
        Gm = cv.f32(NT * 256).rearrange("p (c f) -> p c f", c=NT); B_Gm = Buf()
        PmT = [cv.bf16(2 * 512).rearrange("p (m t) -> p m t", m=2) for _ in range(2)]; B_Pm = [Buf(), Buf()]
        rr = cv.f32(4); B_rr = Buf()
        ym = [cv.bf16(256), cv.bf16(256)]; B_ym = [Buf(), Buf()]
        it = 0
        for hh in range(4):
            wq, wqB = load_w(win(OFF_MQ + 256 * hh, 256), 8, 256)
            wg, wgB = load_w(win(OFF_MG + 256 * hh, 256), 8, 256)
            for db in range(2):
                for tc in range(T // 512):
                    pi = (2 * db + tc) % 2
                    proj_fm(pi, 0, wq, wqB, 128 * db, 128, t0 + 512 * tc, 512)
                    evac(QmT[:, db, 512 * tc:512 * tc + 512], psum[pi][:, :], [B_ps[pi]], [B_Qm])
            for c in range(NT):
                pi = 2 + c % 2
                proj_tm(pi, 0, wg, wgB, 0, 256, t0 + 128 * c)
                P.op("act", "activation", A(out=Gm[:, c, :], in_=psum[pi][:, 0:256], func=AF.Silu), reads=[B_ps[pi]], writes=[B_Gm])
            for tc in range(T // 512):
                pb = tc % 2
                for mt in range(2):
                    pi = mt
                    for db in range(2):
                        P.op("pe", "matmul", A(psum[pi][:, :], lhsT=kmT[:, 2 * hh + db, 128 * mt:128 * mt + 128], rhs=QmT[:, db, 512 * tc:512 * tc + 512], start=(db == 0), stop=(db == 1)),
                             reads=[B_kmT, B_Qm], writes=[B_ps[pi]])
                    P.op("act", "activation", A(out=PmT[pb][:, mt, :], in_=psum[pi][:, :], func=AF.Exp, scale=1.0 / 16.0), reads=[B_ps[pi]], writes=[B_Pm[pb]])
                for j in range(4):
                    pi = 2 + j % 2
                    for mt in range(2):
                        P.op("pe", "matmul", A(psum[pi][:, 0:257], lhsT=PmT[pb][:, mt, 128 * j:128 * j + 128], rhs=vma[:, mt, hh, 0:257], start=(mt == 0), stop=(mt == 1)),
                             reads=[B_Pm[pb], B_vma], writes=[B_ps[pi]])
                    yi = it % 2; it += 1
                    P.op("dve", "reciprocal", A(out=rr[:, 0:1], in_=psum[pi][:, 256:257]), reads=[B_ps[pi]], writes=[B_rr])
                    P.op("dve", "scalar_tensor_tensor", A(out=ym[yi], in0=psum[pi][:, 0:256], scalar=rr[:, 0:1], in1=Gm[:, 4 * tc + j, :], op0=ALU.mult, op1=ALU.mult),
                         reads=[B_ps[pi], B_rr, B_Gm], writes=[B_ym[yi]])
                    transpose_to_YT(ym[yi], B_ym[yi], 2, 2 * hh, 512 * tc + 128 * j, 6)
        P.barrier()

    def diff_phase(sq, part):
        t0 = part * T
        nk = t0 + T
        cv = Carver()
        KT = cv.bf16(S); B_KT = Buf()
        QTc = [cv.bf16(T), cv.bf16(T)]; B_QT = Buf()
        P.op("pool", "memset", A(QTc[0][64:128, :], 0.0), writes=[B_QT])
        P.op("pool", "memset", A(QTc[1][0:64, :], 0.0), writes=[B_QT])
        Va = cv.bf16(16 * 130).rearrange("p (t v) -> p t v", t=16); B_Va = Buf()
        Gs = cv.f32(NT * 128).rearrange("p (c f) -> p c f", c=NT); B_Gs = Buf()
        tmpS = [cv.f32(256), cv.f32(256)]; B_tmpS = [Buf(), Buf()]
        PT = [cv.bf16(512), cv.bf16(512)]; B_PT = [Buf(), Buf()]
        Os = [cv.f32(4 * 129).rearrange("p (j v) -> p j v", j=4) for _ in range(2)]; B_Os = [Buf(), Buf()]
        obuf = cv.f32(NT * 128).rearrange("p (c f) -> p c f", c=NT); B_ob = Buf()
        t1 = cv.f32(512).rearrange("p (j v) -> p j v", j=4); t2 = cv.f32(512).rearrange("p (j v) -> p j v", j=4); B_t = Buf()
        rr = cv.f32(8); B_rr = Buf()
        ss = cv.f32(NT); B_ss = Buf()
        junk = cv.f32(128); B_junk = Buf()
        ydt = cv.bf16(NT * 128).rearrange("p (c f) -> p c f", c=NT); B_ydt = Buf()
        biasT = cv.f32(8 * 2 * 128).rearrange("p (h d q) -> p h d q", h=8, d=2); B_biasT = Buf()
        for h in range(8):
            P.dma("sp", "dma_start", A(out=biasT[:, h, :, :], in_=bias_scr.ap()[h, :].rearrange("(d k q) -> k d q", d=2, k=128)), writes=[B_biasT])
        if sq == 0 and part == 0:
            print("diff arena words", cv.off)
        P.op("pool", "memset", A(Va[:, :, 128:130], 1.0), writes=[B_Va])
        sidx = 0
        for h in range(8):
            wq, wqB = load_w(win(OFF_DQ + 128 * h, 128), 8, 128)
            wk, wkB = load_w(win(OFF_DK + 128 * h, 128), 8, 128)
            wv, wvB = load_w(win(OFF_DV + 128 * h, 128), 8, 128)
            wg, wgB = load_w(win(OFF_DG + 128 * h, 128), 8, 128)
            for kc in range(nk // 512):
                pi = kc % 2
                proj_fm(pi, 0, wk, wkB, 0, 128, 512 * kc, 512)
                evac(KT[:, 512 * kc:512 * kc + 512], psum[pi][:, :], [B_ps[pi]], [B_KT])
            for tc in range(T // 512):
                pi = tc % 2
                proj_fm(pi, 0, wq, wqB, 0, 128, t0 + 512 * tc, 512)
                evac(QTc[0][0:64, 512 * tc:512 * tc + 512], psum[pi][0:64, :], [B_ps[pi]], [B_QT])
                evac(QTc[1][64:128, 512 * tc:512 * tc + 512], psum[pi][64:128, :], [B_ps[pi]], [B_QT])
            for kt in range(nk // 128):
                pi = kt % 2
                proj_tm(pi, 0, wv, wvB, 0, 128, 128 * kt)
                evac(Va[:, kt, 0:128], psum[pi][:, 0:128], [B_ps[pi]], [B_Va])
            for c in range(NT):
                pi = c % 2
                proj_tm(pi, 0, wg, wgB, 0, 128, t0 + 128 * c)
                P.op("act", "activation", A(out=Gs[:, c, :], in_=psum[pi][:, 0:128], func=AF.Silu), reads=[B_ps[pi]], writes=[B_Gs])
            for qc in range(T // 512):
                qb0 = (t0 + 512 * qc) // 128
                for c in range(2):
                    r0 = 64 * c
                    nkb_ = qb0 + 4
                    sis = []
                    for kk in range(nkb_):
                        sis.append(sidx % 2); sidx += 1

                    def emit_S(kb_):
                        j0 = kb_ - qb0
                        jlo = max(j0, 0)
                        si = sis[kb_]
                        P.op("pe", "matmul", A(
                            psum[si][:, 128 * jlo:512], lhsT=KT[:, 128 * kb_:128 * kb_ + 128],
                            rhs=QTc[c][:, 512 * qc + 128 * jlo:512 * qc + 512], start=True, stop=True),
                             reads=[B_KT, B_QT], writes=[B_ps[si]])

                    emit_S(0)
                    for kb_ in range(nkb_):
                        if kb_ + 1 < nkb_:
                            emit_S(kb_ + 1)
                        j0 = kb_ - qb0
                        jlo = max(j0, 0)
                        si = sis[kb_]
                        Sps = psum[si]
                        nsp = 0
                        for j in range(jlo, 4):
                            dl = j - j0
                            if dl <= 1:
                                P.op("dve", "scalar_tensor_tensor", A(
                                    out=tmpS[si][:, 128 * nsp:128 * nsp + 128], in0=Sps[:, 128 * j:128 * j + 128], scalar=0.125,
                                    in1=biasT[:, h, dl, :], op0=ALU.mult, op1=ALU.add),
                                     reads=[B_ps[si], B_biasT], writes=[B_tmpS[si]])
                                nsp += 1
                        if nsp:
                            P.op("act", "activation", A(out=PT[si][:, 128 * jlo:128 * (jlo + nsp)], in_=tmpS[si][:, 0:128 * nsp], func=AF.Exp),
                                 reads=[B_tmpS[si]], writes=[B_PT[si]])
                        jf = max(j0 + 2, 0)
                        if jf < 4:
                            P.op("act", "activation", A(out=PT[si][:, 128 * jf:512], in_=Sps[:, 128 * jf:512], func=AF.Exp, scale=0.125, bias=sm[:, SM_CFAR + h:SM_CFAR + h + 1]),
                                 reads=[B_ps[si], B_sm], writes=[B_PT[si]])
                        for j in range(jlo, 4):
                            P.op("pe", "matmul", A(psum[2 + j][:, 0:129], lhsT=PT[si][:, 128 * j:128 * j + 128], rhs=Va[:, kb_, 0:129],
                                                   start=(kb_ == 0), stop=(kb_ == qb0 + j)),
                                 reads=[B_PT[si], B_Va], writes=[B_ps[2 + j]])
                    for j in range(4):
                        evac(Os[c][:, j, :], psum[2 + j][:, 0:129], [B_ps[2 + j]], [B_Os[c]])
                P.op("dve", "reciprocal", A(out=rr[:, 0:4], in_=Os[0][:, :, 128]), reads=[B_Os[0]], writes=[B_rr])
                P.op("dve", "reciprocal", A(out=rr[:, 4:8], in_=Os[1][:, :, 128]), reads=[B_Os[1], B_rr], writes=[B_rr])
                P.op("dve", "tensor_scalar", A(out=rr[:, 4:8], in0=rr[:, 4:8], scalar1=sm[:, SM_NLAM:SM_NLAM + 1], scalar2=None, op0=ALU.mult), reads=[B_rr, B_sm], writes=[B_rr])
                P.op("dve", "tensor_tensor", A(out=t1, in0=Os[0][:, :, 0:128], in1=rr[:, 0:4].unsqueeze(2).broadcast_to([128, 4, 128]), op=ALU.mult), reads=[B_Os[0], B_rr, B_t], writes=[B_t])
                P.op("dve", "tensor_tensor", A(out=t2, in0=Os[1][:, :, 0:128], in1=rr[:, 4:8].unsqueeze(2).broadcast_to([128, 4, 128]), op=ALU.mult), reads=[B_Os[1], B_rr, B_t], writes=[B_t])
                P.op("dve", "tensor_tensor", A(out=obuf[:, 4 * qc:4 * qc + 4, :], in0=t1, in1=t2, op=ALU.add), reads=[B_t], writes=[B_ob])
            for c in range(NT):
                P.op("act", "activation", A(out=junk, in_=obuf[:, c, :], func=AF.Square, accum_out=ss[:, c:c + 1]), reads=[B_ob], writes=[B_junk, B_ss])
            rstd_from_ss(ss[:, 0:NT], NT, 1.0 / 128, [B_ss])
            P.op("dve", "tensor_tensor", A(out=obuf, in0=obuf, in1=ss[:, 0:NT].unsqueeze(2).broadcast_to([128, NT, 128]), op=ALU.mult), reads=[B_ob, B_ss], writes=[B_ob])
            P.op("dve", "tensor_tensor", A(out=obuf, in0=obuf, in1=sm[:, SM_SUBG:SM_SUBG + 128].unsqueeze(1).broadcast_to([128, NT, 128]), op=ALU.mult), reads=[B_ob, B_sm], writes=[B_ob])
            P.op("dve", "tensor_tensor", A(out=ydt, in0=obuf, in1=Gs, op=ALU.mult), reads=[B_ob, B_Gs], writes=[B_ydt])
            for c in range(NT):
                transpose_to_YT(ydt[:, c, :], B_ydt, 1, h, 128 * c, 6 + c % 2)
        P.barrier()

    def ssm_phase(sq, part):
        t0 = part * T
        cv = Carver()
        Us = [cv.f32(T + 4), cv.f32(T + 4)]; B_Us = [Buf(), Buf()]
        acc = cv.f32(T); B_acc = Buf()
        xsT = cv.bf16(T); B_xsT = Buf()
        xs_tok2 = [cv.bf16(NT * 256).rearrange("p (c f) -> p c f", c=NT) for _ in range(2)]; B_xs2 = [Buf(), Buf()]
        BT2 = [cv.bf16(T), cv.bf16(T)]; B_BT2 = [Buf(), Buf()]
        Btok2 = [cv.bf16(NT * 128).rearrange("p (c f) -> p c f", c=NT) for _ in range(2)]; B_Btok2 = [Buf(), Buf()]
        CT2 = [cv.bf16(T), cv.bf16(T)]; B_CT2 = [Buf(), Buf()]
        zs2 = [cv.bf16(NT * 256).rearrange("p (c f) -> p c f", c=NT) for _ in range(2)]; B_zs2 = [Buf(), Buf()]
        dskI2 = [cv.bf16(4 * 128).rearrange("p (h l) -> p h l", h=4) for _ in range(2)]; B_dskI2 = [Buf(), Buf()]
        sg_b2 = [cv.f32(256), cv.f32(256)]; B_sg2 = [Buf(), Buf()]
        dt = cv.f32(NT * 32).rearrange("p (c f) -> p c f", c=NT)
        adt = cv.f32(NT * 32).rearrange("p (c f) -> p c f", c=NT); B_dt = Buf()
        E = cv.f32(NT * 96).rearrange("p (c f) -> p c f", c=NT); B_E = Buf()
        R = [cv.bf16(512), cv.bf16(512)]; B_R = [Buf(), Buf()]
        LT = [cv.bf16(512), cv.bf16(512)]; B_LT = [Buf(), Buf()]
        MT = [cv.bf16(512), cv.bf16(512)]; B_MT = [Buf(), Buf()]
        xdt = [cv.bf16(256), cv.bf16(256)]; xdd = [cv.bf16(256), cv.bf16(256)]; B_xd = [Buf(), Buf()]; B_xdd = [Buf(), Buf()]
        y1 = cv.f32(256); B_y1 = Buf()
        y2 = cv.f32(NT * 256).rearrange("p (c f) -> p c f", c=NT); B_y2c = [Buf() for _ in range(NT)]; B_y2 = B_y2c
        yn = cv.bf16(NT * 256).rearrange("p (c f) -> p c f", c=NT); B_yn = Buf()
        st_b = cv.bf16(256); B_stb = Buf()
        stmp = cv.f32(256); B_stmp = Buf()
        ss = cv.f32(NT); B_ss = Buf()
        junk = cv.bf16(256); B_junk = Buf()
        if sq == 0 and part == 0:
            print("ssm arena words", cv.off)

        wd, wdB = load_w(win(OFF_DT, 32), 8, 32)
        for c in range(NT):
            pi = c % 2
            proj_tm(pi, 0, wd, wdB, 0, 32, t0 + 128 * c)
            P.op("dve", "tensor_tensor", A(out=dt[:, c, :], in0=psum[pi][:, 0:32], in1=sm[:, SM_DTB:SM_DTB + 32], op=ALU.add), reads=[B_ps[pi], B_sm], writes=[B_dt])
        dtf = dt.rearrange("p c f -> p (c f)")
        P.op("act", "activation", A(out=dtf, in_=dtf, func=AF.Exp), reads=[B_dt], writes=[B_dt])
        P.op("act", "activation", A(out=dtf, in_=dtf, func=AF.Ln, bias=sm[:, SM_ONE:SM_ONE + 1]), reads=[B_dt, B_sm], writes=[B_dt])
        P.op("dve", "tensor_tensor", A(out=adt, in0=dt, in1=sm[:, SM_A:SM_A + 32].unsqueeze(1).broadcast_to([128, NT, 32]), op=ALU.mult), reads=[B_dt, B_sm], writes=[B_dt])
        for c in range(NT):
            pi = c % 2
            for i, cc in enumerate((C_TRIL, C_SUP, C_ONES)):
                P.op("pe", "matmul", A(psum[pi][:, 32 * i:32 * i + 32], lhsT=cst[:, cc:cc + 128], rhs=adt[:, c, :], start=True, stop=True),
                     reads=[B_cst, B_dt], writes=[B_ps[pi]])
            P.op("act", "activation", A(out=E[:, c, :], in_=psum[pi][:, 0:96], func=AF.Exp), reads=[B_ps[pi]], writes=[B_E])

        def prep(g):
            q = g % 2
            xs_tok, BT, Btok, CT, zs, dskI, sg_b = xs_tok2[q], BT2[q], Btok2[q], CT2[q], zs2[q], dskI2[q], sg_b2[q]
            B_xs, B_BT, B_Btok, B_CT, B_zs, B_dskI, B_sg = B_xs2[q], B_BT2[q], B_Btok2[q], B_CT2[q], B_zs2[q], B_dskI2[q], B_sg2[q]
            if part == 0:
                P.op("pool", "memset", A(state_f[:, g, :], 0.0), writes=[B_state[g]])
            bcast_load(sg_b, sgain_d.ap()[:, 256 * g:256 * g + 256], [B_sg])
            for hh in range(4):
                P.op("pool", "tensor_scalar", A(out=dskI[:, hh, :], in0=cst[:, C_ID:C_ID + 128], scalar1=sm[:, SM_DSK + 4 * g + hh:SM_DSK + 4 * g + hh + 1], scalar2=None, op0=ALU.mult),
                     reads=[B_cst, B_sm], writes=[B_dskI])
            blocks = [("xs", 2 * g, OFF_XBC + 256 * g), ("xs", 2 * g + 1, OFF_XBC + 256 * g + 128),
                      ("B", 16 + g, OFF_XBC + 2048 + 128 * g), ("C", 24 + g, OFF_XBC + 3072 + 128 * g)]
            for bi_, (kind, blk, col) in enumerate(blocks):
                wt, wB = load_w(win(col, 128), 8, 128)
                U = Us[bi_ % 2]; B_U = B_Us[bi_ % 2]
                if part == 0:
                    P.op("pool", "memset", A(U[:, 0:4], 0.0), writes=[B_U])
                else:
                    P.op("pool", "tensor_copy", A(out=U[:, 0:4], in_=halo[:, blk, :]), reads=[B_halo], writes=[B_U])
                for tc in range(T // 512):
                    pi = 6 + tc % 2
                    proj_fm(pi, 0, wt, wB, 0, 128, t0 + 512 * tc, 512)
                    evac(U[:, 4 + 512 * tc:4 + 512 * tc + 512], psum[pi][:, :], [B_ps[pi]], [B_U])
                    yield
                P.op("pool", "tensor_copy", A(out=halo[:, blk, :], in_=U[:, T:T + 4]), reads=[B_U], writes=[B_halo])
                P.op("dve", "tensor_scalar", A(out=acc, in0=U[:, 4:4 + T], scalar1=cw[:, blk, 3:4], scalar2=cb[:, blk:blk + 1], op0=ALU.mult, op1=ALU.add),
                     reads=[B_U, B_cw], writes=[B_acc])
                for k in (2, 1, 0):
                    P.op("dve", "scalar_tensor_tensor", A(out=acc, in0=U[:, 1 + k:1 + k + T], scalar=cw[:, blk, k:k + 1], in1=acc, op0=ALU.mult, op1=ALU.add),
                         reads=[B_U, B_cw, B_acc], writes=[B_acc])
                    if k == 1:
                        yield
                if kind == "C":
                    P.op("act", "activation", A(out=CT, in_=acc, func=AF.Silu), reads=[B_acc], writes=[B_CT])
                elif kind == "B":
                    P.op("act", "activation", A(out=BT, in_=acc, func=AF.Silu), reads=[B_acc], writes=[B_BT])
                    for c in range(NT):
                        P.op("pe", "transpose", A(out=psb(6)[:, 128 * c:128 * c + 128], in_=BT[:, 128 * c:128 * c + 128], identity=identb[:]),
                             reads=[B_BT, B_cst], writes=[B_ps[6]])
                    evac(Btok, psb(6)[:, 0:128 * NT].rearrange("p (c f) -> p c f", c=NT), [B_ps[6]], [B_Btok])
                else:
                    jx = bi_
                    P.op("act", "activation", A(out=xsT, in_=acc, func=AF.Silu), reads=[B_acc], writes=[B_xsT])
                    for c in range(NT):
                        P.op("pe", "transpose", A(out=psb(7)[:, 128 * c:128 * c + 128], in_=xsT[:, 128 * c:128 * c + 128], identity=identb[:]),
                             reads=[B_xsT, B_cst], writes=[B_ps[7]])
                    evac(xs_tok[:, :, 128 * jx:128 * jx + 128], psb(7)[:, 0:128 * NT].rearrange("p (c f) -> p c f", c=NT), [B_ps[7]], [B_xs])
                yield
            wz, wzB = load_w(win(OFF_Z + 256 * g, 256), 8, 256)
            for c in range(NT):
                pi = 6 + c % 2
                proj_tm(pi, 0, wz, wzB, 0, 256, t0 + 128 * c)
                P.op("act", "activation", A(out=zs[:, c, :], in_=psum[pi][:, 0:256], func=AF.Silu), reads=[B_ps[pi]], writes=[B_zs])
                if c % 2:
                    yield

        def scan(g):
            q = g % 2
            xs_tok, BT, Btok, CT, zs, dskI, sg_b = xs_tok2[q], BT2[q], Btok2[q], CT2[q], zs2[q], dskI2[q], sg_b2[q]
            B_xs, B_BT, B_Btok, B_CT, B_zs, B_dskI, B_sg = B_xs2[q], B_BT2[q], B_Btok2[q], B_CT2[q], B_zs2[q], B_dskI2[q], B_sg2[q]
            P.op("act", "copy", A(out=st_b, in_=state_f[:, g, :]), reads=[B_state[g]], writes=[B_stb])
            CBb = (1, 5)

            def S1(c):
                i2 = c % 2
                tk = slice(128 * c, 128 * c + 128)
                ag = adt[:, c, 4 * g:4 * g + 4]
                P.op("dve", "tensor_tensor", A(out=R[i2].rearrange("p (h l) -> p h l", h=4), in0=cst[:, C_TRIL:C_TRIL + 128].unsqueeze(1).broadcast_to([128, 4, 128]),
                                               in1=ag.unsqueeze(2).broadcast_to([128, 4, 128]), op=ALU.mult),
                     reads=[B_cst, B_dt], writes=[B_R[i2]])
                P.op("pe", "matmul", A(psum[0][:, :], lhsT=cstb[:, 0:128], rhs=R[i2], start=True, stop=False), reads=[B_cst, B_R[i2]], writes=[B_ps[0]])
                P.op("pe", "matmul", A(psum[0][:, :], lhsT=cstb[:, 128:256], rhs=cstb[:, 256:768], start=False, stop=True), reads=[B_cst], writes=[B_ps[0]])
                P.op("act", "activation", A(out=LT[i2], in_=psum[0][:, :], func=AF.Exp), reads=[B_ps[0]], writes=[B_LT[i2]])
                P.op("pe", "matmul", A(psum[CBb[i2]][:, 0:128], lhsT=BT[:, tk], rhs=CT[:, tk], start=True, stop=True), reads=[B_BT, B_CT], writes=[B_ps[CBb[i2]]])

            def S2(c):
                i2 = c % 2
                P.op("dve", "tensor_tensor", A(out=xdt[i2].rearrange("p (h q) -> p h q", h=4), in0=xs_tok[:, c, :].rearrange("p (h q) -> p h q", h=4),
                                               in1=dt[:, c, 4 * g:4 * g + 4].unsqueeze(2).broadcast_to([128, 4, 64]), op=ALU.mult),
                     reads=[B_xs, B_dt], writes=[B_xd[i2]])
                P.op("dve", "tensor_tensor", A(out=xdd[i2].rearrange("p (h q) -> p h q", h=4), in0=xdt[i2].rearrange("p (h q) -> p h q", h=4),
                                               in1=E[:, c, 32 + 4 * g:32 + 4 * g + 4].unsqueeze(2).broadcast_to([128, 4, 64]), op=ALU.mult),
                     reads=[B_xd[i2], B_E], writes=[B_xdd[i2]])
                P.op("dve", "tensor_tensor", A(out=MT[i2].rearrange("p (h l) -> p h l", h=4), in0=LT[i2].rearrange("p (h l) -> p h l", h=4),
                                               in1=psum[CBb[i2]][:, 0:128].unsqueeze(1).broadcast_to([128, 4, 128]), op=ALU.mult),
                     reads=[B_LT[i2], B_ps[CBb[i2]]], writes=[B_MT[i2]])

            def S3(c):
                i2 = c % 2
                tk = slice(128 * c, 128 * c + 128)
                for hh in range(4):
                    P.op("pe", "matmul", A(psum[2][:, 64 * hh:64 * hh + 64], lhsT=MT[i2][:, 128 * hh:128 * hh + 128], rhs=xdt[i2][:, 64 * hh:64 * hh + 64], start=True, stop=False),
                         reads=[B_MT[i2], B_xd[i2]], writes=[B_ps[2]])
                    P.op("pe", "matmul", A(psum[2][:, 64 * hh:64 * hh + 64], lhsT=dskI[:, hh, :], rhs=xs_tok[:, c, 64 * hh:64 * hh + 64], start=False, stop=True),
                         reads=[B_dskI, B_xs], writes=[B_ps[2]])
                P.op("pe", "matmul", A(psum[4][:, 0:256], lhsT=Btok[:, c, :], rhs=xdd[i2], start=True, stop=True), reads=[B_Btok, B_xdd[i2]], writes=[B_ps[4]])
                P.op("pe", "matmul", A(psum[3][:, 0:256], lhsT=CT[:, tk], rhs=st_b, start=True, stop=True), reads=[B_CT, B_stb], writes=[B_ps[3]])
                P.op("dve", "tensor_tensor", A(out=y1.rearrange("p (h q) -> p h q", h=4), in0=psum[3][:, 0:256].rearrange("p (h q) -> p h q", h=4),
                                               in1=E[:, c, 4 * g:4 * g + 4].unsqueeze(2).broadcast_to([128, 4, 64]), op=ALU.mult),
                     reads=[B_ps[3], B_E], writes=[B_y1])
                P.op("dve", "tensor_tensor", A(out=stmp.rearrange("p (h q) -> p h q", h=4), in0=state_f[:, g, :].rearrange("p (h q) -> p h q", h=4),
                                               in1=E[:, c, 64 + 4 * g:64 + 4 * g + 4].unsqueeze(2).broadcast_to([128, 4, 64]), op=ALU.mult),
                     reads=[B_state[g], B_E], writes=[B_stmp])
                P.op("dve", "tensor_tensor", A(out=state_f[:, g, :], in0=stmp, in1=psum[4][:, 0:256], op=ALU.add), reads=[B_stmp, B_ps[4]], writes=[B_state[g]])
                P.op("act", "copy", A(out=st_b, in_=state_f[:, g, :]), reads=[B_state[g]], writes=[B_stb])
                P.op("dve", "tensor_tensor", A(out=y2[:, c, :], in0=y1, in1=psum[2][:, 0:256], op=ALU.add), reads=[B_y1, B_ps[2]], writes=[B_y2c[c]])

            S1(0)
            if NT > 1:
                S1(1)
            S2(0)
            for c in range(NT):
                if c + 2 < NT:
                    S1(c + 2)
                if c + 1 < NT:
                    S2(c + 1)
                S3(c)
                yield
            P.op("dve", "tensor_tensor", A(out=y2, in0=y2, in1=zs, op=ALU.mult), reads=B_y2 + [B_zs], writes=B_y2)
            for c in range(NT):
                P.op("act", "activation", A(out=junk, in_=y2[:, c, :], func=AF.Square, accum_out=ss[:, c:c + 1]), reads=B_y2, writes=[B_junk, B_ss])
            rstd_from_ss(ss[:, 0:NT], NT, 1.0 / 256, [B_ss])
            yield
            P.op("dve", "tensor_tensor", A(out=y2, in0=y2, in1=ss[:, 0:NT].unsqueeze(2).broadcast_to([128, NT, 256]), op=ALU.mult), reads=B_y2 + [B_ss], writes=B_y2)
            P.op("dve", "tensor_tensor", A(out=yn, in0=y2, in1=sg_b.unsqueeze(1).broadcast_to([128, NT, 256]), op=ALU.mult), reads=B_y2 + [B_sg], writes=[B_yn])
            yield
            for c in range(NT):
                transpose_to_YT(yn[:, c, :], B_yn, 2, 2 * g, 128 * c, c % 2)
                if c % 2:
                    yield

        def run_interleaved(gens):
            gens = [g_ for g_ in gens if g_ is not None]
            while gens:
                for g_ in list(gens):
                    try:
                        next(g_)
                    except StopIteration:
                        gens.remove(g_)

        run_interleaved([prep(0)])
        for g in range(8):
            run_interleaved([scan(g), prep(g + 1) if g + 1 < 8 else None])
        P.barrier()

    def merge_phase(sq, part, bi, wbr_d, nkb, first):
        t0 = part * T
        it = 0
        for ob in range(8):
            wb_, wbB = load_w(wbr_d.ap()[:, 128 * ob:128 * ob + 128], nkb, 128)
            wg, wgB = load_w(win(OFF_GATE + 1024 * bi + 128 * ob, 128), 8, 128)
            for tc in range(T // 512):
                pa, pg = 2 * (it % 2), 2 * (it % 2) + 1
                it += 1
                for kb in range(nkb):
                    P.op("pe", "matmul", A(psum[pa][:, :], lhsT=wb_[:, kb, :], rhs=YT[:, kb, 512 * tc:512 * tc + 512], start=(kb == 0), stop=(kb == nkb - 1)),
                         reads=[wbB, B_YT], writes=[B_ps[pa]])
                proj_fm(pg, 0, wg, wgB, 0, 128, t0 + 512 * tc, 512)
                P.op("act", "activation", A(out=msig[:], in_=psum[pg][:, :], func=AF.Sigmoid), reads=[B_ps[pg]], writes=[B_msig])
                dst = mrg[:, ob, 512 * tc:512 * tc + 512]
                if first:
                    P.op("dve", "tensor_tensor", A(out=dst, in0=msig[:], in1=psum[pa][:, :], op=ALU.mult), reads=[B_msig, B_ps[pa]], writes=[B_mrg])
                else:
                    P.op("dve", "tensor_tensor", A(out=mtmp[:], in0=msig[:], in1=psum[pa][:, :], op=ALU.mult), reads=[B_msig, B_ps[pa]], writes=[B_mtmp])
                    P.op("dve", "tensor_tensor", A(out=dst, in0=dst, in1=mtmp[:], op=ALU.add), reads=[B_mtmp, B_mrg], writes=[B_mrg])

    def out_phase(sq, part):
        t0 = part * T
        cv = Carver()
        fg_b = cv.f32(1024); B_fg = Buf()
        xt = [cv.f32(1024), cv.f32(1024)]; B_xt = [Buf(), Buf()]
        r = [cv.f32(1024), cv.f32(1024)]; B_r = [Buf(), Buf()]
        junk = cv.f32(1024); B_junk = Buf()
        ss = cv.f32(2); B_ss = Buf()
        bcast_load(fg_b, fgain_d.ap(), [B_fg])
        wts = [load_w(w_out.ap()[:, 256 * i:256 * i + 256], 8, 256) for i in range(4)]
        for c in range(NT):
            xi = c % 2
            rows = slice(t0 + 128 * c, t0 + 128 * c + 128)
            P.dma("sp", "dma_start", A(out=xt[xi], in_=x_d.ap()[sq, rows, :]), writes=[B_xt[xi]])
            for hf in range(2):
                pi = 2 * xi + hf
                for q4 in range(2):
                    wt, wB = wts[2 * hf + q4]
                    for kb in range(8):
                        P.op("pe", "matmul", A(psum[pi][:, 256 * q4:256 * q4 + 256], lhsT=mrg[:, kb, 128 * c:128 * c + 128], rhs=wt[:, kb, :], start=(kb == 0), stop=(kb == 7)),
                             reads=[wB, B_mrg], writes=[B_ps[pi]])
                P.op("dve", "tensor_tensor", A(out=r[xi][:, 512 * hf:512 * hf + 512], in0=xt[xi][:, 512 * hf:512 * hf + 512], in1=psum[pi][:, :], op=ALU.add),
                     reads=[B_xt[xi], B_ps[pi]], writes=[B_r[xi]])
            P.op("act", "activation", A(out=junk, in_=r[xi], func=AF.Square, accum_out=ss[:, 0:1]), reads=[B_r[xi]], writes=[B_junk, B_ss])
            rstd_from_ss(ss[:, 0:1], 1, 1.0 / D, [B_ss])
            P.op("dve", "scalar_tensor_tensor", A(out=r[xi], in0=r[xi], scalar=ss[:, 0:1], in1=fg_b, op0=ALU.mult, op1=ALU.mult), reads=[B_r[xi], B_ss, B_fg], writes=[B_r[xi]])
            P.dma("sp", "dma_start", A(out=out_d.ap()[sq, rows, :], in_=r[xi]), reads=[B_r[xi]])
        P.barrier()

    for sq in range(nseq):
        prologue(sq)
        for part in range(NPART):
            first = True
            for bi, (name, fn, wbr, nkb) in enumerate((("ssm", ssm_phase, w_brs, 16), ("diff", diff_phase, w_brd, 8), ("mem", mem_phase, w_brm, 8))):
                if name not in branches:
                    continue
                fn(sq, part)
                merge_phase(sq, part, bi, wbr, nkb, first)
                first = False
            out_phase(sq, part)
    P.finish()
    P.emit()
    return nc, P


_CACHE = {}


def kernel(**inputs):
    ncores = _CFG["ncores"]; nseq = _CFG["nseq"]
    key = (nseq, tuple(_CFG["branches"]), _CFG["same_engine_sync"], _CFG["schedule"])
    if key not in _CACHE:
        _CACHE[key] = build(nseq, _CFG["branches"], _CFG["same_engine_sync"], _CFG["schedule"])
    nc, P = _CACHE[key]
    f = lambda a: np.ascontiguousarray(np.asarray(a, dtype=np.float32))
    cst, oh = host_consts()
    shared = {
        "w_in": f(inputs["w_in"][0]), "w_mem_kv": f(inputs["w_mem_kv"][0]), "w_br_ssm": f(inputs["w_br_ssm"][0]),
        "w_br_diff": f(inputs["w_br_diff"][0]), "w_br_mem": f(inputs["w_br_mem"][0]), "w_out": f(inputs["w_out"][0]),
        "norm_gain": f(inputs["norm_gain"]).reshape(1, D), "mem_norm_gain": f(inputs["mem_norm_gain"]).reshape(1, D),
        "final_norm_gain": f(inputs["final_norm_gain"]).reshape(1, D), "ssm_norm_gain": f(inputs["ssm_norm_gain"]).reshape(1, 2048),
        "subln_gain": f(inputs["subln_gain"]).reshape(1, 128), "dt_bias": f(inputs["dt_bias"]).reshape(1, 32),
        "a_log": f(inputs["a_log"]).reshape(1, 32), "d_skip": f(inputs["d_skip"]).reshape(1, 32),
        "lambda_q1": f(inputs["lambda_q1"]).reshape(1, 64), "lambda_k1": f(inputs["lambda_k1"]).reshape(1, 64),
        "lambda_q2": f(inputs["lambda_q2"]).reshape(1, 64), "lambda_k2": f(inputs["lambda_k2"]).reshape(1, 64),
        "rel_bias": f(inputs["rel_bias"]),
        "conv_w_l": f(np.asarray(inputs["conv_w"][0]).reshape(4, 32, 128).transpose(2, 1, 0)),
        "conv_b_l": f(np.asarray(inputs["conv_b"][0]).reshape(32, 128).transpose(1, 0)),
        "cst": cst, "onehot": oh,
    }
    x = np.asarray(inputs["x"]); mem = np.asarray(inputs["mem"])
    in_maps = []
    for c in range(ncores):
        m = dict(shared)
        m["x"] = f(x[c * nseq:(c + 1) * nseq])
        m["mem"] = f(mem[c * nseq:(c + 1) * nseq])
        in_maps.append(m)
    if _CFG.get("trace"):
        res = run_bass_kernel_spmd(nc, in_maps, core_ids=list(range(ncores)), trace=True)
        print("EXEC_TIME_NS", res.exec_time_ns)
    else:
        res = run_bass_kernel_spmd(nc, in_maps, core_ids=list(range(ncores)))
    return np.concatenate([np.asarray(r["out"]) for r in res.results], axis=0).astype(np.float32)
```

```python
import math
import numpy as np
import concourse.bass as bass
import concourse.mybir as mybir
from concourse.bass_utils import run_bass_kernel_spmd

F32 = mybir.dt.float32
BF16 = mybir.dt.bfloat16
ALU = mybir.AluOpType
AF = mybir.ActivationFunctionType

D = 1024
S = 2048
MEM = 256
IN_DIM = 15392
OFF_Z, OFF_XBC, OFF_DT, OFF_DQ, OFF_DK, OFF_DV, OFF_DG, OFF_MQ, OFF_MG, OFF_GATE = (
    0, 2048, 6144, 6176, 7200, 8224, 9248, 10272, 11296, 12320)
EPS = 1e-5
NEG = -30000.0
T = 1024
NT = T // 128
NPART = S // T

_CFG = {"branches": ("ssm", "diff", "mem"), "ncores": 8, "nseq": 4, "same_engine_sync": "raw", "schedule": True}

COMPUTE = ("pe", "act", "dve", "pool")
QUEUES = ("pe", "act", "dve", "pool", "sp")
DMAQ = ("sp", "pool", "act")
NDMASEM = 12


def A(*a, **k):
    return (a, k)


class Buf:
    __slots__ = ("name", "last_w", "readers")

    def __init__(self, name=""):
        self.name = name
        self.last_w = None
        self.readers = []


def _free(ap):
    n = 1
    for d in ap.shape[1:]:
        n *= int(d)
    return n


class Prog:
    def __init__(self, nc, same_engine_sync="raw", schedule=True):
        self.nc = nc
        self.ins = []
        self.segs = [[]]
        self.same_engine_sync = same_engine_sync
        self.schedule = schedule
        self.sem = {e: nc.alloc_semaphore("s_" + e) for e in COMPUTE}
        self.dsem = {e: [nc.alloc_semaphore(f"d_{e}{i}") for i in range(NDMASEM)] for e in DMAQ}

    def _deps(self, reads, writes):
        deps = set()
        raw = set()
        for b in reads:
            if b.last_w is not None:
                deps.add(b.last_w)
                raw.add(b.last_w)
        for b in writes:
            if b.last_w is not None:
                deps.add(b.last_w)
            deps.update(b.readers)
        return deps, raw

    def _commit(self, me, reads, writes):
        for b in reads:
            b.readers.append(me)
        for b in writes:
            b.last_w = me
            b.readers = []

    def _est(self, eng, meth, args, is_dma):
        a, k = args
        try:
            out = k.get("out", a[0] if a else None)
            n = _free(out)
            if is_dma:
                return 60.0, 2000.0 + n * (4 if out.dtype == F32 else 2) * 128 / 150.0
            if eng == "pe":
                if meth == "transpose":
                    return 110.0, 0.0
                f32 = k["lhsT"].dtype == F32
                return max(64.0, n * 0.42) * (4 if f32 else 1), 0.0
            if eng == "act":
                return 224.0 + 0.75 * n, 0.0
            if eng == "dve":
                return 60.0 + 1.05 * n, 0.0
            return 100.0 + 2.3 * n, 0.0
        except Exception:
            return 300.0, 0.0

    def _add(self, eng, meth, args, reads, writes, is_dma):
        deps, raw = self._deps(reads, writes)
        gid = len(self.ins)
        dur, lat = self._est(eng, meth, args, is_dma)
        tab = None
        if eng == "act" and meth == "activation":
            f = args[1].get("func")
            tab = "explog" if f in (AF.Exp, AF.Ln) else str(f)
        self.ins.append(dict(eng=eng, fn=(meth, args), deps=deps, raw=raw, dma=is_dma, dur=dur, lat=lat, tab=tab))
        self.segs[-1].append(gid)
        self._commit(gid, reads, writes)
        return gid

    def op(self, eng, meth, args, reads=(), writes=()):
        return self._add(eng, meth, args, reads, writes, False)

    def dma(self, eng, meth, args, reads=(), writes=()):
        return self._add(eng, meth, args, reads, writes, True)

    def barrier(self):
        self.segs.append([])

    def finish(self):
        self.segs.append([])

    def _sched(self, seg):
        ins = self.ins
        pend = {e: [] for e in QUEUES}
        for g in seg:
            pend[ins[g]["eng"]].append(g)
        if not self.schedule:
            return pend
        segset = set(seg)
        W = _CFG.get('W', 32)
        fin = {}
        nun = {}
        users = {}
        for g in seg:
            c = 0
            for d in ins[g]["deps"]:
                if d in segset:
                    c += 1
                    users.setdefault(d, []).append(g)
            nun[g] = c
        blev = {}
        for g in reversed(seg):
            m = 0.0
            for u in users.get(g, ()):
                if blev[u] > m:
                    m = blev[u]
            blev[g] = ins[g]["dur"] + ins[g]["lat"] + m
        free = {e: 0.0 for e in QUEUES}
        curtab = [None]
        order = {e: [] for e in QUEUES}
        cand = {e: None for e in QUEUES}
        dirty = set(QUEUES)
        remaining = len(seg)
        while remaining:
            for e in list(dirty):
                best = None
                lst = pend[e]
                for g in lst[:W]:
                    if nun[g]:
                        continue
                    I = ins[g]
                    r = 0.0
                    for d in I["deps"]:
                        if d in fin:
                            t = fin[d] + (150.0 if ins[d]["eng"] != e else 0.0)
                            if t > r:
                                r = t
                    st = r if r > free[e] else free[e]
                    if e == "act" and I["tab"] is not None and I["tab"] != curtab[0]:
                        st += 1300.0
                    if best is None or st < best[0] or (st == best[0] and blev[g] > blev[best[1]]):
                        best = (st, g)
                cand[e] = best
            dirty.clear()
            be = None
            for e in QUEUES:
                c = cand[e]
                if c is not None and (be is None or c[0] < cand[be][0]):
                    be = e
            if be is None:
                raise RuntimeError("scheduler deadlock")
            st, g = cand[be]
            I = ins[g]
            if be == "act" and I["tab"] is not None:
                curtab[0] = I["tab"]
            free[be] = st + I["dur"]
            fin[g] = st + I["dur"] + I["lat"]
            pend[be].remove(g)
            order[be].append(g)
            remaining -= 1
            dirty.add(be)
            for u in users.get(g, ()):
                nun[u] -= 1
                if nun[u] == 0:
                    dirty.add(ins[u]["eng"])
        return order

    def emit(self):
        nc = self.nc
        ins = self.ins
        stream = {e: [] for e in QUEUES}
        pos = {}
        dma_list = {e: [] for e in DMAQ}

        def tails():
            t = set()
            for e in QUEUES:
                for x in reversed(stream[e]):
                    if not isinstance(x, tuple) and not ins[x]["dma"]:
                        t.add(x)
                        break
            for e in DMAQ:
                t.update(dma_list[e][-NDMASEM:])
            return t

        nseg = len(self.segs)
        for si, seg in enumerate(self.segs):
            if seg:
                order = self._sched(seg)
                for e in QUEUES:
                    for g in order[e]:
                        I = ins[g]
                        if I["dma"]:
                            dl = dma_list[e]
                            I["k"] = len(dl)
                            if len(dl) >= NDMASEM:
                                I["deps"] = set(I["deps"]) | {dl[-NDMASEM]}
                            dl.append(g)
                        pos[g] = len(stream[e])
                        stream[e].append(g)
            if si < nseg - 1:
                t = tails()
                last = (si == nseg - 2)
                for e in (("sp",) if last else QUEUES):
                    stream[e].append(("wait", t))
        signal = set()
        plan = {e: [] for e in QUEUES}
        for e in QUEUES:
            seen = {x: -1 for x in COMPUTE}
            seen_slot = {}
            for ent in stream[e]:
                if isinstance(ent, tuple):
                    deps, raw, g = ent[1], ent[1], None
                else:
                    g = ent
                    deps, raw = ins[g]["deps"], ins[g]["raw"]
                best = {}
                for d in deps:
                    D = ins[d]
                    if D["dma"]:
                        slot = (D["eng"], D["k"] % NDMASEM)
                        if seen_slot.get(slot, -1) >= D["k"] // NDMASEM:
                            continue
                        if slot not in best or ins[best[slot]]["k"] < D["k"]:
                            best[slot] = d
                    else:
                        x = D["eng"]
                        if x == e and (x == "pe" or not self.same_engine_sync):
                            continue
                        if x == e and self.same_engine_sync == "raw" and d not in raw:
                            continue
                        if seen[x] >= pos[d]:
                            continue
                        if x not in best or pos[best[x]] < pos[d]:
                            best[x] = d
                final = list(best.values())
                for d in final:
                    D = ins[d]
                    if D["dma"]:
                        seen_slot[(D["eng"], D["k"] % NDMASEM)] = D["k"] // NDMASEM
                    else:
                        seen[D["eng"]] = pos[d]
                        signal.add(d)
                plan[e].append((final, g))
        count = {}
        for e in COMPUTE:
            c = 0
            for ent in stream[e]:
                if not isinstance(ent, tuple) and ent in signal:
                    c += 1
                    count[ent] = c
        self.stats = {e: len(stream[e]) for e in QUEUES}

        def replay(e, eng):
            for final, g in plan[e]:
                for d in final:
                    D = ins[d]
                    if D["dma"]:
                        eng.wait_ge(self.dsem[D["eng"]][D["k"] % NDMASEM], 16 * (D["k"] // NDMASEM + 1))
                    else:
                        eng.wait_ge(self.sem[D["eng"]], count[d])
                if g is None:
                    continue
                I = ins[g]
                fn = I["fn"]
                o = getattr(eng, fn[0])(*fn[1][0], **fn[1][1])
                if I["dma"]:
                    o.then_inc(self.dsem[e][I["k"] % NDMASEM], 16)
                elif g in signal:
                    o.then_inc(self.sem[e], 1)

        with nc.Block() as block:
            @block.tensor
            def _(eng):
                replay("pe", eng)

            @block.scalar
            def _(eng):
                replay("act", eng)

            @block.vector
            def _(eng):
                replay("dve", eng)

            @block.gpsimd
            def _(eng):
                replay("pool", eng)

            @block.sync
            def _(eng):
                replay("sp", eng)


def t5_bucket_np(rel):
    n = np.maximum(rel, 0)
    max_exact = 16
    nf = np.maximum(n, 1).astype(np.float32)
    large = max_exact + (np.log(nf / max_exact) / np.float32(math.log(128 / max_exact))
                         * (32 - max_exact)).astype(np.int32)
    large = np.minimum(large, 31)
    return np.where(n < max_exact, n, large)


def host_consts():
    k = np.arange(128)[:, None]
    l = np.arange(128)[None, :]
    ident = (k == l).astype(np.float32)
    tril1 = (k <= l).astype(np.float32)
    sup = (k > l).astype(np.float32)
    ones = np.ones((128, 128), np.float32)
    negi = ident * NEG
    slow4 = np.tile(sup, (1, 4))
    cst = np.concatenate([ident, tril1, sup, ones, negi, slow4], axis=1)
    oh = np.zeros((33, 2, 128, 128), np.float32)
    for dlt in range(2):
        rel = 128 * dlt + (l - k)
        b = t5_bucket_np(rel)
        for kk in range(128):
            for qq in range(128):
                if rel[kk, qq] >= 0:
                    oh[b[kk, qq], dlt, kk, qq] = 1.0
                else:
                    oh[32, dlt, kk, qq] = 1.0
    return cst, oh.reshape(33, 2 * 128 * 128)


C_ID, C_TRIL, C_SUP, C_ONES, C_NEGI, C_SLOW4 = 0, 128, 256, 384, 512, 640


def build(nseq, branches, same_engine_sync="raw", schedule=True):
    nc = bass.Bass("TRN2", target_bir_lowering=False)
    P = Prog(nc, same_engine_sync, schedule)

    def din(name, shape, dt=F32):
        return nc.dram_tensor(name, list(shape), dt, kind="ExternalInput")

    x_d = din("x", [nseq, S, D])
    mem_d = din("mem", [nseq, MEM, D])
    w_in = din("w_in", [D, IN_DIM])
    w_kv = din("w_mem_kv", [D, 2048])
    w_brs = din("w_br_ssm", [2048, D])
    w_brd = din("w_br_diff", [D, D])
    w_brm = din("w_br_mem", [D, D])
    w_out = din("w_out", [D, D])
    gain_d = din("norm_gain", [1, D])
    mgain_d = din("mem_norm_gain", [1, D])
    fgain_d = din("final_norm_gain", [1, D])
    sgain_d = din("ssm_norm_gain", [1, 2048])
    subg_d = din("subln_gain", [1, 128])
    dtb_d = din("dt_bias", [1, 32])
    alog_d = din("a_log", [1, 32])
    dsk_d = din("d_skip", [1, 32])
    lq1_d, lk1_d, lq2_d, lk2_d = (din(n, [1, 64]) for n in ("lambda_q1", "lambda_k1", "lambda_q2", "lambda_k2"))
    rb_d = din("rel_bias", [32, 8])
    cw_d = din("conv_w_l", [128, 32, 4])
    cb_d = din("conv_b_l", [128, 32])
    sgl_d = din("sgain_l", [128, 16])
    cst_d = din("cst", [128, 1152])
    oh_d = din("onehot", [33, 32768])
    out_d = nc.dram_tensor("out", [nseq, S, D], F32, kind="ExternalOutput")
    bias_scr = nc.dram_tensor("bias_scr", [8, 32768], F32, kind="Internal")
    w_in_f, w_kv_f, w_brs_f, w_brd_f, w_brm_f, w_out_f = w_in, w_kv, w_brs, w_brd, w_brm, w_out
    w_in = nc.dram_tensor("w_in_b", [D, IN_DIM], BF16, kind="Internal")
    w_kv = nc.dram_tensor("w_kv_b", [D, 2048], BF16, kind="Internal")
    w_brs = nc.dram_tensor("w_brs_b", [2048, D], BF16, kind="Internal")
    w_brd = nc.dram_tensor("w_brd_b", [D, D], BF16, kind="Internal")
    w_brm = nc.dram_tensor("w_brm_b", [D, D], BF16, kind="Internal")
    w_out = nc.dram_tensor("w_out_b", [D, D], BF16, kind="Internal")

    def sb(name, shape, dt=F32):
        return nc.alloc_sbuf_tensor("sb_" + name, list(shape), dt)

    cst = sb("cst", [128, 1152]); B_cst = Buf("cst")
    identb = sb("identb", [128, 128], BF16)
    trilb = sb("trilb", [128, 128], BF16)
    cstb = sb("cstb", [128, 768], BF16)
    sm = sb("small", [128, 512]); B_sm = Buf("small")
    SM_CFAR, SM_DTB, SM_A, SM_DSK, SM_LAM, SM_NLAM, SM_ONE, SM_EPS = 0, 8, 40, 72, 104, 105, 106, 107
    SM_SUBG = 128
    SM_TMP = 256
    cw = sb("cw", [128, 32, 4]); cb = sb("cb", [128, 32]); B_cw = Buf("cw")
    sgl = sb("sgl", [128, 16])
    hT = sb("hT", [128, 8, S], BF16)
    B_hT = [Buf(f"hT{i}") for i in range(S // 128)]
    kmT = sb("kmT", [128, 8, MEM], BF16); B_kmT = Buf("kmT")
    vma = sb("vma", [128, 2, 4, 258], BF16); B_vma = Buf("vma")
    YT = sb("YT", [128, 16, T], BF16); B_YT = Buf("YT")
    mrg = sb("mrg", [128, 8, T], BF16); B_mrg = Buf("mrg")
    state_f = sb("state_f", [128, 8, 256]); B_state = [Buf(f"st{g}") for g in range(8)]
    halo = sb("halo", [128, 32, 4]); B_halo = Buf("halo")
    NWB = 6
    wbf = [sb(f"wbf{i}", [128, 2048], BF16) for i in range(NWB)]; B_wbf = [Buf(f"wbf{i}") for i in range(NWB)]
    ARENA_W = 18976
    arena = sb("arena", [128, ARENA_W])
    arena_b = arena[:].bitcast(BF16)

    class Carver:
        def __init__(self):
            self.off = 0

        def f32(self, n):
            a = arena[:, self.off:self.off + n]
            self.off += n
            assert self.off <= ARENA_W, self.off
            return a

        def bf16(self, n):
            w = (n + 1) // 2
            a = arena_b[:, 2 * self.off:2 * self.off + n]
            self.off += w
            assert self.off <= ARENA_W, self.off
            return a

    msig = sb("msig", [128, 512]); B_msig = Buf("msig")
    mtmp = sb("mtmp", [128, 512]); B_mtmp = Buf("mtmp")
    psum = [nc.alloc_psum_tensor(f"ps{i}", [128, 512], F32) for i in range(8)]
    B_ps = [Buf(f"ps{i}") for i in range(8)]

    def psb(i):
        return psum[i][:].bitcast(BF16)

    wctr = [0, 0]

    def load_w(src_ap, nkb, ncols):
        assert nkb * ncols <= 2048
        bi = wctr[1] % NWB; wctr[1] += 1
        bf = wbf[bi][:, 0:nkb * ncols].rearrange("p (k c) -> p k c", k=nkb)
        src = src_ap.rearrange("(k p) c -> p k c", p=128)
        P.dma("sp", "dma_start", A(out=bf, in_=src), writes=[B_wbf[bi]])
        return bf, B_wbf[bi]

    def win(c0, ncols):
        return w_in.ap()[:, c0:c0 + ncols]

    def hbufs(t0, n):
        return B_hT[t0 // 128:(t0 + n + 127) // 128]

    def proj_fm(ps_i, ps_cols, wt, wB, c_lo, ncol, tok0, ntok):
        out = psum[ps_i][0:ncol, ps_cols:ps_cols + ntok]
        for kb in range(8):
            P.op("pe", "matmul", A(out, lhsT=wt[:, kb, c_lo:c_lo + ncol], rhs=hT[:, kb, tok0:tok0 + ntok],
                                                 start=(kb == 0), stop=(kb == 7)),
                 reads=[wB] + hbufs(tok0, ntok), writes=[B_ps[ps_i]])

    def proj_tm(ps_i, ps_cols, wt, wB, c_lo, ncol, tok0):
        out = psum[ps_i][:, ps_cols:ps_cols + ncol]
        for kb in range(8):
            P.op("pe", "matmul", A(out, lhsT=hT[:, kb, tok0:tok0 + 128], rhs=wt[:, kb, c_lo:c_lo + ncol],
                                                 start=(kb == 0), stop=(kb == 7)),
                 reads=[wB] + hbufs(tok0, 128), writes=[B_ps[ps_i]])

    evac_rr = [0]
    evac_mode = ["alt"]

    def evac(out, in_, reads, writes):
        evac_rr[0] += 1
        if evac_mode[0] == "act" or (evac_mode[0] == "alt" and evac_rr[0] % 2):
            P.op("act", "copy", A(out=out, in_=in_), reads=reads, writes=writes)
        else:
            P.op("dve", "tensor_copy", A(out=out, in_=in_), reads=reads, writes=writes)

    def bcast_load(dst, src_row_ap, writes):
        P.dma("sp", "dma_start", A(out=dst, in_=src_row_ap.partition_broadcast(128)), writes=writes)

    def rstd_from_ss(ss_ap, n, inv_count, Bs):
        P.op("dve", "tensor_scalar", A(out=ss_ap, in0=ss_ap, scalar1=inv_count, scalar2=EPS, op0=ALU.mult, op1=ALU.add),
             reads=Bs, writes=Bs)
        P.op("act", "activation", A(out=ss_ap, in_=ss_ap, func=AF.Sqrt), reads=Bs, writes=Bs)
        P.op("dve", "reciprocal", A(out=ss_ap, in_=ss_ap), reads=Bs, writes=Bs)

    P.dma("sp", "dma_start", A(out=cst[:], in_=cst_d.ap()), writes=[B_cst])
    P.op("dve", "tensor_copy", A(out=identb[:], in_=cst[:, C_ID:C_ID + 128]), reads=[B_cst], writes=[B_cst])
    P.op("dve", "tensor_copy", A(out=trilb[:], in_=cst[:, C_TRIL:C_TRIL + 128]), reads=[B_cst], writes=[B_cst])
    P.op("dve", "tensor_copy", A(out=cstb[:, 0:128], in_=cst[:, C_SUP:C_SUP + 128]), reads=[B_cst], writes=[B_cst])
    P.op("dve", "tensor_copy", A(out=cstb[:, 128:768], in_=cst[:, C_NEGI:C_NEGI + 640]), reads=[B_cst], writes=[B_cst])
    P.dma("sp", "dma_start", A(out=cw[:], in_=cw_d.ap()), writes=[B_cw])
    P.dma("sp", "dma_start", A(out=cb[:], in_=cb_d.ap()), writes=[B_cw])
    P.dma("sp", "dma_start", A(out=sgl[:], in_=sgl_d.ap()), writes=[B_cw])
    P.op("pool", "memset", A(sm[:], 0.0), writes=[B_sm])
    P.op("pool", "memset", A(sm[:, SM_ONE:SM_ONE + 1], 1.0), writes=[B_sm])
    P.op("pool", "memset", A(sm[:, SM_EPS:SM_EPS + 1], EPS), writes=[B_sm])
    bcast_load(sm[:, SM_CFAR:SM_CFAR + 8], rb_d.ap()[31:32, :], [B_sm])
    bcast_load(sm[:, SM_DTB:SM_DTB + 32], dtb_d.ap(), [B_sm])
    bcast_load(sm[:, SM_A:SM_A + 32], alog_d.ap(), [B_sm])
    bcast_load(sm[:, SM_DSK:SM_DSK + 32], dsk_d.ap(), [B_sm])
    bcast_load(sm[:, SM_SUBG:SM_SUBG + 128], subg_d.ap(), [B_sm])
    for i, ld in enumerate((lq1_d, lk1_d, lq2_d, lk2_d)):
        bcast_load(sm[:, SM_TMP + 64 * i:SM_TMP + 64 * i + 64], ld.ap(), [B_sm])
    Bs = [B_sm]
    P.op("act", "activation", A(out=sm[:, SM_A:SM_A + 32], in_=sm[:, SM_A:SM_A + 32], func=AF.Exp), reads=Bs, writes=Bs)
    P.op("dve", "tensor_scalar_mul", A(out=sm[:, SM_A:SM_A + 32], in0=sm[:, SM_A:SM_A + 32], scalar1=-1.0), reads=Bs, writes=Bs)
    LAM_INIT = 0.8 - 0.6 * math.exp(-0.3 * 0)
    P.op("dve", "tensor_scalar_mul", A(out=sm[:, SM_SUBG:SM_SUBG + 128], in0=sm[:, SM_SUBG:SM_SUBG + 128], scalar1=1.0 - LAM_INIT), reads=Bs, writes=Bs)
    for i in range(2):
        a0 = SM_TMP + 128 * i
        P.op("dve", "tensor_tensor", A(out=sm[:, a0:a0 + 64], in0=sm[:, a0:a0 + 64], in1=sm[:, a0 + 64:a0 + 128], op=ALU.mult), reads=Bs, writes=Bs)
        P.op("dve", "reduce_sum", A(out=sm[:, 110 + i:111 + i], in_=sm[:, a0:a0 + 64], axis=mybir.AxisListType.X), reads=Bs, writes=Bs)
    P.op("act", "activation", A(out=sm[:, 110:112], in_=sm[:, 110:112], func=AF.Exp), reads=Bs, writes=Bs)
    P.op("dve", "tensor_tensor", A(out=sm[:, SM_LAM:SM_LAM + 1], in0=sm[:, 110:111], in1=sm[:, 111:112], op=ALU.subtract), reads=Bs, writes=Bs)
    P.op("dve", "tensor_scalar_add", A(out=sm[:, SM_LAM:SM_LAM + 1], in0=sm[:, SM_LAM:SM_LAM + 1], scalar1=LAM_INIT), reads=Bs, writes=Bs)
    P.op("dve", "tensor_scalar_mul", A(out=sm[:, SM_NLAM:SM_NLAM + 1], in0=sm[:, SM_LAM:SM_LAM + 1], scalar1=-1.0), reads=Bs, writes=Bs)

    cv = Carver()
    NPS = 4
    pst = [cv.f32(2048) for _ in range(NPS)]; B_pst = [Buf() for _ in range(NPS)]
    pbf = [cv.bf16(2048) for _ in range(NPS)]; B_pbf = [Buf() for _ in range(NPS)]
    pc_i = 0
    for (srcw, dstw, rows, cols) in ((w_in_f, w_in, D, IN_DIM), (w_kv_f, w_kv, D, 2048), (w_brs_f, w_brs, 2048, D),
                                     (w_brd_f, w_brd, D, D), (w_brm_f, w_brm, D, D), (w_out_f, w_out, D, D)):
        for rb in range(rows // 128):
            for c0 in range(0, cols, 2048):
                n = min(2048, cols - c0)
                si = pc_i % NPS
                P.dma("sp", "dma_start", A(out=pst[si][:, 0:n], in_=srcw.ap()[128 * rb:128 * rb + 128, c0:c0 + n]), writes=[B_pst[si]])
                ce = ("act", "dve")[pc_i % 2]
                if srcw is w_brs_f:
                    P.op("dve", "tensor_scalar", A(out=pbf[si][:, 0:n], in0=pst[si][:, 0:n], scalar1=sgl[:, rb:rb + 1], scalar2=None, op0=ALU.mult),
                         reads=[B_pst[si], B_cw], writes=[B_pbf[si]])
                else:
                    P.op(ce, "copy" if ce == "act" else "tensor_copy", A(out=pbf[si][:, 0:n], in_=pst[si][:, 0:n]), reads=[B_pst[si]], writes=[B_pbf[si]])
                P.dma("sp", "dma_start", A(out=dstw.ap()[128 * rb:128 * rb + 128, c0:c0 + n], in_=pbf[si][:, 0:n]), reads=[B_pbf[si]])
                pc_i += 1
    P.barrier()

    if "diff" in branches:
        cv = Carver()
        rbx = cv.f32(8)
        ohs = cv.f32(4096)
        stg = cv.f32(4096)
        B_rbx, B_ohs, B_stg, B_scr = Buf(), Buf(), Buf(), Buf()
        P.op("pool", "memset", A(rbx[32:33, :], NEG), writes=[B_rbx])
        P.dma("sp", "dma_start", A(out=rbx[0:32, :], in_=rb_d.ap()), writes=[B_rbx])
        for pc in range(8):
            P.dma("sp", "dma_start", A(out=ohs[0:33, :], in_=oh_d.ap()[:, 4096 * pc:4096 * pc + 4096]), writes=[B_ohs])
            for i in range(8):
                pi = i % 2
                P.op("pe", "matmul", A(psum[pi][0:8, :], lhsT=rbx[0:33, :], rhs=ohs[0:33, 512 * i:512 * i + 512], start=True, stop=True),
                     reads=[B_rbx, B_ohs], writes=[B_ps[pi]])
                evac(stg[0:8, 512 * i:512 * i + 512], psum[pi][0:8, :], [B_ps[pi]], [B_stg])
            P.dma("sp", "dma_start", A(out=bias_scr.ap()[:, 4096 * pc:4096 * pc + 4096], in_=stg[0:8, :]), reads=[B_stg], writes=[B_scr])
        P.barrier()

    def prologue(sq):
        cv = Carver()
        gain_b = cv.f32(1024); B_gain = Buf()
        xt = [cv.f32(1024), cv.f32(1024)]; B_xt = [Buf(), Buf()]
        junk = cv.f32(1024); B_junk = Buf()
        hb = cv.bf16(1024); B_hb = Buf()
        ss = cv.f32(2); B_ss = Buf()
        memT = cv.bf16(8 * MEM); B_memT = Buf()
        memT3 = memT.rearrange("p (k m) -> p k m", k=8)

        def norm_transpose(src_ap, i, dst3, dstB, col0):
            xi = i % 2
            P.dma("sp", "dma_start", A(out=xt[xi], in_=src_ap), writes=[B_xt[xi]])
            P.op("act", "activation", A(out=junk, in_=xt[xi], func=AF.Square, accum_out=ss[:, 0:1]), reads=[B_xt[xi]], writes=[B_junk, B_ss])
            rstd_from_ss(ss[:, 0:1], 1, 1.0 / D, [B_ss])
            P.op("dve", "scalar_tensor_tensor", A(out=hb, in0=xt[xi], scalar=ss[:, 0:1], in1=gain_b, op0=ALU.mult, op1=ALU.mult),
                 reads=[B_xt[xi], B_ss, B_gain], writes=[B_hb])
            for kb in range(8):
                P.op("pe", "transpose", A(out=psb(6)[:, 128 * kb:128 * kb + 128], in_=hb[:, 128 * kb:128 * kb + 128], identity=identb[:]),
                     reads=[B_hb, B_cst], writes=[B_ps[6]])
            evac(dst3[:, :, col0:col0 + 128], psb(6)[:, 0:1024].rearrange("p (k t) -> p k t", k=8), [B_ps[6]], dstB)

        bcast_load(gain_b, gain_d.ap(), [B_gain])
        for i in range(S // 128):
            norm_transpose(x_d.ap()[sq, 128 * i:128 * i + 128, :], i, hT, [B_hT[i]], 128 * i)
        if "mem" in branches:
            bcast_load(gain_b, mgain_d.ap(), [B_gain])
            for i in range(2):
                norm_transpose(mem_d.ap()[sq, 128 * i:128 * i + 128, :], i, memT3, [B_memT], 128 * i)
            for fb in range(8):
                wt, wB = load_w(w_kv.ap()[:, 128 * fb:128 * fb + 128], 8, 128)
                pi = fb % 2
                for kb in range(8):
                    P.op("pe", "matmul", A(psum[pi][:, 0:MEM], lhsT=wt[:, kb, :], rhs=memT3[:, kb, :], start=(kb == 0), stop=(kb == 7)),
                         reads=[wB, B_memT], writes=[B_ps[pi]])
                evac(kmT[:, fb, :], psum[pi][:, 0:MEM], [B_ps[pi]], [B_kmT])
            P.op("pool", "memset", A(vma[:, :, :, 256:258], 1.0), writes=[B_vma])
            for hh in range(4):
                wt, wB = load_w(w_kv.ap()[:, 1024 + 256 * hh:1024 + 256 * hh + 256], 8, 256)
                for mt in range(2):
                    pi = mt
                    for kb in range(8):
                        P.op("pe", "matmul", A(psum[pi][:, 0:256], lhsT=memT3[:, kb, 128 * mt:128 * mt + 128], rhs=wt[:, kb, :], start=(kb == 0), stop=(kb == 7)),
                             reads=[wB, B_memT], writes=[B_ps[pi]])
                    evac(vma[:, mt, hh, 0:256], psum[pi][:, 0:256], [B_ps[pi]], [B_vma])
        P.barrier()

    def transpose_to_YT(src_bf, srcB, nblk, yt_blk0, tokcol0, ps_i):
        for j in range(nblk):
            P.op("pe", "transpose", A(out=psb(ps_i)[:, 128 * j:128 * j + 128], in_=src_bf[:, 128 * j:128 * j + 128], identity=identb[:]),
                 reads=[srcB, B_cst], writes=[B_ps[ps_i]])
        evac(YT[:, yt_blk0:yt_blk0 + nblk, tokcol0:tokcol0 + 128], psb(ps_i)[:, 0:128 * nblk].rearrange("p (j t) -> p j t", j=nblk), [B_ps[ps_i]], [B_YT])

    def mem_phase(sq, part):
        t0 = part * T
        evac_mode[0] = "alt"
        cv = Carver()
        QmT = cv.bf16(2 * T).rearrange("p (d t) -> p d t", d=2); B_Qm = Buf()
        Gm = cv.f32(NT * 256).rearrange("p (c f) -> p c f", c=NT); B_Gm = Buf()
        PmT = [cv.bf16(2 * 512).rearrange("p (m t) -> p m t", m=2) for _ in range(2)]; B_Pm = [Buf(), Buf()]
        rr = cv.f32(4); B_rr = Buf()
        ym = [cv.bf16(256), cv.bf16(256)]; B_ym = [Buf(), Buf()]
        it = 0
        for hh in range(4):
            wq, wqB = load_w(win(OFF_MQ + 256 * hh, 256), 8, 256)
            wg, wgB = load_w(win(OFF_MG + 256 * hh, 256), 8, 256)
            for db in range(2):
                for tc in range(T // 512):
                    pi = (2 * db + tc) % 2
                    proj_fm(pi, 0, wq, wqB, 128 * db, 128, t0 + 512 * tc, 512)
                    evac(QmT[:, db, 512 * tc:512 * tc + 512], psum[pi][:, :], [B_ps[pi]], [B_Qm])
            for c in range(NT):
                pi = 2 + c % 2
                proj_tm(pi, 0, wg, wgB, 0, 256, t0 + 128 * c)
                P.op("act", "activation", A(out=Gm[:, c, :], in_=psum[pi][:, 0:256], func=AF.Silu), reads=[B_ps[pi]], writes=[B_Gm])
            for tc in range(T // 512):
                pb = tc % 2
                for mt in range(2):
                    pi = mt
                    for db in range(2):
                        P.op("pe", "matmul", A(psum[pi][:, :], lhsT=kmT[:, 2 * hh + db, 128 * mt:128 * mt + 128], rhs=QmT[:, db, 512 * tc:512 * tc + 512], start=(db == 0), stop=(db == 1)),
                             reads=[B_kmT, B_Qm], writes=[B_ps[pi]])
                    P.op("act", "activation", A(out=PmT[pb][:, mt, :], in_=psum[pi][:, :], func=AF.Exp, scale=1.0 / 16.0), reads=[B_ps[pi]], writes=[B_Pm[pb]])
                for j in range(4):
                    pi = 2 + j % 2
                    for mt in range(2):
                        P.op("pe", "matmul", A(psum[pi][:, 0:257], lhsT=PmT[pb][:, mt, 128 * j:128 * j + 128], rhs=vma[:, mt, hh, 0:257], start=(mt == 0), stop=(mt == 1)),
                             reads=[B_Pm[pb], B_vma], writes=[B_ps[pi]])
                    yi = it % 2; it += 1
                    P.op("dve", "reciprocal", A(out=rr[:, 0:1], in_=psum[pi][:, 256:257]), reads=[B_ps[pi]], writes=[B_rr])
                    P.op("dve", "scalar_tensor_tensor", A(out=ym[yi], in0=psum[pi][:, 0:256], scalar=rr[:, 0:1], in1=Gm[:, 4 * tc + j, :], op0=ALU.mult, op1=ALU.mult),
                         reads=[B_ps[pi], B_rr, B_Gm], writes=[B_ym[yi]])
                    transpose_to_YT(ym[yi], B_ym[yi], 2, 2 * hh, 512 * tc + 128 * j, 6)
        P.barrier()

    def diff_phase(sq, part):
        t0 = part * T
        evac_mode[0] = "dve"
        nk = t0 + T
        cv = Carver()
        KT = cv.bf16(S); B_KT = Buf()
        QTc = [cv.bf16(T), cv.bf16(T)]; B_QT = Buf()
        P.op("pool", "memset", A(QTc[0][64:128, :], 0.0), writes=[B_QT])
        P.op("pool", "memset", A(QTc[1][0:64, :], 0.0), writes=[B_QT])
        Va = cv.bf16(16 * 130).rearrange("p (t v) -> p t v", t=16); B_Va = Buf()
        Gs = cv.f32(NT * 128).rearrange("p (c f) -> p c f", c=NT); B_Gs = Buf()
        tmpS = [cv.f32(256), cv.f32(256)]; B_tmpS = [Buf(), Buf()]
        PT = [cv.bf16(512), cv.bf16(512)]; B_PT = [Buf(), Buf()]
        Os = [cv.f32(4 * 129).rearrange("p (j v) -> p j v", j=4) for _ in range(2)]; B_Os = [Buf(), Buf()]
        obuf = cv.f32(NT * 128).rearrange("p (c f) -> p c f", c=NT); B_ob = Buf()
        t1 = cv.f32(512).rearrange("p (j v) -> p j v", j=4); t2 = cv.f32(512).rearrange("p (j v) -> p j v", j=4); B_t = Buf()
        rr = cv.f32(8); B_rr = Buf()
        ss = cv.f32(NT); B_ss = Buf()
        junk = cv.f32(128); B_junk = Buf()
        ydt = cv.bf16(NT * 128).rearrange("p (c f) -> p c f", c=NT); B_ydt = Buf()
        biasT = cv.f32(8 * 2 * 128).rearrange("p (h d q) -> p h d q", h=8, d=2); B_biasT = Buf()
        for h in range(8):
            P.dma("sp", "dma_start", A(out=biasT[:, h, :, :], in_=bias_scr.ap()[h, :].rearrange("(d k q) -> k d q", d=2, k=128)), writes=[B_biasT])
        if sq == 0 and part == 0:
            print("diff arena words", cv.off)
        P.op("pool", "memset", A(Va[:, :, 128:130], 1.0), writes=[B_Va])
        sidx = 0
        for h in range(8):
            wq, wqB = load_w(win(OFF_DQ + 128 * h, 128), 8, 128)
            wk, wkB = load_w(win(OFF_DK + 128 * h, 128), 8, 128)
            wv, wvB = load_w(win(OFF_DV + 128 * h, 128), 8, 128)
            wg, wgB = load_w(win(OFF_DG + 128 * h, 128), 8, 128)
            for kc in range(nk // 512):
                pi = kc % 2
                proj_fm(pi, 0, wk, wkB, 0, 128, 512 * kc, 512)
                evac(KT[:, 512 * kc:512 * kc + 512], psum[pi][:, :], [B_ps[pi]], [B_KT])
            for tc in range(T // 512):
                pi = tc % 2
                proj_fm(pi, 0, wq, wqB, 0, 128, t0 + 512 * tc, 512)
                evac(QTc[0][0:64, 512 * tc:512 * tc + 512], psum[pi][0:64, :], [B_ps[pi]], [B_QT])
                evac(QTc[1][64:128, 512 * tc:512 * tc + 512], psum[pi][64:128, :], [B_ps[pi]], [B_QT])
            for kt in range(nk // 128):
                pi = kt % 2
                proj_tm(pi, 0, wv, wvB, 0, 128, 128 * kt)
                evac(Va[:, kt, 0:128], psum[pi][:, 0:128], [B_ps[pi]], [B_Va])
            for c in range(NT):
                pi = c % 2
                proj_tm(pi, 0, wg, wgB, 0, 128, t0 + 128 * c)
                P.op("act", "activation", A(out=Gs[:, c, :], in_=psum[pi][:, 0:128], func=AF.Silu), reads=[B_ps[pi]], writes=[B_Gs])
            for qc in range(T // 512):
                qb0 = (t0 + 512 * qc) // 128
                for c in range(2):
                    r0 = 64 * c
                    nkb_ = qb0 + 4
                    sis = []
                    for kk in range(nkb_):
                        sis.append(sidx % 2); sidx += 1

                    def emit_S(kb_):
                        j0 = kb_ - qb0
                        jlo = max(j0, 0)
                        si = sis[kb_]
                        P.op("pe", "matmul", A(
                            psum[si][:, 128 * jlo:512], lhsT=KT[:, 128 * kb_:128 * kb_ + 128],
                            rhs=QTc[c][:, 512 * qc + 128 * jlo:512 * qc + 512], start=True, stop=True),
                             reads=[B_KT, B_QT], writes=[B_ps[si]])

                    emit_S(0)
                    for kb_ in range(nkb_):
                        if kb_ + 1 < nkb_:
                            emit_S(kb_ + 1)
                        j0 = kb_ - qb0
                        jlo = max(j0, 0)
                        si = sis[kb_]
                        Sps = psum[si]
                        nsp = 0
                        for j in range(jlo, 4):
                            dl = j - j0
                            if dl <= 1:
                                P.op("dve", "scalar_tensor_tensor", A(
                                    out=tmpS[si][:, 128 * nsp:128 * nsp + 128], in0=Sps[:, 128 * j:128 * j + 128], scalar=0.125,
                                    in1=biasT[:, h, dl, :], op0=ALU.mult, op1=ALU.add),
                                     reads=[B_ps[si], B_biasT], writes=[B_tmpS[si]])
                                nsp += 1
                        if nsp:
                            P.op("act", "activation", A(out=PT[si][:, 128 * jlo:128 * (jlo + nsp)], in_=tmpS[si][:, 0:128 * nsp], func=AF.Exp),
                                 reads=[B_tmpS[si]], writes=[B_PT[si]])
                        jf = max(j0 + 2, 0)
                        if jf < 4:
                            P.op("act", "activation", A(out=PT[si][:, 128 * jf:512], in_=Sps[:, 128 * jf:512], func=AF.Exp, scale=0.125, bias=sm[:, SM_CFAR + h:SM_CFAR + h + 1]),
                                 reads=[B_ps[si], B_sm], writes=[B_PT[si]])
                        for j in range(jlo, 4):
                            P.op("pe", "matmul", A(psum[2 + j][:, 0:129], lhsT=PT[si][:, 128 * j:128 * j + 128], rhs=Va[:, kb_, 0:129],
                                                   start=(kb_ == 0), stop=(kb_ == qb0 + j)),
                                 reads=[B_PT[si], B_Va], writes=[B_ps[2 + j]])
                    for j in range(4):
                        evac(Os[c][:, j, :], psum[2 + j][:, 0:129], [B_ps[2 + j]], [B_Os[c]])
                P.op("dve", "reciprocal", A(out=rr[:, 0:4], in_=Os[0][:, :, 128]), reads=[B_Os[0]], writes=[B_rr])
                P.op("dve", "reciprocal", A(out=rr[:, 4:8], in_=Os[1][:, :, 128]), reads=[B_Os[1], B_rr], writes=[B_rr])
                P.op("dve", "tensor_scalar", A(out=rr[:, 4:8], in0=rr[:, 4:8], scalar1=sm[:, SM_NLAM:SM_NLAM + 1], scalar2=None, op0=ALU.mult), reads=[B_rr, B_sm], writes=[B_rr])
                P.op("dve", "tensor_tensor", A(out=t1, in0=Os[0][:, :, 0:128], in1=rr[:, 0:4].unsqueeze(2).broadcast_to([128, 4, 128]), op=ALU.mult), reads=[B_Os[0], B_rr, B_t], writes=[B_t])
                P.op("dve", "tensor_tensor", A(out=t2, in0=Os[1][:, :, 0:128], in1=rr[:, 4:8].unsqueeze(2).broadcast_to([128, 4, 128]), op=ALU.mult), reads=[B_Os[1], B_rr, B_t], writes=[B_t])
                P.op("dve", "tensor_tensor", A(out=obuf[:, 4 * qc:4 * qc + 4, :], in0=t1, in1=t2, op=ALU.add), reads=[B_t], writes=[B_ob])
            for c in range(NT):
                P.op("act", "activation", A(out=junk, in_=obuf[:, c, :], func=AF.Square, accum_out=ss[:, c:c + 1]), reads=[B_ob], writes=[B_junk, B_ss])
            rstd_from_ss(ss[:, 0:NT], NT, 1.0 / 128, [B_ss])
            P.op("dve", "tensor_tensor", A(out=obuf, in0=obuf, in1=ss[:, 0:NT].unsqueeze(2).broadcast_to([128, NT, 128]), op=ALU.mult), reads=[B_ob, B_ss], writes=[B_ob])
            P.op("dve", "tensor_tensor", A(out=obuf, in0=obuf, in1=sm[:, SM_SUBG:SM_SUBG + 128].unsqueeze(1).broadcast_to([128, NT, 128]), op=ALU.mult), reads=[B_ob, B_sm], writes=[B_ob])
            P.op("dve", "tensor_tensor", A(out=ydt, in0=obuf, in1=Gs, op=ALU.mult), reads=[B_ob, B_Gs], writes=[B_ydt])
            for c in range(NT):
                transpose_to_YT(ydt[:, c, :], B_ydt, 1, h, 128 * c, 6 + c % 2)
        P.barrier()

    def ssm_phase(sq, part):
        t0 = part * T
        evac_mode[0] = _CFG.get("ssm_evac", "act")
        cv = Carver()
        Us = [cv.bf16(T + 4), cv.bf16(T + 4)]; B_Us = [Buf(), Buf()]
        dgs = [cv.bf16(4 * 128).rearrange("p (k c) -> p k c", k=4) for _ in range(2)]; B_dgs = [Buf(), Buf()]
        xsT = cv.bf16(T); B_xsT = Buf()
        xs_tok2 = [cv.bf16(NT * 256).rearrange("p (c f) -> p c f", c=NT) for _ in range(2)]; B_xs2 = [Buf(), Buf()]
        BT2 = [cv.bf16(T), cv.bf16(T)]; B_BT2 = [Buf(), Buf()]
        Btok2 = [cv.bf16(NT * 128).rearrange("p (c f) -> p c f", c=NT) for _ in range(2)]; B_Btok2 = [Buf(), Buf()]
        CT2 = [cv.bf16(T), cv.bf16(T)]; B_CT2 = [Buf(), Buf()]
        zs2 = [cv.bf16(NT * 256).rearrange("p (c f) -> p c f", c=NT) for _ in range(2)]; B_zs2 = [Buf(), Buf()]
        dskI2 = [cv.bf16(4 * 128).rearrange("p (h l) -> p h l", h=4) for _ in range(2)]; B_dskI2 = [Buf(), Buf()]
        sg_b2 = [cv.f32(256), cv.f32(256)]; B_sg2 = [Buf(), Buf()]
        dt = cv.f32(NT * 32).rearrange("p (c f) -> p c f", c=NT)
        adt = cv.f32(NT * 32).rearrange("p (c f) -> p c f", c=NT); B_dt = Buf()
        E = cv.f32(NT * 96).rearrange("p (c f) -> p c f", c=NT); B_E = Buf()
        R = [cv.bf16(512), cv.bf16(512)]; B_R = [Buf(), Buf()]
        LT = [cv.bf16(512), cv.bf16(512)]; B_LT = [Buf(), Buf()]
        MT = [cv.bf16(512), cv.bf16(512)]; B_MT = [Buf(), Buf()]
        xdt = [cv.bf16(256), cv.bf16(256)]; xdd = [cv.bf16(256), cv.bf16(256)]; B_xd = [Buf(), Buf()]; B_xdd = [Buf(), Buf()]
        y1 = cv.f32(256); B_y1 = Buf()
        y2 = cv.f32(NT * 256).rearrange("p (c f) -> p c f", c=NT); B_y2c = [Buf() for _ in range(NT)]; B_y2 = B_y2c
        yn = cv.bf16(NT * 256).rearrange("p (c f) -> p c f", c=NT); B_ync = [Buf() for _ in range(NT)]
        st_b = cv.bf16(256); B_stb = Buf()
        stmp = cv.f32(256); B_stmp = Buf()
        ss = cv.f32(NT); B_ss = Buf()
        junk = cv.bf16(256); B_junk = Buf()
        if sq == 0 and part == 0:
            print("ssm arena words", cv.off)

        wd, wdB = load_w(win(OFF_DT, 32), 8, 32)
        for c in range(NT):
            pi = c % 2
            proj_tm(pi, 0, wd, wdB, 0, 32, t0 + 128 * c)
            P.op("dve", "tensor_tensor", A(out=dt[:, c, :], in0=psum[pi][:, 0:32], in1=sm[:, SM_DTB:SM_DTB + 32], op=ALU.add), reads=[B_ps[pi], B_sm], writes=[B_dt])
        dtf = dt.rearrange("p c f -> p (c f)")
        P.op("act", "activation", A(out=dtf, in_=dtf, func=AF.Exp), reads=[B_dt], writes=[B_dt])
        P.op("act", "activation", A(out=dtf, in_=dtf, func=AF.Ln, bias=sm[:, SM_ONE:SM_ONE + 1]), reads=[B_dt, B_sm], writes=[B_dt])
        P.op("dve", "tensor_tensor", A(out=adt, in0=dt, in1=sm[:, SM_A:SM_A + 32].unsqueeze(1).broadcast_to([128, NT, 32]), op=ALU.mult), reads=[B_dt, B_sm], writes=[B_dt])
        for c in range(NT):
            pi = c % 2
            for i, cc in enumerate((C_TRIL, C_SUP, C_ONES)):
                P.op("pe", "matmul", A(psum[pi][:, 32 * i:32 * i + 32], lhsT=cst[:, cc:cc + 128], rhs=adt[:, c, :], start=True, stop=True),
                     reads=[B_cst, B_dt], writes=[B_ps[pi]])
            P.op("act", "activation", A(out=E[:, c, :], in_=psum[pi][:, 0:96], func=AF.Exp), reads=[B_ps[pi]], writes=[B_E])

        def prep(g):
            q = g % 2
            xs_tok, BT, Btok, CT, zs, dskI, sg_b = xs_tok2[q], BT2[q], Btok2[q], CT2[q], zs2[q], dskI2[q], sg_b2[q]
            B_xs, B_BT, B_Btok, B_CT, B_zs, B_dskI, B_sg = B_xs2[q], B_BT2[q], B_Btok2[q], B_CT2[q], B_zs2[q], B_dskI2[q], B_sg2[q]
            if part == 0:
                P.op("pool", "memset", A(state_f[:, g, :], 0.0), writes=[B_state[g]])
            for hh in range(4):
                P.op("pool", "tensor_scalar", A(out=dskI[:, hh, :], in0=cst[:, C_ID:C_ID + 128], scalar1=sm[:, SM_DSK + 4 * g + hh:SM_DSK + 4 * g + hh + 1], scalar2=None, op0=ALU.mult),
                     reads=[B_cst, B_sm], writes=[B_dskI])
            blocks = [("xs", 2 * g, OFF_XBC + 256 * g), ("xs", 2 * g + 1, OFF_XBC + 256 * g + 128),
                      ("B", 16 + g, OFF_XBC + 2048 + 128 * g), ("C", 24 + g, OFF_XBC + 3072 + 128 * g)]
            for bi_, (kind, blk, col) in enumerate(blocks):
                wt, wB = load_w(win(col, 128), 8, 128)
                U = Us[bi_ % 2]; B_U = B_Us[bi_ % 2]
                dg = dgs[bi_ % 2]; B_dg = B_dgs[bi_ % 2]
                for k in range(4):
                    P.op("dve", "tensor_scalar", A(out=dg[:, k, :], in0=cst[:, C_ID:C_ID + 128], scalar1=cw[:, blk, k:k + 1], scalar2=None, op0=ALU.mult),
                         reads=[B_cst, B_cw], writes=[B_dg])
                if part == 0:
                    P.op("pool", "memset", A(U[:, 0:4], 0.0), writes=[B_U])
                else:
                    P.op("pool", "tensor_copy", A(out=U[:, 0:4], in_=halo[:, blk, :]), reads=[B_halo], writes=[B_U])
                for tc in range(T // 512):
                    pi = 6 + tc % 2
                    proj_fm(pi, 0, wt, wB, 0, 128, t0 + 512 * tc, 512)
                    evac(U[:, 4 + 512 * tc:4 + 512 * tc + 512], psum[pi][:, :], [B_ps[pi]], [B_U])
                    yield
                P.op("pool", "tensor_copy", A(out=halo[:, blk, :], in_=U[:, T:T + 4]), reads=[B_U], writes=[B_halo])
                dst = {"C": (CT, B_CT), "B": (BT, B_BT), "xs": (xsT, B_xsT)}[kind]
                for tc in range(T // 512):
                    pi = 6 + tc % 2
                    for k in range(4):
                        P.op("pe", "matmul", A(psum[pi][:, :], lhsT=dg[:, k, :], rhs=U[:, 1 + k + 512 * tc:1 + k + 512 * tc + 512], start=(k == 0), stop=(k == 3)),
                             reads=[B_dg, B_U], writes=[B_ps[pi]])
                    P.op("act", "activation", A(out=dst[0][:, 512 * tc:512 * tc + 512], in_=psum[pi][:, :], func=AF.Silu, bias=cb[:, blk:blk + 1]),
                         reads=[B_ps[pi], B_cw], writes=[dst[1]])
                    yield
                if kind == "B":
                    for c in range(NT):
                        P.op("pe", "transpose", A(out=psb(6)[:, 128 * c:128 * c + 128], in_=BT[:, 128 * c:128 * c + 128], identity=identb[:]),
                             reads=[B_BT, B_cst], writes=[B_ps[6]])
                    evac(Btok, psb(6)[:, 0:128 * NT].rearrange("p (c f) -> p c f", c=NT), [B_ps[6]], [B_Btok])
                elif kind == "xs":
                    jx = bi_
                    for c in range(NT):
                        P.op("pe", "transpose", A(out=psb(7)[:, 128 * c:128 * c + 128], in_=xsT[:, 128 * c:128 * c + 128], identity=identb[:]),
                             reads=[B_xsT, B_cst], writes=[B_ps[7]])
                    evac(xs_tok[:, :, 128 * jx:128 * jx + 128], psb(7)[:, 0:128 * NT].rearrange("p (c f) -> p c f", c=NT), [B_ps[7]], [B_xs])
                yield
            wz, wzB = load_w(win(OFF_Z + 256 * g, 256), 8, 256)
            for c in range(NT):
                pi = 6 + c % 2
                proj_tm(pi, 0, wz, wzB, 0, 256, t0 + 128 * c)
                P.op("act", "activation", A(out=zs[:, c, :], in_=psum[pi][:, 0:256], func=AF.Silu), reads=[B_ps[pi]], writes=[B_zs])
                if c % 2:
                    yield

        def scan(g):
            q = g % 2
            xs_tok, BT, Btok, CT, zs, dskI, sg_b = xs_tok2[q], BT2[q], Btok2[q], CT2[q], zs2[q], dskI2[q], sg_b2[q]
            B_xs, B_BT, B_Btok, B_CT, B_zs, B_dskI, B_sg = B_xs2[q], B_BT2[q], B_Btok2[q], B_CT2[q], B_zs2[q], B_dskI2[q], B_sg2[q]
            P.op("act", "copy", A(out=st_b, in_=state_f[:, g, :]), reads=[B_state[g]], writes=[B_stb])
            CBb = (1, 5)

            def S1(c):
                i2 = c % 2
                tk = slice(128 * c, 128 * c + 128)
                ag = adt[:, c, 4 * g:4 * g + 4]
                P.op("dve", "tensor_tensor", A(out=R[i2].rearrange("p (h l) -> p h l", h=4), in0=cst[:, C_TRIL:C_TRIL + 128].unsqueeze(1).broadcast_to([128, 4, 128]),
                                               in1=ag.unsqueeze(2).broadcast_to([128, 4, 128]), op=ALU.mult),
                     reads=[B_cst, B_dt], writes=[B_R[i2]])
                P.op("pe", "matmul", A(psum[0][:, :], lhsT=cstb[:, 0:128], rhs=R[i2], start=True, stop=False), reads=[B_cst, B_R[i2]], writes=[B_ps[0]])
                P.op("pe", "matmul", A(psum[0][:, :], lhsT=cstb[:, 128:256], rhs=cstb[:, 256:768], start=False, stop=True), reads=[B_cst], writes=[B_ps[0]])
                P.op("act", "activation", A(out=LT[i2], in_=psum[0][:, :], func=AF.Exp), reads=[B_ps[0]], writes=[B_LT[i2]])
                P.op("pe", "matmul", A(psum[CBb[i2]][:, 0:128], lhsT=BT[:, tk], rhs=CT[:, tk], start=True, stop=True), reads=[B_BT, B_CT], writes=[B_ps[CBb[i2]]])

            def S2(c):
                i2 = c % 2
                P.op("dve", "tensor_tensor", A(out=xdt[i2].rearrange("p (h q) -> p h q", h=4), in0=xs_tok[:, c, :].rearrange("p (h q) -> p h q", h=4),
                                               in1=dt[:, c, 4 * g:4 * g + 4].unsqueeze(2).broadcast_to([128, 4, 64]), op=ALU.mult),
                     reads=[B_xs, B_dt], writes=[B_xd[i2]])
                P.op("dve", "tensor_tensor", A(out=xdd[i2].rearrange("p (h q) -> p h q", h=4), in0=xdt[i2].rearrange("p (h q) -> p h q", h=4),
                                               in1=E[:, c, 32 + 4 * g:32 + 4 * g + 4].unsqueeze(2).broadcast_to([128, 4, 64]), op=ALU.mult),
                     reads=[B_xd[i2], B_E], writes=[B_xdd[i2]])
                P.op("dve", "tensor_tensor", A(out=MT[i2].rearrange("p (h l) -> p h l", h=4), in0=LT[i2].rearrange("p (h l) -> p h l", h=4),
                                               in1=psum[CBb[i2]][:, 0:128].unsqueeze(1).broadcast_to([128, 4, 128]), op=ALU.mult),
                     reads=[B_LT[i2], B_ps[CBb[i2]]], writes=[B_MT[i2]])

            def S3(c):
                i2 = c % 2
                tk = slice(128 * c, 128 * c + 128)
                for hh in range(4):
                    P.op("pe", "matmul", A(psum[2][:, 64 * hh:64 * hh + 64], lhsT=MT[i2][:, 128 * hh:128 * hh + 128], rhs=xdt[i2][:, 64 * hh:64 * hh + 64], start=True, stop=False),
                         reads=[B_MT[i2], B_xd[i2]], writes=[B_ps[2]])
                    P.op("pe", "matmul", A(psum[2][:, 64 * hh:64 * hh + 64], lhsT=dskI[:, hh, :], rhs=xs_tok[:, c, 64 * hh:64 * hh + 64], start=False, stop=True),
                         reads=[B_dskI, B_xs], writes=[B_ps[2]])
                P.op("pe", "matmul", A(psum[4][:, 0:256], lhsT=Btok[:, c, :], rhs=xdd[i2], start=True, stop=True), reads=[B_Btok, B_xdd[i2]], writes=[B_ps[4]])
                P.op("pe", "matmul", A(psum[3][:, 0:256], lhsT=CT[:, tk], rhs=st_b, start=True, stop=True), reads=[B_CT, B_stb], writes=[B_ps[3]])
                P.op("dve", "tensor_tensor", A(out=y1.rearrange("p (h q) -> p h q", h=4), in0=psum[3][:, 0:256].rearrange("p (h q) -> p h q", h=4),
                                               in1=E[:, c, 4 * g:4 * g + 4].unsqueeze(2).broadcast_to([128, 4, 64]), op=ALU.mult),
                     reads=[B_ps[3], B_E], writes=[B_y1])
                P.op("dve", "tensor_tensor", A(out=stmp.rearrange("p (h q) -> p h q", h=4), in0=state_f[:, g, :].rearrange("p (h q) -> p h q", h=4),
                                               in1=E[:, c, 64 + 4 * g:64 + 4 * g + 4].unsqueeze(2).broadcast_to([128, 4, 64]), op=ALU.mult),
                     reads=[B_state[g], B_E], writes=[B_stmp])
                P.op("dve", "tensor_tensor", A(out=state_f[:, g, :], in0=stmp, in1=psum[4][:, 0:256], op=ALU.add), reads=[B_stmp, B_ps[4]], writes=[B_state[g]])
                P.op("act", "copy", A(out=st_b, in_=state_f[:, g, :]), reads=[B_state[g]], writes=[B_stb])
                P.op("dve", "tensor_tensor", A(out=y2[:, c, :], in0=y1, in1=psum[2][:, 0:256], op=ALU.add), reads=[B_y1, B_ps[2]], writes=[B_y2c[c]])

            S1(0)
            if NT > 1:
                S1(1)
            S2(0)
            for c in range(NT):
                if c + 2 < NT:
                    S1(c + 2)
                if c + 1 < NT:
                    S2(c + 1)
                S3(c)
                yield
            P.op("dve", "tensor_tensor", A(out=y2, in0=y2, in1=zs, op=ALU.mult), reads=B_y2 + [B_zs], writes=B_y2)
            for c in range(NT):
                P.op("act", "activation", A(out=junk, in_=y2[:, c, :], func=AF.Square, accum_out=ss[:, c:c + 1]), reads=B_y2, writes=[B_junk, B_ss])
            rstd_from_ss(ss[:, 0:NT], NT, 1.0 / 256, [B_ss])
            yield
            for c in range(NT):
                P.op("act", "activation", A(out=yn[:, c, :], in_=y2[:, c, :], func=AF.Copy, scale=ss[:, c:c + 1]), reads=[B_y2c[c], B_ss], writes=[B_ync[c]])
            yield
            for c in range(NT):
                transpose_to_YT(yn[:, c, :], B_ync[c], 2, 2 * g, 128 * c, c % 2)
                if c % 2:
                    yield

        def run_interleaved(gens):
            gens = [g_ for g_ in gens if g_ is not None]
            while gens:
                for g_ in list(gens):
                    try:
                        next(g_)
                    except StopIteration:
                        gens.remove(g_)

        run_interleaved([prep(0)])
        for g in range(8):
            run_interleaved([scan(g), prep(g + 1) if g + 1 < 8 else None])
        P.barrier()

    def merge_phase(sq, part, bi, wbr_d, nkb, first):
        t0 = part * T
        it = 0
        for ob in range(8):
            wb_, wbB = load_w(wbr_d.ap()[:, 128 * ob:128 * ob + 128], nkb, 128)
            wg, wgB = load_w(win(OFF_GATE + 1024 * bi + 128 * ob, 128), 8, 128)
            for tc in range(T // 512):
                pa, pg = 2 * (it % 2), 2 * (it % 2) + 1
                it += 1
                for kb in range(nkb):
                    P.op("pe", "matmul", A(psum[pa][:, :], lhsT=wb_[:, kb, :], rhs=YT[:, kb, 512 * tc:512 * tc + 512], start=(kb == 0), stop=(kb == nkb - 1)),
                         reads=[wbB, B_YT], writes=[B_ps[pa]])
                proj_fm(pg, 0, wg, wgB, 0, 128, t0 + 512 * tc, 512)
                P.op("act", "activation", A(out=msig[:], in_=psum[pg][:, :], func=AF.Sigmoid), reads=[B_ps[pg]], writes=[B_msig])
                dst = mrg[:, ob, 512 * tc:512 * tc + 512]
                if first:
                    P.op("dve", "tensor_tensor", A(out=dst, in0=msig[:], in1=psum[pa][:, :], op=ALU.mult), reads=[B_msig, B_ps[pa]], writes=[B_mrg])
                else:
                    P.op("dve", "tensor_tensor", A(out=mtmp[:], in0=msig[:], in1=psum[pa][:, :], op=ALU.mult), reads=[B_msig, B_ps[pa]], writes=[B_mtmp])
                    P.op("dve", "tensor_tensor", A(out=dst, in0=dst, in1=mtmp[:], op=ALU.add), reads=[B_mtmp, B_mrg], writes=[B_mrg])

    def out_phase(sq, part):
        t0 = part * T
        cv = Carver()
        fg_b = cv.f32(1024); B_fg = Buf()
        xt = [cv.f32(1024), cv.f32(1024)]; B_xt = [Buf(), Buf()]
        r = [cv.f32(1024), cv.f32(1024)]; B_r = [Buf(), Buf()]
        junk = cv.f32(1024); B_junk = Buf()
        ss = cv.f32(2); B_ss = Buf()
        bcast_load(fg_b, fgain_d.ap(), [B_fg])
        wts = [load_w(w_out.ap()[:, 256 * i:256 * i + 256], 8, 256) for i in range(4)]
        for c in range(NT):
            xi = c % 2
            rows = slice(t0 + 128 * c, t0 + 128 * c + 128)
            P.dma("sp", "dma_start", A(out=xt[xi], in_=x_d.ap()[sq, rows, :]), writes=[B_xt[xi]])
            for hf in range(2):
                pi = 2 * xi + hf
                for q4 in range(2):
                    wt, wB = wts[2 * hf + q4]
                    for kb in range(8):
                        P.op("pe", "matmul", A(psum[pi][:, 256 * q4:256 * q4 + 256], lhsT=mrg[:, kb, 128 * c:128 * c + 128], rhs=wt[:, kb, :], start=(kb == 0), stop=(kb == 7)),
                             reads=[wB, B_mrg], writes=[B_ps[pi]])
                P.op("dve", "tensor_tensor", A(out=r[xi][:, 512 * hf:512 * hf + 512], in0=xt[xi][:, 512 * hf:512 * hf + 512], in1=psum[pi][:, :], op=ALU.add),
                     reads=[B_xt[xi], B_ps[pi]], writes=[B_r[xi]])
            P.op("act", "activation", A(out=junk, in_=r[xi], func=AF.Square, accum_out=ss[:, 0:1]), reads=[B_r[xi]], writes=[B_junk, B_ss])
            rstd_from_ss(ss[:, 0:1], 1, 1.0 / D, [B_ss])
            P.op("dve", "scalar_tensor_tensor", A(out=r[xi], in0=r[xi], scalar=ss[:, 0:1], in1=fg_b, op0=ALU.mult, op1=ALU.mult), reads=[B_r[xi], B_ss, B_fg], writes=[B_r[xi]])
            P.dma("sp", "dma_start", A(out=out_d.ap()[sq, rows, :], in_=r[xi]), reads=[B_r[xi]])
        P.barrier()

    for sq in range(nseq):
        prologue(sq)
        for part in range(NPART):
            first = True
            for bi, (name, fn, wbr, nkb) in enumerate((("ssm", ssm_phase, w_brs, 16), ("diff", diff_phase, w_brd, 8), ("mem", mem_phase, w_brm, 8))):
                if name not in branches:
                    continue
                fn(sq, part)
                merge_phase(sq, part, bi, wbr, nkb, first)
                first = False
            out_phase(sq, part)
    P.finish()
    P.emit()
    return nc, P


_CACHE = {}


def kernel(**inputs):
    ncores = _CFG["ncores"]; nseq = _CFG["nseq"]
    key = (nseq, tuple(_CFG["branches"]), _CFG["same_engine_sync"], _CFG["schedule"], _CFG.get("ssm_evac"))
    if key not in _CACHE:
        _CACHE[key] = build(nseq, _CFG["branches"], _CFG["same_engine_sync"], _CFG["schedule"])
    nc, P = _CACHE[key]
    f = lambda a: np.ascontiguousarray(np.asarray(a, dtype=np.float32))
    cst, oh = host_consts()
    shared = {
        "w_in": f(inputs["w_in"][0]), "w_mem_kv": f(inputs["w_mem_kv"][0]), "w_br_ssm": f(inputs["w_br_ssm"][0]),
        "w_br_diff": f(inputs["w_br_diff"][0]), "w_br_mem": f(inputs["w_br_mem"][0]), "w_out": f(inputs["w_out"][0]),
        "norm_gain": f(inputs["norm_gain"]).reshape(1, D), "mem_norm_gain": f(inputs["mem_norm_gain"]).reshape(1, D),
        "final_norm_gain": f(inputs["final_norm_gain"]).reshape(1, D), "ssm_norm_gain": f(inputs["ssm_norm_gain"]).reshape(1, 2048),
        "subln_gain": f(inputs["subln_gain"]).reshape(1, 128), "dt_bias": f(inputs["dt_bias"]).reshape(1, 32),
        "a_log": f(inputs["a_log"]).reshape(1, 32), "d_skip": f(inputs["d_skip"]).reshape(1, 32),
        "lambda_q1": f(inputs["lambda_q1"]).reshape(1, 64), "lambda_k1": f(inputs["lambda_k1"]).reshape(1, 64),
        "lambda_q2": f(inputs["lambda_q2"]).reshape(1, 64), "lambda_k2": f(inputs["lambda_k2"]).reshape(1, 64),
        "rel_bias": f(inputs["rel_bias"]),
        "conv_w_l": f(np.asarray(inputs["conv_w"][0]).reshape(4, 32, 128).transpose(2, 1, 0)),
        "conv_b_l": f(np.asarray(inputs["conv_b"][0]).reshape(32, 128).transpose(1, 0)),
        "sgain_l": f(np.asarray(inputs["ssm_norm_gain"][0]).reshape(16, 128).transpose(1, 0)),
        "cst": cst, "onehot": oh,
    }
    x = np.asarray(inputs["x"]); mem = np.asarray(inputs["mem"])
    in_maps = []
    for c in range(ncores):
        m = dict(shared)
        m["x"] = f(x[c * nseq:(c + 1) * nseq])
        m["mem"] = f(mem[c * nseq:(c + 1) * nseq])
        in_maps.append(m)
    if _CFG.get("trace"):
        res = run_bass_kernel_spmd(nc, in_maps, core_ids=list(range(ncores)), trace=True)
        print("EXEC_TIME_NS", res.exec_time_ns)
    else:
        res = run_bass_kernel_spmd(nc, in_maps, core_ids=list(range(ncores)))
    return np.concatenate([np.asarray(r["out"]) for r in res.results], axis=0).astype(np.float32)
```

```python
import math
import numpy as np
import concourse.bass as bass
import concourse.mybir as mybir
from concourse.bass_utils import run_bass_kernel_spmd

F32 = mybir.dt.float32
BF16 = mybir.dt.bfloat16
ALU = mybir.AluOpType
AF = mybir.ActivationFunctionType

D = 1024
S = 2048
MEM = 256
IN_DIM = 15392
OFF_Z, OFF_XBC, OFF_DT, OFF_DQ, OFF_DK, OFF_DV, OFF_DG, OFF_MQ, OFF_MG, OFF_GATE = (
    0, 2048, 6144, 6176, 7200, 8224, 9248, 10272, 11296, 12320)
EPS = 1e-5
NEG = -30000.0
T = 1024
NT = T // 128
NPART = S // T

_CFG = {"branches": ("ssm", "diff", "mem"), "ncores": 8, "nseq": 4, "same_engine_sync": "raw", "schedule": True}

COMPUTE = ("pe", "act", "dve", "pool")
QUEUES = ("pe", "act", "dve", "pool", "sp")
DMAQ = ("sp", "pool", "act")
NDMASEM = 12


def A(*a, **k):
    return (a, k)


class Buf:
    __slots__ = ("name", "last_w", "readers")

    def __init__(self, name=""):
        self.name = name
        self.last_w = None
        self.readers = []


def _free(ap):
    n = 1
    for d in ap.shape[1:]:
        n *= int(d)
    return n


class Prog:
    def __init__(self, nc, same_engine_sync="raw", schedule=True):
        self.nc = nc
        self.ins = []
        self.segs = [[]]
        self.same_engine_sync = same_engine_sync
        self.schedule = schedule
        self.sem = {e: nc.alloc_semaphore("s_" + e) for e in COMPUTE}
        self.dsem = {e: [nc.alloc_semaphore(f"d_{e}{i}") for i in range(NDMASEM)] for e in DMAQ}

    def _deps(self, reads, writes):
        deps = set()
        raw = set()
        for b in reads:
            if b.last_w is not None:
                deps.add(b.last_w)
                raw.add(b.last_w)
        for b in writes:
            if b.last_w is not None:
                deps.add(b.last_w)
            deps.update(b.readers)
        return deps, raw

    def _commit(self, me, reads, writes):
        for b in reads:
            b.readers.append(me)
        for b in writes:
            b.last_w = me
            b.readers = []

    def _est(self, eng, meth, args, is_dma):
        a, k = args
        try:
            out = k.get("out", a[0] if a else None)
            n = _free(out)
            if is_dma:
                return 60.0, 2000.0 + n * (4 if out.dtype == F32 else 2) * 128 / 150.0
            if eng == "pe":
                if meth == "transpose":
                    return 110.0, 0.0
                f32 = k["lhsT"].dtype == F32
                return max(64.0, n * 0.42) * (4 if f32 else 1), 0.0
            if eng == "act":
                return 224.0 + 0.75 * n, 0.0
            if eng == "dve":
                return 60.0 + 1.05 * n, 0.0
            return 100.0 + 2.3 * n, 0.0
        except Exception:
            return 300.0, 0.0

    def _add(self, eng, meth, args, reads, writes, is_dma):
        deps, raw = self._deps(reads, writes)
        gid = len(self.ins)
        dur, lat = self._est(eng, meth, args, is_dma)
        tab = None
        if eng == "act" and meth == "activation":
            f = args[1].get("func")
            tab = "explog" if f in (AF.Exp, AF.Ln) else str(f)
        self.ins.append(dict(eng=eng, fn=(meth, args), deps=deps, raw=raw, dma=is_dma, dur=dur, lat=lat, tab=tab))
        self.segs[-1].append(gid)
        self._commit(gid, reads, writes)
        return gid

    def op(self, eng, meth, args, reads=(), writes=()):
        return self._add(eng, meth, args, reads, writes, False)

    def dma(self, eng, meth, args, reads=(), writes=()):
        return self._add(eng, meth, args, reads, writes, True)

    def barrier(self):
        self.segs.append([])

    def finish(self):
        self.segs.append([])

    def _sched(self, seg):
        ins = self.ins
        pend = {e: [] for e in QUEUES}
        for g in seg:
            pend[ins[g]["eng"]].append(g)
        if not self.schedule:
            return pend
        segset = set(seg)
        W = _CFG.get('W', 32)
        fin = {}
        nun = {}
        users = {}
        for g in seg:
            c = 0
            for d in ins[g]["deps"]:
                if d in segset:
                    c += 1
                    users.setdefault(d, []).append(g)
            nun[g] = c
        blev = {}
        for g in reversed(seg):
            m = 0.0
            for u in users.get(g, ()):
                if blev[u] > m:
                    m = blev[u]
            blev[g] = ins[g]["dur"] + ins[g]["lat"] + m
        free = {e: 0.0 for e in QUEUES}
        curtab = [None]
        order = {e: [] for e in QUEUES}
        cand = {e: None for e in QUEUES}
        dirty = set(QUEUES)
        remaining = len(seg)
        while remaining:
            for e in list(dirty):
                best = None
                lst = pend[e]
                for g in lst[:W]:
                    if nun[g]:
                        continue
                    I = ins[g]
                    r = 0.0
                    for d in I["deps"]:
                        if d in fin:
                            t = fin[d] + (150.0 if ins[d]["eng"] != e else 0.0)
                            if t > r:
                                r = t
                    st = r if r > free[e] else free[e]
                    if e == "act" and I["tab"] is not None and I["tab"] != curtab[0]:
                        st += 1300.0
                    if best is None or st < best[0] or (st == best[0] and blev[g] > blev[best[1]]):
                        best = (st, g)
                cand[e] = best
            dirty.clear()
            be = None
            for e in QUEUES:
                c = cand[e]
                if c is not None and (be is None or c[0] < cand[be][0]):
                    be = e
            if be is None:
                raise RuntimeError("scheduler deadlock")
            st, g = cand[be]
            I = ins[g]
            if be == "act" and I["tab"] is not None:
                curtab[0] = I["tab"]
            free[be] = st + I["dur"]
            fin[g] = st + I["dur"] + I["lat"]
            pend[be].remove(g)
            order[be].append(g)
            remaining -= 1
            dirty.add(be)
            for u in users.get(g, ()):
                nun[u] -= 1
                if nun[u] == 0:
                    dirty.add(ins[u]["eng"])
        return order

    def emit(self):
        nc = self.nc
        ins = self.ins
        stream = {e: [] for e in QUEUES}
        pos = {}
        dma_list = {e: [] for e in DMAQ}

        def tails():
            t = set()
            for e in QUEUES:
                for x in reversed(stream[e]):
                    if not isinstance(x, tuple) and not ins[x]["dma"]:
                        t.add(x)
                        break
            for e in DMAQ:
                t.update(dma_list[e][-NDMASEM:])
            return t

        nseg = len(self.segs)
        for si, seg in enumerate(self.segs):
            if seg:
                order = self._sched(seg)
                for e in QUEUES:
                    for g in order[e]:
                        I = ins[g]
                        if I["dma"]:
                            dl = dma_list[e]
                            I["k"] = len(dl)
                            if len(dl) >= NDMASEM:
                                I["deps"] = set(I["deps"]) | {dl[-NDMASEM]}
                            dl.append(g)
                        pos[g] = len(stream[e])
                        stream[e].append(g)
            if si < nseg - 1:
                t = tails()
                last = (si == nseg - 2)
                for e in (("sp",) if last else QUEUES):
                    stream[e].append(("wait", t))
        signal = set()
        plan = {e: [] for e in QUEUES}
        for e in QUEUES:
            seen = {x: -1 for x in COMPUTE}
            seen_slot = {}
            for ent in stream[e]:
                if isinstance(ent, tuple):
                    deps, raw, g = ent[1], ent[1], None
                else:
                    g = ent
                    deps, raw = ins[g]["deps"], ins[g]["raw"]
                best = {}
                for d in deps:
                    D = ins[d]
                    if D["dma"]:
                        slot = (D["eng"], D["k"] % NDMASEM)
                        if seen_slot.get(slot, -1) >= D["k"] // NDMASEM:
                            continue
                        if slot not in best or ins[best[slot]]["k"] < D["k"]:
                            best[slot] = d
                    else:
                        x = D["eng"]
                        if x == e and (x == "pe" or not self.same_engine_sync):
                            continue
                        if x == e and self.same_engine_sync == "raw" and d not in raw:
                            continue
                        if seen[x] >= pos[d]:
                            continue
                        if x not in best or pos[best[x]] < pos[d]:
                            best[x] = d
                final = list(best.values())
                for d in final:
                    D = ins[d]
                    if D["dma"]:
                        seen_slot[(D["eng"], D["k"] % NDMASEM)] = D["k"] // NDMASEM
                    else:
                        seen[D["eng"]] = pos[d]
                        signal.add(d)
                plan[e].append((final, g))
        count = {}
        for e in COMPUTE:
            c = 0
            for ent in stream[e]:
                if not isinstance(ent, tuple) and ent in signal:
                    c += 1
                    count[ent] = c
        self.stats = {e: len(stream[e]) for e in QUEUES}

        def replay(e, eng):
            for final, g in plan[e]:
                for d in final:
                    D = ins[d]
                    if D["dma"]:
                        eng.wait_ge(self.dsem[D["eng"]][D["k"] % NDMASEM], 16 * (D["k"] // NDMASEM + 1))
                    else:
                        eng.wait_ge(self.sem[D["eng"]], count[d])
                if g is None:
                    continue
                I = ins[g]
                fn = I["fn"]
                o = getattr(eng, fn[0])(*fn[1][0], **fn[1][1])
                if I["dma"]:
                    o.then_inc(self.dsem[e][I["k"] % NDMASEM], 16)
                elif g in signal:
                    o.then_inc(self.sem[e], 1)

        with nc.Block() as block:
            @block.tensor
            def _(eng):
                replay("pe", eng)

            @block.scalar
            def _(eng):
                replay("act", eng)

            @block.vector
            def _(eng):
                replay("dve", eng)

            @block.gpsimd
            def _(eng):
                replay("pool", eng)

            @block.sync
            def _(eng):
                replay("sp", eng)


def t5_bucket_np(rel):
    n = np.maximum(rel, 0)
    max_exact = 16
    nf = np.maximum(n, 1).astype(np.float32)
    large = max_exact + (np.log(nf / max_exact) / np.float32(math.log(128 / max_exact))
                         * (32 - max_exact)).astype(np.int32)
    large = np.minimum(large, 31)
    return np.where(n < max_exact, n, large)


def host_consts():
    k = np.arange(128)[:, None]
    l = np.arange(128)[None, :]
    ident = (k == l).astype(np.float32)
    tril1 = (k <= l).astype(np.float32)
    sup = (k > l).astype(np.float32)
    ones = np.ones((128, 128), np.float32)
    negi = ident * NEG
    slow4 = np.tile(sup, (1, 4))
    cst = np.concatenate([ident, tril1, sup, ones, negi, slow4], axis=1)
    oh = np.zeros((33, 2, 128, 128), np.float32)
    for dlt in range(2):
        rel = 128 * dlt + (l - k)
        b = t5_bucket_np(rel)
        for kk in range(128):
            for qq in range(128):
                if rel[kk, qq] >= 0:
                    oh[b[kk, qq], dlt, kk, qq] = 1.0
                else:
                    oh[32, dlt, kk, qq] = 1.0
    return cst, oh.reshape(33, 2 * 128 * 128)


C_ID, C_TRIL, C_SUP, C_ONES, C_NEGI, C_SLOW4 = 0, 128, 256, 384, 512, 640


def build(nseq, branches, same_engine_sync="raw", schedule=True):
    nc = bass.Bass("TRN2", target_bir_lowering=False)
    P = Prog(nc, same_engine_sync, schedule)

    def din(name, shape, dt=F32):
        return nc.dram_tensor(name, list(shape), dt, kind="ExternalInput")

    x_d = din("x", [nseq, S, D])
    mem_d = din("mem", [nseq, MEM, D])
    w_in = din("w_in", [D, IN_DIM])
    w_kv = din("w_mem_kv", [D, 2048])
    w_brs = din("w_br_ssm", [2048, D])
    w_brd = din("w_br_diff", [D, D])
    w_brm = din("w_br_mem", [D, D])
    w_out = din("w_out", [D, D])
    gain_d = din("norm_gain", [1, D])
    mgain_d = din("mem_norm_gain", [1, D])
    fgain_d = din("final_norm_gain", [1, D])
    sgain_d = din("ssm_norm_gain", [1, 2048])
    subg_d = din("subln_gain", [1, 128])
    dtb_d = din("dt_bias", [1, 32])
    alog_d = din("a_log", [1, 32])
    dsk_d = din("d_skip", [1, 32])
    lq1_d, lk1_d, lq2_d, lk2_d = (din(n, [1, 64]) for n in ("lambda_q1", "lambda_k1", "lambda_q2", "lambda_k2"))
    rb_d = din("rel_bias", [32, 8])
    cw_d = din("conv_w_l", [128, 32, 4])
    cb_d = din("conv_b_l", [128, 32])
    sgl_d = din("sgain_l", [128, 16])
    cst_d = din("cst", [128, 1152])
    oh_d = din("onehot", [33, 32768])
    out_d = nc.dram_tensor("out", [nseq, S, D], F32, kind="ExternalOutput")
    bias_scr = nc.dram_tensor("bias_scr", [8, 32768], F32, kind="Internal")
    w_in_f, w_kv_f, w_brs_f, w_brd_f, w_brm_f, w_out_f = w_in, w_kv, w_brs, w_brd, w_brm, w_out
    w_in = nc.dram_tensor("w_in_b", [D, IN_DIM], BF16, kind="Internal")
    w_kv = nc.dram_tensor("w_kv_b", [D, 2048], BF16, kind="Internal")
    w_brs = nc.dram_tensor("w_brs_b", [2048, D], BF16, kind="Internal")
    w_brd = nc.dram_tensor("w_brd_b", [D, D], BF16, kind="Internal")
    w_brm = nc.dram_tensor("w_brm_b", [D, D], BF16, kind="Internal")
    w_out = nc.dram_tensor("w_out_b", [D, D], BF16, kind="Internal")

    def sb(name, shape, dt=F32):
        return nc.alloc_sbuf_tensor("sb_" + name, list(shape), dt)

    cst = sb("cst", [128, 1152]); B_cst = Buf("cst")
    identb = sb("identb", [128, 128], BF16)
    trilb = sb("trilb", [128, 128], BF16)
    cstb = sb("cstb", [128, 768], BF16)
    sm = sb("small", [128, 512]); B_sm = Buf("small")
    SM_CFAR, SM_DTB, SM_A, SM_DSK, SM_LAM, SM_NLAM, SM_ONE, SM_EPS = 0, 8, 40, 72, 104, 105, 106, 107
    SM_SUBG = 128
    SM_TMP = 256
    cw = sb("cw", [128, 32, 4]); cb = sb("cb", [128, 32]); B_cw = Buf("cw")
    sgl = sb("sgl", [128, 16])
    hT = sb("hT", [128, 8, S], BF16)
    B_hT = [Buf(f"hT{i}") for i in range(S // 128)]
    kmT = sb("kmT", [128, 8, MEM], BF16); B_kmT = Buf("kmT")
    vma = sb("vma", [128, 2, 4, 258], BF16); B_vma = Buf("vma")
    YT = sb("YT", [128, 16, T], BF16); B_YT = Buf("YT")
    mrg = sb("mrg", [128, 8, T], BF16); B_mrg = Buf("mrg")
    state_f = sb("state_f", [128, 8, 256]); B_state = [Buf(f"st{g}") for g in range(8)]
    halo = sb("halo", [128, 32, 4]); B_halo = Buf("halo")
    NWB = 6
    wbf = [sb(f"wbf{i}", [128, 2048], BF16) for i in range(NWB)]; B_wbf = [Buf(f"wbf{i}") for i in range(NWB)]
    ARENA_W = 18976
    arena = sb("arena", [128, ARENA_W])
    arena_b = arena[:].bitcast(BF16)

    class Carver:
        def __init__(self, base=0):
            self.off = base

        def f32(self, n):
            a = arena[:, self.off:self.off + n]
            self.off += n
            assert self.off <= ARENA_W, self.off
            return a

        def bf16(self, n):
            w = (n + 1) // 2
            a = arena_b[:, 2 * self.off:2 * self.off + n]
            self.off += w
            assert self.off <= ARENA_W, self.off
            return a

    msig = sb("msig", [128, 512]); B_msig = Buf("msig")
    mtmp = sb("mtmp", [128, 512]); B_mtmp = Buf("mtmp")
    psum = [nc.alloc_psum_tensor(f"ps{i}", [128, 512], F32) for i in range(8)]
    B_ps = [Buf(f"ps{i}") for i in range(8)]

    def psb(i):
        return psum[i][:].bitcast(BF16)

    wctr = [0, 0]

    def load_w(src_ap, nkb, ncols):
        assert nkb * ncols <= 2048
        bi = wctr[1] % NWB; wctr[1] += 1
        bf = wbf[bi][:, 0:nkb * ncols].rearrange("p (k c) -> p k c", k=nkb)
        src = src_ap.rearrange("(k p) c -> p k c", p=128)
        P.dma("sp", "dma_start", A(out=bf, in_=src), writes=[B_wbf[bi]])
        return bf, B_wbf[bi]

    def win(c0, ncols):
        return w_in.ap()[:, c0:c0 + ncols]

    def hbufs(t0, n):
        return B_hT[t0 // 128:(t0 + n + 127) // 128]

    def proj_fm(ps_i, ps_cols, wt, wB, c_lo, ncol, tok0, ntok):
        out = psum[ps_i][0:ncol, ps_cols:ps_cols + ntok]
        for kb in range(8):
            P.op("pe", "matmul", A(out, lhsT=wt[:, kb, c_lo:c_lo + ncol], rhs=hT[:, kb, tok0:tok0 + ntok],
                                                 start=(kb == 0), stop=(kb == 7)),
                 reads=[wB] + hbufs(tok0, ntok), writes=[B_ps[ps_i]])

    def proj_tm(ps_i, ps_cols, wt, wB, c_lo, ncol, tok0):
        out = psum[ps_i][:, ps_cols:ps_cols + ncol]
        for kb in range(8):
            P.op("pe", "matmul", A(out, lhsT=hT[:, kb, tok0:tok0 + 128], rhs=wt[:, kb, c_lo:c_lo + ncol],
                                                 start=(kb == 0), stop=(kb == 7)),
                 reads=[wB] + hbufs(tok0, 128), writes=[B_ps[ps_i]])

    evac_rr = [0]
    evac_mode = ["alt"]

    def evac(out, in_, reads, writes):
        evac_rr[0] += 1
        if evac_mode[0] == "act" or (evac_mode[0] == "alt" and evac_rr[0] % 2):
            P.op("act", "copy", A(out=out, in_=in_), reads=reads, writes=writes)
        else:
            P.op("dve", "tensor_copy", A(out=out, in_=in_), reads=reads, writes=writes)

    def bcast_load(dst, src_row_ap, writes):
        P.dma("sp", "dma_start", A(out=dst, in_=src_row_ap.partition_broadcast(128)), writes=writes)

    def rstd_from_ss(ss_ap, n, inv_count, Bs):
        P.op("dve", "tensor_scalar", A(out=ss_ap, in0=ss_ap, scalar1=inv_count, scalar2=EPS, op0=ALU.mult, op1=ALU.add),
             reads=Bs, writes=Bs)
        P.op("act", "activation", A(out=ss_ap, in_=ss_ap, func=AF.Sqrt), reads=Bs, writes=Bs)
        P.op("dve", "reciprocal", A(out=ss_ap, in_=ss_ap), reads=Bs, writes=Bs)

    P.dma("sp", "dma_start", A(out=cst[:], in_=cst_d.ap()), writes=[B_cst])
    P.op("dve", "tensor_copy", A(out=identb[:], in_=cst[:, C_ID:C_ID + 128]), reads=[B_cst], writes=[B_cst])
    P.op("dve", "tensor_copy", A(out=trilb[:], in_=cst[:, C_TRIL:C_TRIL + 128]), reads=[B_cst], writes=[B_cst])
    P.op("dve", "tensor_copy", A(out=cstb[:, 0:128], in_=cst[:, C_SUP:C_SUP + 128]), reads=[B_cst], writes=[B_cst])
    P.op("dve", "tensor_copy", A(out=cstb[:, 128:768], in_=cst[:, C_NEGI:C_NEGI + 640]), reads=[B_cst], writes=[B_cst])
    P.dma("sp", "dma_start", A(out=cw[:], in_=cw_d.ap()), writes=[B_cw])
    P.dma("sp", "dma_start", A(out=cb[:], in_=cb_d.ap()), writes=[B_cw])
    P.dma("sp", "dma_start", A(out=sgl[:], in_=sgl_d.ap()), writes=[B_cw])
    P.op("pool", "memset", A(sm[:], 0.0), writes=[B_sm])
    P.op("pool", "memset", A(sm[:, SM_ONE:SM_ONE + 1], 1.0), writes=[B_sm])
    P.op("pool", "memset", A(sm[:, SM_EPS:SM_EPS + 1], EPS), writes=[B_sm])
    bcast_load(sm[:, SM_CFAR:SM_CFAR + 8], rb_d.ap()[31:32, :], [B_sm])
    bcast_load(sm[:, SM_DTB:SM_DTB + 32], dtb_d.ap(), [B_sm])
    bcast_load(sm[:, SM_A:SM_A + 32], alog_d.ap(), [B_sm])
    bcast_load(sm[:, SM_DSK:SM_DSK + 32], dsk_d.ap(), [B_sm])
    bcast_load(sm[:, SM_SUBG:SM_SUBG + 128], subg_d.ap(), [B_sm])
    for i, ld in enumerate((lq1_d, lk1_d, lq2_d, lk2_d)):
        bcast_load(sm[:, SM_TMP + 64 * i:SM_TMP + 64 * i + 64], ld.ap(), [B_sm])
    Bs = [B_sm]
    P.op("act", "activation", A(out=sm[:, SM_A:SM_A + 32], in_=sm[:, SM_A:SM_A + 32], func=AF.Exp), reads=Bs, writes=Bs)
    P.op("dve", "tensor_scalar_mul", A(out=sm[:, SM_A:SM_A + 32], in0=sm[:, SM_A:SM_A + 32], scalar1=-1.0), reads=Bs, writes=Bs)
    LAM_INIT = 0.8 - 0.6 * math.exp(-0.3 * 0)
    P.op("dve", "tensor_scalar_mul", A(out=sm[:, SM_SUBG:SM_SUBG + 128], in0=sm[:, SM_SUBG:SM_SUBG + 128], scalar1=1.0 - LAM_INIT), reads=Bs, writes=Bs)
    for i in range(2):
        a0 = SM_TMP + 128 * i
        P.op("dve", "tensor_tensor", A(out=sm[:, a0:a0 + 64], in0=sm[:, a0:a0 + 64], in1=sm[:, a0 + 64:a0 + 128], op=ALU.mult), reads=Bs, writes=Bs)
        P.op("dve", "reduce_sum", A(out=sm[:, 110 + i:111 + i], in_=sm[:, a0:a0 + 64], axis=mybir.AxisListType.X), reads=Bs, writes=Bs)
    P.op("act", "activation", A(out=sm[:, 110:112], in_=sm[:, 110:112], func=AF.Exp), reads=Bs, writes=Bs)
    P.op("dve", "tensor_tensor", A(out=sm[:, SM_LAM:SM_LAM + 1], in0=sm[:, 110:111], in1=sm[:, 111:112], op=ALU.subtract), reads=Bs, writes=Bs)
    P.op("dve", "tensor_scalar_add", A(out=sm[:, SM_LAM:SM_LAM + 1], in0=sm[:, SM_LAM:SM_LAM + 1], scalar1=LAM_INIT), reads=Bs, writes=Bs)
    P.op("dve", "tensor_scalar_mul", A(out=sm[:, SM_NLAM:SM_NLAM + 1], in0=sm[:, SM_LAM:SM_LAM + 1], scalar1=-1.0), reads=Bs, writes=Bs)

    cv = Carver()
    NPS = 4
    pst = [cv.f32(2048) for _ in range(NPS)]; B_pst = [Buf() for _ in range(NPS)]
    pbf = [cv.bf16(2048) for _ in range(NPS)]; B_pbf = [Buf() for _ in range(NPS)]
    pc_i = 0
    for (srcw, dstw, rows, cols) in ((w_in_f, w_in, D, IN_DIM), (w_kv_f, w_kv, D, 2048), (w_brs_f, w_brs, 2048, D),
                                     (w_brd_f, w_brd, D, D), (w_brm_f, w_brm, D, D), (w_out_f, w_out, D, D)):
        for rb in range(rows // 128):
            for c0 in range(0, cols, 2048):
                n = min(2048, cols - c0)
                si = pc_i % NPS
                P.dma("sp", "dma_start", A(out=pst[si][:, 0:n], in_=srcw.ap()[128 * rb:128 * rb + 128, c0:c0 + n]), writes=[B_pst[si]])
                ce = ("act", "dve")[pc_i % 2]
                if srcw is w_brs_f:
                    P.op("dve", "tensor_scalar", A(out=pbf[si][:, 0:n], in0=pst[si][:, 0:n], scalar1=sgl[:, rb:rb + 1], scalar2=None, op0=ALU.mult),
                         reads=[B_pst[si], B_cw], writes=[B_pbf[si]])
                else:
                    P.op(ce, "copy" if ce == "act" else "tensor_copy", A(out=pbf[si][:, 0:n], in_=pst[si][:, 0:n]), reads=[B_pst[si]], writes=[B_pbf[si]])
                P.dma("sp", "dma_start", A(out=dstw.ap()[128 * rb:128 * rb + 128, c0:c0 + n], in_=pbf[si][:, 0:n]), reads=[B_pbf[si]])
                pc_i += 1
    P.barrier()

    if "diff" in branches:
        cv = Carver()
        rbx = cv.f32(8)
        ohs = cv.f32(4096)
        stg = cv.f32(4096)
        B_rbx, B_ohs, B_stg, B_scr = Buf(), Buf(), Buf(), Buf()
        P.op("pool", "memset", A(rbx[32:33, :], NEG), writes=[B_rbx])
        P.dma("sp", "dma_start", A(out=rbx[0:32, :], in_=rb_d.ap()), writes=[B_rbx])
        for pc in range(8):
            P.dma("sp", "dma_start", A(out=ohs[0:33, :], in_=oh_d.ap()[:, 4096 * pc:4096 * pc + 4096]), writes=[B_ohs])
            for i in range(8):
                pi = i % 2
                P.op("pe", "matmul", A(psum[pi][0:8, :], lhsT=rbx[0:33, :], rhs=ohs[0:33, 512 * i:512 * i + 512], start=True, stop=True),
                     reads=[B_rbx, B_ohs], writes=[B_ps[pi]])
                evac(stg[0:8, 512 * i:512 * i + 512], psum[pi][0:8, :], [B_ps[pi]], [B_stg])
            P.dma("sp", "dma_start", A(out=bias_scr.ap()[:, 4096 * pc:4096 * pc + 4096], in_=stg[0:8, :]), reads=[B_stg], writes=[B_scr])
        P.barrier()

    def prologue(sq):
        cv = Carver(5200)
        gain_b = cv.f32(1024); B_gain = Buf()
        xt = [cv.f32(1024), cv.f32(1024)]; B_xt = [Buf(), Buf()]
        junk = cv.f32(1024); B_junk = Buf()
        hb = cv.bf16(1024); B_hb = Buf()
        ss = cv.f32(2); B_ss = Buf()
        memT = cv.bf16(8 * MEM); B_memT = Buf()
        memT3 = memT.rearrange("p (k m) -> p k m", k=8)

        def norm_transpose(src_ap, i, dst3, dstB, col0):
            xi = i % 2
            P.dma("sp", "dma_start", A(out=xt[xi], in_=src_ap), writes=[B_xt[xi]])
            P.op("act", "activation", A(out=junk, in_=xt[xi], func=AF.Square, accum_out=ss[:, 0:1]), reads=[B_xt[xi]], writes=[B_junk, B_ss])
            rstd_from_ss(ss[:, 0:1], 1, 1.0 / D, [B_ss])
            P.op("dve", "scalar_tensor_tensor", A(out=hb, in0=xt[xi], scalar=ss[:, 0:1], in1=gain_b, op0=ALU.mult, op1=ALU.mult),
                 reads=[B_xt[xi], B_ss, B_gain], writes=[B_hb])
            for kb in range(8):
                P.op("pe", "transpose", A(out=psb(6)[:, 128 * kb:128 * kb + 128], in_=hb[:, 128 * kb:128 * kb + 128], identity=identb[:]),
                     reads=[B_hb, B_cst], writes=[B_ps[6]])
            evac(dst3[:, :, col0:col0 + 128], psb(6)[:, 0:1024].rearrange("p (k t) -> p k t", k=8), [B_ps[6]], dstB)

        bcast_load(gain_b, gain_d.ap(), [B_gain])
        for i in range(S // 128):
            norm_transpose(x_d.ap()[sq, 128 * i:128 * i + 128, :], i, hT, [B_hT[i]], 128 * i)
        if "mem" in branches:
            bcast_load(gain_b, mgain_d.ap(), [B_gain])
            for i in range(2):
                norm_transpose(mem_d.ap()[sq, 128 * i:128 * i + 128, :], i, memT3, [B_memT], 128 * i)
            for fb in range(8):
                wt, wB = load_w(w_kv.ap()[:, 128 * fb:128 * fb + 128], 8, 128)
                pi = fb % 2
                for kb in range(8):
                    P.op("pe", "matmul", A(psum[pi][:, 0:MEM], lhsT=wt[:, kb, :], rhs=memT3[:, kb, :], start=(kb == 0), stop=(kb == 7)),
                         reads=[wB, B_memT], writes=[B_ps[pi]])
                evac(kmT[:, fb, :], psum[pi][:, 0:MEM], [B_ps[pi]], [B_kmT])
            P.op("pool", "memset", A(vma[:, :, :, 256:258], 1.0), writes=[B_vma])
            for hh in range(4):
                wt, wB = load_w(w_kv.ap()[:, 1024 + 256 * hh:1024 + 256 * hh + 256], 8, 256)
                for mt in range(2):
                    pi = mt
                    for kb in range(8):
                        P.op("pe", "matmul", A(psum[pi][:, 0:256], lhsT=memT3[:, kb, 128 * mt:128 * mt + 128], rhs=wt[:, kb, :], start=(kb == 0), stop=(kb == 7)),
                             reads=[wB, B_memT], writes=[B_ps[pi]])
                    evac(vma[:, mt, hh, 0:256], psum[pi][:, 0:256], [B_ps[pi]], [B_vma])
        P.barrier()

    def transpose_to_YT(src_bf, srcB, nblk, yt_blk0, tokcol0, ps_i):
        for j in range(nblk):
            P.op("pe", "transpose", A(out=psb(ps_i)[:, 128 * j:128 * j + 128], in_=src_bf[:, 128 * j:128 * j + 128], identity=identb[:]),
                 reads=[srcB, B_cst], writes=[B_ps[ps_i]])
        evac(YT[:, yt_blk0:yt_blk0 + nblk, tokcol0:tokcol0 + 128], psb(ps_i)[:, 0:128 * nblk].rearrange("p (j t) -> p j t", j=nblk), [B_ps[ps_i]], [B_YT])

    def mem_phase(sq, part):
        t0 = part * T
        evac_mode[0] = "alt"
        cv = Carver()
        QmT = cv.bf16(2 * T).rearrange("p (d t) -> p d t", d=2); B_Qm = Buf()
        Gm = cv.f32(NT * 256).rearrange("p (c f) -> p c f", c=NT); B_Gm = Buf()
        PmT = [cv.bf16(2 * 512).rearrange("p (m t) -> p m t", m=2) for _ in range(2)]; B_Pm = [Buf(), Buf()]
        rr = cv.f32(4); B_rr = Buf()
        ym = [cv.bf16(256), cv.bf16(256)]; B_ym = [Buf(), Buf()]
        it = 0
        for hh in range(4):
            wq, wqB = load_w(win(OFF_MQ + 256 * hh, 256), 8, 256)
            wg, wgB = load_w(win(OFF_MG + 256 * hh, 256), 8, 256)
            for db in range(2):
                for tc in range(T // 512):
                    pi = (2 * db + tc) % 2
                    proj_fm(pi, 0, wq, wqB, 128 * db, 128, t0 + 512 * tc, 512)
                    evac(QmT[:, db, 512 * tc:512 * tc + 512], psum[pi][:, :], [B_ps[pi]], [B_Qm])
            for c in range(NT):
                pi = 2 + c % 2
                proj_tm(pi, 0, wg, wgB, 0, 256, t0 + 128 * c)
                P.op("act", "activation", A(out=Gm[:, c, :], in_=psum[pi][:, 0:256], func=AF.Silu), reads=[B_ps[pi]], writes=[B_Gm])
            for tc in range(T // 512):
                pb = tc % 2
                for mt in range(2):
                    pi = mt
                    for db in range(2):
                        P.op("pe", "matmul", A(psum[pi][:, :], lhsT=kmT[:, 2 * hh + db, 128 * mt:128 * mt + 128], rhs=QmT[:, db, 512 * tc:512 * tc + 512], start=(db == 0), stop=(db == 1)),
                             reads=[B_kmT, B_Qm], writes=[B_ps[pi]])
                    P.op("act", "activation", A(out=PmT[pb][:, mt, :], in_=psum[pi][:, :], func=AF.Exp, scale=1.0 / 16.0), reads=[B_ps[pi]], writes=[B_Pm[pb]])
                for j in range(4):
                    pi = 2 + j % 2
                    for mt in range(2):
                        P.op("pe", "matmul", A(psum[pi][:, 0:257], lhsT=PmT[pb][:, mt, 128 * j:128 * j + 128], rhs=vma[:, mt, hh, 0:257], start=(mt == 0), stop=(mt == 1)),
                             reads=[B_Pm[pb], B_vma], writes=[B_ps[pi]])
                    yi = it % 2; it += 1
                    P.op("dve", "reciprocal", A(out=rr[:, 0:1], in_=psum[pi][:, 256:257]), reads=[B_ps[pi]], writes=[B_rr])
                    P.op("dve", "scalar_tensor_tensor", A(out=ym[yi], in0=psum[pi][:, 0:256], scalar=rr[:, 0:1], in1=Gm[:, 4 * tc + j, :], op0=ALU.mult, op1=ALU.mult),
                         reads=[B_ps[pi], B_rr, B_Gm], writes=[B_ym[yi]])
                    transpose_to_YT(ym[yi], B_ym[yi], 2, 2 * hh, 512 * tc + 128 * j, 6)

    def diff_phase(sq, part):
        t0 = part * T
        evac_mode[0] = "dve"
        nk = t0 + T
        cv = Carver()
        KT = cv.bf16(S); B_KT = Buf()
        QTc = [cv.bf16(T), cv.bf16(T)]; B_QT = Buf()
        P.op("pool", "memset", A(QTc[0][64:128, :], 0.0), writes=[B_QT])
        P.op("pool", "memset", A(QTc[1][0:64, :], 0.0), writes=[B_QT])
        Va = cv.bf16(16 * 130).rearrange("p (t v) -> p t v", t=16); B_Va = Buf()
        Gs = cv.f32(NT * 128).rearrange("p (c f) -> p c f", c=NT); B_Gs = Buf()
        tmpS = [cv.f32(256), cv.f32(256)]; B_tmpS = [Buf(), Buf()]
        PT = [cv.bf16(512), cv.bf16(512)]; B_PT = [Buf(), Buf()]
        Os = [cv.f32(4 * 129).rearrange("p (j v) -> p j v", j=4) for _ in range(2)]; B_Os = [Buf(), Buf()]
        obuf = cv.f32(NT * 128).rearrange("p (c f) -> p c f", c=NT); B_ob = Buf()
        t1 = cv.f32(512).rearrange("p (j v) -> p j v", j=4); t2 = cv.f32(512).rearrange("p (j v) -> p j v", j=4); B_t = Buf()
        rr = cv.f32(8); B_rr = Buf()
        ss = cv.f32(NT); B_ss = Buf()
        junk = cv.f32(128); B_junk = Buf()
        ydt = cv.bf16(NT * 128).rearrange("p (c f) -> p c f", c=NT); B_ydt = Buf()
        biasT = cv.f32(8 * 2 * 128).rearrange("p (h d q) -> p h d q", h=8, d=2); B_biasT = Buf()
        for h in range(8):
            P.dma("sp", "dma_start", A(out=biasT[:, h, :, :], in_=bias_scr.ap()[h, :].rearrange("(d k q) -> k d q", d=2, k=128)), writes=[B_biasT])
        if sq == 0 and part == 0:
            print("diff arena words", cv.off)
        P.op("pool", "memset", A(Va[:, :, 128:130], 1.0), writes=[B_Va])
        sidx = 0
        for h in range(8):
            wq, wqB = load_w(win(OFF_DQ + 128 * h, 128), 8, 128)
            wk, wkB = load_w(win(OFF_DK + 128 * h, 128), 8, 128)
            wv, wvB = load_w(win(OFF_DV + 128 * h, 128), 8, 128)
            wg, wgB = load_w(win(OFF_DG + 128 * h, 128), 8, 128)
            for kc in range(nk // 512):
                pi = kc % 2
                proj_fm(pi, 0, wk, wkB, 0, 128, 512 * kc, 512)
                evac(KT[:, 512 * kc:512 * kc + 512], psum[pi][:, :], [B_ps[pi]], [B_KT])
            for tc in range(T // 512):
                pi = tc % 2
                proj_fm(pi, 0, wq, wqB, 0, 128, t0 + 512 * tc, 512)
                evac(QTc[0][0:64, 512 * tc:512 * tc + 512], psum[pi][0:64, :], [B_ps[pi]], [B_QT])
                evac(QTc[1][64:128, 512 * tc:512 * tc + 512], psum[pi][64:128, :], [B_ps[pi]], [B_QT])
            for kt in range(nk // 128):
                pi = kt % 2
                proj_tm(pi, 0, wv, wvB, 0, 128, 128 * kt)
                evac(Va[:, kt, 0:128], psum[pi][:, 0:128], [B_ps[pi]], [B_Va])
            for c in range(NT):
                pi = c % 2
                proj_tm(pi, 0, wg, wgB, 0, 128, t0 + 128 * c)
                P.op("act", "activation", A(out=Gs[:, c, :], in_=psum[pi][:, 0:128], func=AF.Silu), reads=[B_ps[pi]], writes=[B_Gs])
            for qc in range(T // 512):
                qb0 = (t0 + 512 * qc) // 128
                for c in range(2):
                    r0 = 64 * c
                    nkb_ = qb0 + 4
                    sis = []
                    for kk in range(nkb_):
                        sis.append(sidx % 2); sidx += 1

                    def emit_S(kb_):
                        j0 = kb_ - qb0
                        jlo = max(j0, 0)
                        si = sis[kb_]
                        P.op("pe", "matmul", A(
                            psum[si][:, 128 * jlo:512], lhsT=KT[:, 128 * kb_:128 * kb_ + 128],
                            rhs=QTc[c][:, 512 * qc + 128 * jlo:512 * qc + 512], start=True, stop=True),
                             reads=[B_KT, B_QT], writes=[B_ps[si]])

                    emit_S(0)
                    for kb_ in range(nkb_):
                        if kb_ + 1 < nkb_:
                            emit_S(kb_ + 1)
                        j0 = kb_ - qb0
                        jlo = max(j0, 0)
                        si = sis[kb_]
                        Sps = psum[si]
                        nsp = 0
                        for j in range(jlo, 4):
                            dl = j - j0
                            if dl <= 1:
                                P.op("dve", "scalar_tensor_tensor", A(
                                    out=tmpS[si][:, 128 * nsp:128 * nsp + 128], in0=Sps[:, 128 * j:128 * j + 128], scalar=0.125,
                                    in1=biasT[:, h, dl, :], op0=ALU.mult, op1=ALU.add),
                                     reads=[B_ps[si], B_biasT], writes=[B_tmpS[si]])
                                nsp += 1
                        if nsp:
                            P.op("act", "activation", A(out=PT[si][:, 128 * jlo:128 * (jlo + nsp)], in_=tmpS[si][:, 0:128 * nsp], func=AF.Exp),
                                 reads=[B_tmpS[si]], writes=[B_PT[si]])
                        jf = max(j0 + 2, 0)
                        if jf < 4:
                            P.op("act", "activation", A(out=PT[si][:, 128 * jf:512], in_=Sps[:, 128 * jf:512], func=AF.Exp, scale=0.125, bias=sm[:, SM_CFAR + h:SM_CFAR + h + 1]),
                                 reads=[B_ps[si], B_sm], writes=[B_PT[si]])
                        for j in range(jlo, 4):
                            P.op("pe", "matmul", A(psum[2 + j][:, 0:129], lhsT=PT[si][:, 128 * j:128 * j + 128], rhs=Va[:, kb_, 0:129],
                                                   start=(kb_ == 0), stop=(kb_ == qb0 + j)),
                                 reads=[B_PT[si], B_Va], writes=[B_ps[2 + j]])
                    for j in range(4):
                        evac(Os[c][:, j, :], psum[2 + j][:, 0:129], [B_ps[2 + j]], [B_Os[c]])
                P.op("dve", "reciprocal", A(out=rr[:, 0:4], in_=Os[0][:, :, 128]), reads=[B_Os[0]], writes=[B_rr])
                P.op("dve", "reciprocal", A(out=rr[:, 4:8], in_=Os[1][:, :, 128]), reads=[B_Os[1], B_rr], writes=[B_rr])
                P.op("dve", "tensor_scalar", A(out=rr[:, 4:8], in0=rr[:, 4:8], scalar1=sm[:, SM_NLAM:SM_NLAM + 1], scalar2=None, op0=ALU.mult), reads=[B_rr, B_sm], writes=[B_rr])
                P.op("dve", "tensor_tensor", A(out=t1, in0=Os[0][:, :, 0:128], in1=rr[:, 0:4].unsqueeze(2).broadcast_to([128, 4, 128]), op=ALU.mult), reads=[B_Os[0], B_rr, B_t], writes=[B_t])
                P.op("dve", "tensor_tensor", A(out=t2, in0=Os[1][:, :, 0:128], in1=rr[:, 4:8].unsqueeze(2).broadcast_to([128, 4, 128]), op=ALU.mult), reads=[B_Os[1], B_rr, B_t], writes=[B_t])
                P.op("dve", "tensor_tensor", A(out=obuf[:, 4 * qc:4 * qc + 4, :], in0=t1, in1=t2, op=ALU.add), reads=[B_t], writes=[B_ob])
            for c in range(NT):
                P.op("act", "activation", A(out=junk, in_=obuf[:, c, :], func=AF.Square, accum_out=ss[:, c:c + 1]), reads=[B_ob], writes=[B_junk, B_ss])
            rstd_from_ss(ss[:, 0:NT], NT, 1.0 / 128, [B_ss])
            P.op("dve", "tensor_tensor", A(out=obuf, in0=obuf, in1=ss[:, 0:NT].unsqueeze(2).broadcast_to([128, NT, 128]), op=ALU.mult), reads=[B_ob, B_ss], writes=[B_ob])
            P.op("dve", "tensor_tensor", A(out=obuf, in0=obuf, in1=sm[:, SM_SUBG:SM_SUBG + 128].unsqueeze(1).broadcast_to([128, NT, 128]), op=ALU.mult), reads=[B_ob, B_sm], writes=[B_ob])
            P.op("dve", "tensor_tensor", A(out=ydt, in0=obuf, in1=Gs, op=ALU.mult), reads=[B_ob, B_Gs], writes=[B_ydt])
            for c in range(NT):
                transpose_to_YT(ydt[:, c, :], B_ydt, 1, h, 128 * c, 6 + c % 2)
        P.barrier()

    def ssm_phase(sq, part):
        t0 = part * T
        evac_mode[0] = _CFG.get("ssm_evac", "act")
        cv = Carver()
        Us = [cv.bf16(T + 4), cv.bf16(T + 4)]; B_Us = [Buf(), Buf()]
        dgs = [cv.bf16(4 * 128).rearrange("p (k c) -> p k c", k=4) for _ in range(2)]; B_dgs = [Buf(), Buf()]
        xsT = cv.bf16(T); B_xsT = Buf()
        xs_tok2 = [cv.bf16(NT * 256).rearrange("p (c f) -> p c f", c=NT) for _ in range(2)]; B_xs2 = [Buf(), Buf()]
        BT2 = [cv.bf16(T), cv.bf16(T)]; B_BT2 = [Buf(), Buf()]
        Btok2 = [cv.bf16(NT * 128).rearrange("p (c f) -> p c f", c=NT) for _ in range(2)]; B_Btok2 = [Buf(), Buf()]
        CT2 = [cv.bf16(T), cv.bf16(T)]; B_CT2 = [Buf(), Buf()]
        zs2 = [cv.bf16(NT * 256).rearrange("p (c f) -> p c f", c=NT) for _ in range(2)]; B_zs2 = [Buf(), Buf()]
        dskI2 = [cv.bf16(4 * 128).rearrange("p (h l) -> p h l", h=4) for _ in range(2)]; B_dskI2 = [Buf(), Buf()]
        sg_b2 = [cv.f32(256), cv.f32(256)]; B_sg2 = [Buf(), Buf()]
        dt = cv.f32(NT * 32).rearrange("p (c f) -> p c f", c=NT)
        adt = cv.f32(NT * 32).rearrange("p (c f) -> p c f", c=NT); B_dt = Buf()
        E = cv.f32(NT * 96).rearrange("p (c f) -> p c f", c=NT); B_E = Buf()
        R = [cv.bf16(512), cv.bf16(512)]; B_R = [Buf(), Buf()]
        LT = [cv.bf16(512), cv.bf16(512)]; B_LT = [Buf(), Buf()]
        MT = [cv.bf16(512), cv.bf16(512)]; B_MT = [Buf(), Buf()]
        xdt = [cv.bf16(256), cv.bf16(256)]; xdd = [cv.bf16(256), cv.bf16(256)]; B_xd = [Buf(), Buf()]; B_xdd = [Buf(), Buf()]
        y1 = cv.f32(256); B_y1 = Buf()
        y2 = cv.f32(NT * 256).rearrange("p (c f) -> p c f", c=NT); B_y2c = [Buf() for _ in range(NT)]; B_y2 = B_y2c
        yn = cv.bf16(NT * 256).rearrange("p (c f) -> p c f", c=NT); B_ync = [Buf() for _ in range(NT)]
        st_b = cv.bf16(256); B_stb = Buf()
        stmp = cv.f32(256); B_stmp = Buf()
        ss = cv.f32(NT); B_ss = Buf()
        junk = cv.bf16(256); B_junk = Buf()
        if sq == 0 and part == 0:
            print("ssm arena words", cv.off)

        wd, wdB = load_w(win(OFF_DT, 32), 8, 32)
        for c in range(NT):
            pi = c % 2
            proj_tm(pi, 0, wd, wdB, 0, 32, t0 + 128 * c)
            P.op("dve", "tensor_tensor", A(out=dt[:, c, :], in0=psum[pi][:, 0:32], in1=sm[:, SM_DTB:SM_DTB + 32], op=ALU.add), reads=[B_ps[pi], B_sm], writes=[B_dt])
        dtf = dt.rearrange("p c f -> p (c f)")
        P.op("act", "activation", A(out=dtf, in_=dtf, func=AF.Exp), reads=[B_dt], writes=[B_dt])
        P.op("act", "activation", A(out=dtf, in_=dtf, func=AF.Ln, bias=sm[:, SM_ONE:SM_ONE + 1]), reads=[B_dt, B_sm], writes=[B_dt])
        P.op("dve", "tensor_tensor", A(out=adt, in0=dt, in1=sm[:, SM_A:SM_A + 32].unsqueeze(1).broadcast_to([128, NT, 32]), op=ALU.mult), reads=[B_dt, B_sm], writes=[B_dt])
        for c in range(NT):
            pi = c % 2
            for i, cc in enumerate((C_TRIL, C_SUP, C_ONES)):
                P.op("pe", "matmul", A(psum[pi][:, 32 * i:32 * i + 32], lhsT=cst[:, cc:cc + 128], rhs=adt[:, c, :], start=True, stop=True),
                     reads=[B_cst, B_dt], writes=[B_ps[pi]])
            P.op("act", "activation", A(out=E[:, c, :], in_=psum[pi][:, 0:96], func=AF.Exp), reads=[B_ps[pi]], writes=[B_E])

        def prep(g):
            q = g % 2
            xs_tok, BT, Btok, CT, zs, dskI, sg_b = xs_tok2[q], BT2[q], Btok2[q], CT2[q], zs2[q], dskI2[q], sg_b2[q]
            B_xs, B_BT, B_Btok, B_CT, B_zs, B_dskI, B_sg = B_xs2[q], B_BT2[q], B_Btok2[q], B_CT2[q], B_zs2[q], B_dskI2[q], B_sg2[q]
            if part == 0:
                P.op("pool", "memset", A(state_f[:, g, :], 0.0), writes=[B_state[g]])
            for hh in range(4):
                P.op("pool", "tensor_scalar", A(out=dskI[:, hh, :], in0=cst[:, C_ID:C_ID + 128], scalar1=sm[:, SM_DSK + 4 * g + hh:SM_DSK + 4 * g + hh + 1], scalar2=None, op0=ALU.mult),
                     reads=[B_cst, B_sm], writes=[B_dskI])
            blocks = [("xs", 2 * g, OFF_XBC + 256 * g), ("xs", 2 * g + 1, OFF_XBC + 256 * g + 128),
                      ("B", 16 + g, OFF_XBC + 2048 + 128 * g), ("C", 24 + g, OFF_XBC + 3072 + 128 * g)]
            for bi_, (kind, blk, col) in enumerate(blocks):
                wt, wB = load_w(win(col, 128), 8, 128)
                U = Us[bi_ % 2]; B_U = B_Us[bi_ % 2]
                dg = dgs[bi_ % 2]; B_dg = B_dgs[bi_ % 2]
                for k in range(4):
                    P.op("dve", "tensor_scalar", A(out=dg[:, k, :], in0=cst[:, C_ID:C_ID + 128], scalar1=cw[:, blk, k:k + 1], scalar2=None, op0=ALU.mult),
                         reads=[B_cst, B_cw], writes=[B_dg])
                if part == 0:
                    P.op("pool", "memset", A(U[:, 0:4], 0.0), writes=[B_U])
                else:
                    P.op("pool", "tensor_copy", A(out=U[:, 0:4], in_=halo[:, blk, :]), reads=[B_halo], writes=[B_U])
                for tc in range(T // 512):
                    pi = 6 + tc % 2
                    proj_fm(pi, 0, wt, wB, 0, 128, t0 + 512 * tc, 512)
                    evac(U[:, 4 + 512 * tc:4 + 512 * tc + 512], psum[pi][:, :], [B_ps[pi]], [B_U])
                    yield
                P.op("pool", "tensor_copy", A(out=halo[:, blk, :], in_=U[:, T:T + 4]), reads=[B_U], writes=[B_halo])
                dst = {"C": (CT, B_CT), "B": (BT, B_BT), "xs": (xsT, B_xsT)}[kind]
                for tc in range(T // 512):
                    pi = 6 + tc % 2
                    for k in range(4):
                        P.op("pe", "matmul", A(psum[pi][:, :], lhsT=dg[:, k, :], rhs=U[:, 1 + k + 512 * tc:1 + k + 512 * tc + 512], start=(k == 0), stop=(k == 3)),
                             reads=[B_dg, B_U], writes=[B_ps[pi]])
                    P.op("act", "activation", A(out=dst[0][:, 512 * tc:512 * tc + 512], in_=psum[pi][:, :], func=AF.Silu, bias=cb[:, blk:blk + 1]),
                         reads=[B_ps[pi], B_cw], writes=[dst[1]])
                    yield
                if kind == "B":
                    for c in range(NT):
                        P.op("pe", "transpose", A(out=psb(6)[:, 128 * c:128 * c + 128], in_=BT[:, 128 * c:128 * c + 128], identity=identb[:]),
                             reads=[B_BT, B_cst], writes=[B_ps[6]])
                    evac(Btok, psb(6)[:, 0:128 * NT].rearrange("p (c f) -> p c f", c=NT), [B_ps[6]], [B_Btok])
                elif kind == "xs":
                    jx = bi_
                    for c in range(NT):
                        P.op("pe", "transpose", A(out=psb(7)[:, 128 * c:128 * c + 128], in_=xsT[:, 128 * c:128 * c + 128], identity=identb[:]),
                             reads=[B_xsT, B_cst], writes=[B_ps[7]])
                    evac(xs_tok[:, :, 128 * jx:128 * jx + 128], psb(7)[:, 0:128 * NT].rearrange("p (c f) -> p c f", c=NT), [B_ps[7]], [B_xs])
                yield
            wz, wzB = load_w(win(OFF_Z + 256 * g, 256), 8, 256)
            for c in range(NT):
                pi = 6 + c % 2
                proj_tm(pi, 0, wz, wzB, 0, 256, t0 + 128 * c)
                P.op("act", "activation", A(out=zs[:, c, :], in_=psum[pi][:, 0:256], func=AF.Silu), reads=[B_ps[pi]], writes=[B_zs])
                if c % 2:
                    yield

        def scan(g):
            q = g % 2
            xs_tok, BT, Btok, CT, zs, dskI, sg_b = xs_tok2[q], BT2[q], Btok2[q], CT2[q], zs2[q], dskI2[q], sg_b2[q]
            B_xs, B_BT, B_Btok, B_CT, B_zs, B_dskI, B_sg = B_xs2[q], B_BT2[q], B_Btok2[q], B_CT2[q], B_zs2[q], B_dskI2[q], B_sg2[q]
            P.op("act", "copy", A(out=st_b, in_=state_f[:, g, :]), reads=[B_state[g]], writes=[B_stb])
            CBb = (1, 5)

            def S1(c):
                i2 = c % 2
                tk = slice(128 * c, 128 * c + 128)
                ag = adt[:, c, 4 * g:4 * g + 4]
                P.op("dve", "tensor_tensor", A(out=R[i2].rearrange("p (h l) -> p h l", h=4), in0=cst[:, C_TRIL:C_TRIL + 128].unsqueeze(1).broadcast_to([128, 4, 128]),
                                               in1=ag.unsqueeze(2).broadcast_to([128, 4, 128]), op=ALU.mult),
                     reads=[B_cst, B_dt], writes=[B_R[i2]])
                P.op("pe", "matmul", A(psum[0][:, :], lhsT=cstb[:, 0:128], rhs=R[i2], start=True, stop=False), reads=[B_cst, B_R[i2]], writes=[B_ps[0]])
                P.op("pe", "matmul", A(psum[0][:, :], lhsT=cstb[:, 128:256], rhs=cstb[:, 256:768], start=False, stop=True), reads=[B_cst], writes=[B_ps[0]])
                P.op("act", "activation", A(out=LT[i2], in_=psum[0][:, :], func=AF.Exp), reads=[B_ps[0]], writes=[B_LT[i2]])
                P.op("pe", "matmul", A(psum[CBb[i2]][:, 0:128], lhsT=BT[:, tk], rhs=CT[:, tk], start=True, stop=True), reads=[B_BT, B_CT], writes=[B_ps[CBb[i2]]])

            def S2(c):
                i2 = c % 2
                P.op("dve", "tensor_tensor", A(out=xdt[i2].rearrange("p (h q) -> p h q", h=4), in0=xs_tok[:, c, :].rearrange("p (h q) -> p h q", h=4),
                                               in1=dt[:, c, 4 * g:4 * g + 4].unsqueeze(2).broadcast_to([128, 4, 64]), op=ALU.mult),
                     reads=[B_xs, B_dt], writes=[B_xd[i2]])
                P.op("dve", "tensor_tensor", A(out=xdd[i2].rearrange("p (h q) -> p h q", h=4), in0=xdt[i2].rearrange("p (h q) -> p h q", h=4),
                                               in1=E[:, c, 32 + 4 * g:32 + 4 * g + 4].unsqueeze(2).broadcast_to([128, 4, 64]), op=ALU.mult),
                     reads=[B_xd[i2], B_E], writes=[B_xdd[i2]])
                P.op("dve", "tensor_tensor", A(out=MT[i2].rearrange("p (h l) -> p h l", h=4), in0=LT[i2].rearrange("p (h l) -> p h l", h=4),
                                               in1=psum[CBb[i2]][:, 0:128].unsqueeze(1).broadcast_to([128, 4, 128]), op=ALU.mult),
                     reads=[B_LT[i2], B_ps[CBb[i2]]], writes=[B_MT[i2]])

            def S3(c):
                i2 = c % 2
                tk = slice(128 * c, 128 * c + 128)
                for hh in range(4):
                    P.op("pe", "matmul", A(psum[2][:, 64 * hh:64 * hh + 64], lhsT=MT[i2][:, 128 * hh:128 * hh + 128], rhs=xdt[i2][:, 64 * hh:64 * hh + 64], start=True, stop=False),
                         reads=[B_MT[i2], B_xd[i2]], writes=[B_ps[2]])
                    P.op("pe", "matmul", A(psum[2][:, 64 * hh:64 * hh + 64], lhsT=dskI[:, hh, :], rhs=xs_tok[:, c, 64 * hh:64 * hh + 64], start=False, stop=True),
                         reads=[B_dskI, B_xs], writes=[B_ps[2]])
                P.op("pe", "matmul", A(psum[4][:, 0:256], lhsT=Btok[:, c, :], rhs=xdd[i2], start=True, stop=True), reads=[B_Btok, B_xdd[i2]], writes=[B_ps[4]])
                P.op("pe", "matmul", A(psum[3][:, 0:256], lhsT=CT[:, tk], rhs=st_b, start=True, stop=True), reads=[B_CT, B_stb], writes=[B_ps[3]])
                P.op("dve", "tensor_tensor", A(out=y1.rearrange("p (h q) -> p h q", h=4), in0=psum[3][:, 0:256].rearrange("p (h q) -> p h q", h=4),
                                               in1=E[:, c, 4 * g:4 * g + 4].unsqueeze(2).broadcast_to([128, 4, 64]), op=ALU.mult),
                     reads=[B_ps[3], B_E], writes=[B_y1])
                P.op("dve", "tensor_tensor", A(out=stmp.rearrange("p (h q) -> p h q", h=4), in0=state_f[:, g, :].rearrange("p (h q) -> p h q", h=4),
                                               in1=E[:, c, 64 + 4 * g:64 + 4 * g + 4].unsqueeze(2).broadcast_to([128, 4, 64]), op=ALU.mult),
                     reads=[B_state[g], B_E], writes=[B_stmp])
                P.op("dve", "tensor_tensor", A(out=state_f[:, g, :], in0=stmp, in1=psum[4][:, 0:256], op=ALU.add), reads=[B_stmp, B_ps[4]], writes=[B_state[g]])
                P.op("act", "copy", A(out=st_b, in_=state_f[:, g, :]), reads=[B_state[g]], writes=[B_stb])
                P.op("dve", "tensor_tensor", A(out=y2[:, c, :], in0=y1, in1=psum[2][:, 0:256], op=ALU.add), reads=[B_y1, B_ps[2]], writes=[B_y2c[c]])

            S1(0)
            if NT > 1:
                S1(1)
            S2(0)
            for c in range(NT):
                if c + 2 < NT:
                    S1(c + 2)
                if c + 1 < NT:
                    S2(c + 1)
                S3(c)
                yield
            P.op("dve", "tensor_tensor", A(out=y2, in0=y2, in1=zs, op=ALU.mult), reads=B_y2 + [B_zs], writes=B_y2)
            for c in range(NT):
                P.op("act", "activation", A(out=junk, in_=y2[:, c, :], func=AF.Square, accum_out=ss[:, c:c + 1]), reads=B_y2, writes=[B_junk, B_ss])
            rstd_from_ss(ss[:, 0:NT], NT, 1.0 / 256, [B_ss])
            yield
            for c in range(NT):
                P.op("act", "activation", A(out=yn[:, c, :], in_=y2[:, c, :], func=AF.Copy, scale=ss[:, c:c + 1]), reads=[B_y2c[c], B_ss], writes=[B_ync[c]])
            yield
            for c in range(NT):
                transpose_to_YT(yn[:, c, :], B_ync[c], 2, 2 * g, 128 * c, c % 2)
                if c % 2:
                    yield

        def run_interleaved(gens):
            gens = [g_ for g_ in gens if g_ is not None]
            while gens:
                for g_ in list(gens):
                    try:
                        next(g_)
                    except StopIteration:
                        gens.remove(g_)

        run_interleaved([prep(0)])
        for g in range(8):
            run_interleaved([scan(g), prep(g + 1) if g + 1 < 8 else None])
        P.barrier()

    def merge_phase(sq, part, bi, wbr_d, nkb, first):
        t0 = part * T
        it = 0
        for ob in range(8):
            wb_, wbB = load_w(wbr_d.ap()[:, 128 * ob:128 * ob + 128], nkb, 128)
            wg, wgB = load_w(win(OFF_GATE + 1024 * bi + 128 * ob, 128), 8, 128)
            for tc in range(T // 512):
                pa, pg = 2 * (it % 2), 2 * (it % 2) + 1
                it += 1
                for kb in range(nkb):
                    P.op("pe", "matmul", A(psum[pa][:, :], lhsT=wb_[:, kb, :], rhs=YT[:, kb, 512 * tc:512 * tc + 512], start=(kb == 0), stop=(kb == nkb - 1)),
                         reads=[wbB, B_YT], writes=[B_ps[pa]])
                proj_fm(pg, 0, wg, wgB, 0, 128, t0 + 512 * tc, 512)
                P.op("act", "activation", A(out=msig[:], in_=psum[pg][:, :], func=AF.Sigmoid), reads=[B_ps[pg]], writes=[B_msig])
                dst = mrg[:, ob, 512 * tc:512 * tc + 512]
                if first:
                    P.op("dve", "tensor_tensor", A(out=dst, in0=msig[:], in1=psum[pa][:, :], op=ALU.mult), reads=[B_msig, B_ps[pa]], writes=[B_mrg])
                else:
                    P.op("dve", "tensor_tensor", A(out=mtmp[:], in0=msig[:], in1=psum[pa][:, :], op=ALU.mult), reads=[B_msig, B_ps[pa]], writes=[B_mtmp])
                    P.op("dve", "tensor_tensor", A(out=dst, in0=dst, in1=mtmp[:], op=ALU.add), reads=[B_mtmp, B_mrg], writes=[B_mrg])

    def out_phase(sq, part):
        t0 = part * T
        cv = Carver(11000)
        fg_b = cv.f32(1024); B_fg = Buf()
        xt = [cv.f32(1024), cv.f32(1024)]; B_xt = [Buf(), Buf()]
        r = [cv.f32(1024), cv.f32(1024)]; B_r = [Buf(), Buf()]
        junk = cv.f32(1024); B_junk = Buf()
        ss = cv.f32(2); B_ss = Buf()
        bcast_load(fg_b, fgain_d.ap(), [B_fg])
        wts = [load_w(w_out.ap()[:, 256 * i:256 * i + 256], 8, 256) for i in range(4)]
        for c in range(NT):
            xi = c % 2
            rows = slice(t0 + 128 * c, t0 + 128 * c + 128)
            P.dma("sp", "dma_start", A(out=xt[xi], in_=x_d.ap()[sq, rows, :]), writes=[B_xt[xi]])
            for hf in range(2):
                pi = 2 * xi + hf
                for q4 in range(2):
                    wt, wB = wts[2 * hf + q4]
                    for kb in range(8):
                        P.op("pe", "matmul", A(psum[pi][:, 256 * q4:256 * q4 + 256], lhsT=mrg[:, kb, 128 * c:128 * c + 128], rhs=wt[:, kb, :], start=(kb == 0), stop=(kb == 7)),
                             reads=[wB, B_mrg], writes=[B_ps[pi]])
                P.op("dve", "tensor_tensor", A(out=r[xi][:, 512 * hf:512 * hf + 512], in0=xt[xi][:, 512 * hf:512 * hf + 512], in1=psum[pi][:, :], op=ALU.add),
                     reads=[B_xt[xi], B_ps[pi]], writes=[B_r[xi]])
            P.op("act", "activation", A(out=junk, in_=r[xi], func=AF.Square, accum_out=ss[:, 0:1]), reads=[B_r[xi]], writes=[B_junk, B_ss])
            rstd_from_ss(ss[:, 0:1], 1, 1.0 / D, [B_ss])
            P.op("dve", "scalar_tensor_tensor", A(out=r[xi], in0=r[xi], scalar=ss[:, 0:1], in1=fg_b, op0=ALU.mult, op1=ALU.mult), reads=[B_r[xi], B_ss, B_fg], writes=[B_r[xi]])
            P.dma("sp", "dma_start", A(out=out_d.ap()[sq, rows, :], in_=r[xi]), reads=[B_r[xi]])

    for sq in range(nseq):
        prologue(sq)
        for part in range(NPART):
            first = True
            for bi, (name, fn, wbr, nkb) in enumerate((("ssm", ssm_phase, w_brs, 16), ("diff", diff_phase, w_brd, 8), ("mem", mem_phase, w_brm, 8))):
                if name not in branches:
                    continue
                fn(sq, part)
                merge_phase(sq, part, bi, wbr, nkb, first)
                first = False
            out_phase(sq, part)
            if part < NPART - 1:
                P.barrier()
    P.barrier()
    P.finish()
    P.emit()
    return nc, P


_CACHE = {}


def kernel(**inputs):
    ncores = _CFG["ncores"]; nseq = _CFG["nseq"]
    key = (nseq, tuple(_CFG["branches"]), _CFG["same_engine_sync"], _CFG["schedule"], _CFG.get("ssm_evac"))
    if key not in _CACHE:
        _CACHE[key] = build(nseq, _CFG["branches"], _CFG["same_engine_sync"], _CFG["schedule"])
    nc, P = _CACHE[key]
    f = lambda a: np.ascontiguousarray(np.asarray(a, dtype=np.float32))
    cst, oh = host_consts()
    shared = {
        "w_in": f(inputs["w_in"][0]), "w_mem_kv": f(inputs["w_mem_kv"][0]), "w_br_ssm": f(inputs["w_br_ssm"][0]),
        "w_br_diff": f(inputs["w_br_diff"][0]), "w_br_mem": f(inputs["w_br_mem"][0]), "w_out": f(inputs["w_out"][0]),
        "norm_gain": f(inputs["norm_gain"]).reshape(1, D), "mem_norm_gain": f(inputs["mem_norm_gain"]).reshape(1, D),
        "final_norm_gain": f(inputs["final_norm_gain"]).reshape(1, D), "ssm_norm_gain": f(inputs["ssm_norm_gain"]).reshape(1, 2048),
        "subln_gain": f(inputs["subln_gain"]).reshape(1, 128), "dt_bias": f(inputs["dt_bias"]).reshape(1, 32),
        "a_log": f(inputs["a_log"]).reshape(1, 32), "d_skip": f(inputs["d_skip"]).reshape(1, 32),
        "lambda_q1": f(inputs["lambda_q1"]).reshape(1, 64), "lambda_k1": f(inputs["lambda_k1"]).reshape(1, 64),
        "lambda_q2": f(inputs["lambda_q2"]).reshape(1, 64), "lambda_k2": f(inputs["lambda_k2"]).reshape(1, 64),
        "rel_bias": f(inputs["rel_bias"]),
        "conv_w_l": f(np.asarray(inputs["conv_w"][0]).reshape(4, 32, 128).transpose(2, 1, 0)),
        "conv_b_l": f(np.asarray(inputs["conv_b"][0]).reshape(32, 128).transpose(1, 0)),
        "sgain_l": f(np.asarray(inputs["ssm_norm_gain"][0]).reshape(16, 128).transpose(1, 0)),
        "cst": cst, "onehot": oh,
    }
    x = np.asarray(inputs["x"]); mem = np.asarray(inputs["mem"])
    in_maps = []
    for c in range(ncores):
        m = dict(shared)
        m["x"] = f(x[c * nseq:(c + 1) * nseq])
        m["mem"] = f(mem[c * nseq:(c + 1) * nseq])
        in_maps.append(m)
    if _CFG.get("trace"):
        res = run_bass_kernel_spmd(nc, in_maps, core_ids=list(range(ncores)), trace=True)
        print("EXEC_TIME_NS", res.exec_time_ns)
    else:
        res = run_bass_kernel_spmd(nc, in_maps, core_ids=list(range(ncores)))
    return np.concatenate([np.asarray(r["out"]) for r in res.results], axis=0).astype(np.float32)
```

```python
import math
import numpy as np
import concourse.bass as bass
import concourse.mybir as mybir
from concourse.bass_utils import run_bass_kernel_spmd

F32 = mybir.dt.float32
BF16 = mybir.dt.bfloat16
ALU = mybir.AluOpType
AF = mybir.ActivationFunctionType

D = 1024
S = 2048
MEM = 256
IN_DIM = 15392
OFF_Z, OFF_XBC, OFF_DT, OFF_DQ, OFF_DK, OFF_DV, OFF_DG, OFF_MQ, OFF_MG, OFF_GATE = (
    0, 2048, 6144, 6176, 7200, 8224, 9248, 10272, 11296, 12320)
EPS = 1e-5
NEG = -30000.0
T = 1024
NT = T // 128
NPART = S // T

_CFG = {"branches": ("ssm", "diff", "mem"), "ncores": 8, "nseq": 4, "same_engine_sync": "raw", "schedule": True}

COMPUTE = ("pe", "act", "dve", "pool")
QUEUES = ("pe", "act", "dve", "pool", "sp")
DMAQ = ("sp", "pool", "act")
NDMASEM = 12


def A(*a, **k):
    return (a, k)


class Buf:
    __slots__ = ("name", "last_w", "readers")

    def __init__(self, name=""):
        self.name = name
        self.last_w = None
        self.readers = []


def _free(ap):
    n = 1
    for d in ap.shape[1:]:
        n *= int(d)
    return n


class Prog:
    def __init__(self, nc, same_engine_sync="raw", schedule=True):
        self.nc = nc
        self.ins = []
        self.segs = [[]]
        self.same_engine_sync = same_engine_sync
        self.schedule = schedule
        self.sem = {e: nc.alloc_semaphore("s_" + e) for e in COMPUTE}
        self.dsem = {e: [nc.alloc_semaphore(f"d_{e}{i}") for i in range(NDMASEM)] for e in DMAQ}

    def _deps(self, reads, writes):
        deps = set()
        raw = set()
        for b in reads:
            if b.last_w is not None:
                deps.add(b.last_w)
                raw.add(b.last_w)
        for b in writes:
            if b.last_w is not None:
                deps.add(b.last_w)
            deps.update(b.readers)
        return deps, raw

    def _commit(self, me, reads, writes):
        for b in reads:
            b.readers.append(me)
        for b in writes:
            b.last_w = me
            b.readers = []

    def _est(self, eng, meth, args, is_dma):
        a, k = args
        try:
            out = k.get("out", a[0] if a else None)
            n = _free(out)
            if is_dma:
                return 60.0, 2000.0 + n * (4 if out.dtype == F32 else 2) * 128 / 150.0
            if eng == "pe":
                if meth == "transpose":
                    return 110.0, 0.0
                f32 = k["lhsT"].dtype == F32
                return max(64.0, n * 0.42) * (4 if f32 else 1), 0.0
            if eng == "act":
                return 224.0 + 0.75 * n, 0.0
            if eng == "dve":
                return 60.0 + 1.05 * n, 0.0
            return 100.0 + 2.3 * n, 0.0
        except Exception:
            return 300.0, 0.0

    def _add(self, eng, meth, args, reads, writes, is_dma):
        deps, raw = self._deps(reads, writes)
        gid = len(self.ins)
        dur, lat = self._est(eng, meth, args, is_dma)
        tab = None
        if eng == "act" and meth == "activation":
            f = args[1].get("func")
            tab = "explog" if f in (AF.Exp, AF.Ln) else str(f)
        self.ins.append(dict(eng=eng, fn=(meth, args), deps=deps, raw=raw, dma=is_dma, dur=dur, lat=lat, tab=tab))
        self.segs[-1].append(gid)
        self._commit(gid, reads, writes)
        return gid

    def op(self, eng, meth, args, reads=(), writes=()):
        return self._add(eng, meth, args, reads, writes, False)

    def dma(self, eng, meth, args, reads=(), writes=()):
        return self._add(eng, meth, args, reads, writes, True)

    def barrier(self):
        self.segs.append([])

    def finish(self):
        self.segs.append([])

    def _sched(self, seg):
        ins = self.ins
        pend = {e: [] for e in QUEUES}
        for g in seg:
            pend[ins[g]["eng"]].append(g)
        if not self.schedule:
            return pend
        segset = set(seg)
        W = _CFG.get('W', 32)
        fin = {}
        nun = {}
        users = {}
        for g in seg:
            c = 0
            for d in ins[g]["deps"]:
                if d in segset:
                    c += 1
                    users.setdefault(d, []).append(g)
            nun[g] = c
        blev = {}
        for g in reversed(seg):
            m = 0.0
            for u in users.get(g, ()):
                if blev[u] > m:
                    m = blev[u]
            blev[g] = ins[g]["dur"] + ins[g]["lat"] + m
        free = {e: 0.0 for e in QUEUES}
        curtab = [None]
        order = {e: [] for e in QUEUES}
        cand = {e: None for e in QUEUES}
        dirty = set(QUEUES)
        remaining = len(seg)
        while remaining:
            for e in list(dirty):
                best = None
                lst = pend[e]
                for g in lst[:W]:
                    if nun[g]:
                        continue
                    I = ins[g]
                    r = 0.0
                    for d in I["deps"]:
                        if d in fin:
                            t = fin[d] + (_CFG.get("xlat", 150.0) if ins[d]["eng"] != e else _CFG.get("slat", 0.0))
                            if t > r:
                                r = t
                    st = r if r > free[e] else free[e]
                    if e == "act" and I["tab"] is not None and I["tab"] != curtab[0]:
                        st += 1300.0
                    if best is None or st < best[0] or (st == best[0] and blev[g] > blev[best[1]]):
                        best = (st, g)
                cand[e] = best
            dirty.clear()
            be = None
            for e in QUEUES:
                c = cand[e]
                if c is not None and (be is None or c[0] < cand[be][0]):
                    be = e
            if be is None:
                raise RuntimeError("scheduler deadlock")
            st, g = cand[be]
            I = ins[g]
            if be == "act" and I["tab"] is not None:
                curtab[0] = I["tab"]
            free[be] = st + I["dur"]
            fin[g] = st + I["dur"] + I["lat"]
            pend[be].remove(g)
            order[be].append(g)
            remaining -= 1
            dirty.add(be)
            for u in users.get(g, ()):
                nun[u] -= 1
                if nun[u] == 0:
                    dirty.add(ins[u]["eng"])
        return order

    def emit(self):
        nc = self.nc
        ins = self.ins
        stream = {e: [] for e in QUEUES}
        pos = {}
        dma_list = {e: [] for e in DMAQ}

        def tails():
            t = set()
            for e in QUEUES:
                for x in reversed(stream[e]):
                    if not isinstance(x, tuple) and not ins[x]["dma"]:
                        t.add(x)
                        break
            for e in DMAQ:
                t.update(dma_list[e][-NDMASEM:])
            return t

        nseg = len(self.segs)
        for si, seg in enumerate(self.segs):
            if seg:
                order = self._sched(seg)
                for e in QUEUES:
                    for g in order[e]:
                        I = ins[g]
                        if I["dma"]:
                            dl = dma_list[e]
                            I["k"] = len(dl)
                            if len(dl) >= NDMASEM:
                                I["deps"] = set(I["deps"]) | {dl[-NDMASEM]}
                            dl.append(g)
                        pos[g] = len(stream[e])
                        stream[e].append(g)
            if si < nseg - 1:
                t = tails()
                last = (si == nseg - 2)
                for e in (("sp",) if last else QUEUES):
                    stream[e].append(("wait", t))
        signal = set()
        plan = {e: [] for e in QUEUES}
        for e in QUEUES:
            seen = {x: -1 for x in COMPUTE}
            seen_slot = {}
            for ent in stream[e]:
                if isinstance(ent, tuple):
                    deps, raw, g = ent[1], ent[1], None
                else:
                    g = ent
                    deps, raw = ins[g]["deps"], ins[g]["raw"]
                best = {}
                for d in deps:
                    D = ins[d]
                    if D["dma"]:
                        slot = (D["eng"], D["k"] % NDMASEM)
                        if seen_slot.get(slot, -1) >= D["k"] // NDMASEM:
                            continue
                        if slot not in best or ins[best[slot]]["k"] < D["k"]:
                            best[slot] = d
                    else:
                        x = D["eng"]
                        if x == e and (x == "pe" or not self.same_engine_sync):
                            continue
                        if x == e and self.same_engine_sync == "raw" and d not in raw:
                            continue
                        if seen[x] >= pos[d]:
                            continue
                        if x not in best or pos[best[x]] < pos[d]:
                            best[x] = d
                final = list(best.values())
                for d in final:
                    D = ins[d]
                    if D["dma"]:
                        seen_slot[(D["eng"], D["k"] % NDMASEM)] = D["k"] // NDMASEM
                    else:
                        seen[D["eng"]] = pos[d]
                        signal.add(d)
                plan[e].append((final, g))
        count = {}
        for e in COMPUTE:
            c = 0
            for ent in stream[e]:
                if not isinstance(ent, tuple) and ent in signal:
                    c += 1
                    count[ent] = c
        self.stats = {e: len(stream[e]) for e in QUEUES}

        def replay(e, eng):
            for final, g in plan[e]:
                for d in final:
                    D = ins[d]
                    if D["dma"]:
                        eng.wait_ge(self.dsem[D["eng"]][D["k"] % NDMASEM], 16 * (D["k"] // NDMASEM + 1))
                    else:
                        eng.wait_ge(self.sem[D["eng"]], count[d])
                if g is None:
                    continue
                I = ins[g]
                fn = I["fn"]
                o = getattr(eng, fn[0])(*fn[1][0], **fn[1][1])
                if I["dma"]:
                    o.then_inc(self.dsem[e][I["k"] % NDMASEM], 16)
                elif g in signal:
                    o.then_inc(self.sem[e], 1)

        with nc.Block() as block:
            @block.tensor
            def _(eng):
                replay("pe", eng)

            @block.scalar
            def _(eng):
                replay("act", eng)

            @block.vector
            def _(eng):
                replay("dve", eng)

            @block.gpsimd
            def _(eng):
                replay("pool", eng)

            @block.sync
            def _(eng):
                replay("sp", eng)


def t5_bucket_np(rel):
    n = np.maximum(rel, 0)
    max_exact = 16
    nf = np.maximum(n, 1).astype(np.float32)
    large = max_exact + (np.log(nf / max_exact) / np.float32(math.log(128 / max_exact))
                         * (32 - max_exact)).astype(np.int32)
    large = np.minimum(large, 31)
    return np.where(n < max_exact, n, large)


def host_consts():
    k = np.arange(128)[:, None]
    l = np.arange(128)[None, :]
    ident = (k == l).astype(np.float32)
    tril1 = (k <= l).astype(np.float32)
    sup = (k > l).astype(np.float32)
    ones = np.ones((128, 128), np.float32)
    negi = ident * NEG
    slow4 = np.tile(sup, (1, 4))
    cst = np.concatenate([ident, tril1, sup, ones, negi, slow4], axis=1)
    oh = np.zeros((33, 2, 128, 128), np.float32)
    for dlt in range(2):
        rel = 128 * dlt + (l - k)
        b = t5_bucket_np(rel)
        for kk in range(128):
            for qq in range(128):
                if rel[kk, qq] >= 0:
                    oh[b[kk, qq], dlt, kk, qq] = 1.0
                else:
                    oh[32, dlt, kk, qq] = 1.0
    return cst, oh.reshape(33, 2 * 128 * 128)


C_ID, C_TRIL, C_SUP, C_ONES, C_NEGI, C_SLOW4 = 0, 128, 256, 384, 512, 640


def build(nseq, branches, same_engine_sync="raw", schedule=True):
    nc = bass.Bass("TRN2", target_bir_lowering=False)
    P = Prog(nc, same_engine_sync, schedule)

    def din(name, shape, dt=F32):
        return nc.dram_tensor(name, list(shape), dt, kind="ExternalInput")

    x_d = din("x", [nseq, S, D])
    mem_d = din("mem", [nseq, MEM, D])
    w_in = din("w_in", [D, IN_DIM])
    w_kv = din("w_mem_kv", [D, 2048])
    w_brs = din("w_br_ssm", [2048, D])
    w_brd = din("w_br_diff", [D, D])
    w_brm = din("w_br_mem", [D, D])
    w_out = din("w_out", [D, D])
    gain_d = din("norm_gain", [1, D])
    mgain_d = din("mem_norm_gain", [1, D])
    fgain_d = din("final_norm_gain", [1, D])
    sgain_d = din("ssm_norm_gain", [1, 2048])
    subg_d = din("subln_gain", [1, 128])
    dtb_d = din("dt_bias", [1, 32])
    alog_d = din("a_log", [1, 32])
    dsk_d = din("d_skip", [1, 32])
    lq1_d, lk1_d, lq2_d, lk2_d = (din(n, [1, 64]) for n in ("lambda_q1", "lambda_k1", "lambda_q2", "lambda_k2"))
    rb_d = din("rel_bias", [32, 8])
    cw_d = din("conv_w_l", [128, 32, 4])
    cb_d = din("conv_b_l", [128, 32])
    sgl_d = din("sgain_l", [128, 16])
    cst_d = din("cst", [128, 1152])
    oh_d = din("onehot", [33, 32768])
    out_d = nc.dram_tensor("out", [nseq, S, D], F32, kind="ExternalOutput")
    bias_scr = nc.dram_tensor("bias_scr", [8, 32768], F32, kind="Internal")
    w_in_f, w_kv_f, w_brs_f, w_brd_f, w_brm_f, w_out_f = w_in, w_kv, w_brs, w_brd, w_brm, w_out
    w_in = nc.dram_tensor("w_in_b", [D, IN_DIM], BF16, kind="Internal")
    w_kv = nc.dram_tensor("w_kv_b", [D, 2048], BF16, kind="Internal")
    w_brs = nc.dram_tensor("w_brs_b", [2048, D], BF16, kind="Internal")
    w_brd = nc.dram_tensor("w_brd_b", [D, D], BF16, kind="Internal")
    w_brm = nc.dram_tensor("w_brm_b", [D, D], BF16, kind="Internal")
    w_out = nc.dram_tensor("w_out_b", [D, D], BF16, kind="Internal")

    def sb(name, shape, dt=F32):
        return nc.alloc_sbuf_tensor("sb_" + name, list(shape), dt)

    cst = sb("cst", [128, 1152]); B_cst = Buf("cst")
    identb = sb("identb", [128, 128], BF16)
    trilb = sb("trilb", [128, 128], BF16)
    cstb = sb("cstb", [128, 768], BF16)
    sm = sb("small", [128, 512]); B_sm = Buf("small")
    SM_CFAR, SM_DTB, SM_A, SM_DSK, SM_LAM, SM_NLAM, SM_ONE, SM_EPS = 0, 8, 40, 72, 104, 105, 106, 107
    SM_SUBG = 128
    SM_TMP = 256
    cw = sb("cw", [128, 32, 4]); cb = sb("cb", [128, 32]); B_cw = Buf("cw")
    sgl = sb("sgl", [128, 16])
    hT = sb("hT", [128, 8, S], BF16)
    B_hT = [Buf(f"hT{i}") for i in range(S // 128)]
    kmT = sb("kmT", [128, 8, MEM], BF16); B_kmT = Buf("kmT")
    vma = sb("vma", [128, 2, 4, 258], BF16); B_vma = Buf("vma")
    YT = sb("YT", [128, 16, T], BF16); B_YT = Buf("YT")
    mrg = sb("mrg", [128, 8, T], BF16); B_mrg = Buf("mrg")
    state_f = sb("state_f", [128, 8, 256]); B_state = [Buf(f"st{g}") for g in range(8)]
    halo = sb("halo", [128, 32, 4]); B_halo = Buf("halo")
    NWB = 6
    wbf = [sb(f"wbf{i}", [128, 2048], BF16) for i in range(NWB)]; B_wbf = [Buf(f"wbf{i}") for i in range(NWB)]
    ARENA_W = 18976
    arena = sb("arena", [128, ARENA_W])
    arena_b = arena[:].bitcast(BF16)

    class Carver:
        def __init__(self, base=0):
            self.off = base

        def f32(self, n):
            a = arena[:, self.off:self.off + n]
            self.off += n
            assert self.off <= ARENA_W, self.off
            return a

        def bf16(self, n):
            w = (n + 1) // 2
            a = arena_b[:, 2 * self.off:2 * self.off + n]
            self.off += w
            assert self.off <= ARENA_W, self.off
            return a

    msig = sb("msig", [128, 512]); B_msig = Buf("msig")
    mtmp = sb("mtmp", [128, 512]); B_mtmp = Buf("mtmp")
    psum = [nc.alloc_psum_tensor(f"ps{i}", [128, 512], F32) for i in range(8)]
    B_ps = [Buf(f"ps{i}") for i in range(8)]

    def psb(i):
        return psum[i][:].bitcast(BF16)

    wctr = [0, 0]

    def load_w(src_ap, nkb, ncols):
        assert nkb * ncols <= 2048
        bi = wctr[1] % NWB; wctr[1] += 1
        bf = wbf[bi][:, 0:nkb * ncols].rearrange("p (k c) -> p k c", k=nkb)
        src = src_ap.rearrange("(k p) c -> p k c", p=128)
        P.dma("sp", "dma_start", A(out=bf, in_=src), writes=[B_wbf[bi]])
        return bf, B_wbf[bi]

    def win(c0, ncols):
        return w_in.ap()[:, c0:c0 + ncols]

    def hbufs(t0, n):
        return B_hT[t0 // 128:(t0 + n + 127) // 128]

    def proj_fm(ps_i, ps_cols, wt, wB, c_lo, ncol, tok0, ntok):
        out = psum[ps_i][0:ncol, ps_cols:ps_cols + ntok]
        for kb in range(8):
            P.op("pe", "matmul", A(out, lhsT=wt[:, kb, c_lo:c_lo + ncol], rhs=hT[:, kb, tok0:tok0 + ntok],
                                                 start=(kb == 0), stop=(kb == 7)),
                 reads=[wB] + hbufs(tok0, ntok), writes=[B_ps[ps_i]])

    def proj_tm(ps_i, ps_cols, wt, wB, c_lo, ncol, tok0):
        out = psum[ps_i][:, ps_cols:ps_cols + ncol]
        for kb in range(8):
            P.op("pe", "matmul", A(out, lhsT=hT[:, kb, tok0:tok0 + 128], rhs=wt[:, kb, c_lo:c_lo + ncol],
                                                 start=(kb == 0), stop=(kb == 7)),
                 reads=[wB] + hbufs(tok0, 128), writes=[B_ps[ps_i]])

    evac_rr = [0]
    evac_mode = ["alt"]

    def evac(out, in_, reads, writes):
        evac_rr[0] += 1
        if evac_mode[0] == "act" or (evac_mode[0] == "alt" and evac_rr[0] % 2):
            P.op("act", "copy", A(out=out, in_=in_), reads=reads, writes=writes)
        else:
            P.op("dve", "tensor_copy", A(out=out, in_=in_), reads=reads, writes=writes)

    def bcast_load(dst, src_row_ap, writes):
        P.dma("sp", "dma_start", A(out=dst, in_=src_row_ap.partition_broadcast(128)), writes=writes)

    def rstd_from_ss(ss_ap, n, inv_count, Bs):
        P.op("dve", "tensor_scalar", A(out=ss_ap, in0=ss_ap, scalar1=inv_count, scalar2=EPS, op0=ALU.mult, op1=ALU.add),
             reads=Bs, writes=Bs)
        P.op("act", "activation", A(out=ss_ap, in_=ss_ap, func=AF.Sqrt), reads=Bs, writes=Bs)
        P.op("dve", "reciprocal", A(out=ss_ap, in_=ss_ap), reads=Bs, writes=Bs)

    P.dma("sp", "dma_start", A(out=cst[:], in_=cst_d.ap()), writes=[B_cst])
    P.op("dve", "tensor_copy", A(out=identb[:], in_=cst[:, C_ID:C_ID + 128]), reads=[B_cst], writes=[B_cst])
    P.op("dve", "tensor_copy", A(out=trilb[:], in_=cst[:, C_TRIL:C_TRIL + 128]), reads=[B_cst], writes=[B_cst])
    P.op("dve", "tensor_copy", A(out=cstb[:, 0:128], in_=cst[:, C_SUP:C_SUP + 128]), reads=[B_cst], writes=[B_cst])
    P.op("dve", "tensor_copy", A(out=cstb[:, 128:768], in_=cst[:, C_NEGI:C_NEGI + 640]), reads=[B_cst], writes=[B_cst])
    P.dma("sp", "dma_start", A(out=cw[:], in_=cw_d.ap()), writes=[B_cw])
    P.dma("sp", "dma_start", A(out=cb[:], in_=cb_d.ap()), writes=[B_cw])
    P.dma("sp", "dma_start", A(out=sgl[:], in_=sgl_d.ap()), writes=[B_cw])
    P.op("pool", "memset", A(sm[:], 0.0), writes=[B_sm])
    P.op("pool", "memset", A(sm[:, SM_ONE:SM_ONE + 1], 1.0), writes=[B_sm])
    P.op("pool", "memset", A(sm[:, SM_EPS:SM_EPS + 1], EPS), writes=[B_sm])
    bcast_load(sm[:, SM_CFAR:SM_CFAR + 8], rb_d.ap()[31:32, :], [B_sm])
    bcast_load(sm[:, SM_DTB:SM_DTB + 32], dtb_d.ap(), [B_sm])
    bcast_load(sm[:, SM_A:SM_A + 32], alog_d.ap(), [B_sm])
    bcast_load(sm[:, SM_DSK:SM_DSK + 32], dsk_d.ap(), [B_sm])
    bcast_load(sm[:, SM_SUBG:SM_SUBG + 128], subg_d.ap(), [B_sm])
    for i, ld in enumerate((lq1_d, lk1_d, lq2_d, lk2_d)):
        bcast_load(sm[:, SM_TMP + 64 * i:SM_TMP + 64 * i + 64], ld.ap(), [B_sm])
    Bs = [B_sm]
    P.op("act", "activation", A(out=sm[:, SM_A:SM_A + 32], in_=sm[:, SM_A:SM_A + 32], func=AF.Exp), reads=Bs, writes=Bs)
    P.op("dve", "tensor_scalar_mul", A(out=sm[:, SM_A:SM_A + 32], in0=sm[:, SM_A:SM_A + 32], scalar1=-1.0), reads=Bs, writes=Bs)
    LAM_INIT = 0.8 - 0.6 * math.exp(-0.3 * 0)
    P.op("dve", "tensor_scalar_mul", A(out=sm[:, SM_SUBG:SM_SUBG + 128], in0=sm[:, SM_SUBG:SM_SUBG + 128], scalar1=1.0 - LAM_INIT), reads=Bs, writes=Bs)
    for i in range(2):
        a0 = SM_TMP + 128 * i
        P.op("dve", "tensor_tensor", A(out=sm[:, a0:a0 + 64], in0=sm[:, a0:a0 + 64], in1=sm[:, a0 + 64:a0 + 128], op=ALU.mult), reads=Bs, writes=Bs)
        P.op("dve", "reduce_sum", A(out=sm[:, 110 + i:111 + i], in_=sm[:, a0:a0 + 64], axis=mybir.AxisListType.X), reads=Bs, writes=Bs)
    P.op("act", "activation", A(out=sm[:, 110:112], in_=sm[:, 110:112], func=AF.Exp), reads=Bs, writes=Bs)
    P.op("dve", "tensor_tensor", A(out=sm[:, SM_LAM:SM_LAM + 1], in0=sm[:, 110:111], in1=sm[:, 111:112], op=ALU.subtract), reads=Bs, writes=Bs)
    P.op("dve", "tensor_scalar_add", A(out=sm[:, SM_LAM:SM_LAM + 1], in0=sm[:, SM_LAM:SM_LAM + 1], scalar1=LAM_INIT), reads=Bs, writes=Bs)
    P.op("dve", "tensor_scalar_mul", A(out=sm[:, SM_NLAM:SM_NLAM + 1], in0=sm[:, SM_LAM:SM_LAM + 1], scalar1=-1.0), reads=Bs, writes=Bs)

    cv = Carver()
    NPS = 4
    pst = [cv.f32(2048) for _ in range(NPS)]; B_pst = [Buf() for _ in range(NPS)]
    pbf = [cv.bf16(2048) for _ in range(NPS)]; B_pbf = [Buf() for _ in range(NPS)]
    pc_i = 0
    for (srcw, dstw, rows, cols) in ((w_in_f, w_in, D, IN_DIM), (w_kv_f, w_kv, D, 2048), (w_brs_f, w_brs, 2048, D),
                                     (w_brd_f, w_brd, D, D), (w_brm_f, w_brm, D, D), (w_out_f, w_out, D, D)):
        for rb in range(rows // 128):
            for c0 in range(0, cols, 2048):
                n = min(2048, cols - c0)
                si = pc_i % NPS
                P.dma("sp", "dma_start", A(out=pst[si][:, 0:n], in_=srcw.ap()[128 * rb:128 * rb + 128, c0:c0 + n]), writes=[B_pst[si]])
                ce = ("act", "dve")[pc_i % 2]
                if srcw is w_brs_f:
                    P.op("dve", "tensor_scalar", A(out=pbf[si][:, 0:n], in0=pst[si][:, 0:n], scalar1=sgl[:, rb:rb + 1], scalar2=None, op0=ALU.mult),
                         reads=[B_pst[si], B_cw], writes=[B_pbf[si]])
                else:
                    P.op(ce, "copy" if ce == "act" else "tensor_copy", A(out=pbf[si][:, 0:n], in_=pst[si][:, 0:n]), reads=[B_pst[si]], writes=[B_pbf[si]])
                P.dma("sp", "dma_start", A(out=dstw.ap()[128 * rb:128 * rb + 128, c0:c0 + n], in_=pbf[si][:, 0:n]), reads=[B_pbf[si]])
                pc_i += 1
    P.barrier()

    if "diff" in branches:
        cv = Carver()
        rbx = cv.f32(8)
        ohs = cv.f32(4096)
        stg = cv.f32(4096)
        B_rbx, B_ohs, B_stg, B_scr = Buf(), Buf(), Buf(), Buf()
        P.op("pool", "memset", A(rbx[32:33, :], NEG), writes=[B_rbx])
        P.dma("sp", "dma_start", A(out=rbx[0:32, :], in_=rb_d.ap()), writes=[B_rbx])
        for pc in range(8):
            P.dma("sp", "dma_start", A(out=ohs[0:33, :], in_=oh_d.ap()[:, 4096 * pc:4096 * pc + 4096]), writes=[B_ohs])
            for i in range(8):
                pi = i % 2
                P.op("pe", "matmul", A(psum[pi][0:8, :], lhsT=rbx[0:33, :], rhs=ohs[0:33, 512 * i:512 * i + 512], start=True, stop=True),
                     reads=[B_rbx, B_ohs], writes=[B_ps[pi]])
                evac(stg[0:8, 512 * i:512 * i + 512], psum[pi][0:8, :], [B_ps[pi]], [B_stg])
            P.dma("sp", "dma_start", A(out=bias_scr.ap()[:, 4096 * pc:4096 * pc + 4096], in_=stg[0:8, :]), reads=[B_stg], writes=[B_scr])
        P.barrier()

    def prologue(sq):
        cv = Carver(5200)
        gain_b = cv.f32(1024); B_gain = Buf()
        xt = [cv.f32(1024), cv.f32(1024)]; B_xt = [Buf(), Buf()]
        junk = cv.f32(1024); B_junk = Buf()
        hb = cv.bf16(1024); B_hb = Buf()
        ss = cv.f32(2); B_ss = Buf()
        memT = cv.bf16(8 * MEM); B_memT = Buf()
        memT3 = memT.rearrange("p (k m) -> p k m", k=8)

        def norm_transpose(src_ap, i, dst3, dstB, col0):
            xi = i % 2
            P.dma("sp", "dma_start", A(out=xt[xi], in_=src_ap), writes=[B_xt[xi]])
            P.op("act", "activation", A(out=junk, in_=xt[xi], func=AF.Square, accum_out=ss[:, 0:1]), reads=[B_xt[xi]], writes=[B_junk, B_ss])
            rstd_from_ss(ss[:, 0:1], 1, 1.0 / D, [B_ss])
            P.op("dve", "scalar_tensor_tensor", A(out=hb, in0=xt[xi], scalar=ss[:, 0:1], in1=gain_b, op0=ALU.mult, op1=ALU.mult),
                 reads=[B_xt[xi], B_ss, B_gain], writes=[B_hb])
            for kb in range(8):
                P.op("pe", "transpose", A(out=psb(6)[:, 128 * kb:128 * kb + 128], in_=hb[:, 128 * kb:128 * kb + 128], identity=identb[:]),
                     reads=[B_hb, B_cst], writes=[B_ps[6]])
            evac(dst3[:, :, col0:col0 + 128], psb(6)[:, 0:1024].rearrange("p (k t) -> p k t", k=8), [B_ps[6]], dstB)

        bcast_load(gain_b, gain_d.ap(), [B_gain])
        for i in range(S // 128):
            norm_transpose(x_d.ap()[sq, 128 * i:128 * i + 128, :], i, hT, [B_hT[i]], 128 * i)
        if "mem" in branches:
            bcast_load(gain_b, mgain_d.ap(), [B_gain])
            for i in range(2):
                norm_transpose(mem_d.ap()[sq, 128 * i:128 * i + 128, :], i, memT3, [B_memT], 128 * i)
            for fb in range(8):
                wt, wB = load_w(w_kv.ap()[:, 128 * fb:128 * fb + 128], 8, 128)
                pi = fb % 2
                for kb in range(8):
                    P.op("pe", "matmul", A(psum[pi][:, 0:MEM], lhsT=wt[:, kb, :], rhs=memT3[:, kb, :], start=(kb == 0), stop=(kb == 7)),
                         reads=[wB, B_memT], writes=[B_ps[pi]])
                evac(kmT[:, fb, :], psum[pi][:, 0:MEM], [B_ps[pi]], [B_kmT])
            P.op("pool", "memset", A(vma[:, :, :, 256:258], 1.0), writes=[B_vma])
            for hh in range(4):
                wt, wB = load_w(w_kv.ap()[:, 1024 + 256 * hh:1024 + 256 * hh + 256], 8, 256)
                for mt in range(2):
                    pi = mt
                    for kb in range(8):
                        P.op("pe", "matmul", A(psum[pi][:, 0:256], lhsT=memT3[:, kb, 128 * mt:128 * mt + 128], rhs=wt[:, kb, :], start=(kb == 0), stop=(kb == 7)),
                             reads=[wB, B_memT], writes=[B_ps[pi]])
                    evac(vma[:, mt, hh, 0:256], psum[pi][:, 0:256], [B_ps[pi]], [B_vma])
        P.barrier()

    def transpose_to_YT(src_bf, srcB, nblk, yt_blk0, tokcol0, ps_i):
        for j in range(nblk):
            P.op("pe", "transpose", A(out=psb(ps_i)[:, 128 * j:128 * j + 128], in_=src_bf[:, 128 * j:128 * j + 128], identity=identb[:]),
                 reads=[srcB, B_cst], writes=[B_ps[ps_i]])
        evac(YT[:, yt_blk0:yt_blk0 + nblk, tokcol0:tokcol0 + 128], psb(ps_i)[:, 0:128 * nblk].rearrange("p (j t) -> p j t", j=nblk), [B_ps[ps_i]], [B_YT])

    def mem_phase(sq, part):
        t0 = part * T
        evac_mode[0] = "alt"
        cv = Carver()
        QmT = cv.bf16(2 * T).rearrange("p (d t) -> p d t", d=2); B_Qm = Buf()
        Gm = cv.f32(NT * 256).rearrange("p (c f) -> p c f", c=NT); B_Gm = Buf()
        PmT = [cv.bf16(2 * 512).rearrange("p (m t) -> p m t", m=2) for _ in range(2)]; B_Pm = [Buf(), Buf()]
        rr = cv.f32(4); B_rr = Buf()
        ym = [cv.bf16(256), cv.bf16(256)]; B_ym = [Buf(), Buf()]
        it = 0
        for hh in range(4):
            wq, wqB = load_w(win(OFF_MQ + 256 * hh, 256), 8, 256)
            wg, wgB = load_w(win(OFF_MG + 256 * hh, 256), 8, 256)
            for db in range(2):
                for tc in range(T // 512):
                    pi = (2 * db + tc) % 2
                    proj_fm(pi, 0, wq, wqB, 128 * db, 128, t0 + 512 * tc, 512)
                    evac(QmT[:, db, 512 * tc:512 * tc + 512], psum[pi][:, :], [B_ps[pi]], [B_Qm])
            for c in range(NT):
                pi = 2 + c % 2
                proj_tm(pi, 0, wg, wgB, 0, 256, t0 + 128 * c)
                P.op("act", "activation", A(out=Gm[:, c, :], in_=psum[pi][:, 0:256], func=AF.Silu), reads=[B_ps[pi]], writes=[B_Gm])
            for tc in range(T // 512):
                pb = tc % 2
                for mt in range(2):
                    pi = mt
                    for db in range(2):
                        P.op("pe", "matmul", A(psum[pi][:, :], lhsT=kmT[:, 2 * hh + db, 128 * mt:128 * mt + 128], rhs=QmT[:, db, 512 * tc:512 * tc + 512], start=(db == 0), stop=(db == 1)),
                             reads=[B_kmT, B_Qm], writes=[B_ps[pi]])
                    P.op("act", "activation", A(out=PmT[pb][:, mt, :], in_=psum[pi][:, :], func=AF.Exp, scale=1.0 / 16.0), reads=[B_ps[pi]], writes=[B_Pm[pb]])
                for j in range(4):
                    pi = 2 + j % 2
                    for mt in range(2):
                        P.op("pe", "matmul", A(psum[pi][:, 0:257], lhsT=PmT[pb][:, mt, 128 * j:128 * j + 128], rhs=vma[:, mt, hh, 0:257], start=(mt == 0), stop=(mt == 1)),
                             reads=[B_Pm[pb], B_vma], writes=[B_ps[pi]])
                    yi = it % 2; it += 1
                    P.op("dve", "reciprocal", A(out=rr[:, 0:1], in_=psum[pi][:, 256:257]), reads=[B_ps[pi]], writes=[B_rr])
                    P.op("dve", "scalar_tensor_tensor", A(out=ym[yi], in0=psum[pi][:, 0:256], scalar=rr[:, 0:1], in1=Gm[:, 4 * tc + j, :], op0=ALU.mult, op1=ALU.mult),
                         reads=[B_ps[pi], B_rr, B_Gm], writes=[B_ym[yi]])
                    transpose_to_YT(ym[yi], B_ym[yi], 2, 2 * hh, 512 * tc + 128 * j, 6)

    def diff_phase(sq, part):
        t0 = 0
        TQ = S
        NTQ = S // 128
        evac_mode[0] = "dve"
        nk = S
        cv = Carver()
        KT = cv.bf16(S); B_KT = Buf()
        QTc = [cv.bf16(TQ), cv.bf16(TQ)]; B_QT = Buf()
        P.op("pool", "memset", A(QTc[0][64:128, :], 0.0), writes=[B_QT])
        P.op("pool", "memset", A(QTc[1][0:64, :], 0.0), writes=[B_QT])
        Va = cv.bf16(16 * 130).rearrange("p (t v) -> p t v", t=16); B_Va = Buf()
        Gs = cv.bf16(NTQ * 128).rearrange("p (c f) -> p c f", c=NTQ); B_Gs = Buf()
        tmpS = [cv.f32(256), cv.f32(256)]; B_tmpS = [Buf(), Buf()]
        PT = [cv.bf16(512), cv.bf16(512)]; B_PT = [Buf(), Buf()]
        Os = [cv.f32(4 * 129).rearrange("p (j v) -> p j v", j=4) for _ in range(2)]; B_Os = [Buf(), Buf()]
        obuf = cv.f32(NTQ * 128).rearrange("p (c f) -> p c f", c=NTQ); B_ob = Buf()
        t1 = cv.f32(512).rearrange("p (j v) -> p j v", j=4); t2 = cv.f32(512).rearrange("p (j v) -> p j v", j=4); B_t = Buf()
        rr = cv.f32(8); B_rr = Buf()
        ss = cv.f32(NTQ); B_ss = Buf()
        junk = cv.f32(128); B_junk = Buf()
        ydt = cv.bf16(NTQ * 128).rearrange("p (c f) -> p c f", c=NTQ); B_ydt = Buf()
        biasT = cv.f32(8 * 2 * 128).rearrange("p (h d q) -> p h d q", h=8, d=2); B_biasT = Buf()
        for h in range(8):
            P.dma("sp", "dma_start", A(out=biasT[:, h, :, :], in_=bias_scr.ap()[h, :].rearrange("(d k q) -> k d q", d=2, k=128)), writes=[B_biasT])
        if sq == 0 and part == 0:
            print("diff arena words", cv.off)
        P.op("pool", "memset", A(Va[:, :, 128:130], 1.0), writes=[B_Va])
        sidx = 0
        for h in range(8):
            wq, wqB = load_w(win(OFF_DQ + 128 * h, 128), 8, 128)
            wk, wkB = load_w(win(OFF_DK + 128 * h, 128), 8, 128)
            wv, wvB = load_w(win(OFF_DV + 128 * h, 128), 8, 128)
            wg, wgB = load_w(win(OFF_DG + 128 * h, 128), 8, 128)
            for kc in range(nk // 512):
                pi = kc % 2
                proj_fm(pi, 0, wk, wkB, 0, 128, 512 * kc, 512)
                evac(KT[:, 512 * kc:512 * kc + 512], psum[pi][:, :], [B_ps[pi]], [B_KT])
            for tc in range(TQ // 512):
                pi = tc % 2
                proj_fm(pi, 0, wq, wqB, 0, 128, t0 + 512 * tc, 512)
                evac(QTc[0][0:64, 512 * tc:512 * tc + 512], psum[pi][0:64, :], [B_ps[pi]], [B_QT])
                evac(QTc[1][64:128, 512 * tc:512 * tc + 512], psum[pi][64:128, :], [B_ps[pi]], [B_QT])
            for kt in range(nk // 128):
                pi = kt % 2
                proj_tm(pi, 0, wv, wvB, 0, 128, 128 * kt)
                evac(Va[:, kt, 0:128], psum[pi][:, 0:128], [B_ps[pi]], [B_Va])
            for c in range(NTQ):
                pi = c % 2
                proj_tm(pi, 0, wg, wgB, 0, 128, t0 + 128 * c)
                P.op("act", "activation", A(out=Gs[:, c, :], in_=psum[pi][:, 0:128], func=AF.Silu), reads=[B_ps[pi]], writes=[B_Gs])
            for qc in range(TQ // 512):
                qb0 = (t0 + 512 * qc) // 128
                for c in range(2):
                    r0 = 64 * c
                    nkb_ = qb0 + 4
                    sis = []
                    for kk in range(nkb_):
                        sis.append(sidx % 2); sidx += 1

                    def emit_S(kb_):
                        j0 = kb_ - qb0
                        jlo = max(j0, 0)
                        si = sis[kb_]
                        P.op("pe", "matmul", A(
                            psum[si][:, 128 * jlo:512], lhsT=KT[:, 128 * kb_:128 * kb_ + 128],
                            rhs=QTc[c][:, 512 * qc + 128 * jlo:512 * qc + 512], start=True, stop=True),
                             reads=[B_KT, B_QT], writes=[B_ps[si]])

                    emit_S(0)
                    for kb_ in range(nkb_):
                        if kb_ + 1 < nkb_:
                            emit_S(kb_ + 1)
                        j0 = kb_ - qb0
                        jlo = max(j0, 0)
                        si = sis[kb_]
                        Sps = psum[si]
                        nsp = 0
                        for j in range(jlo, 4):
                            dl = j - j0
                            if dl <= 1:
                                P.op("dve", "scalar_tensor_tensor", A(
                                    out=tmpS[si][:, 128 * nsp:128 * nsp + 128], in0=Sps[:, 128 * j:128 * j + 128], scalar=0.125,
                                    in1=biasT[:, h, dl, :], op0=ALU.mult, op1=ALU.add),
                                     reads=[B_ps[si], B_biasT], writes=[B_tmpS[si]])
                                nsp += 1
                        if nsp:
                            P.op("act", "activation", A(out=PT[si][:, 128 * jlo:128 * (jlo + nsp)], in_=tmpS[si][:, 0:128 * nsp], func=AF.Exp),
                                 reads=[B_tmpS[si]], writes=[B_PT[si]])
                        jf = max(j0 + 2, 0)
                        if jf < 4:
                            P.op("act", "activation", A(out=PT[si][:, 128 * jf:512], in_=Sps[:, 128 * jf:512], func=AF.Exp, scale=0.125, bias=sm[:, SM_CFAR + h:SM_CFAR + h + 1]),
                                 reads=[B_ps[si], B_sm], writes=[B_PT[si]])
                        for j in range(jlo, 4):
                            P.op("pe", "matmul", A(psum[2 + j][:, 0:129], lhsT=PT[si][:, 128 * j:128 * j + 128], rhs=Va[:, kb_, 0:129],
                                                   start=(kb_ == 0), stop=(kb_ == qb0 + j)),
                                 reads=[B_PT[si], B_Va], writes=[B_ps[2 + j]])
                    for j in range(4):
                        evac(Os[c][:, j, :], psum[2 + j][:, 0:129], [B_ps[2 + j]], [B_Os[c]])
                P.op("dve", "reciprocal", A(out=rr[:, 0:4], in_=Os[0][:, :, 128]), reads=[B_Os[0]], writes=[B_rr])
                P.op("dve", "reciprocal", A(out=rr[:, 4:8], in_=Os[1][:, :, 128]), reads=[B_Os[1], B_rr], writes=[B_rr])
                P.op("dve", "tensor_scalar", A(out=rr[:, 4:8], in0=rr[:, 4:8], scalar1=sm[:, SM_NLAM:SM_NLAM + 1], scalar2=None, op0=ALU.mult), reads=[B_rr, B_sm], writes=[B_rr])
                P.op("dve", "tensor_tensor", A(out=t1, in0=Os[0][:, :, 0:128], in1=rr[:, 0:4].unsqueeze(2).broadcast_to([128, 4, 128]), op=ALU.mult), reads=[B_Os[0], B_rr, B_t], writes=[B_t])
                P.op("dve", "tensor_tensor", A(out=t2, in0=Os[1][:, :, 0:128], in1=rr[:, 4:8].unsqueeze(2).broadcast_to([128, 4, 128]), op=ALU.mult), reads=[B_Os[1], B_rr, B_t], writes=[B_t])
                P.op("dve", "tensor_tensor", A(out=obuf[:, 4 * qc:4 * qc + 4, :], in0=t1, in1=t2, op=ALU.add), reads=[B_t], writes=[B_ob])
            for c in range(NTQ):
                P.op("act", "activation", A(out=junk, in_=obuf[:, c, :], func=AF.Square, accum_out=ss[:, c:c + 1]), reads=[B_ob], writes=[B_junk, B_ss])
            rstd_from_ss(ss[:, 0:NTQ], NTQ, 1.0 / 128, [B_ss])
            P.op("dve", "tensor_tensor", A(out=obuf, in0=obuf, in1=ss[:, 0:NTQ].unsqueeze(2).broadcast_to([128, NTQ, 128]), op=ALU.mult), reads=[B_ob, B_ss], writes=[B_ob])
            P.op("dve", "tensor_tensor", A(out=obuf, in0=obuf, in1=sm[:, SM_SUBG:SM_SUBG + 128].unsqueeze(1).broadcast_to([128, NTQ, 128]), op=ALU.mult), reads=[B_ob, B_sm], writes=[B_ob])
            P.op("dve", "tensor_tensor", A(out=ydt, in0=obuf, in1=Gs, op=ALU.mult), reads=[B_ob, B_Gs], writes=[B_ydt])
            for c in range(NTQ):
                transpose_to_YT(ydt[:, c, :], B_ydt, 1, h + 8 * (c // NT), 128 * (c % NT), 6 + c % 2)
        P.barrier()

    def ssm_phase(sq, part):
        t0 = part * T
        evac_mode[0] = _CFG.get("ssm_evac", "act")
        cv = Carver()
        Us = [cv.bf16(T + 4), cv.bf16(T + 4)]; B_Us = [Buf(), Buf()]
        dgs = [cv.bf16(4 * 128).rearrange("p (k c) -> p k c", k=4) for _ in range(2)]; B_dgs = [Buf(), Buf()]
        xsT = cv.bf16(T); B_xsT = Buf()
        xs_tok2 = [cv.bf16(NT * 256).rearrange("p (c f) -> p c f", c=NT) for _ in range(2)]; B_xs2 = [Buf(), Buf()]
        BT2 = [cv.bf16(T), cv.bf16(T)]; B_BT2 = [Buf(), Buf()]
        Btok2 = [cv.bf16(NT * 128).rearrange("p (c f) -> p c f", c=NT) for _ in range(2)]; B_Btok2 = [Buf(), Buf()]
        CT2 = [cv.bf16(T), cv.bf16(T)]; B_CT2 = [Buf(), Buf()]
        zs2 = [cv.bf16(NT * 256).rearrange("p (c f) -> p c f", c=NT) for _ in range(2)]; B_zs2 = [Buf(), Buf()]
        dskI2 = [cv.bf16(4 * 128).rearrange("p (h l) -> p h l", h=4) for _ in range(2)]; B_dskI2 = [Buf(), Buf()]
        sg_b2 = [cv.f32(256), cv.f32(256)]; B_sg2 = [Buf(), Buf()]
        dt = cv.f32(NT * 32).rearrange("p (c f) -> p c f", c=NT)
        adt = cv.f32(NT * 32).rearrange("p (c f) -> p c f", c=NT); B_dt = Buf()
        E = cv.f32(NT * 96).rearrange("p (c f) -> p c f", c=NT); B_E = Buf()
        R = [cv.bf16(512), cv.bf16(512)]; B_R = [Buf(), Buf()]
        LT = [cv.bf16(512), cv.bf16(512)]; B_LT = [Buf(), Buf()]
        MT = [cv.bf16(512), cv.bf16(512)]; B_MT = [Buf(), Buf()]
        xdt = [cv.bf16(256), cv.bf16(256)]; xdd = [cv.bf16(256), cv.bf16(256)]; B_xd = [Buf(), Buf()]; B_xdd = [Buf(), Buf()]
        y1 = cv.f32(256); B_y1 = Buf()
        y2 = cv.f32(NT * 256).rearrange("p (c f) -> p c f", c=NT); B_y2c = [Buf() for _ in range(NT)]; B_y2 = B_y2c
        yn = cv.bf16(NT * 256).rearrange("p (c f) -> p c f", c=NT); B_ync = [Buf() for _ in range(NT)]
        st_b = cv.bf16(256); B_stb = Buf()
        stmp = cv.f32(256); B_stmp = Buf()
        ss = cv.f32(NT); B_ss = Buf()
        junk = cv.bf16(256); B_junk = Buf()
        if sq == 0 and part == 0:
            print("ssm arena words", cv.off)

        wd, wdB = load_w(win(OFF_DT, 32), 8, 32)
        for c in range(NT):
            pi = c % 2
            proj_tm(pi, 0, wd, wdB, 0, 32, t0 + 128 * c)
            P.op("dve", "tensor_tensor", A(out=dt[:, c, :], in0=psum[pi][:, 0:32], in1=sm[:, SM_DTB:SM_DTB + 32], op=ALU.add), reads=[B_ps[pi], B_sm], writes=[B_dt])
        dtf = dt.rearrange("p c f -> p (c f)")
        P.op("act", "activation", A(out=dtf, in_=dtf, func=AF.Exp), reads=[B_dt], writes=[B_dt])
        P.op("act", "activation", A(out=dtf, in_=dtf, func=AF.Ln, bias=sm[:, SM_ONE:SM_ONE + 1]), reads=[B_dt, B_sm], writes=[B_dt])
        P.op("dve", "tensor_tensor", A(out=adt, in0=dt, in1=sm[:, SM_A:SM_A + 32].unsqueeze(1).broadcast_to([128, NT, 32]), op=ALU.mult), reads=[B_dt, B_sm], writes=[B_dt])
        for c in range(NT):
            pi = c % 2
            for i, cc in enumerate((C_TRIL, C_SUP, C_ONES)):
                P.op("pe", "matmul", A(psum[pi][:, 32 * i:32 * i + 32], lhsT=cst[:, cc:cc + 128], rhs=adt[:, c, :], start=True, stop=True),
                     reads=[B_cst, B_dt], writes=[B_ps[pi]])
            P.op("act", "activation", A(out=E[:, c, :], in_=psum[pi][:, 0:96], func=AF.Exp), reads=[B_ps[pi]], writes=[B_E])

        def prep(g):
            q = g % 2
            xs_tok, BT, Btok, CT, zs, dskI, sg_b = xs_tok2[q], BT2[q], Btok2[q], CT2[q], zs2[q], dskI2[q], sg_b2[q]
            B_xs, B_BT, B_Btok, B_CT, B_zs, B_dskI, B_sg = B_xs2[q], B_BT2[q], B_Btok2[q], B_CT2[q], B_zs2[q], B_dskI2[q], B_sg2[q]
            if part == 0:
                P.op("pool", "memset", A(state_f[:, g, :], 0.0), writes=[B_state[g]])
            for hh in range(4):
                P.op("pool", "tensor_scalar", A(out=dskI[:, hh, :], in0=cst[:, C_ID:C_ID + 128], scalar1=sm[:, SM_DSK + 4 * g + hh:SM_DSK + 4 * g + hh + 1], scalar2=None, op0=ALU.mult),
                     reads=[B_cst, B_sm], writes=[B_dskI])
            blocks = [("xs", 2 * g, OFF_XBC + 256 * g), ("xs", 2 * g + 1, OFF_XBC + 256 * g + 128),
                      ("B", 16 + g, OFF_XBC + 2048 + 128 * g), ("C", 24 + g, OFF_XBC + 3072 + 128 * g)]
            for bi_, (kind, blk, col) in enumerate(blocks):
                wt, wB = load_w(win(col, 128), 8, 128)
                U = Us[bi_ % 2]; B_U = B_Us[bi_ % 2]
                dg = dgs[bi_ % 2]; B_dg = B_dgs[bi_ % 2]
                for k in range(4):
                    P.op("dve", "tensor_scalar", A(out=dg[:, k, :], in0=cst[:, C_ID:C_ID + 128], scalar1=cw[:, blk, k:k + 1], scalar2=None, op0=ALU.mult),
                         reads=[B_cst, B_cw], writes=[B_dg])
                if part == 0:
                    P.op("pool", "memset", A(U[:, 0:4], 0.0), writes=[B_U])
                else:
                    P.op("pool", "tensor_copy", A(out=U[:, 0:4], in_=halo[:, blk, :]), reads=[B_halo], writes=[B_U])
                for tc in range(T // 512):
                    pi = 6 + tc % 2
                    proj_fm(pi, 0, wt, wB, 0, 128, t0 + 512 * tc, 512)
                    evac(U[:, 4 + 512 * tc:4 + 512 * tc + 512], psum[pi][:, :], [B_ps[pi]], [B_U])
                    yield
                P.op("pool", "tensor_copy", A(out=halo[:, blk, :], in_=U[:, T:T + 4]), reads=[B_U], writes=[B_halo])
                dst = {"C": (CT, B_CT), "B": (BT, B_BT), "xs": (xsT, B_xsT)}[kind]
                for tc in range(T // 512):
                    pi = 6 + tc % 2
                    for k in range(4):
                        P.op("pe", "matmul", A(psum[pi][:, :], lhsT=dg[:, k, :], rhs=U[:, 1 + k + 512 * tc:1 + k + 512 * tc + 512], start=(k == 0), stop=(k == 3)),
                             reads=[B_dg, B_U], writes=[B_ps[pi]])
                    P.op("act", "activation", A(out=dst[0][:, 512 * tc:512 * tc + 512], in_=psum[pi][:, :], func=AF.Silu, bias=cb[:, blk:blk + 1]),
                         reads=[B_ps[pi], B_cw], writes=[dst[1]])
                    yield
                if kind == "B":
                    for c in range(NT):
                        P.op("pe", "transpose", A(out=psb(6)[:, 128 * c:128 * c + 128], in_=BT[:, 128 * c:128 * c + 128], identity=identb[:]),
                             reads=[B_BT, B_cst], writes=[B_ps[6]])
                    evac(Btok, psb(6)[:, 0:128 * NT].rearrange("p (c f) -> p c f", c=NT), [B_ps[6]], [B_Btok])
                elif kind == "xs":
                    jx = bi_
                    for c in range(NT):
                        P.op("pe", "transpose", A(out=psb(7)[:, 128 * c:128 * c + 128], in_=xsT[:, 128 * c:128 * c + 128], identity=identb[:]),
                             reads=[B_xsT, B_cst], writes=[B_ps[7]])
                    evac(xs_tok[:, :, 128 * jx:128 * jx + 128], psb(7)[:, 0:128 * NT].rearrange("p (c f) -> p c f", c=NT), [B_ps[7]], [B_xs])
                yield
            wz, wzB = load_w(win(OFF_Z + 256 * g, 256), 8, 256)
            for c in range(NT):
                pi = 6 + c % 2
                proj_tm(pi, 0, wz, wzB, 0, 256, t0 + 128 * c)
                P.op("act", "activation", A(out=zs[:, c, :], in_=psum[pi][:, 0:256], func=AF.Silu), reads=[B_ps[pi]], writes=[B_zs])
                if c % 2:
                    yield

        def scan(g):
            q = g % 2
            xs_tok, BT, Btok, CT, zs, dskI, sg_b = xs_tok2[q], BT2[q], Btok2[q], CT2[q], zs2[q], dskI2[q], sg_b2[q]
            B_xs, B_BT, B_Btok, B_CT, B_zs, B_dskI, B_sg = B_xs2[q], B_BT2[q], B_Btok2[q], B_CT2[q], B_zs2[q], B_dskI2[q], B_sg2[q]
            P.op("act", "copy", A(out=st_b, in_=state_f[:, g, :]), reads=[B_state[g]], writes=[B_stb])
            CBb = (1, 5)

            def S1(c):
                i2 = c % 2
                tk = slice(128 * c, 128 * c + 128)
                ag = adt[:, c, 4 * g:4 * g + 4]
                P.op("dve", "tensor_tensor", A(out=R[i2].rearrange("p (h l) -> p h l", h=4), in0=cst[:, C_TRIL:C_TRIL + 128].unsqueeze(1).broadcast_to([128, 4, 128]),
                                               in1=ag.unsqueeze(2).broadcast_to([128, 4, 128]), op=ALU.mult),
                     reads=[B_cst, B_dt], writes=[B_R[i2]])
                P.op("pe", "matmul", A(psum[0][:, :], lhsT=cstb[:, 0:128], rhs=R[i2], start=True, stop=False), reads=[B_cst, B_R[i2]], writes=[B_ps[0]])
                P.op("pe", "matmul", A(psum[0][:, :], lhsT=cstb[:, 128:256], rhs=cstb[:, 256:768], start=False, stop=True), reads=[B_cst], writes=[B_ps[0]])
                P.op("act", "activation", A(out=LT[i2], in_=psum[0][:, :], func=AF.Exp), reads=[B_ps[0]], writes=[B_LT[i2]])
                P.op("pe", "matmul", A(psum[CBb[i2]][:, 0:128], lhsT=BT[:, tk], rhs=CT[:, tk], start=True, stop=True), reads=[B_BT, B_CT], writes=[B_ps[CBb[i2]]])

            def S2(c):
                i2 = c % 2
                P.op("dve", "tensor_tensor", A(out=xdt[i2].rearrange("p (h q) -> p h q", h=4), in0=xs_tok[:, c, :].rearrange("p (h q) -> p h q", h=4),
                                               in1=dt[:, c, 4 * g:4 * g + 4].unsqueeze(2).broadcast_to([128, 4, 64]), op=ALU.mult),
                     reads=[B_xs, B_dt], writes=[B_xd[i2]])
                P.op("dve", "tensor_tensor", A(out=xdd[i2].rearrange("p (h q) -> p h q", h=4), in0=xdt[i2].rearrange("p (h q) -> p h q", h=4),
                                               in1=E[:, c, 32 + 4 * g:32 + 4 * g + 4].unsqueeze(2).broadcast_to([128, 4, 64]), op=ALU.mult),
                     reads=[B_xd[i2], B_E], writes=[B_xdd[i2]])
                P.op("dve", "tensor_tensor", A(out=MT[i2].rearrange("p (h l) -> p h l", h=4), in0=LT[i2].rearrange("p (h l) -> p h l", h=4),
                                               in1=psum[CBb[i2]][:, 0:128].unsqueeze(1).broadcast_to([128, 4, 128]), op=ALU.mult),
                     reads=[B_LT[i2], B_ps[CBb[i2]]], writes=[B_MT[i2]])

            def S3(c):
                i2 = c % 2
                tk = slice(128 * c, 128 * c + 128)
                for hh in range(4):
                    P.op("pe", "matmul", A(psum[2][:, 64 * hh:64 * hh + 64], lhsT=MT[i2][:, 128 * hh:128 * hh + 128], rhs=xdt[i2][:, 64 * hh:64 * hh + 64], start=True, stop=False),
                         reads=[B_MT[i2], B_xd[i2]], writes=[B_ps[2]])
                    P.op("pe", "matmul", A(psum[2][:, 64 * hh:64 * hh + 64], lhsT=dskI[:, hh, :], rhs=xs_tok[:, c, 64 * hh:64 * hh + 64], start=False, stop=True),
                         reads=[B_dskI, B_xs], writes=[B_ps[2]])
                P.op("pe", "matmul", A(psum[4][:, 0:256], lhsT=Btok[:, c, :], rhs=xdd[i2], start=True, stop=True), reads=[B_Btok, B_xdd[i2]], writes=[B_ps[4]])
                P.op("pe", "matmul", A(psum[3][:, 0:256], lhsT=CT[:, tk], rhs=st_b, start=True, stop=True), reads=[B_CT, B_stb], writes=[B_ps[3]])
                P.op("dve", "tensor_tensor", A(out=y1.rearrange("p (h q) -> p h q", h=4), in0=psum[3][:, 0:256].rearrange("p (h q) -> p h q", h=4),
                                               in1=E[:, c, 4 * g:4 * g + 4].unsqueeze(2).broadcast_to([128, 4, 64]), op=ALU.mult),
                     reads=[B_ps[3], B_E], writes=[B_y1])
                P.op("dve", "tensor_tensor", A(out=stmp.rearrange("p (h q) -> p h q", h=4), in0=state_f[:, g, :].rearrange("p (h q) -> p h q", h=4),
                                               in1=E[:, c, 64 + 4 * g:64 + 4 * g + 4].unsqueeze(2).broadcast_to([128, 4, 64]), op=ALU.mult),
                     reads=[B_state[g], B_E], writes=[B_stmp])
                P.op("dve", "tensor_tensor", A(out=state_f[:, g, :], in0=stmp, in1=psum[4][:, 0:256], op=ALU.add), reads=[B_stmp, B_ps[4]], writes=[B_state[g]])
                P.op("act", "copy", A(out=st_b, in_=state_f[:, g, :]), reads=[B_state[g]], writes=[B_stb])
                P.op("dve", "tensor_tensor", A(out=y2[:, c, :], in0=y1, in1=psum[2][:, 0:256], op=ALU.add), reads=[B_y1, B_ps[2]], writes=[B_y2c[c]])

            S1(0)
            if NT > 1:
                S1(1)
            S2(0)
            for c in range(NT):
                if c + 2 < NT:
                    S1(c + 2)
                if c + 1 < NT:
                    S2(c + 1)
                S3(c)
                yield
            P.op("dve", "tensor_tensor", A(out=y2, in0=y2, in1=zs, op=ALU.mult), reads=B_y2 + [B_zs], writes=B_y2)
            for c in range(NT):
                P.op("act", "activation", A(out=junk, in_=y2[:, c, :], func=AF.Square, accum_out=ss[:, c:c + 1]), reads=B_y2, writes=[B_junk, B_ss])
            rstd_from_ss(ss[:, 0:NT], NT, 1.0 / 256, [B_ss])
            yield
            for c in range(NT):
                P.op("act", "activation", A(out=yn[:, c, :], in_=y2[:, c, :], func=AF.Copy, scale=ss[:, c:c + 1]), reads=[B_y2c[c], B_ss], writes=[B_ync[c]])
            yield
            for c in range(NT):
                transpose_to_YT(yn[:, c, :], B_ync[c], 2, 2 * g, 128 * c, c % 2)
                if c % 2:
                    yield

        def run_interleaved(gens):
            gens = [g_ for g_ in gens if g_ is not None]
            while gens:
                for g_ in list(gens):
                    try:
                        next(g_)
                    except StopIteration:
                        gens.remove(g_)

        run_interleaved([prep(0)])
        for g in range(8):
            run_interleaved([scan(g), prep(g + 1) if g + 1 < 8 else None])
        P.barrier()

    def merge_phase(sq, part, bi, wbr_d, nkb, first, yb0=0):
        t0 = part * T
        it = 0
        for ob in range(8):
            wb_, wbB = load_w(wbr_d.ap()[:, 128 * ob:128 * ob + 128], nkb, 128)
            wg, wgB = load_w(win(OFF_GATE + 1024 * bi + 128 * ob, 128), 8, 128)
            for tc in range(T // 512):
                pa, pg = 2 * (it % 2), 2 * (it % 2) + 1
                it += 1
                for kb in range(nkb):
                    P.op("pe", "matmul", A(psum[pa][:, :], lhsT=wb_[:, kb, :], rhs=YT[:, yb0 + kb, 512 * tc:512 * tc + 512], start=(kb == 0), stop=(kb == nkb - 1)),
                         reads=[wbB, B_YT], writes=[B_ps[pa]])
                proj_fm(pg, 0, wg, wgB, 0, 128, t0 + 512 * tc, 512)
                P.op("act", "activation", A(out=msig[:], in_=psum[pg][:, :], func=AF.Sigmoid), reads=[B_ps[pg]], writes=[B_msig])
                dst = mrg[:, ob, 512 * tc:512 * tc + 512]
                if first:
                    P.op("dve", "tensor_tensor", A(out=dst, in0=msig[:], in1=psum[pa][:, :], op=ALU.mult), reads=[B_msig, B_ps[pa]], writes=[B_mrg])
                else:
                    P.op("dve", "tensor_tensor", A(out=mtmp[:], in0=msig[:], in1=psum[pa][:, :], op=ALU.mult), reads=[B_msig, B_ps[pa]], writes=[B_mtmp])
                    P.op("dve", "tensor_tensor", A(out=dst, in0=dst, in1=mtmp[:], op=ALU.add), reads=[B_mtmp, B_mrg], writes=[B_mrg])

    def out_phase(sq, part):
        t0 = part * T
        cv = Carver(11000)
        fg_b = cv.f32(1024); B_fg = Buf()
        xt = [cv.f32(1024), cv.f32(1024)]; B_xt = [Buf(), Buf()]
        r = [cv.f32(1024), cv.f32(1024)]; B_r = [Buf(), Buf()]
        junk = cv.f32(1024); B_junk = Buf()
        ss = cv.f32(2); B_ss = Buf()
        bcast_load(fg_b, fgain_d.ap(), [B_fg])
        wts = [load_w(w_out.ap()[:, 256 * i:256 * i + 256], 8, 256) for i in range(4)]
        for c in range(NT):
            xi = c % 2
            rows = slice(t0 + 128 * c, t0 + 128 * c + 128)
            P.dma("sp", "dma_start", A(out=xt[xi], in_=x_d.ap()[sq, rows, :]), writes=[B_xt[xi]])
            for hf in range(2):
                pi = 2 * xi + hf
                for q4 in range(2):
                    wt, wB = wts[2 * hf + q4]
                    for kb in range(8):
                        P.op("pe", "matmul", A(psum[pi][:, 256 * q4:256 * q4 + 256], lhsT=mrg[:, kb, 128 * c:128 * c + 128], rhs=wt[:, kb, :], start=(kb == 0), stop=(kb == 7)),
                             reads=[wB, B_mrg], writes=[B_ps[pi]])
                P.op("dve", "tensor_tensor", A(out=r[xi][:, 512 * hf:512 * hf + 512], in0=xt[xi][:, 512 * hf:512 * hf + 512], in1=psum[pi][:, :], op=ALU.add),
                     reads=[B_xt[xi], B_ps[pi]], writes=[B_r[xi]])
            P.op("act", "activation", A(out=junk, in_=r[xi], func=AF.Square, accum_out=ss[:, 0:1]), reads=[B_r[xi]], writes=[B_junk, B_ss])
            rstd_from_ss(ss[:, 0:1], 1, 1.0 / D, [B_ss])
            P.op("dve", "scalar_tensor_tensor", A(out=r[xi], in0=r[xi], scalar=ss[:, 0:1], in1=fg_b, op0=ALU.mult, op1=ALU.mult), reads=[B_r[xi], B_ss, B_fg], writes=[B_r[xi]])
            P.dma("sp", "dma_start", A(out=out_d.ap()[sq, rows, :], in_=r[xi]), reads=[B_r[xi]])

    for sq in range(nseq):
        prologue(sq)
        for part in range(NPART):
            first = True
            if part == 1 and "diff" in branches:
                merge_phase(sq, part, 1, w_brd, 8, True, yb0=8)
                first = False
            if "ssm" in branches:
                ssm_phase(sq, part)
                merge_phase(sq, part, 0, w_brs, 16, first)
                first = False
            if part == 0 and "diff" in branches:
                diff_phase(sq, part)
                merge_phase(sq, part, 1, w_brd, 8, first, yb0=0)
                first = False
            if "mem" in branches:
                mem_phase(sq, part)
                merge_phase(sq, part, 2, w_brm, 8, first)
                first = False
            out_phase(sq, part)
            if part < NPART - 1:
                P.barrier()
    P.barrier()
    P.finish()
    P.emit()
    return nc, P


_CACHE = {}


def kernel(**inputs):
    ncores = _CFG["ncores"]; nseq = _CFG["nseq"]
    key = (nseq, tuple(_CFG["branches"]), _CFG["same_engine_sync"], _CFG["schedule"], _CFG.get("ssm_evac"))
    if key not in _CACHE:
        _CACHE[key] = build(nseq, _CFG["branches"], _CFG["same_engine_sync"], _CFG["schedule"])
    nc, P = _CACHE[key]
    f = lambda a: np.ascontiguousarray(np.asarray(a, dtype=np.float32))
    cst, oh = host_consts()
    shared = {
        "w_in": f(inputs["w_in"][0]), "w_mem_kv": f(inputs["w_mem_kv"][0]), "w_br_ssm": f(inputs["w_br_ssm"][0]),
        "w_br_diff": f(inputs["w_br_diff"][0]), "w_br_mem": f(inputs["w_br_mem"][0]), "w_out": f(inputs["w_out"][0]),
        "norm_gain": f(inputs["norm_gain"]).reshape(1, D), "mem_norm_gain": f(inputs["mem_norm_gain"]).reshape(1, D),
        "final_norm_gain": f(inputs["final_norm_gain"]).reshape(1, D), "ssm_norm_gain": f(inputs["ssm_norm_gain"]).reshape(1, 2048),
        "subln_gain": f(inputs["subln_gain"]).reshape(1, 128), "dt_bias": f(inputs["dt_bias"]).reshape(1, 32),
        "a_log": f(inputs["a_log"]).reshape(1, 32), "d_skip": f(inputs["d_skip"]).reshape(1, 32),
        "lambda_q1": f(inputs["lambda_q1"]).reshape(1, 64), "lambda_k1": f(inputs["lambda_k1"]).reshape(1, 64),
        "lambda_q2": f(inputs["lambda_q2"]).reshape(1, 64), "lambda_k2": f(inputs["lambda_k2"]).reshape(1, 64),
        "rel_bias": f(inputs["rel_bias"]),
        "conv_w_l": f(np.asarray(inputs["conv_w"][0]).reshape(4, 32, 128).transpose(2, 1, 0)),
        "conv_b_l": f(np.asarray(inputs["conv_b"][0]).reshape(32, 128).transpose(1, 0)),
        "sgain_l": f(np.asarray(inputs["ssm_norm_gain"][0]).reshape(16, 128).transpose(1, 0)),
        "cst": cst, "onehot": oh,
    }
    x = np.asarray(inputs["x"]); mem = np.asarray(inputs["mem"])
    in_maps = []
    for c in range(ncores):
        m = dict(shared)
        m["x"] = f(x[c * nseq:(c + 1) * nseq])
        m["mem"] = f(mem[c * nseq:(c + 1) * nseq])
        in_maps.append(m)
    if _CFG.get("trace"):
        res = run_bass_kernel_spmd(nc, in_maps, core_ids=list(range(ncores)), trace=True)
        print("EXEC_TIME_NS", res.exec_time_ns)
    else:
        res = run_bass_kernel_spmd(nc, in_maps, core_ids=list(range(ncores)))
    return np.concatenate([np.asarray(r["out"]) for r in res.results], axis=0).astype(np.float32)
```

```python
import math
import numpy as np
import concourse.bass as bass
import concourse.mybir as mybir
from concourse.bass_utils import run_bass_kernel_spmd

F32 = mybir.dt.float32
BF16 = mybir.dt.bfloat16
ALU = mybir.AluOpType
AF = mybir.ActivationFunctionType

D = 1024
S = 2048
MEM = 256
IN_DIM = 15392
OFF_Z, OFF_XBC, OFF_DT, OFF_DQ, OFF_DK, OFF_DV, OFF_DG, OFF_MQ, OFF_MG, OFF_GATE = (
    0, 2048, 6144, 6176, 7200, 8224, 9248, 10272, 11296, 12320)
EPS = 1e-5
NEG = -30000.0
T = 1024
NT = T // 128
NPART = S // T

_CFG = {"branches": ("ssm", "diff", "mem"), "ncores": 8, "nseq": 4, "same_engine_sync": "raw", "schedule": True}

COMPUTE = ("pe", "act", "dve", "pool")
QUEUES = ("pe", "act", "dve", "pool", "sp")
DMAQ = ("sp", "pool", "act")
NDMASEM = 12


def A(*a, **k):
    return (a, k)


class Buf:
    __slots__ = ("name", "last_w", "readers")

    def __init__(self, name=""):
        self.name = name
        self.last_w = None
        self.readers = []


def _free(ap):
    n = 1
    for d in ap.shape[1:]:
        n *= int(d)
    return n


class Prog:
    def __init__(self, nc, same_engine_sync="raw", schedule=True):
        self.nc = nc
        self.ins = []
        self.segs = [[]]
        self.same_engine_sync = same_engine_sync
        self.schedule = schedule
        self.sem = {e: nc.alloc_semaphore("s_" + e) for e in COMPUTE}
        self.dsem = {e: [nc.alloc_semaphore(f"d_{e}{i}") for i in range(NDMASEM)] for e in DMAQ}

    def _deps(self, reads, writes):
        deps = set()
        raw = set()
        for b in reads:
            if b.last_w is not None:
                deps.add(b.last_w)
                raw.add(b.last_w)
        for b in writes:
            if b.last_w is not None:
                deps.add(b.last_w)
            deps.update(b.readers)
        return deps, raw

    def _commit(self, me, reads, writes):
        for b in reads:
            b.readers.append(me)
        for b in writes:
            b.last_w = me
            b.readers = []

    def _est(self, eng, meth, args, is_dma):
        a, k = args
        try:
            out = k.get("out", a[0] if a else None)
            n = _free(out)
            if is_dma:
                return 60.0, 2000.0 + n * (4 if out.dtype == F32 else 2) * 128 / 150.0
            if eng == "pe":
                if meth == "transpose":
                    return 110.0, 0.0
                f32 = k["lhsT"].dtype == F32
                return max(64.0, n * 0.42) * (4 if f32 else 1), 0.0
            if eng == "act":
                return 224.0 + 0.75 * n, 0.0
            if eng == "dve":
                return 60.0 + 1.05 * n, 0.0
            return 100.0 + 2.3 * n, 0.0
        except Exception:
            return 300.0, 0.0

    def _add(self, eng, meth, args, reads, writes, is_dma):
        deps, raw = self._deps(reads, writes)
        gid = len(self.ins)
        dur, lat = self._est(eng, meth, args, is_dma)
        tab = None
        if eng == "act" and meth == "activation":
            f = args[1].get("func")
            tab = "explog" if f in (AF.Exp, AF.Ln) else str(f)
        self.ins.append(dict(eng=eng, fn=(meth, args), deps=deps, raw=raw, dma=is_dma, dur=dur, lat=lat, tab=tab))
        self.segs[-1].append(gid)
        self._commit(gid, reads, writes)
        return gid

    def op(self, eng, meth, args, reads=(), writes=()):
        return self._add(eng, meth, args, reads, writes, False)

    def dma(self, eng, meth, args, reads=(), writes=()):
        return self._add(eng, meth, args, reads, writes, True)

    def barrier(self):
        self.segs.append([])

    def finish(self):
        self.segs.append([])

    def _sched(self, seg):
        ins = self.ins
        pend = {e: [] for e in QUEUES}
        for g in seg:
            pend[ins[g]["eng"]].append(g)
        if not self.schedule:
            return pend
        segset = set(seg)
        W = _CFG.get('W', 32)
        fin = {}
        nun = {}
        users = {}
        for g in seg:
            c = 0
            for d in ins[g]["deps"]:
                if d in segset:
                    c += 1
                    users.setdefault(d, []).append(g)
            nun[g] = c
        blev = {}
        for g in reversed(seg):
            m = 0.0
            for u in users.get(g, ()):
                if blev[u] > m:
                    m = blev[u]
            blev[g] = ins[g]["dur"] + ins[g]["lat"] + m
        free = {e: 0.0 for e in QUEUES}
        curtab = [None]
        order = {e: [] for e in QUEUES}
        cand = {e: None for e in QUEUES}
        dirty = set(QUEUES)
        remaining = len(seg)
        while remaining:
            for e in list(dirty):
                best = None
                lst = pend[e]
                for g in lst[:W]:
                    if nun[g]:
                        continue
                    I = ins[g]
                    r = 0.0
                    for d in I["deps"]:
                        if d in fin:
                            t = fin[d] + (_CFG.get("xlat", 150.0) if ins[d]["eng"] != e else _CFG.get("slat", 0.0))
                            if t > r:
                                r = t
                    st = r if r > free[e] else free[e]
                    if e == "act" and I["tab"] is not None and I["tab"] != curtab[0]:
                        st += 1300.0
                    if best is None or st < best[0] or (st == best[0] and blev[g] > blev[best[1]]):
                        best = (st, g)
                cand[e] = best
            dirty.clear()
            be = None
            for e in QUEUES:
                c = cand[e]
                if c is not None and (be is None or c[0] < cand[be][0]):
                    be = e
            if be is None:
                raise RuntimeError("scheduler deadlock")
            st, g = cand[be]
            I = ins[g]
            if be == "act" and I["tab"] is not None:
                curtab[0] = I["tab"]
            free[be] = st + I["dur"]
            fin[g] = st + I["dur"] + I["lat"]
            pend[be].remove(g)
            order[be].append(g)
            remaining -= 1
            dirty.add(be)
            for u in users.get(g, ()):
                nun[u] -= 1
                if nun[u] == 0:
                    dirty.add(ins[u]["eng"])
        return order

    def emit(self):
        nc = self.nc
        ins = self.ins
        stream = {e: [] for e in QUEUES}
        pos = {}
        dma_list = {e: [] for e in DMAQ}

        def tails():
            t = set()
            for e in QUEUES:
                for x in reversed(stream[e]):
                    if not isinstance(x, tuple) and not ins[x]["dma"]:
                        t.add(x)
                        break
            for e in DMAQ:
                t.update(dma_list[e][-NDMASEM:])
            return t

        nseg = len(self.segs)
        for si, seg in enumerate(self.segs):
            if seg:
                order = self._sched(seg)
                for e in QUEUES:
                    for g in order[e]:
                        I = ins[g]
                        if I["dma"]:
                            dl = dma_list[e]
                            I["k"] = len(dl)
                            if len(dl) >= NDMASEM:
                                I["deps"] = set(I["deps"]) | {dl[-NDMASEM]}
                            dl.append(g)
                        pos[g] = len(stream[e])
                        stream[e].append(g)
            if si < nseg - 1:
                t = tails()
                last = (si == nseg - 2)
                for e in (("sp",) if last else QUEUES):
                    stream[e].append(("wait", t))
        signal = set()
        plan = {e: [] for e in QUEUES}
        for e in QUEUES:
            seen = {x: -1 for x in COMPUTE}
            seen_slot = {}
            for ent in stream[e]:
                if isinstance(ent, tuple):
                    deps, raw, g = ent[1], ent[1], None
                else:
                    g = ent
                    deps, raw = ins[g]["deps"], ins[g]["raw"]
                best = {}
                for d in deps:
                    D = ins[d]
                    if D["dma"]:
                        slot = (D["eng"], D["k"] % NDMASEM)
                        if seen_slot.get(slot, -1) >= D["k"] // NDMASEM:
                            continue
                        if slot not in best or ins[best[slot]]["k"] < D["k"]:
                            best[slot] = d
                    else:
                        x = D["eng"]
                        if x == e and (x == "pe" or not self.same_engine_sync):
                            continue
                        if x == e and self.same_engine_sync == "raw" and d not in raw:
                            continue
                        if seen[x] >= pos[d]:
                            continue
                        if x not in best or pos[best[x]] < pos[d]:
                            best[x] = d
                final = list(best.values())
                for d in final:
                    D = ins[d]
                    if D["dma"]:
                        seen_slot[(D["eng"], D["k"] % NDMASEM)] = D["k"] // NDMASEM
                    else:
                        seen[D["eng"]] = pos[d]
                        signal.add(d)
                plan[e].append((final, g))
        count = {}
        for e in COMPUTE:
            c = 0
            for ent in stream[e]:
                if not isinstance(ent, tuple) and ent in signal:
                    c += 1
                    count[ent] = c
        self.stats = {e: len(stream[e]) for e in QUEUES}

        def replay(e, eng):
            for final, g in plan[e]:
                for d in final:
                    D = ins[d]
                    if D["dma"]:
                        eng.wait_ge(self.dsem[D["eng"]][D["k"] % NDMASEM], 16 * (D["k"] // NDMASEM + 1))
                    else:
                        eng.wait_ge(self.sem[D["eng"]], count[d])
                if g is None:
                    continue
                I = ins[g]
                fn = I["fn"]
                o = getattr(eng, fn[0])(*fn[1][0], **fn[1][1])
                if I["dma"]:
                    o.then_inc(self.dsem[e][I["k"] % NDMASEM], 16)
                elif g in signal:
                    o.then_inc(self.sem[e], 1)

        with nc.Block() as block:
            @block.tensor
            def _(eng):
                replay("pe", eng)

            @block.scalar
            def _(eng):
                replay("act", eng)

            @block.vector
            def _(eng):
                replay("dve", eng)

            @block.gpsimd
            def _(eng):
                replay("pool", eng)

            @block.sync
            def _(eng):
                replay("sp", eng)


def t5_bucket_np(rel):
    n = np.maximum(rel, 0)
    max_exact = 16
    nf = np.maximum(n, 1).astype(np.float32)
    large = max_exact + (np.log(nf / max_exact) / np.float32(math.log(128 / max_exact))
                         * (32 - max_exact)).astype(np.int32)
    large = np.minimum(large, 31)
    return np.where(n < max_exact, n, large)


def host_consts():
    k = np.arange(128)[:, None]
    l = np.arange(128)[None, :]
    ident = (k == l).astype(np.float32)
    tril1 = (k <= l).astype(np.float32)
    sup = (k > l).astype(np.float32)
    ones = np.ones((128, 128), np.float32)
    negi = ident * NEG
    slow4 = np.tile(sup, (1, 4))
    cst = np.concatenate([ident, tril1, sup, ones, negi, slow4], axis=1)
    oh = np.zeros((33, 2, 128, 128), np.float32)
    for dlt in range(2):
        rel = 128 * dlt + (l - k)
        b = t5_bucket_np(rel)
        for kk in range(128):
            for qq in range(128):
                if rel[kk, qq] >= 0:
                    oh[b[kk, qq], dlt, kk, qq] = 1.0
                else:
                    oh[32, dlt, kk, qq] = 1.0
    return cst, oh.reshape(33, 2 * 128 * 128)


C_ID, C_TRIL, C_SUP, C_ONES, C_NEGI, C_SLOW4 = 0, 128, 256, 384, 512, 640


def build(nseq, branches, same_engine_sync="raw", schedule=True):
    nc = bass.Bass("TRN2", target_bir_lowering=False)
    P = Prog(nc, same_engine_sync, schedule)

    def din(name, shape, dt=F32):
        return nc.dram_tensor(name, list(shape), dt, kind="ExternalInput")

    x_d = din("x", [nseq, S, D])
    mem_d = din("mem", [nseq, MEM, D])
    w_in = din("w_in", [D, IN_DIM])
    w_kv = din("w_mem_kv", [D, 2048])
    w_brs = din("w_br_ssm", [2048, D])
    w_brd = din("w_br_diff", [D, D])
    w_brm = din("w_br_mem", [D, D])
    w_out = din("w_out", [D, D])
    gain_d = din("norm_gain", [1, D])
    mgain_d = din("mem_norm_gain", [1, D])
    fgain_d = din("final_norm_gain", [1, D])
    sgain_d = din("ssm_norm_gain", [1, 2048])
    subg_d = din("subln_gain", [1, 128])
    dtb_d = din("dt_bias", [1, 32])
    alog_d = din("a_log", [1, 32])
    dsk_d = din("d_skip", [1, 32])
    lq1_d, lk1_d, lq2_d, lk2_d = (din(n, [1, 64]) for n in ("lambda_q1", "lambda_k1", "lambda_q2", "lambda_k2"))
    rb_d = din("rel_bias", [32, 8])
    cw_d = din("conv_w_l", [128, 32, 4])
    cb_d = din("conv_b_l", [128, 32])
    sgl_d = din("sgain_l", [128, 16])
    cst_d = din("cst", [128, 1152])
    oh_d = din("onehot", [33, 32768])
    out_d = nc.dram_tensor("out", [nseq, S, D], F32, kind="ExternalOutput")
    bias_scr = nc.dram_tensor("bias_scr", [8, 32768], F32, kind="Internal")
    w_in_f, w_kv_f, w_brs_f, w_brd_f, w_brm_f, w_out_f = w_in, w_kv, w_brs, w_brd, w_brm, w_out
    w_in = nc.dram_tensor("w_in_b", [D, IN_DIM], BF16, kind="Internal")
    w_kv = nc.dram_tensor("w_kv_b", [D, 2048], BF16, kind="Internal")
    w_brs = nc.dram_tensor("w_brs_b", [2048, D], BF16, kind="Internal")
    w_brd = nc.dram_tensor("w_brd_b", [D, D], BF16, kind="Internal")
    w_brm = nc.dram_tensor("w_brm_b", [D, D], BF16, kind="Internal")
    w_out = nc.dram_tensor("w_out_b", [D, D], BF16, kind="Internal")

    def sb(name, shape, dt=F32):
        return nc.alloc_sbuf_tensor("sb_" + name, list(shape), dt)

    cst = sb("cst", [128, 1152]); B_cst = Buf("cst")
    identb = sb("identb", [128, 128], BF16)
    trilb = sb("trilb", [128, 128], BF16)
    cstb = sb("cstb", [128, 768], BF16)
    sm = sb("small", [128, 512]); B_sm = Buf("small")
    SM_CFAR, SM_DTB, SM_A, SM_DSK, SM_LAM, SM_NLAM, SM_ONE, SM_EPS = 0, 8, 40, 72, 104, 105, 106, 107
    SM_SUBG = 128
    SM_TMP = 256
    cw = sb("cw", [128, 32, 4]); cb = sb("cb", [128, 32]); B_cw = Buf("cw")
    sgl = sb("sgl", [128, 16])
    hT = sb("hT", [128, 8, S], BF16)
    B_hT = [Buf(f"hT{i}") for i in range(S // 128)]
    kmT = sb("kmT", [128, 8, MEM], BF16); B_kmT = Buf("kmT")
    vma = sb("vma", [128, 2, 4, 258], BF16); B_vma = Buf("vma")
    YT = sb("YT", [128, 16, T], BF16); B_YT = Buf("YT")
    mrg = sb("mrg", [128, 8, T], BF16); B_mrg = Buf("mrg")
    state_f = sb("state_f", [128, 8, 256]); B_state = [Buf(f"st{g}") for g in range(8)]
    halo = sb("halo", [128, 32, 4]); B_halo = Buf("halo")
    NWB = 6
    wbf = [sb(f"wbf{i}", [128, 2048], BF16) for i in range(NWB)]; B_wbf = [Buf(f"wbf{i}") for i in range(NWB)]
    ARENA_W = 18976
    arena = sb("arena", [128, ARENA_W])
    arena_b = arena[:].bitcast(BF16)

    class Carver:
        def __init__(self, base=0):
            self.off = base

        def f32(self, n):
            a = arena[:, self.off:self.off + n]
            self.off += n
            assert self.off <= ARENA_W, self.off
            return a

        def bf16(self, n):
            w = (n + 1) // 2
            a = arena_b[:, 2 * self.off:2 * self.off + n]
            self.off += w
            assert self.off <= ARENA_W, self.off
            return a

    msig = sb("msig", [128, 512]); B_msig = Buf("msig")
    mtmp = sb("mtmp", [128, 512]); B_mtmp = Buf("mtmp")
    psum = [nc.alloc_psum_tensor(f"ps{i}", [128, 512], F32) for i in range(8)]
    B_ps = [Buf(f"ps{i}") for i in range(8)]

    def psb(i):
        return psum[i][:].bitcast(BF16)

    wctr = [0, 0]

    def load_w(src_ap, nkb, ncols):
        assert nkb * ncols <= 2048
        bi = wctr[1] % NWB; wctr[1] += 1
        bf = wbf[bi][:, 0:nkb * ncols].rearrange("p (k c) -> p k c", k=nkb)
        src = src_ap.rearrange("(k p) c -> p k c", p=128)
        P.dma("sp", "dma_start", A(out=bf, in_=src), writes=[B_wbf[bi]])
        return bf, B_wbf[bi]

    def win(c0, ncols):
        return w_in.ap()[:, c0:c0 + ncols]

    def hbufs(t0, n):
        return B_hT[t0 // 128:(t0 + n + 127) // 128]

    def proj_fm(ps_i, ps_cols, wt, wB, c_lo, ncol, tok0, ntok):
        out = psum[ps_i][0:ncol, ps_cols:ps_cols + ntok]
        for kb in range(8):
            P.op("pe", "matmul", A(out, lhsT=wt[:, kb, c_lo:c_lo + ncol], rhs=hT[:, kb, tok0:tok0 + ntok],
                                                 start=(kb == 0), stop=(kb == 7)),
                 reads=[wB] + hbufs(tok0, ntok), writes=[B_ps[ps_i]])

    def proj_tm(ps_i, ps_cols, wt, wB, c_lo, ncol, tok0):
        out = psum[ps_i][:, ps_cols:ps_cols + ncol]
        for kb in range(8):
            P.op("pe", "matmul", A(out, lhsT=hT[:, kb, tok0:tok0 + 128], rhs=wt[:, kb, c_lo:c_lo + ncol],
                                                 start=(kb == 0), stop=(kb == 7)),
                 reads=[wB] + hbufs(tok0, 128), writes=[B_ps[ps_i]])

    evac_rr = [0]
    evac_mode = ["alt"]

    def evac(out, in_, reads, writes):
        evac_rr[0] += 1
        if evac_mode[0] == "act" or (evac_mode[0] == "alt" and evac_rr[0] % 2):
            P.op("act", "copy", A(out=out, in_=in_), reads=reads, writes=writes)
        else:
            P.op("dve", "tensor_copy", A(out=out, in_=in_), reads=reads, writes=writes)

    def bcast_load(dst, src_row_ap, writes):
        P.dma("sp", "dma_start", A(out=dst, in_=src_row_ap.partition_broadcast(128)), writes=writes)

    def rstd_from_ss(ss_ap, n, inv_count, Bs):
        P.op("dve", "tensor_scalar", A(out=ss_ap, in0=ss_ap, scalar1=inv_count, scalar2=EPS, op0=ALU.mult, op1=ALU.add),
             reads=Bs, writes=Bs)
        P.op("act", "activation", A(out=ss_ap, in_=ss_ap, func=AF.Sqrt), reads=Bs, writes=Bs)
        P.op("dve", "reciprocal", A(out=ss_ap, in_=ss_ap), reads=Bs, writes=Bs)

    P.dma("sp", "dma_start", A(out=cst[:], in_=cst_d.ap()), writes=[B_cst])
    P.op("dve", "tensor_copy", A(out=identb[:], in_=cst[:, C_ID:C_ID + 128]), reads=[B_cst], writes=[B_cst])
    P.op("dve", "tensor_copy", A(out=trilb[:], in_=cst[:, C_TRIL:C_TRIL + 128]), reads=[B_cst], writes=[B_cst])
    P.op("dve", "tensor_copy", A(out=cstb[:, 0:128], in_=cst[:, C_SUP:C_SUP + 128]), reads=[B_cst], writes=[B_cst])
    P.op("dve", "tensor_copy", A(out=cstb[:, 128:768], in_=cst[:, C_NEGI:C_NEGI + 640]), reads=[B_cst], writes=[B_cst])
    P.dma("sp", "dma_start", A(out=cw[:], in_=cw_d.ap()), writes=[B_cw])
    P.dma("sp", "dma_start", A(out=cb[:], in_=cb_d.ap()), writes=[B_cw])
    P.dma("sp", "dma_start", A(out=sgl[:], in_=sgl_d.ap()), writes=[B_cw])
    P.op("pool", "memset", A(sm[:], 0.0), writes=[B_sm])
    P.op("pool", "memset", A(sm[:, SM_ONE:SM_ONE + 1], 1.0), writes=[B_sm])
    P.op("pool", "memset", A(sm[:, SM_EPS:SM_EPS + 1], EPS), writes=[B_sm])
    bcast_load(sm[:, SM_CFAR:SM_CFAR + 8], rb_d.ap()[31:32, :], [B_sm])
    bcast_load(sm[:, SM_DTB:SM_DTB + 32], dtb_d.ap(), [B_sm])
    bcast_load(sm[:, SM_A:SM_A + 32], alog_d.ap(), [B_sm])
    bcast_load(sm[:, SM_DSK:SM_DSK + 32], dsk_d.ap(), [B_sm])
    bcast_load(sm[:, SM_SUBG:SM_SUBG + 128], subg_d.ap(), [B_sm])
    for i, ld in enumerate((lq1_d, lk1_d, lq2_d, lk2_d)):
        bcast_load(sm[:, SM_TMP + 64 * i:SM_TMP + 64 * i + 64], ld.ap(), [B_sm])
    Bs = [B_sm]
    P.op("act", "activation", A(out=sm[:, SM_A:SM_A + 32], in_=sm[:, SM_A:SM_A + 32], func=AF.Exp), reads=Bs, writes=Bs)
    P.op("dve", "tensor_scalar_mul", A(out=sm[:, SM_A:SM_A + 32], in0=sm[:, SM_A:SM_A + 32], scalar1=-1.0), reads=Bs, writes=Bs)
    LAM_INIT = 0.8 - 0.6 * math.exp(-0.3 * 0)
    P.op("dve", "tensor_scalar_mul", A(out=sm[:, SM_SUBG:SM_SUBG + 128], in0=sm[:, SM_SUBG:SM_SUBG + 128], scalar1=1.0 - LAM_INIT), reads=Bs, writes=Bs)
    for i in range(2):
        a0 = SM_TMP + 128 * i
        P.op("dve", "tensor_tensor", A(out=sm[:, a0:a0 + 64], in0=sm[:, a0:a0 + 64], in1=sm[:, a0 + 64:a0 + 128], op=ALU.mult), reads=Bs, writes=Bs)
        P.op("dve", "reduce_sum", A(out=sm[:, 110 + i:111 + i], in_=sm[:, a0:a0 + 64], axis=mybir.AxisListType.X), reads=Bs, writes=Bs)
    P.op("act", "activation", A(out=sm[:, 110:112], in_=sm[:, 110:112], func=AF.Exp), reads=Bs, writes=Bs)
    P.op("dve", "tensor_tensor", A(out=sm[:, SM_LAM:SM_LAM + 1], in0=sm[:, 110:111], in1=sm[:, 111:112], op=ALU.subtract), reads=Bs, writes=Bs)
    P.op("dve", "tensor_scalar_add", A(out=sm[:, SM_LAM:SM_LAM + 1], in0=sm[:, SM_LAM:SM_LAM + 1], scalar1=LAM_INIT), reads=Bs, writes=Bs)
    P.op("dve", "tensor_scalar_mul", A(out=sm[:, SM_NLAM:SM_NLAM + 1], in0=sm[:, SM_LAM:SM_LAM + 1], scalar1=-1.0), reads=Bs, writes=Bs)

    cv = Carver()
    NPS = 4
    pst = [cv.f32(2048) for _ in range(NPS)]; B_pst = [Buf() for _ in range(NPS)]
    pbf = [cv.bf16(2048) for _ in range(NPS)]; B_pbf = [Buf() for _ in range(NPS)]
    pc_i = 0
    for (srcw, dstw, rows, cols) in ((w_in_f, w_in, D, IN_DIM), (w_kv_f, w_kv, D, 2048), (w_brs_f, w_brs, 2048, D),
                                     (w_brd_f, w_brd, D, D), (w_brm_f, w_brm, D, D), (w_out_f, w_out, D, D)):
        for rb in range(rows // 128):
            for c0 in range(0, cols, 2048):
                n = min(2048, cols - c0)
                si = pc_i % NPS
                P.dma("sp", "dma_start", A(out=pst[si][:, 0:n], in_=srcw.ap()[128 * rb:128 * rb + 128, c0:c0 + n]), writes=[B_pst[si]])
                ce = ("act", "dve")[pc_i % 2]
                if srcw is w_brs_f:
                    P.op("dve", "tensor_scalar", A(out=pbf[si][:, 0:n], in0=pst[si][:, 0:n], scalar1=sgl[:, rb:rb + 1], scalar2=None, op0=ALU.mult),
                         reads=[B_pst[si], B_cw], writes=[B_pbf[si]])
                else:
                    P.op(ce, "copy" if ce == "act" else "tensor_copy", A(out=pbf[si][:, 0:n], in_=pst[si][:, 0:n]), reads=[B_pst[si]], writes=[B_pbf[si]])
                P.dma("sp", "dma_start", A(out=dstw.ap()[128 * rb:128 * rb + 128, c0:c0 + n], in_=pbf[si][:, 0:n]), reads=[B_pbf[si]])
                pc_i += 1
    P.barrier()

    if "diff" in branches:
        cv = Carver()
        rbx = cv.f32(8)
        ohs = cv.f32(4096)
        stg = cv.f32(4096)
        B_rbx, B_ohs, B_stg, B_scr = Buf(), Buf(), Buf(), Buf()
        P.op("pool", "memset", A(rbx[32:33, :], NEG), writes=[B_rbx])
        P.dma("sp", "dma_start", A(out=rbx[0:32, :], in_=rb_d.ap()), writes=[B_rbx])
        for pc in range(8):
            P.dma("sp", "dma_start", A(out=ohs[0:33, :], in_=oh_d.ap()[:, 4096 * pc:4096 * pc + 4096]), writes=[B_ohs])
            for i in range(8):
                pi = i % 2
                P.op("pe", "matmul", A(psum[pi][0:8, :], lhsT=rbx[0:33, :], rhs=ohs[0:33, 512 * i:512 * i + 512], start=True, stop=True),
                     reads=[B_rbx, B_ohs], writes=[B_ps[pi]])
                evac(stg[0:8, 512 * i:512 * i + 512], psum[pi][0:8, :], [B_ps[pi]], [B_stg])
            P.dma("sp", "dma_start", A(out=bias_scr.ap()[:, 4096 * pc:4096 * pc + 4096], in_=stg[0:8, :]), reads=[B_stg], writes=[B_scr])
        P.barrier()

    def prologue(sq):
        cv = Carver(5200)
        gain_b = cv.f32(1024); B_gain = Buf()
        xt = [cv.f32(1024), cv.f32(1024)]; B_xt = [Buf(), Buf()]
        junk = cv.f32(1024); B_junk = Buf()
        hb = cv.bf16(1024); B_hb = Buf()
        ss = cv.f32(2); B_ss = Buf()
        memT = cv.bf16(8 * MEM); B_memT = Buf()
        memT3 = memT.rearrange("p (k m) -> p k m", k=8)

        def norm_transpose(src_ap, i, dst3, dstB, col0):
            xi = i % 2
            P.dma("sp", "dma_start", A(out=xt[xi], in_=src_ap), writes=[B_xt[xi]])
            P.op("act", "activation", A(out=junk, in_=xt[xi], func=AF.Square, accum_out=ss[:, 0:1]), reads=[B_xt[xi]], writes=[B_junk, B_ss])
            rstd_from_ss(ss[:, 0:1], 1, 1.0 / D, [B_ss])
            P.op("dve", "scalar_tensor_tensor", A(out=hb, in0=xt[xi], scalar=ss[:, 0:1], in1=gain_b, op0=ALU.mult, op1=ALU.mult),
                 reads=[B_xt[xi], B_ss, B_gain], writes=[B_hb])
            for kb in range(8):
                P.op("pe", "transpose", A(out=psb(6)[:, 128 * kb:128 * kb + 128], in_=hb[:, 128 * kb:128 * kb + 128], identity=identb[:]),
                     reads=[B_hb, B_cst], writes=[B_ps[6]])
            evac(dst3[:, :, col0:col0 + 128], psb(6)[:, 0:1024].rearrange("p (k t) -> p k t", k=8), [B_ps[6]], dstB)

        bcast_load(gain_b, gain_d.ap(), [B_gain])
        for i in range(S // 128):
            norm_transpose(x_d.ap()[sq, 128 * i:128 * i + 128, :], i, hT, [B_hT[i]], 128 * i)
        if "mem" in branches:
            bcast_load(gain_b, mgain_d.ap(), [B_gain])
            for i in range(2):
                norm_transpose(mem_d.ap()[sq, 128 * i:128 * i + 128, :], i, memT3, [B_memT], 128 * i)
            for fb in range(8):
                wt, wB = load_w(w_kv.ap()[:, 128 * fb:128 * fb + 128], 8, 128)
                pi = fb % 2
                for kb in range(8):
                    P.op("pe", "matmul", A(psum[pi][:, 0:MEM], lhsT=wt[:, kb, :], rhs=memT3[:, kb, :], start=(kb == 0), stop=(kb == 7)),
                         reads=[wB, B_memT], writes=[B_ps[pi]])
                evac(kmT[:, fb, :], psum[pi][:, 0:MEM], [B_ps[pi]], [B_kmT])
            P.op("pool", "memset", A(vma[:, :, :, 256:258], 1.0), writes=[B_vma])
            for hh in range(4):
                wt, wB = load_w(w_kv.ap()[:, 1024 + 256 * hh:1024 + 256 * hh + 256], 8, 256)
                for mt in range(2):
                    pi = mt
                    for kb in range(8):
                        P.op("pe", "matmul", A(psum[pi][:, 0:256], lhsT=memT3[:, kb, 128 * mt:128 * mt + 128], rhs=wt[:, kb, :], start=(kb == 0), stop=(kb == 7)),
                             reads=[wB, B_memT], writes=[B_ps[pi]])
                    evac(vma[:, mt, hh, 0:256], psum[pi][:, 0:256], [B_ps[pi]], [B_vma])
        P.barrier()

    def transpose_to_YT(src_bf, srcB, nblk, yt_blk0, tokcol0, ps_i):
        for j in range(nblk):
            P.op("pe", "transpose", A(out=psb(ps_i)[:, 128 * j:128 * j + 128], in_=src_bf[:, 128 * j:128 * j + 128], identity=identb[:]),
                 reads=[srcB, B_cst], writes=[B_ps[ps_i]])
        evac(YT[:, yt_blk0:yt_blk0 + nblk, tokcol0:tokcol0 + 128], psb(ps_i)[:, 0:128 * nblk].rearrange("p (j t) -> p j t", j=nblk), [B_ps[ps_i]], [B_YT])

    def mem_phase(sq, part):
        t0 = part * T
        evac_mode[0] = "alt"
        cv = Carver()
        QmT = cv.bf16(2 * T).rearrange("p (d t) -> p d t", d=2); B_Qm = Buf()
        Gm = cv.f32(NT * 256).rearrange("p (c f) -> p c f", c=NT); B_Gm = Buf()
        PmT = [cv.bf16(2 * 512).rearrange("p (m t) -> p m t", m=2) for _ in range(2)]; B_Pm = [Buf(), Buf()]
        rr = cv.f32(4); B_rr = Buf()
        ym = [cv.bf16(256), cv.bf16(256)]; B_ym = [Buf(), Buf()]
        it = 0
        for hh in range(4):
            wq, wqB = load_w(win(OFF_MQ + 256 * hh, 256), 8, 256)
            wg, wgB = load_w(win(OFF_MG + 256 * hh, 256), 8, 256)
            for db in range(2):
                for tc in range(T // 512):
                    pi = (2 * db + tc) % 2
                    proj_fm(pi, 0, wq, wqB, 128 * db, 128, t0 + 512 * tc, 512)
                    evac(QmT[:, db, 512 * tc:512 * tc + 512], psum[pi][:, :], [B_ps[pi]], [B_Qm])
            for c in range(NT):
                pi = 2 + c % 2
                proj_tm(pi, 0, wg, wgB, 0, 256, t0 + 128 * c)
                P.op("act", "activation", A(out=Gm[:, c, :], in_=psum[pi][:, 0:256], func=AF.Silu), reads=[B_ps[pi]], writes=[B_Gm])
            for tc in range(T // 512):
                pb = tc % 2
                for mt in range(2):
                    pi = mt
                    for db in range(2):
                        P.op("pe", "matmul", A(psum[pi][:, :], lhsT=kmT[:, 2 * hh + db, 128 * mt:128 * mt + 128], rhs=QmT[:, db, 512 * tc:512 * tc + 512], start=(db == 0), stop=(db == 1)),
                             reads=[B_kmT, B_Qm], writes=[B_ps[pi]])
                    P.op("act", "activation", A(out=PmT[pb][:, mt, :], in_=psum[pi][:, :], func=AF.Exp, scale=1.0 / 16.0), reads=[B_ps[pi]], writes=[B_Pm[pb]])
                for j in range(4):
                    pi = 2 + j % 2
                    for mt in range(2):
                        P.op("pe", "matmul", A(psum[pi][:, 0:257], lhsT=PmT[pb][:, mt, 128 * j:128 * j + 128], rhs=vma[:, mt, hh, 0:257], start=(mt == 0), stop=(mt == 1)),
                             reads=[B_Pm[pb], B_vma], writes=[B_ps[pi]])
                    yi = it % 2; it += 1
                    P.op("dve", "reciprocal", A(out=rr[:, 0:1], in_=psum[pi][:, 256:257]), reads=[B_ps[pi]], writes=[B_rr])
                    P.op("dve", "scalar_tensor_tensor", A(out=ym[yi], in0=psum[pi][:, 0:256], scalar=rr[:, 0:1], in1=Gm[:, 4 * tc + j, :], op0=ALU.mult, op1=ALU.mult),
                         reads=[B_ps[pi], B_rr, B_Gm], writes=[B_ym[yi]])
                    transpose_to_YT(ym[yi], B_ym[yi], 2, 2 * hh, 512 * tc + 128 * j, 6)

    def diff_phase(sq, part):
        t0 = 0
        TQ = S
        NTQ = S // 128
        evac_mode[0] = "dve"
        nk = S
        cv = Carver()
        KT = cv.bf16(S); B_KT = Buf()
        QTc = [cv.bf16(TQ), cv.bf16(TQ)]; B_QT = Buf()
        P.op("pool", "memset", A(QTc[0][64:128, :], 0.0), writes=[B_QT])
        P.op("pool", "memset", A(QTc[1][0:64, :], 0.0), writes=[B_QT])
        Va = cv.bf16(16 * 130).rearrange("p (t v) -> p t v", t=16); B_Va = Buf()
        Gs = cv.bf16(NTQ * 128).rearrange("p (c f) -> p c f", c=NTQ); B_Gs = Buf()
        QW = 256; NQB = QW // 128; NS = 4; SB = (0, 1, 4, 5)
        tmpS = [cv.f32(256) for _ in range(NS)]; B_tmpS = [Buf() for _ in range(NS)]
        PT = [cv.bf16(QW) for _ in range(NS)]; B_PT = [Buf() for _ in range(NS)]
        Os = [cv.f32(NQB * 129).rearrange("p (j v) -> p j v", j=NQB) for _ in range(2)]; B_Os = [Buf(), Buf()]
        obuf = cv.f32(NTQ * 128).rearrange("p (c f) -> p c f", c=NTQ); B_ob = Buf()
        t1 = cv.f32(NQB * 128).rearrange("p (j v) -> p j v", j=NQB); t2 = cv.f32(NQB * 128).rearrange("p (j v) -> p j v", j=NQB); B_t = Buf()
        rr = cv.f32(8); B_rr = Buf()
        ss = cv.f32(NTQ); B_ss = Buf()
        junk = cv.f32(128); B_junk = Buf()
        ydt = cv.bf16(NTQ * 128).rearrange("p (c f) -> p c f", c=NTQ); B_ydt = Buf()
        biasT = cv.f32(8 * 2 * 128).rearrange("p (h d q) -> p h d q", h=8, d=2); B_biasT = Buf()
        for h in range(8):
            P.dma("sp", "dma_start", A(out=biasT[:, h, :, :], in_=bias_scr.ap()[h, :].rearrange("(d k q) -> k d q", d=2, k=128)), writes=[B_biasT])
        if sq == 0 and part == 0:
            print("diff arena words", cv.off)
        P.op("pool", "memset", A(Va[:, :, 128:130], 1.0), writes=[B_Va])
        sidx = 0
        for h in range(8):
            wq, wqB = load_w(win(OFF_DQ + 128 * h, 128), 8, 128)
            wk, wkB = load_w(win(OFF_DK + 128 * h, 128), 8, 128)
            wv, wvB = load_w(win(OFF_DV + 128 * h, 128), 8, 128)
            wg, wgB = load_w(win(OFF_DG + 128 * h, 128), 8, 128)
            for kc in range(nk // 512):
                pi = kc % 2
                proj_fm(pi, 0, wk, wkB, 0, 128, 512 * kc, 512)
                evac(KT[:, 512 * kc:512 * kc + 512], psum[pi][:, :], [B_ps[pi]], [B_KT])
            for tc in range(TQ // 512):
                pi = tc % 2
                proj_fm(pi, 0, wq, wqB, 0, 128, t0 + 512 * tc, 512)
                evac(QTc[0][0:64, 512 * tc:512 * tc + 512], psum[pi][0:64, :], [B_ps[pi]], [B_QT])
                evac(QTc[1][64:128, 512 * tc:512 * tc + 512], psum[pi][64:128, :], [B_ps[pi]], [B_QT])
            for kt in range(nk // 128):
                pi = kt % 2
                proj_tm(pi, 0, wv, wvB, 0, 128, 128 * kt)
                evac(Va[:, kt, 0:128], psum[pi][:, 0:128], [B_ps[pi]], [B_Va])
            for c in range(NTQ):
                pi = c % 2
                proj_tm(pi, 0, wg, wgB, 0, 128, t0 + 128 * c)
                P.op("act", "activation", A(out=Gs[:, c, :], in_=psum[pi][:, 0:128], func=AF.Silu), reads=[B_ps[pi]], writes=[B_Gs])
            for qc in range(TQ // QW):
                qb0 = (t0 + QW * qc) // 128
                for c in range(2):
                    r0 = 64 * c
                    nkb_ = qb0 + NQB
                    sis = []
                    for kk in range(nkb_):
                        sis.append(sidx % NS); sidx += 1

                    def emit_S(kb_):
                        j0 = kb_ - qb0
                        jlo = max(j0, 0)
                        si = sis[kb_]
                        P.op("pe", "matmul", A(
                            psum[SB[si]][:, 128 * jlo:QW], lhsT=KT[:, 128 * kb_:128 * kb_ + 128],
                            rhs=QTc[c][:, QW * qc + 128 * jlo:QW * qc + QW], start=True, stop=True),
                             reads=[B_KT, B_QT], writes=[B_ps[SB[si]]])

                    for kk in range(min(NS - 1, nkb_)):
                        emit_S(kk)
                    for kb_ in range(nkb_):
                        if kb_ + NS - 1 < nkb_:
                            emit_S(kb_ + NS - 1)
                        j0 = kb_ - qb0
                        jlo = max(j0, 0)
                        si = sis[kb_]
                        Sps = psum[SB[si]]
                        nsp = 0
                        for j in range(jlo, NQB):
                            dl = j - j0
                            if dl <= 1:
                                P.op("dve", "scalar_tensor_tensor", A(
                                    out=tmpS[si][:, 128 * nsp:128 * nsp + 128], in0=Sps[:, 128 * j:128 * j + 128], scalar=0.125,
                                    in1=biasT[:, h, dl, :], op0=ALU.mult, op1=ALU.add),
                                     reads=[B_ps[SB[si]], B_biasT], writes=[B_tmpS[si]])
                                nsp += 1
                        if nsp:
                            P.op("act", "activation", A(out=PT[si][:, 128 * jlo:128 * (jlo + nsp)], in_=tmpS[si][:, 0:128 * nsp], func=AF.Exp),
                                 reads=[B_tmpS[si]], writes=[B_PT[si]])
                        jf = max(j0 + 2, 0)
                        if jf < NQB:
                            P.op("act", "activation", A(out=PT[si][:, 128 * jf:QW], in_=Sps[:, 128 * jf:QW], func=AF.Exp, scale=0.125, bias=sm[:, SM_CFAR + h:SM_CFAR + h + 1]),
                                 reads=[B_ps[SB[si]], B_sm], writes=[B_PT[si]])
                        for j in range(jlo, NQB):
                            P.op("pe", "matmul", A(psum[2 + j][:, 0:129], lhsT=PT[si][:, 128 * j:128 * j + 128], rhs=Va[:, kb_, 0:129],
                                                   start=(kb_ == 0), stop=(kb_ == qb0 + j)),
                                 reads=[B_PT[si], B_Va], writes=[B_ps[2 + j]])
                    for j in range(NQB):
                        evac(Os[c][:, j, :], psum[2 + j][:, 0:129], [B_ps[2 + j]], [B_Os[c]])
                P.op("dve", "reciprocal", A(out=rr[:, 0:NQB], in_=Os[0][:, :, 128]), reads=[B_Os[0]], writes=[B_rr])
                P.op("dve", "reciprocal", A(out=rr[:, 4:4 + NQB], in_=Os[1][:, :, 128]), reads=[B_Os[1], B_rr], writes=[B_rr])
                P.op("dve", "tensor_scalar", A(out=rr[:, 4:4 + NQB], in0=rr[:, 4:4 + NQB], scalar1=sm[:, SM_NLAM:SM_NLAM + 1], scalar2=None, op0=ALU.mult), reads=[B_rr, B_sm], writes=[B_rr])
                P.op("dve", "tensor_tensor", A(out=t1, in0=Os[0][:, :, 0:128], in1=rr[:, 0:NQB].unsqueeze(2).broadcast_to([128, NQB, 128]), op=ALU.mult), reads=[B_Os[0], B_rr, B_t], writes=[B_t])
                P.op("dve", "tensor_tensor", A(out=t2, in0=Os[1][:, :, 0:128], in1=rr[:, 4:4 + NQB].unsqueeze(2).broadcast_to([128, NQB, 128]), op=ALU.mult), reads=[B_Os[1], B_rr, B_t], writes=[B_t])
                P.op("dve", "tensor_tensor", A(out=obuf[:, NQB * qc:NQB * qc + NQB, :], in0=t1, in1=t2, op=ALU.add), reads=[B_t], writes=[B_ob])
            for c in range(NTQ):
                P.op("act", "activation", A(out=junk, in_=obuf[:, c, :], func=AF.Square, accum_out=ss[:, c:c + 1]), reads=[B_ob], writes=[B_junk, B_ss])
            rstd_from_ss(ss[:, 0:NTQ], NTQ, 1.0 / 128, [B_ss])
            P.op("dve", "tensor_tensor", A(out=obuf, in0=obuf, in1=ss[:, 0:NTQ].unsqueeze(2).broadcast_to([128, NTQ, 128]), op=ALU.mult), reads=[B_ob, B_ss], writes=[B_ob])
            P.op("dve", "tensor_tensor", A(out=obuf, in0=obuf, in1=sm[:, SM_SUBG:SM_SUBG + 128].unsqueeze(1).broadcast_to([128, NTQ, 128]), op=ALU.mult), reads=[B_ob, B_sm], writes=[B_ob])
            P.op("dve", "tensor_tensor", A(out=ydt, in0=obuf, in1=Gs, op=ALU.mult), reads=[B_ob, B_Gs], writes=[B_ydt])
            for c in range(NTQ):
                transpose_to_YT(ydt[:, c, :], B_ydt, 1, h + 8 * (c // NT), 128 * (c % NT), 6 + c % 2)
        P.barrier()

    def ssm_phase(sq, part):
        t0 = part * T
        evac_mode[0] = _CFG.get("ssm_evac", "act")
        cv = Carver()
        Us = [cv.bf16(T + 4), cv.bf16(T + 4)]; B_Us = [Buf(), Buf()]
        dgs = [cv.bf16(4 * 128).rearrange("p (k c) -> p k c", k=4) for _ in range(2)]; B_dgs = [Buf(), Buf()]
        xsT = cv.bf16(T); B_xsT = Buf()
        xs_tok2 = [cv.bf16(NT * 256).rearrange("p (c f) -> p c f", c=NT) for _ in range(2)]; B_xs2 = [Buf(), Buf()]
        BT2 = [cv.bf16(T), cv.bf16(T)]; B_BT2 = [Buf(), Buf()]
        Btok2 = [cv.bf16(NT * 128).rearrange("p (c f) -> p c f", c=NT) for _ in range(2)]; B_Btok2 = [Buf(), Buf()]
        CT2 = [cv.bf16(T), cv.bf16(T)]; B_CT2 = [Buf(), Buf()]
        zs2 = [cv.bf16(NT * 256).rearrange("p (c f) -> p c f", c=NT) for _ in range(2)]; B_zs2 = [Buf(), Buf()]
        dskI2 = [cv.bf16(4 * 128).rearrange("p (h l) -> p h l", h=4) for _ in range(2)]; B_dskI2 = [Buf(), Buf()]
        sg_b2 = [cv.f32(256), cv.f32(256)]; B_sg2 = [Buf(), Buf()]
        dt = cv.f32(NT * 32).rearrange("p (c f) -> p c f", c=NT)
        adt = cv.f32(NT * 32).rearrange("p (c f) -> p c f", c=NT); B_dt = Buf()
        E = cv.f32(NT * 96).rearrange("p (c f) -> p c f", c=NT); B_E = Buf()
        R = [cv.bf16(512), cv.bf16(512)]; B_R = [Buf(), Buf()]
        LT = [cv.bf16(512), cv.bf16(512)]; B_LT = [Buf(), Buf()]
        MT = [cv.bf16(512), cv.bf16(512)]; B_MT = [Buf(), Buf()]
        xdt = [cv.bf16(256), cv.bf16(256)]; xdd = [cv.bf16(256), cv.bf16(256)]; B_xd = [Buf(), Buf()]; B_xdd = [Buf(), Buf()]
        y1 = cv.f32(256); B_y1 = Buf()
        y2 = cv.f32(NT * 256).rearrange("p (c f) -> p c f", c=NT); B_y2c = [Buf() for _ in range(NT)]; B_y2 = B_y2c
        yn = cv.bf16(NT * 256).rearrange("p (c f) -> p c f", c=NT); B_ync = [Buf() for _ in range(NT)]
        st_b = cv.bf16(256); B_stb = Buf()
        stmp = cv.f32(256); B_stmp = Buf()
        ss = cv.f32(NT); B_ss = Buf()
        junk = cv.bf16(256); B_junk = Buf()
        if sq == 0 and part == 0:
            print("ssm arena words", cv.off)

        wd, wdB = load_w(win(OFF_DT, 32), 8, 32)
        for c in range(NT):
            pi = c % 2
            proj_tm(pi, 0, wd, wdB, 0, 32, t0 + 128 * c)
            P.op("dve", "tensor_tensor", A(out=dt[:, c, :], in0=psum[pi][:, 0:32], in1=sm[:, SM_DTB:SM_DTB + 32], op=ALU.add), reads=[B_ps[pi], B_sm], writes=[B_dt])
        dtf = dt.rearrange("p c f -> p (c f)")
        P.op("act", "activation", A(out=dtf, in_=dtf, func=AF.Exp), reads=[B_dt], writes=[B_dt])
        P.op("act", "activation", A(out=dtf, in_=dtf, func=AF.Ln, bias=sm[:, SM_ONE:SM_ONE + 1]), reads=[B_dt, B_sm], writes=[B_dt])
        P.op("dve", "tensor_tensor", A(out=adt, in0=dt, in1=sm[:, SM_A:SM_A + 32].unsqueeze(1).broadcast_to([128, NT, 32]), op=ALU.mult), reads=[B_dt, B_sm], writes=[B_dt])
        for c in range(NT):
            pi = c % 2
            for i, cc in enumerate((C_TRIL, C_SUP, C_ONES)):
                P.op("pe", "matmul", A(psum[pi][:, 32 * i:32 * i + 32], lhsT=cst[:, cc:cc + 128], rhs=adt[:, c, :], start=True, stop=True),
                     reads=[B_cst, B_dt], writes=[B_ps[pi]])
            P.op("act", "activation", A(out=E[:, c, :], in_=psum[pi][:, 0:96], func=AF.Exp), reads=[B_ps[pi]], writes=[B_E])

        def prep(g):
            q = g % 2
            xs_tok, BT, Btok, CT, zs, dskI, sg_b = xs_tok2[q], BT2[q], Btok2[q], CT2[q], zs2[q], dskI2[q], sg_b2[q]
            B_xs, B_BT, B_Btok, B_CT, B_zs, B_dskI, B_sg = B_xs2[q], B_BT2[q], B_Btok2[q], B_CT2[q], B_zs2[q], B_dskI2[q], B_sg2[q]
            if part == 0:
                P.op("pool", "memset", A(state_f[:, g, :], 0.0), writes=[B_state[g]])
            for hh in range(4):
                P.op("pool", "tensor_scalar", A(out=dskI[:, hh, :], in0=cst[:, C_ID:C_ID + 128], scalar1=sm[:, SM_DSK + 4 * g + hh:SM_DSK + 4 * g + hh + 1], scalar2=None, op0=ALU.mult),
                     reads=[B_cst, B_sm], writes=[B_dskI])
            blocks = [("xs", 2 * g, OFF_XBC + 256 * g), ("xs", 2 * g + 1, OFF_XBC + 256 * g + 128),
                      ("B", 16 + g, OFF_XBC + 2048 + 128 * g), ("C", 24 + g, OFF_XBC + 3072 + 128 * g)]
            for bi_, (kind, blk, col) in enumerate(blocks):
                wt, wB = load_w(win(col, 128), 8, 128)
                U = Us[bi_ % 2]; B_U = B_Us[bi_ % 2]
                dg = dgs[bi_ % 2]; B_dg = B_dgs[bi_ % 2]
                for k in range(4):
                    P.op("dve", "tensor_scalar", A(out=dg[:, k, :], in0=cst[:, C_ID:C_ID + 128], scalar1=cw[:, blk, k:k + 1], scalar2=None, op0=ALU.mult),
                         reads=[B_cst, B_cw], writes=[B_dg])
                if part == 0:
                    P.op("pool", "memset", A(U[:, 0:4], 0.0), writes=[B_U])
                else:
                    P.op("pool", "tensor_copy", A(out=U[:, 0:4], in_=halo[:, blk, :]), reads=[B_halo], writes=[B_U])
                for tc in range(T // 512):
                    pi = 6 + tc % 2
                    proj_fm(pi, 0, wt, wB, 0, 128, t0 + 512 * tc, 512)
                    evac(U[:, 4 + 512 * tc:4 + 512 * tc + 512], psum[pi][:, :], [B_ps[pi]], [B_U])
                    yield
                P.op("pool", "tensor_copy", A(out=halo[:, blk, :], in_=U[:, T:T + 4]), reads=[B_U], writes=[B_halo])
                dst = {"C": (CT, B_CT), "B": (BT, B_BT), "xs": (xsT, B_xsT)}[kind]
                for tc in range(T // 512):
                    pi = 6 + tc % 2
                    for k in range(4):
                        P.op("pe", "matmul", A(psum[pi][:, :], lhsT=dg[:, k, :], rhs=U[:, 1 + k + 512 * tc:1 + k + 512 * tc + 512], start=(k == 0), stop=(k == 3)),
                             reads=[B_dg, B_U], writes=[B_ps[pi]])
                    P.op("act", "activation", A(out=dst[0][:, 512 * tc:512 * tc + 512], in_=psum[pi][:, :], func=AF.Silu, bias=cb[:, blk:blk + 1]),
                         reads=[B_ps[pi], B_cw], writes=[dst[1]])
                    yield
                if kind == "B":
                    for c in range(NT):
                        P.op("pe", "transpose", A(out=psb(6)[:, 128 * c:128 * c + 128], in_=BT[:, 128 * c:128 * c + 128], identity=identb[:]),
                             reads=[B_BT, B_cst], writes=[B_ps[6]])
                    evac(Btok, psb(6)[:, 0:128 * NT].rearrange("p (c f) -> p c f", c=NT), [B_ps[6]], [B_Btok])
                elif kind == "xs":
                    jx = bi_
                    for c in range(NT):
                        P.op("pe", "transpose", A(out=psb(7)[:, 128 * c:128 * c + 128], in_=xsT[:, 128 * c:128 * c + 128], identity=identb[:]),
                             reads=[B_xsT, B_cst], writes=[B_ps[7]])
                    evac(xs_tok[:, :, 128 * jx:128 * jx + 128], psb(7)[:, 0:128 * NT].rearrange("p (c f) -> p c f", c=NT), [B_ps[7]], [B_xs])
                yield
            wz, wzB = load_w(win(OFF_Z + 256 * g, 256), 8, 256)
            for c in range(NT):
                pi = 6 + c % 2
                proj_tm(pi, 0, wz, wzB, 0, 256, t0 + 128 * c)
                P.op("act", "activation", A(out=zs[:, c, :], in_=psum[pi][:, 0:256], func=AF.Silu), reads=[B_ps[pi]], writes=[B_zs])
                if c % 2:
                    yield

        def scan(g):
            q = g % 2
            xs_tok, BT, Btok, CT, zs, dskI, sg_b = xs_tok2[q], BT2[q], Btok2[q], CT2[q], zs2[q], dskI2[q], sg_b2[q]
            B_xs, B_BT, B_Btok, B_CT, B_zs, B_dskI, B_sg = B_xs2[q], B_BT2[q], B_Btok2[q], B_CT2[q], B_zs2[q], B_dskI2[q], B_sg2[q]
            P.op("act", "copy", A(out=st_b, in_=state_f[:, g, :]), reads=[B_state[g]], writes=[B_stb])
            CBb = (1, 5)

            def S1(c):
                i2 = c % 2
                tk = slice(128 * c, 128 * c + 128)
                ag = adt[:, c, 4 * g:4 * g + 4]
                P.op("dve", "tensor_tensor", A(out=R[i2].rearrange("p (h l) -> p h l", h=4), in0=cst[:, C_TRIL:C_TRIL + 128].unsqueeze(1).broadcast_to([128, 4, 128]),
                                               in1=ag.unsqueeze(2).broadcast_to([128, 4, 128]), op=ALU.mult),
                     reads=[B_cst, B_dt], writes=[B_R[i2]])
                P.op("pe", "matmul", A(psum[0][:, :], lhsT=cstb[:, 0:128], rhs=R[i2], start=True, stop=False), reads=[B_cst, B_R[i2]], writes=[B_ps[0]])
                P.op("pe", "matmul", A(psum[0][:, :], lhsT=cstb[:, 128:256], rhs=cstb[:, 256:768], start=False, stop=True), reads=[B_cst], writes=[B_ps[0]])
                P.op("act", "activation", A(out=LT[i2], in_=psum[0][:, :], func=AF.Exp), reads=[B_ps[0]], writes=[B_LT[i2]])
                P.op("pe", "matmul", A(psum[CBb[i2]][:, 0:128], lhsT=BT[:, tk], rhs=CT[:, tk], start=True, stop=True), reads=[B_BT, B_CT], writes=[B_ps[CBb[i2]]])

            def S2(c):
                i2 = c % 2
                P.op("dve", "tensor_tensor", A(out=xdt[i2].rearrange("p (h q) -> p h q", h=4), in0=xs_tok[:, c, :].rearrange("p (h q) -> p h q", h=4),
                                               in1=dt[:, c, 4 * g:4 * g + 4].unsqueeze(2).broadcast_to([128, 4, 64]), op=ALU.mult),
                     reads=[B_xs, B_dt], writes=[B_xd[i2]])
                P.op("dve", "tensor_tensor", A(out=xdd[i2].rearrange("p (h q) -> p h q", h=4), in0=xdt[i2].rearrange("p (h q) -> p h q", h=4),
                                               in1=E[:, c, 32 + 4 * g:32 + 4 * g + 4].unsqueeze(2).broadcast_to([128, 4, 64]), op=ALU.mult),
                     reads=[B_xd[i2], B_E], writes=[B_xdd[i2]])
                P.op("dve", "tensor_tensor", A(out=MT[i2].rearrange("p (h l) -> p h l", h=4), in0=LT[i2].rearrange("p (h l) -> p h l", h=4),
                                               in1=psum[CBb[i2]][:, 0:128].unsqueeze(1).broadcast_to([128, 4, 128]), op=ALU.mult),
                     reads=[B_LT[i2], B_ps[CBb[i2]]], writes=[B_MT[i2]])

            def S3(c):
                i2 = c % 2
                tk = slice(128 * c, 128 * c + 128)
                for hh in range(4):
                    P.op("pe", "matmul", A(psum[2][:, 64 * hh:64 * hh + 64], lhsT=MT[i2][:, 128 * hh:128 * hh + 128], rhs=xdt[i2][:, 64 * hh:64 * hh + 64], start=True, stop=False),
                         reads=[B_MT[i2], B_xd[i2]], writes=[B_ps[2]])
                    P.op("pe", "matmul", A(psum[2][:, 64 * hh:64 * hh + 64], lhsT=dskI[:, hh, :], rhs=xs_tok[:, c, 64 * hh:64 * hh + 64], start=False, stop=True),
                         reads=[B_dskI, B_xs], writes=[B_ps[2]])
                P.op("pe", "matmul", A(psum[4][:, 0:256], lhsT=Btok[:, c, :], rhs=xdd[i2], start=True, stop=True), reads=[B_Btok, B_xdd[i2]], writes=[B_ps[4]])
                P.op("pe", "matmul", A(psum[3][:, 0:256], lhsT=CT[:, tk], rhs=st_b, start=True, stop=True), reads=[B_CT, B_stb], writes=[B_ps[3]])
                P.op("dve", "tensor_tensor", A(out=y1.rearrange("p (h q) -> p h q", h=4), in0=psum[3][:, 0:256].rearrange("p (h q) -> p h q", h=4),
                                               in1=E[:, c, 4 * g:4 * g + 4].unsqueeze(2).broadcast_to([128, 4, 64]), op=ALU.mult),
                     reads=[B_ps[3], B_E], writes=[B_y1])
                P.op("dve", "tensor_tensor", A(out=stmp.rearrange("p (h q) -> p h q", h=4), in0=state_f[:, g, :].rearrange("p (h q) -> p h q", h=4),
                                               in1=E[:, c, 64 + 4 * g:64 + 4 * g + 4].unsqueeze(2).broadcast_to([128, 4, 64]), op=ALU.mult),
                     reads=[B_state[g], B_E], writes=[B_stmp])
                P.op("dve", "tensor_tensor", A(out=state_f[:, g, :], in0=stmp, in1=psum[4][:, 0:256], op=ALU.add), reads=[B_stmp, B_ps[4]], writes=[B_state[g]])
                P.op("act", "copy", A(out=st_b, in_=state_f[:, g, :]), reads=[B_state[g]], writes=[B_stb])
                P.op("dve", "tensor_tensor", A(out=y2[:, c, :], in0=y1, in1=psum[2][:, 0:256], op=ALU.add), reads=[B_y1, B_ps[2]], writes=[B_y2c[c]])

            S1(0)
            if NT > 1:
                S1(1)
            S2(0)
            for c in range(NT):
                if c + 2 < NT:
                    S1(c + 2)
                if c + 1 < NT:
                    S2(c + 1)
                S3(c)
                yield
            P.op("dve", "tensor_tensor", A(out=y2, in0=y2, in1=zs, op=ALU.mult), reads=B_y2 + [B_zs], writes=B_y2)
            for c in range(NT):
                P.op("act", "activation", A(out=junk, in_=y2[:, c, :], func=AF.Square, accum_out=ss[:, c:c + 1]), reads=B_y2, writes=[B_junk, B_ss])
            rstd_from_ss(ss[:, 0:NT], NT, 1.0 / 256, [B_ss])
            yield
            for c in range(NT):
                P.op("act", "activation", A(out=yn[:, c, :], in_=y2[:, c, :], func=AF.Copy, scale=ss[:, c:c + 1]), reads=[B_y2c[c], B_ss], writes=[B_ync[c]])
            yield
            for c in range(NT):
                transpose_to_YT(yn[:, c, :], B_ync[c], 2, 2 * g, 128 * c, c % 2)
                if c % 2:
                    yield

        def run_interleaved(gens):
            gens = [g_ for g_ in gens if g_ is not None]
            while gens:
                for g_ in list(gens):
                    try:
                        next(g_)
                    except StopIteration:
                        gens.remove(g_)

        run_interleaved([prep(0)])
        for g in range(8):
            run_interleaved([scan(g), prep(g + 1) if g + 1 < 8 else None])
        P.barrier()

    def merge_phase(sq, part, bi, wbr_d, nkb, first, yb0=0):
        t0 = part * T
        it = 0
        for ob in range(8):
            wb_, wbB = load_w(wbr_d.ap()[:, 128 * ob:128 * ob + 128], nkb, 128)
            wg, wgB = load_w(win(OFF_GATE + 1024 * bi + 128 * ob, 128), 8, 128)
            for tc in range(T // 512):
                pa, pg = 2 * (it % 2), 2 * (it % 2) + 1
                it += 1
                for kb in range(nkb):
                    P.op("pe", "matmul", A(psum[pa][:, :], lhsT=wb_[:, kb, :], rhs=YT[:, yb0 + kb, 512 * tc:512 * tc + 512], start=(kb == 0), stop=(kb == nkb - 1)),
                         reads=[wbB, B_YT], writes=[B_ps[pa]])
                proj_fm(pg, 0, wg, wgB, 0, 128, t0 + 512 * tc, 512)
                P.op("act", "activation", A(out=msig[:], in_=psum[pg][:, :], func=AF.Sigmoid), reads=[B_ps[pg]], writes=[B_msig])
                dst = mrg[:, ob, 512 * tc:512 * tc + 512]
                if first:
                    P.op("dve", "tensor_tensor", A(out=dst, in0=msig[:], in1=psum[pa][:, :], op=ALU.mult), reads=[B_msig, B_ps[pa]], writes=[B_mrg])
                else:
                    P.op("dve", "tensor_tensor", A(out=mtmp[:], in0=msig[:], in1=psum[pa][:, :], op=ALU.mult), reads=[B_msig, B_ps[pa]], writes=[B_mtmp])
                    P.op("dve", "tensor_tensor", A(out=dst, in0=dst, in1=mtmp[:], op=ALU.add), reads=[B_mtmp, B_mrg], writes=[B_mrg])

    def out_phase(sq, part):
        t0 = part * T
        cv = Carver(11000)
        fg_b = cv.f32(1024); B_fg = Buf()
        xt = [cv.f32(1024), cv.f32(1024)]; B_xt = [Buf(), Buf()]
        r = [cv.f32(1024), cv.f32(1024)]; B_r = [Buf(), Buf()]
        junk = cv.f32(1024); B_junk = Buf()
        ss = cv.f32(2); B_ss = Buf()
        bcast_load(fg_b, fgain_d.ap(), [B_fg])
        wts = [load_w(w_out.ap()[:, 256 * i:256 * i + 256], 8, 256) for i in range(4)]
        for c in range(NT):
            xi = c % 2
            rows = slice(t0 + 128 * c, t0 + 128 * c + 128)
            P.dma("sp", "dma_start", A(out=xt[xi], in_=x_d.ap()[sq, rows, :]), writes=[B_xt[xi]])
            for hf in range(2):
                pi = 2 * xi + hf
                for q4 in range(2):
                    wt, wB = wts[2 * hf + q4]
                    for kb in range(8):
                        P.op("pe", "matmul", A(psum[pi][:, 256 * q4:256 * q4 + 256], lhsT=mrg[:, kb, 128 * c:128 * c + 128], rhs=wt[:, kb, :], start=(kb == 0), stop=(kb == 7)),
                             reads=[wB, B_mrg], writes=[B_ps[pi]])
                P.op("dve", "tensor_tensor", A(out=r[xi][:, 512 * hf:512 * hf + 512], in0=xt[xi][:, 512 * hf:512 * hf + 512], in1=psum[pi][:, :], op=ALU.add),
                     reads=[B_xt[xi], B_ps[pi]], writes=[B_r[xi]])
            P.op("act", "activation", A(out=junk, in_=r[xi], func=AF.Square, accum_out=ss[:, 0:1]), reads=[B_r[xi]], writes=[B_junk, B_ss])
            rstd_from_ss(ss[:, 0:1], 1, 1.0 / D, [B_ss])
            P.op("dve", "scalar_tensor_tensor", A(out=r[xi], in0=r[xi], scalar=ss[:, 0:1], in1=fg_b, op0=ALU.mult, op1=ALU.mult), reads=[B_r[xi], B_ss, B_fg], writes=[B_r[xi]])
            P.dma("sp", "dma_start", A(out=out_d.ap()[sq, rows, :], in_=r[xi]), reads=[B_r[xi]])

    for sq in range(nseq):
        prologue(sq)
        for part in range(NPART):
            first = True
            if part == 1 and "diff" in branches:
                merge_phase(sq, part, 1, w_brd, 8, True, yb0=8)
                first = False
            if "ssm" in branches:
                ssm_phase(sq, part)
                merge_phase(sq, part, 0, w_brs, 16, first)
                first = False
            if part == 0 and "diff" in branches:
                diff_phase(sq, part)
                merge_phase(sq, part, 1, w_brd, 8, first, yb0=0)
                first = False
            if "mem" in branches:
                mem_phase(sq, part)
                merge_phase(sq, part, 2, w_brm, 8, first)
                first = False
            out_phase(sq, part)
            if part < NPART - 1:
                P.barrier()
    P.barrier()
    P.finish()
    P.emit()
    return nc, P


_CACHE = {}


def kernel(**inputs):
    ncores = _CFG["ncores"]; nseq = _CFG["nseq"]
    key = (nseq, tuple(_CFG["branches"]), _CFG["same_engine_sync"], _CFG["schedule"], _CFG.get("ssm_evac"))
    if key not in _CACHE:
        _CACHE[key] = build(nseq, _CFG["branches"], _CFG["same_engine_sync"], _CFG["schedule"])
    nc, P = _CACHE[key]
    f = lambda a: np.ascontiguousarray(np.asarray(a, dtype=np.float32))
    cst, oh = host_consts()
    shared = {
        "w_in": f(inputs["w_in"][0]), "w_mem_kv": f(inputs["w_mem_kv"][0]), "w_br_ssm": f(inputs["w_br_ssm"][0]),
        "w_br_diff": f(inputs["w_br_diff"][0]), "w_br_mem": f(inputs["w_br_mem"][0]), "w_out": f(inputs["w_out"][0]),
        "norm_gain": f(inputs["norm_gain"]).reshape(1, D), "mem_norm_gain": f(inputs["mem_norm_gain"]).reshape(1, D),
        "final_norm_gain": f(inputs["final_norm_gain"]).reshape(1, D), "ssm_norm_gain": f(inputs["ssm_norm_gain"]).reshape(1, 2048),
        "subln_gain": f(inputs["subln_gain"]).reshape(1, 128), "dt_bias": f(inputs["dt_bias"]).reshape(1, 32),
        "a_log": f(inputs["a_log"]).reshape(1, 32), "d_skip": f(inputs["d_skip"]).reshape(1, 32),
        "lambda_q1": f(inputs["lambda_q1"]).reshape(1, 64), "lambda_k1": f(inputs["lambda_k1"]).reshape(1, 64),
        "lambda_q2": f(inputs["lambda_q2"]).reshape(1, 64), "lambda_k2": f(inputs["lambda_k2"]).reshape(1, 64),
        "rel_bias": f(inputs["rel_bias"]),
        "conv_w_l": f(np.asarray(inputs["conv_w"][0]).reshape(4, 32, 128).transpose(2, 1, 0)),
        "conv_b_l": f(np.asarray(inputs["conv_b"][0]).reshape(32, 128).transpose(1, 0)),
        "sgain_l": f(np.asarray(inputs["ssm_norm_gain"][0]).reshape(16, 128).transpose(1, 0)),
        "cst": cst, "onehot": oh,
    }
    x = np.asarray(inputs["x"]); mem = np.asarray(inputs["mem"])
    in_maps = []
    for c in range(ncores):
        m = dict(shared)
        m["x"] = f(x[c * nseq:(c + 1) * nseq])
        m["mem"] = f(mem[c * nseq:(c + 1) * nseq])
        in_maps.append(m)
    if _CFG.get("trace"):
        res = run_bass_kernel_spmd(nc, in_maps, core_ids=list(range(ncores)), trace=True)
        print("EXEC_TIME_NS", res.exec_time_ns)
    else:
        res = run_bass_kernel_spmd(nc, in_maps, core_ids=list(range(ncores)))
    return np.concatenate([np.asarray(r["out"]) for r in res.results], axis=0).astype(np.float32)
```
